# Optimizing a Trainium2 kernel written in Bass

```python
import math
import jax, jax.numpy as jnp
from jax import lax
import numpy as np

D_MODEL = 1024
BATCH = 8
SEQ = 2048
DEPTH = 4
DEC_BATCH = 128
DEC_SEQ = 4
PAST_LEN = 16384
PAGE_SIZE = 128

N_MIXERS = 3
N_RET = (DEPTH + 2) // 3
N_SSD = (DEPTH + 1) // 3
N_GDN = DEPTH // 3
RET_HEADS = 4
RET_DK = D_MODEL // RET_HEADS
RET_DV = 2 * RET_DK
RET_QK = RET_HEADS * RET_DK
RET_V = RET_HEADS * RET_DV
RET_IN = 2 * RET_QK + 2 * RET_V
ROPE_BASE = 10000.0
SSD_INNER = 2 * D_MODEL
SSD_HEADDIM = 64
SSD_HEADS = SSD_INNER // SSD_HEADDIM
SSD_GROUPS = 8
SSD_STATE = 128
SSD_CONV_DIM = SSD_INNER + 2 * SSD_GROUPS * SSD_STATE
SSD_IN = SSD_INNER + SSD_CONV_DIM + SSD_HEADS
SSD_NORM_GROUP = SSD_INNER // SSD_GROUPS
GDN_KH = 8
GDN_VH = 16
GDN_DK = 128
GDN_DV = 128
GDN_KD = GDN_KH * GDN_DK
GDN_V = GDN_VH * GDN_DV
GDN_CONV_DIM = 2 * GDN_KD + GDN_V
GDN_IN = GDN_CONV_DIM + GDN_V + 2 * GDN_VH
CONV_W = 4
D_FF = 4 * D_MODEL
CHUNK = 64
RMS_EPS = 1e-6

kernel_name = "hybrid_retention_ssd_gdn_decoder_step"


def _chunk_len(L):
    return CHUNK if L % CHUNK == 0 else L


def _rms(x, w=None):
    xf = x.astype(jnp.float32)
    y = xf * lax.rsqrt(jnp.mean(xf * xf, axis=-1, keepdims=True) + RMS_EPS)
    if w is not None:
        y = y * w.astype(jnp.float32)
    return y.astype(x.dtype)


def _l2norm(x):
    xf = x.astype(jnp.float32)
    return xf * lax.rsqrt(jnp.sum(xf * xf, axis=-1, keepdims=True) + RMS_EPS)


def _causal_conv(u, buf, w):
    L = u.shape[1]
    up = jnp.concatenate([buf.astype(u.dtype), u], axis=1)
    out = up[:, 0:L] * w[0]
    for tap in range(1, CONV_W):
        out = out + up[:, tap:tap + L] * w[tap]
    return out, up[:, L:]


def _rotary(x, pos0):
    L, half = x.shape[1], x.shape[-1] // 2
    inv = ROPE_BASE ** (-jnp.arange(half, dtype=jnp.float32) / half)
    ang = (jnp.arange(L, dtype=jnp.float32) + pos0)[:, None] * inv[None, :]
    cos, sin = jnp.cos(ang)[None, :, None, :], jnp.sin(ang)[None, :, None, :]
    xf = x.astype(jnp.float32)
    x1, x2 = xf[..., :half], xf[..., half:]
    return jnp.concatenate([x1 * cos - x2 * sin, x1 * sin + x2 * cos], axis=-1).astype(x.dtype)


def _decay_scan(q, k, v, log_a, s0):
    Bsz, L = q.shape[:2]
    chunk = _chunk_len(L)
    nc = L // chunk

    def blocks(t):
        return t.astype(jnp.float32).reshape((Bsz, nc, chunk) + t.shape[2:])

    qc, kc, vc = blocks(q), blocks(k), blocks(v)
    ct = jnp.moveaxis(jnp.cumsum(blocks(log_a), axis=2), 2, -1)
    incl = jnp.tril(jnp.ones((chunk, chunk), dtype=bool))
    diff = ct[..., :, None] - ct[..., None, :]
    decay = jnp.where(incl, jnp.exp(jnp.where(incl, diff, 0.0)), 0.0)
    scores = jnp.einsum('bctgn,bcsgn->bcgts', qc, kc)
    y_intra = jnp.einsum('bcgrts,bcsgrp->bctgrp', scores[:, :, :, None] * decay, vc)

    def step(S, inp):
        q_c, k_c, v_c, ct_c = inp
        y_c = jnp.einsum('btgn,bgrnp,bgrt->btgrp', q_c, S, jnp.exp(ct_c))
        w_end = jnp.exp(ct_c[..., -1:] - ct_c)
        S = S * jnp.exp(ct_c[..., -1])[..., None, None] + jnp.einsum('bsgn,bgrs,bsgrp->bgrnp', k_c, w_end, v_c)
        return S, y_c

    xs = tuple(jnp.moveaxis(t, 1, 0) for t in (qc, kc, vc, ct))
    s_fin, y_inter = lax.scan(step, s0.astype(jnp.float32), xs)
    y = y_intra + jnp.moveaxis(y_inter, 0, 1)
    return y.reshape((Bsz, L) + v.shape[2:]), s_fin


def _gated_delta_scan(q, k, v, g, beta, s0):
    Bsz, L, H, K = q.shape
    V = v.shape[-1]
    chunk = _chunk_len(L)
    nc = L // chunk

    def blocks(t):
        return t.astype(jnp.float32).reshape((Bsz, nc, chunk) + t.shape[2:])

    qc, kc, vc, bc = blocks(q), blocks(k), blocks(v), blocks(beta)
    gc = jnp.cumsum(blocks(g), axis=2)
    gt = jnp.moveaxis(gc, 2, -1)
    incl = jnp.tril(jnp.ones((chunk, chunk), dtype=bool))
    strict = jnp.tril(jnp.ones((chunk, chunk), dtype=bool), k=-1)
    diff = gt[..., :, None] - gt[..., None, :]
    decay = jnp.where(incl, jnp.exp(jnp.where(incl, diff, 0.0)), 0.0)
    kb = kc * bc[..., None]
    lmat = jnp.where(strict, jnp.einsum('bcthk,bcshk->bchts', kb, kc) * decay, 0.0)
    rhs = jnp.concatenate([jnp.moveaxis(vc * bc[..., None], 2, 3),
                           jnp.moveaxis(kb * jnp.exp(gc)[..., None], 2, 3)], axis=-1)
    eye = jnp.eye(chunk, dtype=jnp.float32)
    sol = lax.linalg.triangular_solve(eye + lmat, rhs, left_side=True, lower=True, unit_diagonal=True)
    u, w = sol[..., :V], sol[..., V:]
    attn = jnp.einsum('bcthk,bcshk->bchts', qc, kc) * decay
    q_dec = qc * jnp.exp(gc)[..., None]
    k_dec = kc * jnp.exp(gc[:, :, -1:] - gc)[..., None]
    c_dec = jnp.exp(gc[:, :, -1])

    def step(S, inp):
        u_c, w_c, attn_c, qd_c, kd_c, cd_c = inp
        v_new = u_c - jnp.einsum('bhtk,bhkv->bhtv', w_c, S)
        o_c = jnp.einsum('bthk,bhkv->bthv', qd_c, S) + jnp.einsum('bhts,bhsv->bthv', attn_c, v_new)
        S = S * cd_c[..., None, None] + jnp.einsum('bshk,bhsv->bhkv', kd_c, v_new)
        return S, o_c

    xs = tuple(jnp.moveaxis(t, 1, 0) for t in (u, w, attn, q_dec, k_dec, c_dec))
    s_fin, o = lax.scan(step, s0.astype(jnp.float32), xs)
    return jnp.moveaxis(o, 0, 1).reshape(Bsz, L, H, V), s_fin


def _retention(h, pos0, s0, w_in, w_out):
    Bsz, L, _ = h.shape
    q, k, v, g = jnp.split(h @ w_in, [RET_QK, 2 * RET_QK, 2 * RET_QK + RET_V], axis=-1)
    q = _rotary(q.reshape(Bsz, L, RET_HEADS, RET_DK), pos0)
    k = _rotary(k.reshape(Bsz, L, RET_HEADS, RET_DK), pos0) * (RET_DK ** -0.5)
    v = v.reshape(Bsz, L, RET_HEADS, 1, RET_DV)
    log_gamma = jnp.log1p(-jnp.exp2(-5.0 - jnp.arange(RET_HEADS, dtype=jnp.float32)))
    log_a = jnp.broadcast_to(log_gamma[:, None], (Bsz, L, RET_HEADS, 1))
    y, s_new = _decay_scan(q, k, v, log_a, s0[:, :, None])
    y = _rms(y[:, :, :, 0])
    out = (jax.nn.silu(g.astype(jnp.float32)) * y.reshape(Bsz, L, RET_V)).astype(h.dtype) @ w_out
    return out, s_new[:, :, 0].astype(s0.dtype)


def _ssd(h, conv_buf, s0, w_in, conv_w, conv_b, dt_bias, a_log, d_skip, norm_w, w_out):
    Bsz, L, _ = h.shape
    R = SSD_HEADS // SSD_GROUPS
    z, xbc, dt = jnp.split(h @ w_in, [SSD_INNER, SSD_INNER + SSD_CONV_DIM], axis=-1)
    xbc, new_buf = _causal_conv(xbc, conv_buf, conv_w)
    xbc = jax.nn.silu(xbc + conv_b)
    xs, b_in, c_out = jnp.split(xbc, [SSD_INNER, SSD_INNER + SSD_GROUPS * SSD_STATE], axis=-1)
    xs = xs.astype(jnp.float32).reshape(Bsz, L, SSD_GROUPS, R, SSD_HEADDIM)
    b_in = b_in.reshape(Bsz, L, SSD_GROUPS, SSD_STATE)
    c_out = c_out.reshape(Bsz, L, SSD_GROUPS, SSD_STATE)
    dt = jax.nn.softplus(dt.astype(jnp.float32) + dt_bias.astype(jnp.float32)).reshape(Bsz, L, SSD_GROUPS, R)
    log_a = dt * (-jnp.exp(a_log.astype(jnp.float32))).reshape(SSD_GROUPS, R)
    s0g = jnp.swapaxes(s0.reshape(Bsz, SSD_GROUPS, R, SSD_HEADDIM, SSD_STATE), -1, -2)
    y, s_new = _decay_scan(c_out, b_in, xs * dt[..., None], log_a, s0g)
    y = y + d_skip.astype(jnp.float32).reshape(SSD_GROUPS, R, 1) * xs
    y = y.reshape(Bsz, L, SSD_INNER) * jax.nn.silu(z.astype(jnp.float32))
    y = _rms(y.reshape(Bsz, L, SSD_GROUPS, SSD_NORM_GROUP)).reshape(Bsz, L, SSD_INNER) * norm_w
    out = y.astype(h.dtype) @ w_out
    s_new = jnp.swapaxes(s_new, -1, -2).reshape(Bsz, SSD_HEADS, SSD_HEADDIM, SSD_STATE)
    return out, s_new.astype(s0.dtype), new_buf


def _gdn(h, conv_buf, s0, w_in, conv_w, dt_bias, a_log, norm_w, w_out):
    Bsz, L, _ = h.shape
    rep = GDN_VH // GDN_KH
    qkv, z, b, a = jnp.split(h @ w_in, [GDN_CONV_DIM, GDN_CONV_DIM + GDN_V, GDN_CONV_DIM + GDN_V + GDN_VH], axis=-1)
    qkv, new_buf = _causal_conv(qkv, conv_buf, conv_w)
    q, k, v = jnp.split(jax.nn.silu(qkv), [GDN_KD, 2 * GDN_KD], axis=-1)
    q = jnp.repeat(_l2norm(q.reshape(Bsz, L, GDN_KH, GDN_DK)), rep, axis=2) * (GDN_DK ** -0.5)
    k = jnp.repeat(_l2norm(k.reshape(Bsz, L, GDN_KH, GDN_DK)), rep, axis=2)
    v = v.reshape(Bsz, L, GDN_VH, GDN_DV)
    beta = jax.nn.sigmoid(b.astype(jnp.float32))
    g = -jnp.exp(a_log.astype(jnp.float32)) * jax.nn.softplus(a.astype(jnp.float32) + dt_bias.astype(jnp.float32))
    o, s_new = _gated_delta_scan(q, k, v, g, beta, s0)
    o = _rms(o, norm_w) * jax.nn.silu(z.astype(jnp.float32).reshape(Bsz, L, GDN_VH, GDN_DV))
    out = o.reshape(Bsz, L, GDN_V).astype(h.dtype) @ w_out
    return out, s_new.astype(s0.dtype), new_buf


def _sqrelu_mlp(h, w_up, w_down):
    return jnp.square(jax.nn.relu(h @ w_up)) @ w_down


def setup_inputs(seed: int = 0) -> dict:
    key = jax.random.key(seed)
    ks = iter(jax.random.split(key, 40))

    def nrm(shape, scale):
        return scale * jax.random.normal(next(ks), shape, jnp.float32)

    def dense(shape):
        return nrm(shape, shape[-2] ** -0.5)

    def gain(shape):
        return 1.0 + nrm(shape, 0.02)

    def dt_bias(shape):
        uu = jax.random.uniform(next(ks), shape, jnp.float32)
        dt = jnp.exp(uu * (math.log(0.1) - math.log(0.001)) + math.log(0.001))
        return dt + jnp.log(-jnp.expm1(-dt))

    def a_log(shape):
        return jnp.log(jax.random.uniform(next(ks), shape, jnp.float32, 1.0, 16.0))

    return {
        "x_prompt": nrm((BATCH, SEQ, D_MODEL), 1.0),
        "x_sample": nrm((DEC_BATCH, DEC_SEQ, D_MODEL), 1.0),
        "state_ret": nrm((N_RET, DEC_BATCH, RET_HEADS, RET_DK, RET_DV), 0.5),
        "state_ssd": nrm((N_SSD, DEC_BATCH, SSD_HEADS, SSD_HEADDIM, SSD_STATE), 0.1),
        "state_ssd_conv": nrm((N_SSD, DEC_BATCH, CONV_W - 1, SSD_CONV_DIM), 1.0),
        "state_gdn": nrm((N_GDN, DEC_BATCH, GDN_VH, GDN_DK, GDN_DV), 0.3),
        "state_gdn_conv": nrm((N_GDN, DEC_BATCH, CONV_W - 1, GDN_CONV_DIM), 1.0),
        "norm_mix": gain((DEPTH, D_MODEL)),
        "norm_mlp": gain((DEPTH, D_MODEL)),
        "norm_final": gain((D_MODEL,)),
        "ret_w_in": dense((N_RET, D_MODEL, RET_IN)),
        "ret_w_out": dense((N_RET, RET_V, D_MODEL)),
        "ssd_w_in": dense((N_SSD, D_MODEL, SSD_IN)),
        "ssd_conv_w": nrm((N_SSD, CONV_W, SSD_CONV_DIM), CONV_W ** -0.5),
        "ssd_conv_b": nrm((N_SSD, SSD_CONV_DIM), 0.01),
        "ssd_dt_bias": dt_bias((N_SSD, SSD_HEADS)),
        "ssd_a_log": a_log((N_SSD, SSD_HEADS)),
        "ssd_d": gain((N_SSD, SSD_HEADS)),
        "ssd_norm": gain((N_SSD, SSD_INNER)),
        "ssd_w_out": dense((N_SSD, SSD_INNER, D_MODEL)),
        "gdn_w_in": dense((N_GDN, D_MODEL, GDN_IN)),
        "gdn_conv_w": nrm((N_GDN, CONV_W, GDN_CONV_DIM), CONV_W ** -0.5),
        "gdn_dt_bias": dt_bias((N_GDN, GDN_VH)),
        "gdn_a_log": a_log((N_GDN, GDN_VH)),
        "gdn_norm": gain((N_GDN, GDN_DV)),
        "gdn_w_out": dense((N_GDN, GDN_V, D_MODEL)),
        "mlp_w_up": dense((DEPTH, D_MODEL, D_FF)),
        "mlp_w_down": dense((DEPTH, D_FF, D_MODEL)),
    }


def reference(x_prompt, x_sample, state_ret, state_ssd, state_ssd_conv, state_gdn, state_gdn_conv,
              norm_mix, norm_mlp, norm_final, ret_w_in, ret_w_out,
              ssd_w_in, ssd_conv_w, ssd_conv_b, ssd_dt_bias, ssd_a_log, ssd_d, ssd_norm, ssd_w_out,
              gdn_w_in, gdn_conv_w, gdn_dt_bias, gdn_a_log, gdn_norm, gdn_w_out,
              mlp_w_up, mlp_w_down):

    def trunk(x, pos0, s_ret, s_ssd, c_ssd, s_gdn, c_gdn):
        n_ret, n_ssd, n_ssdc, n_gdn, n_gdnc = [], [], [], [], []
        for i in range(DEPTH):
            kind, j = i % N_MIXERS, i // N_MIXERS
            h = _rms(x, norm_mix[i])
            if kind == 0:
                out, s = _retention(h, pos0, s_ret[j], ret_w_in[j], ret_w_out[j])
                n_ret.append(s)
            elif kind == 1:
                out, s, c = _ssd(h, c_ssd[j], s_ssd[j], ssd_w_in[j], ssd_conv_w[j], ssd_conv_b[j],
                                 ssd_dt_bias[j], ssd_a_log[j], ssd_d[j], ssd_norm[j], ssd_w_out[j])
                n_ssd.append(s)
                n_ssdc.append(c)
            else:
                out, s, c = _gdn(h, c_gdn[j], s_gdn[j], gdn_w_in[j], gdn_conv_w[j], gdn_dt_bias[j],
                                 gdn_a_log[j], gdn_norm[j], gdn_w_out[j])
                n_gdn.append(s)
                n_gdnc.append(c)
            x = x + out
            x = x + _sqrelu_mlp(_rms(x, norm_mlp[i]), mlp_w_up[i], mlp_w_down[i])
        return (_rms(x, norm_final), jnp.stack(n_ret), jnp.stack(n_ssd), jnp.stack(n_ssdc),
                jnp.stack(n_gdn), jnp.stack(n_gdnc))

    bp = x_prompt.shape[0]
    dt_p = x_prompt.dtype
    y_prompt, p_ret, p_ssd, p_ssd_conv, p_gdn, p_gdn_conv = trunk(
        x_prompt, 0,
        jnp.zeros((N_RET, bp, RET_HEADS, RET_DK, RET_DV), dt_p),
        jnp.zeros((N_SSD, bp, SSD_HEADS, SSD_HEADDIM, SSD_STATE), dt_p),
        jnp.zeros((N_SSD, bp, CONV_W - 1, SSD_CONV_DIM), dt_p),
        jnp.zeros((N_GDN, bp, GDN_VH, GDN_DK, GDN_DV), dt_p),
        jnp.zeros((N_GDN, bp, CONV_W - 1, GDN_CONV_DIM), dt_p))
    y_sample, s_ret, s_ssd, s_ssd_conv, s_gdn, s_gdn_conv = trunk(
        x_sample, PAST_LEN, state_ret, state_ssd, state_ssd_conv, state_gdn, state_gdn_conv)
    return (y_prompt, y_sample, p_ret, p_ssd, p_ssd_conv, p_gdn, p_gdn_conv,
            s_ret, s_ssd, s_ssd_conv, s_gdn, s_gdn_conv)
```

```python
import contextlib
import math
import numpy as np
import concourse.bass as bass
import concourse.mybir as mybir
from concourse.bass_utils import run_bass_kernel_spmd

F32 = mybir.dt.float32
BF16 = mybir.dt.bfloat16
ALU = mybir.AluOpType
AF = mybir.ActivationFunctionType

ENGS = ("pe", "act", "dve", "pool", "sp")
NDMA_SEMS = 40

D = 1024
SEQ = 2048
NPT = 16
NT = 17
TS = 64
NTOK = SEQ + TS
DEPTH = 4
PAST_LEN = 16384
RMS_EPS = 1e-6
D_FF = 4096

CFG = {"mixers": (0, 1, 2, 3), "mlp": True, "nlayers": 4}


class Op:
    __slots__ = ("eng", "fn", "deps", "is_dma", "pos", "inc", "semi", "semv", "waits")

    def __init__(self, eng, fn, deps, is_dma):
        self.eng = eng
        self.fn = fn
        self.deps = deps
        self.is_dma = is_dma
        self.inc = False
        self.semi = -1
        self.semv = 0
        self.waits = None


import types as _types


def _freeze(fn):
    if fn.__closure__ is None:
        return fn
    cells = []
    for c in fn.__closure__:
        try:
            cells.append(_types.CellType(c.cell_contents))
        except ValueError:
            cells.append(c)
    g = _types.FunctionType(fn.__code__, fn.__globals__, fn.__name__, fn.__defaults__, tuple(cells))
    g.__kwdefaults__ = fn.__kwdefaults__
    return g


class Prog:
    def __init__(self, nc):
        self.nc = nc
        self.ops = []
        self.lastw = {}
        self.readers = {}

    def op(self, eng, fn, r=(), w=(), dma=False):
        deps = set()
        lastw = self.lastw
        readers = self.readers
        for k in r:
            lw = lastw.get(k)
            if lw is not None:
                deps.add(lw)
            if type(k) is tuple and k[0] == "ps":
                rd = readers.get(k)
                if rd:
                    for j_ in rd:
                        if self.ops[j_].eng != eng:
                            deps.add(j_)
        for k in w:
            lw = lastw.get(k)
            if lw is not None:
                deps.add(lw)
            rd = readers.get(k)
            if rd:
                deps.update(rd)
        idx = len(self.ops)
        self.ops.append(Op(eng, _freeze(fn), deps, dma))
        for k in w:
            lastw[k] = idx
            readers[k] = []
        for k in r:
            if k in w:
                continue
            readers.setdefault(k, []).append(idx)
        return idx

    def plan(self):
        ops = self.ops
        streams = {e: [] for e in ENGS}
        for i, o in enumerate(ops):
            o.pos = len(streams[o.eng])
            streams[o.eng].append(i)
        clock = {e: {f: -1 for f in ENGS} for e in ENGS}
        dma_seen = {e: set() for e in ENGS}
        vcs = [None] * len(ops)
        dma_count = {"sp": 0, "pool": 0, "act": 0}
        dma_base = {"sp": (0, 28), "pool": (28, 12), "act": (0, 28)}
        sem_last = [None] * NDMA_SEMS
        sem_val = [0] * NDMA_SEMS
        for i, o in enumerate(ops):
            E = o.eng
            ck = clock[E]
            waits = []
            deps = o.deps
            if o.is_dma:
                base_, n_ = dma_base[E]
                s = base_ + dma_count[E] % n_
                dma_count[E] += 1
                if sem_last[s] is not None:
                    deps = set(deps)
                    deps.add(sem_last[s])
                sem_last[s] = i
                sem_val[s] += 16
                o.semi = s
                o.semv = sem_val[s]
            need = {}
            for j in deps:
                oj = ops[j]
                if oj.is_dma:
                    if j not in dma_seen[E]:
                        waits.append(("dma", j))
                        dma_seen[E].add(j)
                        vj = vcs[j]
                        for f in ENGS:
                            if vj[f] > ck[f]:
                                ck[f] = vj[f]
                else:
                    F = oj.eng
                    if F == E and E in ("pe", "sp"):
                        continue
                    if oj.pos > ck[F] and oj.pos > need.get(F, -1):
                        need[F] = oj.pos
            for F, p in need.items():
                if p > ck[F]:
                    j = streams[F][p]
                    ops[j].inc = True
                    waits.append(("eng", F, j))
                    vj = vcs[j]
                    for f in ENGS:
                        if vj[f] > ck[f]:
                            ck[f] = vj[f]
                    if p > ck[F]:
                        ck[F] = p
            o.waits = waits
            o.deps = None
            if o.is_dma:
                vcs[i] = dict(ck)
            else:
                v = dict(ck)
                v[E] = o.pos
                vcs[i] = v
                if E == "pe":
                    ck[E] = o.pos
        cnt = {e: 0 for e in ENGS}
        for e in ENGS:
            for i in streams[e]:
                o = ops[i]
                if (not o.is_dma) and o.inc:
                    cnt[e] += 1
                    o.semv = cnt[e]
        self.streams = streams
        self.counts = cnt
        return streams

    def emit(self):
        nc = self.nc
        ops = self.ops
        streams = self.plan()
        with contextlib.ExitStack() as es:
            esem = {e: es.enter_context(nc.semaphore("s_" + e)) for e in ENGS}
            dsem = [es.enter_context(nc.semaphore("d%d" % k)) for k in range(NDMA_SEMS)]
            block = es.enter_context(nc.Block())

            def run(e, eng):
                for i in streams[e]:
                    o = ops[i]
                    for wt in o.waits:
                        if wt[0] == "dma":
                            oj = ops[wt[1]]
                            eng.wait_ge(dsem[oj.semi], oj.semv)
                        else:
                            oj = ops[wt[2]]
                            eng.wait_ge(esem[oj.eng], oj.semv)
                    ins = o.fn(eng)
                    if o.is_dma:
                        ins.then_inc(dsem[o.semi], 16)
                    elif o.inc:
                        ins.then_inc(esem[e], 1)
                    o.fn = None

            @block.tensor
            def _(eng):
                run("pe", eng)

            @block.scalar
            def _(eng):
                run("act", eng)

            @block.vector
            def _(eng):
                run("dve", eng)

            @block.gpsimd
            def _(eng):
                run("pool", eng)

            @block.sync
            def _(eng):
                run("sp", eng)
                vals = {}
                for o in ops:
                    if o.is_dma:
                        vals[o.semi] = max(vals.get(o.semi, 0), o.semv)
                for s, v in sorted(vals.items()):
                    eng.wait_ge(dsem[s], v)
                for e in ("pe", "act", "dve", "pool"):
                    if self.counts[e] > 0:
                        eng.wait_ge(esem[e], self.counts[e])


def _mask_consts(T, L):
    idx = np.arange(T)
    same = (idx[:, None] // L) == (idx[None, :] // L)
    tri = (same & (idx[:, None] <= idx[None, :])).astype(np.float32)
    seq = same.astype(np.float32)
    neg = np.where(same & (idx[None, :] >= idx[:, None]), 0.0, -30000.0).astype(np.float32)
    negt = np.where(same & (idx[None, :] <= idx[:, None]), 0.0, -30000.0).astype(np.float32)
    strict = (same & (idx[None, :] < idx[:, None])).astype(np.float32)
    return tri, seq, neg, negt, strict


def build_consts():
    items = []
    items.append(("ident", np.eye(128, dtype=np.float32)))
    items.append(("ones", np.ones((128, 128), np.float32)))
    for nm, a in zip(("tri_p", "seq_p", "neg_p", "negt_p", "strict_p"), _mask_consts(128, 128)):
        if nm != "seq_p":
            items.append((nm, a))
    for nm, a in zip(("tri_s", "seq_s", "neg_s", "negt_s", "strict_s"), _mask_consts(TS, 4)):
        items.append((nm, a))
    rowmask = (np.arange(TS)[:, None] // 4 == np.arange(16)[None, :]).astype(np.float32)
    items.append(("rowmask_s", rowmask))
    colmask = np.ascontiguousarray(np.broadcast_to(rowmask.T[None, :, :], (128, 16, TS)).reshape(128, 16 * TS))
    ii = np.arange(128)
    for lv in range(7):
        bsz = 1 << lv
        same = (ii[:, None] // (2 * bsz)) == (ii[None, :] // (2 * bsz))
        mT = same & ((ii[None, :] % (2 * bsz)) >= bsz) & ((ii[:, None] % (2 * bsz)) < bsz)
        items.append(("bmT%d" % lv, mT.astype(np.float32)))
    lg = np.log1p(-np.exp2(-5.0 - np.arange(4, dtype=np.float32))).astype(np.float32)
    items.append(("laret", np.broadcast_to(lg[None, :], (128, 4)).copy()))
    offs = {}
    tot = 0
    for nm, a in items:
        offs[nm] = (tot, a.shape[0], a.shape[1])
        tot += a.shape[1]
    pack = np.zeros((128, tot), np.float32)
    for nm, a in items:
        o, r, c = offs[nm]
        pack[:r, o:o + c] = a
    half = 128
    inv = (np.float32(10000.0) ** (-np.arange(half, dtype=np.float32) / np.float32(half))).astype(np.float32)
    pos = np.zeros((NT, 128), np.float32)
    for i in range(NPT):
        pos[i] = np.arange(128, dtype=np.float32) + 128 * i
    pos[NPT, :TS] = (np.arange(TS) % 4).astype(np.float32) + np.float32(PAST_LEN)
    ang = (pos[:, :, None] * inv[None, None, :]).astype(np.float32)
    cs = np.stack([np.cos(ang.astype(np.float64)), np.sin(ang.astype(np.float64))], axis=2).astype(np.float32)
    return pack, offs, np.ascontiguousarray(cs), colmask


class TileInfo:
    def __init__(self, i):
        self.i = i
        self.sample = i == NPT
        self.T = TS if self.sample else 128
        self.c0 = i * 128
        self.kind = "s" if self.sample else "p"
        self.nseq = 16 if self.sample else 1


TILES = [TileInfo(i) for i in range(NT)]


NFA = 12
NBA = 16


class Buf:
    __slots__ = ("pool", "i", "t", "key")

    def __init__(self, pool, i):
        self.pool = pool
        self.i = i
        self.t = pool.ts[i]
        self.key = (pool.name, i)

    def free(self):
        self.pool.free.append(self.i)


class BufPool:
    def __init__(self, name, ts):
        self.name = name
        self.ts = ts
        self.free = list(range(len(ts)))

    def get(self):
        assert self.free, "pool %s exhausted" % self.name
        return Buf(self, self.free.pop(0))


class K:
    def __init__(self, nc, offs):
        self.nc = nc
        self.P = Prog(nc)
        self.offs = offs
        self.uid = 0
        nc_ = nc
        dt = nc_.dram_tensor
        class _Lazy(dict):
            def __missing__(d, nm):
                v = dt(nm, list(IN_SHAPES[nm]), F32, kind="ExternalInput").ap()
                d[nm] = v
                return v
        self.din = _Lazy()
        self.dout = {}
        for nm, shp in OUT_SHAPES.items():
            self.dout[nm] = dt(nm, list(shp), F32, kind="ExternalOutput").ap()
        a = nc_.alloc_sbuf_tensor
        self.X = a("X", [128, NT, D], F32)
        self.XNT = a("XNT", [128, 8, NTOK], BF16)
        self.NRING = 4
        self.ring = [a("wr%d" % k, [128, 4096], BF16) for k in range(self.NRING)]
        self.ring_i = 0
        self.CONST = a("CONST", [128, offs["_tot"]], F32)
        self.identb = a("identb", [128, 128], BF16)
        self.colmaskb = a("colmaskb", [128, 16, TS], BF16)
        self.NW = a("NW", [128, 9, 8], F32)
        self.xnb = [a("xnb%d" % k, [128, D], BF16) for k in range(2)]
        self.st = [a("st%d" % k, [128, 8], F32) for k in range(4)]
        self.PAR = a("PAR", [128, D], F32)
        self.SF = a("SF", [128, 2, 512], F32)
        self.SB = a("SB", [128, 2, 512], BF16)
        self.SM = [a("SM%d" % k, [128, 160], F32) for k in range(4)]
        self.PRM = a("PRM", [128, 128], F32)
        self.CW = a("CW", [128, 32, 4], F32)
        self.CB = a("CB", [128, 32], F32)
        self.UB = [a("UB%d" % k, [128, 528], F32) for k in range(2)]
        self.LAB = [a("LAB%d" % k, [128, 40], F32) for k in range(4)]
        self.fpool = BufPool("fa", [a("fa%d" % k, [128, 512], F32) for k in range(NFA)])
        self.bpool = BufPool("ba", [a("ba%d" % k, [128, 512], BF16) for k in range(NBA)])
        self.ps = [nc_.alloc_psum_tensor("ps%d" % k, [128, 512], F32) for k in range(8)]
        self.ps_free = list(range(8))
        self.cnt = {}
        print("sbuf bytes remaining", nc_.sbuf_bytes_remaining, flush=True)

    def c(self, name):
        o, r, cc = self.offs[name]
        return self.CONST[0:r, o:o + cc]

    def nid(self, pfx):
        self.uid += 1
        return "%s#%d" % (pfx, self.uid)

    def rr(self, key, n=2):
        v = self.cnt.get(key, 0)
        self.cnt[key] = v + 1
        return v % n

    def psum(self):
        b = self.ps_free.pop(0)
        return b

    def pfree(self, b):
        self.ps_free.append(b)

    def ringslot(self):
        s = self.ring_i % self.NRING
        self.ring_i += 1
        return s

    def op(self, *a, **k):
        return self.P.op(*a, **k)

    def dma(self, out, in_, r=(), w=(), eng="sp", slow=False):
        if slow:
            fn = lambda e: e.dma_start(out=out, in_=in_, allow_slow_non_contiguous=True)
        else:
            fn = lambda e: e.dma_start(out=out, in_=in_)
        return self.P.op(eng, fn, r=r, w=w, dma=True)

    def load_w(self, src_pieces, r_extra=()):
        s = self.ringslot()
        t = self.ring[s]
        for dstf, src in src_pieces:
            self.dma(dstf(t), src, w=[("ring", s)], eng="pool")
        return s

    def mm(self, out, lhsT, rhs, start, stop, r, w):
        self.P.op("pe", lambda e: e.matmul(out, lhsT=lhsT, rhs=rhs, start=start, stop=stop), r=r, w=w)

    def tr(self, out, in_, ident, r, w):
        self.P.op("pe", lambda e: e.transpose(out=out, in_=in_, identity=ident), r=r, w=w)


IN_SHAPES = {}
OUT_SHAPES = {}


def _set_shapes(ncst):
    IN_SHAPES.clear()
    IN_SHAPES.update({
        "x_prompt": (SEQ, D), "x_sample": (TS, D),
        "state_ret": (2, 16, 4, 256, 512), "state_ssd": (16, 32, 64, 128), "state_ssd_conv": (16, 3, 4096),
        "state_gdn": (16, 16, 128, 128), "state_gdn_conv": (16, 3, 4096),
        "norms": (9, D),
        "ret_w_in": (2, D, 6144), "ret_w_out": (2, 2048, D),
        "ssd_w_in": (D, 6176), "ssd_conv_w": (4, 4096), "ssd_conv_b": (1, 4096), "ssd_dt_bias": (1, 32),
        "ssd_a_log": (1, 32), "ssd_d": (1, 32), "ssd_norm": (1, 2048), "ssd_w_out": (2048, D),
        "gdn_w_in": (D, 6176), "gdn_conv_w": (4, 4096), "gdn_dt_bias": (1, 16), "gdn_a_log": (1, 16),
        "gdn_norm": (1, 128), "gdn_w_out": (2048, D),
        "mlp_w_up": (4, D, D_FF), "mlp_w_down": (4, D_FF, D),
        "cpack": (128, ncst), "ropecs": (NT, 128, 2, 128), "colmask": (128, 16 * TS),
    })
    OUT_SHAPES.clear()
    OUT_SHAPES.update({
        "y_prompt": (SEQ, D), "y_sample": (TS, D),
        "p_ret": (2, 4, 256, 512), "p_ssd": (32, 64, 128), "p_ssd_conv": (3, 4096),
        "p_gdn": (16, 128, 128), "p_gdn_conv": (3, 4096),
        "s_ret": (2, 16, 4, 256, 512), "s_ssd": (16, 32, 64, 128), "s_ssd_conv": (16, 3, 4096),
        "s_gdn": (16, 16, 128, 128), "s_gdn_conv": (16, 3, 4096),
    })


def emit_setup(k):
    k.dma(k.CONST[:, :], k.din["cpack"][:, :], w=["CONST"])
    k.op("act", lambda e: e.activation(out=k.identb[:, :], in_=k.c("ident"), func=AF.Copy), r=["CONST"], w=["identb"])
    for q in range(2):
        cb = k.fpool.get()
        k.dma(cb.t[:, :], k.din["colmask"][:, q * 512:(q + 1) * 512], w=[cb.key])
        k.op("dve", lambda e, cb=cb, q=q: e.tensor_copy(out=k.colmaskb[0:128, :, :].rearrange("p a b -> p (a b)")[:, q * 512:(q + 1) * 512], in_=cb.t[:, :]),
             r=[cb.key], w=["colmaskb"])
        cb.free()
    k.dma(k.PAR[0:9, :], k.din["norms"][:, :], w=["PAR"])
    b = k.psum()
    pv = k.ps[b][:, 0:72].rearrange("p (c l) -> p c l", c=8)
    for ch in range(8):
        k.tr(pv[:, ch, :], k.PAR[0:9, ch * 128:(ch + 1) * 128], k.c("ident")[0:9, 0:9], r=["PAR", "CONST"], w=[("ps", b)])
    k.op("dve", lambda e: e.tensor_copy(out=k.NW[:, :, :].rearrange("p l c -> p c l"), in_=pv), r=[("ps", b)], w=["NW"])
    k.pfree(b)
    for t in TILES:
        src = k.din["x_sample"][:, :] if t.sample else k.din["x_prompt"][t.c0:t.c0 + 128, :]
        k.dma(k.X[0:t.T, t.i, :], src, w=[("X", t.i)])


def emit_rstd(k, t, stt, src_ap, src_res, n, col):
    T = t.T
    jb = [k.bpool.get() for _ in range((n + 511) // 512)]
    for q, jbq in enumerate(jb):
        n0, n1 = q * 512, min(n, q * 512 + 512)
        k.op("act", lambda e, jbq=jbq, n0=n0, n1=n1, q=q: e.activation(out=jbq.t[0:T, 0:n1 - n0], in_=src_ap[:, n0:n1], func=AF.Square,
                                                                      accum_out=stt[0:T, col + 2 + q:col + 3 + q]),
             r=list(src_res), w=[jbq.key, ("st", id(stt))])
        jbq.free()
    if len(jb) == 2:
        k.op("dve", lambda e: e.tensor_tensor(out=stt[0:T, col:col + 1], in0=stt[0:T, col + 2:col + 3], in1=stt[0:T, col + 3:col + 4], op=ALU.add),
             r=[("st", id(stt))], w=[("st", id(stt))])
    else:
        k.op("dve", lambda e: e.tensor_copy(out=stt[0:T, col:col + 1], in_=stt[0:T, col + 2:col + 3]),
             r=[("st", id(stt))], w=[("st", id(stt))])
    k.op("dve", lambda e: e.tensor_scalar(out=stt[0:T, col + 1:col + 2], in0=stt[0:T, col:col + 1], scalar1=1.0 / n, scalar2=RMS_EPS,
                                          op0=ALU.mult, op1=ALU.add), r=[("st", id(stt))], w=[("st", id(stt))])
    k.op("act", lambda e: e.activation(out=stt[0:T, col + 1:col + 2], in_=stt[0:T, col + 1:col + 2], func=AF.Ln),
         r=[("st", id(stt))], w=[("st", id(stt))])
    k.op("act", lambda e: e.activation(out=stt[0:T, col + 1:col + 2], in_=stt[0:T, col + 1:col + 2], func=AF.Exp, scale=-0.5),
         r=[("st", id(stt))], w=[("st", id(stt))])


def emit_norm_xnt(k, widx):
    for t in TILES[:CFG.get("norm_tiles", NT)]:
        T = t.T
        stt = k.st[k.rr("st", 4)]
        xb = k.xnb[k.rr("xnb", 2)]
        xbk = ("xnb", id(xb))
        emit_rstd(k, t, stt, k.X[0:T, t.i, :], [("X", t.i)], D, 0)
        k.op("dve", lambda e, T=T, xb=xb, stt=stt, t=t: e.tensor_scalar_mul(out=xb[0:T, :], in0=k.X[0:T, t.i, :], scalar1=stt[0:T, 1:2]),
             r=[("X", t.i), ("st", id(stt))], w=[xbk])
        b = k.psum()
        pv = k.ps[b][:, :].bitcast(BF16).rearrange("p (c m) -> p c m", c=8)
        for ch in range(8):
            k.tr(pv[:, ch, 0:T], xb[0:T, ch * 128:(ch + 1) * 128], k.identb[0:T, 0:T], r=[xbk, "identb"], w=[("ps", b)])
        nwb = k.NW[:, widx, :].unsqueeze(2).to_broadcast([128, 8, T])
        k.op("dve", lambda e, T=T, t=t, pv=pv, nwb=nwb: e.tensor_tensor(out=k.XNT[:, :, t.c0:t.c0 + T], in0=pv[:, :, 0:T], in1=nwb, op=ALU.mult),
             r=[("ps", b), "NW"], w=[("XNT", t.i)])
        k.pfree(b)


def emit_mlp(k, layer):
    wu = k.din["mlp_w_up"]
    wd = k.din["mlp_w_down"]
    blocks = [(0, 512, [0, 1, 2, 3]), (512, 512, [4, 5, 6, 7]), (1024, 512, [8, 9, 10, 11]), (1536, 512, [12, 13, 14, 15]),
              (2048, TS, [16])]
    for g in range(8):
        su = k.load_w([(lambda t: t[:, :].rearrange("p (k n) -> p k n", k=8),
                        wu[layer, :, g * 512:(g + 1) * 512].rearrange("(k p) n -> p k n", p=128))])
        sd = k.load_w([(lambda t: t[:, :].rearrange("p (c n) -> p c n", c=4),
                        wd[layer, g * 512:(g + 1) * 512, :].rearrange("(c p) n -> p c n", p=128))])
        Wu = k.ring[su][:, :].rearrange("p (k n) -> p k n", k=8)
        Wd = k.ring[sd][:, :].rearrange("p (c n) -> p c n", c=4)
        for (c0, ncol, tl) in blocks:
            hTb = [k.bpool.get() for _ in range(4)]
            for c in range(4):
                b = k.psum()
                for kk in range(8):
                    k.mm(k.ps[b][:, 0:ncol], Wu[:, kk, c * 128:(c + 1) * 128], k.XNT[:, kk, c0:c0 + ncol], kk == 0, kk == 7,
                         r=[("ring", su)] + [("XNT", ti) for ti in tl], w=[("ps", b)])
                hr = k.bpool.get()
                k.op("act", lambda e, b=b, hr=hr, ncol=ncol: e.activation(out=hr.t[:, 0:ncol], in_=k.ps[b][:, 0:ncol], func=AF.Relu),
                     r=[("ps", b)], w=[hr.key])
                k.op("pool", lambda e, hr=hr, hTc=hTb[c], ncol=ncol: e.tensor_tensor(out=hTc.t[:, 0:ncol], in0=hr.t[:, 0:ncol], in1=hr.t[:, 0:ncol], op=ALU.mult),
                     r=[hr.key], w=[hTb[c].key])
                hr.free()
                k.pfree(b)
            for j, ti in enumerate(tl):
                t = TILES[ti]
                T = t.T
                for half in range(2):
                    b = k.psum()
                    for c in range(4):
                        k.mm(k.ps[b][0:T, :], hTb[c].t[:, j * 128:j * 128 + T], Wd[:, c, half * 512:(half + 1) * 512], c == 0, c == 3,
                             r=[hTb[c].key, ("ring", sd)], w=[("ps", b)])
                    k.op("dve", lambda e, b=b, T=T, ti=ti, half=half: e.tensor_tensor(
                        out=k.X[0:T, ti, half * 512:(half + 1) * 512], in0=k.X[0:T, ti, half * 512:(half + 1) * 512],
                        in1=k.ps[b][0:T, :], op=ALU.add), r=[("ps", b), ("X", ti)], w=[("X", ti)])
                    k.pfree(b)
            for hb in hTb:
                hb.free()


def emit_final(k):
    k.dma(k.PAR[:, :], k.din["norms"][8, :].partition_broadcast(128), w=["PAR"])
    for t in TILES:
        T = t.T
        stt = k.st[k.rr("st", 4)]
        emit_rstd(k, t, stt, k.X[0:T, t.i, :], [("X", t.i)], D, 0)
        k.op("dve", lambda e, T=T, t=t, stt=stt: e.scalar_tensor_tensor(out=k.X[0:T, t.i, :], in0=k.X[0:T, t.i, :], scalar=stt[0:T, 1:2],
                                                                         in1=k.PAR[0:T, :], op0=ALU.mult, op1=ALU.mult),
             r=[("X", t.i), ("st", id(stt)), "PAR"], w=[("X", t.i)])
        dst = k.dout["y_sample"][:, :] if t.sample else k.dout["y_prompt"][t.c0:t.c0 + 128, :]
        k.dma(dst, k.X[0:T, t.i, :], r=[("X", t.i)])


def build_program(nc, offs):
    k = K(nc, offs)
    emit_setup(k)
    for layer in range(CFG["nlayers"]):
        emit_norm_xnt(k, layer)
        if layer in CFG["mixers"]:
            kind = layer % 3
            if kind == 0:
                from_ret(k, layer)
            elif kind == 1:
                from_ssd(k, layer)
            else:
                from_gdn(k, layer)
        if CFG["mlp"]:
            emit_norm_xnt(k, 4 + layer)
            emit_mlp(k, layer)
    emit_final(k)
    zero_unwritten_outputs(k)
    k.P.emit()
    return k


def kconst(k, kind, nm):
    return k.c("%s_%s" % (nm, kind)) if not (nm == "seq" and kind == "p") else k.c("ones")


def decay_prep(k, kind, la, la_res, R):
    T = 128 if kind == "p" else TS
    nseq = 1 if kind == "p" else 16
    smi = k.rr("SM", 4)
    sm = k.SM[smi]
    smk = ("SM", smi)
    tri = kconst(k, kind, "tri")
    seq = kconst(k, kind, "seq")
    neg = kconst(k, kind, "neg")
    ones = k.c("ones")
    b = k.psum()
    k.mm(k.ps[b][0:T, 0:R], tri, la, True, True, r=["CONST"] + la_res, w=[("ps", b)])
    k.mm(k.ps[b][0:T, R:2 * R], seq, la, True, True, r=["CONST"] + la_res, w=[("ps", b)])
    k.op("act", lambda e: e.activation(out=sm[0:T, 0:2 * R], in_=k.ps[b][0:T, 0:2 * R], func=AF.Copy), r=[("ps", b)], w=[smk])
    k.pfree(b)
    bm = k.fpool.get()
    bmv = bm.t[0:T, 0:R * T].rearrange("p (r t) -> p r t", r=R)
    k.op("pool", lambda e: e.tensor_tensor(out=bmv, in0=tri.unsqueeze(1).to_broadcast([T, R, T]), in1=la.unsqueeze(2).to_broadcast([T, R, T]),
                                           op=ALU.mult), r=["CONST"] + la_res, w=[bm.key])
    b2 = k.psum()
    k.mm(k.ps[b2][0:T, 0:R * T], ones[0:T, 0:T], bm.t[0:T, 0:R * T], True, True, r=["CONST", bm.key], w=[("ps", b2)])
    bm.free()
    dec = k.fpool.get()
    decv = dec.t[0:T, 0:R * T].rearrange("p (r t) -> p r t", r=R)
    crv = k.ps[b2][0:T, 0:R * T].rearrange("p (r t) -> p r t", r=R)
    for r_ in range(R):
        k.op("dve", lambda e, r_=r_: e.scalar_tensor_tensor(out=decv[:, r_, :], in0=crv[:, r_, :], scalar=sm[0:T, r_:r_ + 1], in1=neg,
                                                              op0=ALU.subtract, op1=ALU.add), r=[("ps", b2), smk, "CONST"], w=[dec.key])
    k.pfree(b2)
    k.op("act", lambda e: e.activation(out=dec.t[0:T, 0:R * T], in_=dec.t[0:T, 0:R * T], func=AF.Exp), r=[dec.key], w=[dec.key])
    k.op("act", lambda e: e.activation(out=sm[0:T, 2 * R:3 * R], in_=sm[0:T, 0:R], func=AF.Exp), r=[smk], w=[smk])
    k.op("dve", lambda e: e.tensor_tensor(out=sm[0:T, 3 * R:4 * R], in0=sm[0:T, R:2 * R], in1=sm[0:T, 0:R], op=ALU.subtract), r=[smk], w=[smk])
    k.op("act", lambda e: e.activation(out=sm[0:T, 3 * R:4 * R], in_=sm[0:T, 3 * R:4 * R], func=AF.Exp), r=[smk], w=[smk])
    if nseq == 1:
        k.op("act", lambda e: e.activation(out=sm[0:T, 4 * R:5 * R], in_=sm[0:T, R:2 * R], func=AF.Exp), r=[smk], w=[smk])
    else:
        lam = sm[0:T, 96:96 + 16 * R].rearrange("p (b r) -> p b r", b=16)
        k.op("pool", lambda e: e.tensor_tensor(out=lam, in0=la.unsqueeze(1).to_broadcast([T, 16, R]),
                                               in1=k.c("rowmask_s").unsqueeze(2).to_broadcast([T, 16, R]), op=ALU.mult),
             r=["CONST", smk] + la_res, w=[smk])
        b3 = k.psum()
        k.mm(k.ps[b3][0:128, 0:16 * R], ones[0:T, 0:128], sm[0:T, 96:96 + 16 * R], True, True, r=["CONST", smk], w=[("ps", b3)])
        k.op("act", lambda e: e.activation(out=sm[0:128, 16:16 + 16 * R], in_=k.ps[b3][0:128, 0:16 * R], func=AF.Exp), r=[("ps", b3)], w=[smk])
        k.pfree(b3)
    return {"sm": sm, "smk": smk, "dec": dec, "T": T, "R": R, "kind": kind}


def scan_tile(k, t, prep, qT, qT_key, kT, kT_key, ktok, ktok_key, v, v_key, R, Pd, NC, first, sload, sstore, pstore):
    T = t.T
    W = R * Pd
    sm, smk, dec = prep["sm"], prep["smk"], prep["dec"]
    nseq = t.nseq
    bs = k.psum()
    for nc_ in range(NC):
        k.mm(k.ps[bs][0:T, 0:T], kT[:, nc_, 0:T], qT[:, nc_, 0:T], nc_ == 0, nc_ == NC - 1, r=[kT_key, qT_key], w=[("ps", bs)])
    at = k.bpool.get()
    k.op("dve", lambda e: e.tensor_tensor(out=at.t[0:T, 0:R * T].rearrange("p (r t) -> p r t", r=R),
                                          in0=dec.t[0:T, 0:R * T].rearrange("p (r t) -> p r t", r=R),
                                          in1=k.ps[bs][0:T, 0:T].unsqueeze(1).to_broadcast([T, R, T]), op=ALU.mult),
         r=[dec.key, ("ps", bs)], w=[at.key])
    k.pfree(bs)
    by1 = k.psum()
    for r_ in range(R):
        k.mm(k.ps[by1][0:T, r_ * Pd:(r_ + 1) * Pd], at.t[0:T, r_ * T:(r_ + 1) * T], v[0:T, r_ * Pd:(r_ + 1) * Pd], True, True,
             r=[at.key, v_key], w=[("ps", by1)])
    at.free()
    vw = k.bpool.get()
    k.op("pool", lambda e: e.tensor_tensor(out=vw.t[0:T, 0:W].rearrange("p (r d) -> p r d", r=R), in0=v[0:T, 0:W].rearrange("p (r d) -> p r d", r=R),
                                           in1=sm[0:T, 3 * R:4 * R].unsqueeze(2).to_broadcast([T, R, Pd]), op=ALU.mult),
         r=[v_key, smk], w=[vw.key])
    y = k.fpool.get()
    if nseq == 1:
        if first:
            k.op("act", lambda e: e.activation(out=y.t[0:T, 0:W], in_=k.ps[by1][0:T, 0:W], func=AF.Copy), r=[("ps", by1)], w=[y.key])
            k.pfree(by1)
        else:
            by2 = k.psum()
            for nc_ in range(NC):
                k.mm(k.ps[by2][0:T, 0:W], qT[:, nc_, 0:T], k.SB[:, nc_, 0:W], nc_ == 0, nc_ == NC - 1, r=[qT_key, ("SB", nc_)], w=[("ps", by2)])
            tmp = k.fpool.get()
            k.op("dve", lambda e: e.tensor_tensor(out=tmp.t[0:T, 0:W].rearrange("p (r d) -> p r d", r=R),
                                                  in0=k.ps[by2][0:T, 0:W].rearrange("p (r d) -> p r d", r=R),
                                                  in1=sm[0:T, 2 * R:3 * R].unsqueeze(2).to_broadcast([T, R, Pd]), op=ALU.mult),
                 r=[("ps", by2), smk], w=[tmp.key])
            k.pfree(by2)
            k.op("dve", lambda e: e.tensor_tensor(out=y.t[0:T, 0:W], in0=tmp.t[0:T, 0:W], in1=k.ps[by1][0:T, 0:W], op=ALU.add),
                 r=[tmp.key, ("ps", by1)], w=[y.key])
            tmp.free()
            k.pfree(by1)
        for nc_ in range(NC):
            bd = k.psum()
            k.mm(k.ps[bd][0:128, 0:W], ktok[0:T, nc_ * 128:(nc_ + 1) * 128], vw.t[0:T, 0:W], True, True, r=[ktok_key, vw.key], w=[("ps", bd)])
            if first:
                k.op("act", lambda e, bd=bd, nc_=nc_: e.activation(out=k.SF[:, nc_, 0:W], in_=k.ps[bd][0:128, 0:W], func=AF.Copy),
                     r=[("ps", bd)], w=[("SF", nc_)])
            else:
                k.op("pool", lambda e, nc_=nc_: e.tensor_tensor(out=k.SF[:, nc_, 0:W].rearrange("p (r d) -> p r d", r=R),
                                                                 in0=k.SF[:, nc_, 0:W].rearrange("p (r d) -> p r d", r=R),
                                                                 in1=sm[0:128, 4 * R:5 * R].unsqueeze(2).to_broadcast([128, R, Pd]), op=ALU.mult),
                     r=[("SF", nc_), smk], w=[("SF", nc_)])
                k.op("dve", lambda e, bd=bd, nc_=nc_: e.tensor_tensor(out=k.SF[:, nc_, 0:W], in0=k.SF[:, nc_, 0:W], in1=k.ps[bd][0:128, 0:W], op=ALU.add),
                     r=[("SF", nc_), ("ps", bd)], w=[("SF", nc_)])
            k.pfree(bd)
            k.op("act", lambda e, nc_=nc_: e.activation(out=k.SB[:, nc_, 0:W], in_=k.SF[:, nc_, 0:W], func=AF.Copy), r=[("SF", nc_)], w=[("SB", nc_)])
            if pstore is not None:
                pstore(nc_)
    else:
        by2 = k.psum()
        for b_ in range(nseq):
            sf = [k.fpool.get() for _ in range(NC)]
            sb = [k.bpool.get() for _ in range(NC)]
            for nc_ in range(NC):
                sload(b_, nc_, sf[nc_])
                k.op("act", lambda e, nc_=nc_, sf=sf, sb=sb: e.activation(out=sb[nc_].t[:, 0:W], in_=sf[nc_].t[:, 0:W], func=AF.Copy),
                     r=[sf[nc_].key], w=[sb[nc_].key])
            qm = k.bpool.get()
            qmv = qm.t[:, 0:NC * T].rearrange("p (c t) -> p c t", c=NC)
            k.op("pool", lambda e, b_=b_, qmv=qmv: e.tensor_tensor(out=qmv, in0=qT[:, :, 0:T], in1=k.colmaskb[:, b_, :].unsqueeze(1).to_broadcast([128, NC, T]),
                                                                   op=ALU.mult), r=[qT_key, "colmaskb"], w=[qm.key])
            for nc_ in range(NC):
                k.mm(k.ps[by2][0:T, 0:W], qmv[:, nc_, :], sb[nc_].t[:, 0:W], (b_ == 0 and nc_ == 0), (b_ == nseq - 1 and nc_ == NC - 1),
                     r=[qm.key, sb[nc_].key], w=[("ps", by2)])
            qm.free()
            km = k.bpool.get()
            k.op("pool", lambda e, b_=b_, km=km: e.tensor_scalar_mul(out=km.t[0:T, 0:NC * 128], in0=ktok[0:T, 0:NC * 128], scalar1=k.c("rowmask_s")[0:T, b_:b_ + 1]),
                 r=[ktok_key, "CONST"], w=[km.key])
            for nc_ in range(NC):
                bd = k.psum()
                k.mm(k.ps[bd][0:128, 0:W], km.t[0:T, nc_ * 128:(nc_ + 1) * 128], vw.t[0:T, 0:W], True, True, r=[km.key, vw.key], w=[("ps", bd)])
                k.op("pool", lambda e, nc_=nc_, sf=sf, b_=b_: e.tensor_tensor(out=sf[nc_].t[:, 0:W].rearrange("p (r d) -> p r d", r=R),
                                                                             in0=sf[nc_].t[:, 0:W].rearrange("p (r d) -> p r d", r=R),
                                                                             in1=sm[0:128, 16 + b_ * R:16 + (b_ + 1) * R].unsqueeze(2).to_broadcast([128, R, Pd]),
                                                                             op=ALU.mult), r=[sf[nc_].key, smk], w=[sf[nc_].key])
                k.op("dve", lambda e, nc_=nc_, sf=sf, bd=bd: e.tensor_tensor(out=sf[nc_].t[:, 0:W], in0=sf[nc_].t[:, 0:W], in1=k.ps[bd][0:128, 0:W], op=ALU.add),
                     r=[sf[nc_].key, ("ps", bd)], w=[sf[nc_].key])
                k.pfree(bd)
                sstore(b_, nc_, sf[nc_])
            km.free()
            for x_ in sf + sb:
                x_.free()
        tmp = k.fpool.get()
        k.op("dve", lambda e: e.tensor_tensor(out=tmp.t[0:T, 0:W].rearrange("p (r d) -> p r d", r=R),
                                              in0=k.ps[by2][0:T, 0:W].rearrange("p (r d) -> p r d", r=R),
                                              in1=sm[0:T, 2 * R:3 * R].unsqueeze(2).to_broadcast([T, R, Pd]), op=ALU.mult),
             r=[("ps", by2), smk], w=[tmp.key])
        k.pfree(by2)
        k.op("dve", lambda e: e.tensor_tensor(out=y.t[0:T, 0:W], in0=tmp.t[0:T, 0:W], in1=k.ps[by1][0:T, 0:W], op=ALU.add),
             r=[tmp.key, ("ps", by1)], w=[y.key])
        tmp.free()
        k.pfree(by1)
    vw.free()
    return y


def proj_tok(k, t, slot, ncols, c_off=0):
    Wv = k.ring[slot][:, :].rearrange("p (k n) -> p k n", k=8)
    b = k.psum()
    for kk in range(8):
        k.mm(k.ps[b][0:t.T, 0:ncols], k.XNT[:, kk, t.c0:t.c0 + t.T], Wv[:, kk, c_off:c_off + ncols], kk == 0, kk == 7,
             r=[("XNT", t.i), ("ring", slot)], w=[("ps", b)])
    return b


def out_proj_add(k, t, ygT, nchunks, slot):
    T = t.T
    Wo = k.ring[slot][:, 0:nchunks * 1024].rearrange("p (c n) -> p c n", c=nchunks)
    yv = ygT.t[:, 0:nchunks * 128].rearrange("p (c t) -> p c t", c=nchunks)
    for half in range(2):
        b = k.psum()
        for c in range(nchunks):
            k.mm(k.ps[b][0:T, :], yv[:, c, 0:T], Wo[:, c, half * 512:(half + 1) * 512], c == 0, c == nchunks - 1,
                 r=[ygT.key, ("ring", slot)], w=[("ps", b)])
        k.op("dve", lambda e, b=b, half=half: e.tensor_tensor(out=k.X[0:T, t.i, half * 512:(half + 1) * 512], in0=k.X[0:T, t.i, half * 512:(half + 1) * 512],
                                                              in1=k.ps[b][0:T, :], op=ALU.add), r=[("ps", b), ("X", t.i)], w=[("X", t.i)])
        k.pfree(b)


def transpose_to(k, t, src, nchunks, eng="act"):
    T = t.T
    b = k.psum()
    pv = k.ps[b][:, :].bitcast(BF16).rearrange("p (c m) -> p c m", c=8)
    for c in range(nchunks):
        k.tr(pv[:, c, 0:T], src.t[0:T, c * 128:(c + 1) * 128], k.identb[0:T, 0:T], r=[src.key, "identb"], w=[("ps", b)])
    dst = k.bpool.get()
    dv = dst.t[:, 0:nchunks * 128].rearrange("p (c t) -> p c t", c=nchunks)
    if eng == "act":
        k.op("act", lambda e: e.activation(out=dv[:, :, 0:T], in_=pv[:, 0:nchunks, 0:T], func=AF.Copy), r=[("ps", b)], w=[dst.key])
    else:
        k.op("dve", lambda e: e.tensor_copy(out=dv[:, :, 0:T], in_=pv[:, 0:nchunks, 0:T]), r=[("ps", b)], w=[dst.key])
    k.pfree(b)
    return dst


def from_ret(k, layer):
    j = layer // 3
    win = k.din["ret_w_in"]
    wout = k.din["ret_w_out"]
    s_ret_in = k.din["state_ret"]
    for h in range(4):
        v8 = lambda tt: tt[:, :].rearrange("p (k n) -> p k n", k=8)
        s_qk = k.load_w([(lambda tt: v8(tt)[:, :, 0:256], win[j, :, h * 256:(h + 1) * 256].rearrange("(k p) n -> p k n", p=128)),
                         (lambda tt: v8(tt)[:, :, 256:512], win[j, :, 1024 + h * 256:1024 + (h + 1) * 256].rearrange("(k p) n -> p k n", p=128))])
        s_v = k.load_w([(v8, win[j, :, 2048 + h * 512:2048 + (h + 1) * 512].rearrange("(k p) n -> p k n", p=128))])
        s_g = k.load_w([(v8, win[j, :, 4096 + h * 512:4096 + (h + 1) * 512].rearrange("(k p) n -> p k n", p=128))])
        s_o = k.load_w([(lambda tt: tt[:, :].rearrange("p (c n) -> p c n", c=4), wout[j, h * 512:(h + 1) * 512, :].rearrange("(c p) n -> p c n", p=128))])
        preps = {}
        for kind in ("p", "s"):
            T_ = 128 if kind == "p" else TS
            preps[kind] = decay_prep(k, kind, k.c("laret")[0:T_, h:h + 1], ["CONST"], 1)

        def stage_a(t):
            T = t.T
            bqk = proj_tok(k, t, s_qk, 512)
            qk = k.fpool.get()
            k.op("act", lambda e: e.activation(out=qk.t[0:T, 0:256], in_=k.ps[bqk][0:T, 0:256], func=AF.Copy), r=[("ps", bqk)], w=[qk.key])
            k.op("act", lambda e: e.activation(out=qk.t[0:T, 256:512], in_=k.ps[bqk][0:T, 256:512], func=AF.Copy, scale=1.0 / 16.0),
                 r=[("ps", bqk)], w=[qk.key])
            k.pfree(bqk)
            cs = k.fpool.get()
            k.dma(cs.t[0:T, 0:256], k.din["ropecs"][t.i, 0:T, :, :].rearrange("p a b -> p (a b)"), w=[cs.key])
            qv = qk.t[0:T, :].rearrange("p (a h m) -> p a h m", a=2, h=2)
            x1, x2 = qv[:, :, 0, :], qv[:, :, 1, :]
            cosb = cs.t[0:T, 0:128].unsqueeze(1).to_broadcast([T, 2, 128])
            sinb = cs.t[0:T, 128:256].unsqueeze(1).to_broadcast([T, 2, 128])
            ta = k.fpool.get()
            tb = k.fpool.get()
            tav = ta.t[0:T, :].rearrange("p (u a m) -> p u a m", u=2, a=2)
            tbv = tb.t[0:T, :].rearrange("p (u a m) -> p u a m", u=2, a=2)
            k.op("pool", lambda e: e.tensor_tensor(out=tav[:, 0], in0=x1, in1=cosb, op=ALU.mult), r=[qk.key, cs.key], w=[ta.key])
            k.op("pool", lambda e: e.tensor_tensor(out=tav[:, 1], in0=x2, in1=sinb, op=ALU.mult), r=[qk.key, cs.key], w=[ta.key])
            k.op("pool", lambda e: e.tensor_tensor(out=tbv[:, 0], in0=x1, in1=sinb, op=ALU.mult), r=[qk.key, cs.key], w=[tb.key])
            k.op("pool", lambda e: e.tensor_tensor(out=tbv[:, 1], in0=x2, in1=cosb, op=ALU.mult), r=[qk.key, cs.key], w=[tb.key])
            rot = k.bpool.get()
            rv = rot.t[0:T, :].rearrange("p (a h m) -> p a h m", a=2, h=2)
            k.op("dve", lambda e: e.tensor_tensor(out=rv[:, :, 0, :], in0=tav[:, 0], in1=tav[:, 1], op=ALU.subtract), r=[ta.key], w=[rot.key])
            k.op("dve", lambda e: e.tensor_tensor(out=rv[:, :, 1, :], in0=tbv[:, 0], in1=tbv[:, 1], op=ALU.add), r=[tb.key], w=[rot.key])
            for x_ in (qk, cs, ta, tb):
                x_.free()
            qkT = transpose_to(k, t, rot, 4)
            bv = proj_tok(k, t, s_v, 512)
            vb = k.bpool.get()
            k.op("act", lambda e: e.activation(out=vb.t[0:T, :], in_=k.ps[bv][0:T, :], func=AF.Copy), r=[("ps", bv)], w=[vb.key])
            k.pfree(bv)
            bg = proj_tok(k, t, s_g, 512)
            sg = k.bpool.get()
            k.op("act", lambda e: e.activation(out=sg.t[0:T, :], in_=k.ps[bg][0:T, :], func=AF.Silu), r=[("ps", bg)], w=[sg.key])
            k.pfree(bg)
            return {"rot": rot, "qkT": qkT, "v": vb, "sg": sg}

        def stage_b(t, A):
            T = t.T
            qkv = A["qkT"].t[:, 0:512].rearrange("p (c t) -> p c t", c=4)
            first = (t.i == 0)

            def sload(b_, nc_, dst):
                k.dma(dst.t[:, :], s_ret_in[j, b_, h, nc_ * 128:(nc_ + 1) * 128, :], w=[dst.key])

            def sstore(b_, nc_, src):
                k.dma(k.dout["s_ret"][j, b_, h, nc_ * 128:(nc_ + 1) * 128, :], src.t[:, :], r=[src.key])

            pstore = None
            if t.i == NPT - 1:
                def pstore(nc_):
                    k.dma(k.dout["p_ret"][j, h, nc_ * 128:(nc_ + 1) * 128, :], k.SF[:, nc_, :], r=[("SF", nc_)])
            y = scan_tile(k, t, preps[t.kind], qkv[:, 0:2, :], A["qkT"].key, qkv[:, 2:4, :], A["qkT"].key,
                          A["rot"].t[:, 256:512], A["rot"].key, A["v"].t, A["v"].key, 1, 512, 2, first, sload, sstore, pstore)
            stt = k.st[k.rr("st", 4)]
            emit_rstd(k, t, stt, y.t[0:T, 0:512], [y.key], 512, 0)
            yg = k.bpool.get()
            k.op("dve", lambda e: e.scalar_tensor_tensor(out=yg.t[0:T, :], in0=y.t[0:T, :], scalar=stt[0:T, 1:2], in1=A["sg"].t[0:T, :],
                                                          op0=ALU.mult, op1=ALU.mult), r=[y.key, ("st", id(stt)), A["sg"].key], w=[yg.key])
            y.free()
            for nm in ("rot", "qkT", "v", "sg"):
                A[nm].free()
            ygT = transpose_to(k, t, yg, 4)
            yg.free()
            out_proj_add(k, t, ygT, 4, s_o)
            ygT.free()

        A = stage_a(TILES[0])
        for ti in range(NT):
            An = stage_a(TILES[ti + 1]) if ti + 1 < NT else None
            stage_b(TILES[ti], A)
            A = An
        for kind in ("p", "s"):
            preps[kind]["dec"].free()


def silu_to(k, out_ap, in_ap, r, w):
    k.op("act", lambda e: e.activation(out=out_ap, in_=in_ap, func=AF.Silu), r=r, w=w)


def load_conv_params(k, convw, convb):
    k.dma(k.PAR[0:16, :], convw.rearrange("t (q n) -> (t q) n", q=4), w=["PAR"])
    if convb is not None:
        k.dma(k.PAR[32:36, :], convb.rearrange("o (q n) -> (o q) n", q=4), w=["PAR"])
    b = k.psum()
    pv = k.ps[b][:, 0:128].rearrange("p (c x) -> p c x", c=8)
    for c8 in range(8):
        k.tr(pv[:, c8, :], k.PAR[0:16, c8 * 128:(c8 + 1) * 128], k.c("ident")[0:16, 0:16], r=["PAR", "CONST"], w=[("ps", b)])
    for q in range(4):
        k.op("dve", lambda e, q=q: e.tensor_copy(out=k.CW[:, q * 8:(q + 1) * 8, :], in_=pv.rearrange("p c (t q) -> p c t q", q=4)[:, :, :, q]),
             r=[("ps", b)], w=["CW"])
    k.pfree(b)
    if convb is not None:
        b = k.psum()
        pv2 = k.ps[b][:, 0:32].rearrange("p (c q) -> p c q", c=8)
        for c8 in range(8):
            k.tr(pv2[:, c8, :], k.PAR[32:36, c8 * 128:(c8 + 1) * 128], k.c("ident")[32:36, 32:36], r=["PAR", "CONST"], w=[("ps", b)])
        k.op("dve", lambda e: e.tensor_copy(out=k.CB[:, :].rearrange("p (q c) -> p c q", q=4), in_=pv2), r=[("ps", b)], w=["CB"])
        k.pfree(b)


def conv_feature_major(k, t, slot, chunk_ids, halo_src, first, conv_in, has_bias):
    T = t.T
    Wv = k.ring[slot][:, :].rearrange("p (k n) -> p k n", k=8)
    b = k.psum()
    for c in range(4):
        for kk in range(8):
            k.mm(k.ps[b][:, c * T:(c + 1) * T], Wv[:, kk, c * 128:(c + 1) * 128], k.XNT[:, kk, t.c0:t.c0 + T], kk == 0, kk == 7,
                 r=[("ring", slot), ("XNT", t.i)], w=[("ps", b)])
    ui = k.rr("UB", 2)
    U = k.UB[ui]
    uk = ("UB", ui)
    acc = k.fpool.get()
    if not t.sample:
        Uv = U[:, 0:4 * 131].rearrange("p (c x) -> p c x", c=4)
        k.op("act", lambda e: e.activation(out=Uv[:, :, 3:131], in_=k.ps[b][:, 0:512].rearrange("p (c x) -> p c x", c=4), func=AF.Copy),
             r=[("ps", b)], w=[uk])
        if first:
            k.op("pool", lambda e: e.memset(Uv[:, :, 0:3], 0.0), w=[uk])
        else:
            pU = k.UB[1 - ui][:, 0:4 * 131].rearrange("p (c x) -> p c x", c=4)
            k.op("pool", lambda e: e.tensor_copy(out=Uv[:, :, 0:3], in_=pU[:, :, 128:131]), r=[("UB", 1 - ui)], w=[uk])
        accv = acc.t[:, 0:512].rearrange("p (c x) -> p c x", c=4)
        srcs = lambda c, tap: Uv[:, c, tap:tap + 128]
        outs = lambda c: accv[:, c, :]
    else:
        Uv = U[:, 0:4 * 112].rearrange("p (c b x) -> p c b x", c=4, b=16)
        k.op("act", lambda e: e.activation(out=Uv[:, :, :, 3:7], in_=k.ps[b][:, 0:256].rearrange("p (c b x) -> p c b x", c=4, b=16), func=AF.Copy),
             r=[("ps", b)], w=[uk])
        stg = k.fpool.get()
        for (c0_, n_, ch0) in conv_in["segs"]:
            k.dma(stg.t[0:48, c0_:c0_ + n_], conv_in["src"][:, :, ch0:ch0 + n_].rearrange("b j c -> (b j) c"), w=[stg.key])
        bh = k.psum()
        ph = k.ps[bh][:, 0:192].rearrange("p (c x) -> p c x", c=4)
        for c in range(4):
            k.tr(ph[:, c, :], stg.t[0:48, c * 128:(c + 1) * 128], k.c("ident")[0:48, 0:48], r=[stg.key, "CONST"], w=[("ps", bh)])
        stg.free()
        k.op("act", lambda e: e.activation(out=Uv[:, :, :, 0:3], in_=ph.rearrange("p c (b x) -> p c b x", b=16), func=AF.Copy), r=[("ps", bh)], w=[uk])
        k.pfree(bh)
        accv = acc.t[:, 0:256].rearrange("p (c b x) -> p c b x", c=4, b=16)
        srcs = lambda c, tap: Uv[:, c, :, tap:tap + 4]
        outs = lambda c: accv[:, c, :, :]
    k.pfree(b)
    for c in range(4):
        ch = chunk_ids[c]
        eng = "dve"
        if has_bias:
            k.op(eng, lambda e, c=c, ch=ch: e.tensor_scalar(out=outs(c), in0=srcs(c, 0), scalar1=k.CW[:, ch, 0:1], scalar2=k.CB[:, ch:ch + 1],
                                                            op0=ALU.mult, op1=ALU.add), r=[uk, "CW", "CB"], w=[acc.key])
        else:
            k.op(eng, lambda e, c=c, ch=ch: e.tensor_scalar_mul(out=outs(c), in0=srcs(c, 0), scalar1=k.CW[:, ch, 0:1]), r=[uk, "CW"], w=[acc.key])
        for tap in range(1, 4):
            k.op(eng, lambda e, c=c, ch=ch, tap=tap: e.scalar_tensor_tensor(out=outs(c), in0=srcs(c, tap), scalar=k.CW[:, ch, tap:tap + 1], in1=outs(c),
                                                                           op0=ALU.mult, op1=ALU.add), r=[uk, "CW", acc.key], w=[acc.key])
    return acc


def conv_state_out(k, t, slot, segs, dst_p, dst_s):
    T = t.T
    b = proj_tok(k, t, slot, 512)
    stg = k.fpool.get()
    k.op("act", lambda e: e.activation(out=stg.t[0:T, :], in_=k.ps[b][0:T, :], func=AF.Copy), r=[("ps", b)], w=[stg.key])
    k.pfree(b)
    for (c0_, n_, ch0) in segs:
        if not t.sample:
            k.dma(dst_p[0:3, ch0:ch0 + n_], stg.t[125:128, c0_:c0_ + n_], r=[stg.key])
        else:
            for jj in range(3):
                k.dma(dst_s[:, jj, ch0:ch0 + n_], stg.t[1 + jj:64:4, c0_:c0_ + n_], r=[stg.key])
    stg.free()


def from_ssd(k, layer):
    win = k.din["ssd_w_in"]
    wout = k.din["ssd_w_out"]
    k.dma(k.PRM[:, 0:32], k.din["ssd_dt_bias"][0, :].partition_broadcast(128), w=["PRM"])
    k.dma(k.PRM[:, 32:64], k.din["ssd_a_log"][0, :].partition_broadcast(128), w=["PRM"])
    k.dma(k.PRM[:, 64:96], k.din["ssd_d"][0, :].partition_broadcast(128), w=["PRM"])
    k.op("act", lambda e: e.activation(out=k.PRM[:, 96:128], in_=k.PRM[:, 32:64], func=AF.Exp), r=["PRM"], w=["PRM"])
    k.op("dve", lambda e: e.tensor_scalar(out=k.PRM[:, 96:128], in0=k.PRM[:, 96:128], scalar1=-1.0, scalar2=None, op0=ALU.mult), r=["PRM"], w=["PRM"])
    load_conv_params(k, k.din["ssd_conv_w"], k.din["ssd_conv_b"])
    for g in range(8):
        v8 = lambda tt: tt[:, :].rearrange("p (k n) -> p k n", k=8)
        segs = [(0, 256, g * 256), (256, 128, 2048 + g * 128), (384, 128, 3072 + g * 128)]
        s_x = k.load_w([((lambda tt, c0_=c0_, n_=n_: v8(tt)[:, :, c0_:c0_ + n_]),
                         win[:, 2048 + ch0:2048 + ch0 + n_].rearrange("(k p) n -> p k n", p=128)) for (c0_, n_, ch0) in segs])
        s_z = k.load_w([(lambda tt: v8(tt)[:, :, 0:256], win[:, g * 256:(g + 1) * 256].rearrange("(k p) n -> p k n", p=128)),
                        (lambda tt: v8(tt)[:, :, 256:260], win[:, 6144 + 4 * g:6144 + 4 * g + 4].rearrange("(k p) n -> p k n", p=128))])
        s_o = k.load_w([(lambda tt: tt[:, 0:2048].rearrange("p (c n) -> p c n", c=2), wout[g * 256:(g + 1) * 256, :].rearrange("(c p) n -> p c n", p=128))])
        nwb = k.fpool.get()
        k.dma(nwb.t[:, 0:256], k.din["ssd_norm"][0, g * 256:(g + 1) * 256].partition_broadcast(128), w=[nwb.key])
        chunk_ids = [2 * g, 2 * g + 1, 16 + g, 24 + g]
        conv_in = {"src": k.din["state_ssd_conv"], "segs": segs}

        def stage_a(t):
            T = t.T
            acc = conv_feature_major(k, t, s_x, chunk_ids, None, t.i == 0, conv_in, True)
            n4 = 4 * T
            xsT = k.fpool.get()
            bcT = k.bpool.get()
            silu_to(k, xsT.t[:, 0:2 * T], acc.t[:, 0:2 * T], [acc.key], [xsT.key])
            silu_to(k, bcT.t[:, 0:2 * T], acc.t[:, 2 * T:4 * T], [acc.key], [bcT.key])
            acc.free()
            b = k.psum()
            for c in range(2):
                k.tr(k.ps[b][0:T, c * 128:(c + 1) * 128], xsT.t[:, c * T:(c + 1) * T], k.c("ident"), r=[xsT.key, "CONST"], w=[("ps", b)])
            xs = k.fpool.get()
            k.op("act", lambda e: e.activation(out=xs.t[0:T, 0:256], in_=k.ps[b][0:T, 0:256], func=AF.Copy), r=[("ps", b)], w=[xs.key])
            k.pfree(b)
            xsT.free()
            b = k.psum()
            pvb = k.ps[b][:, :].bitcast(BF16)
            k.tr(pvb[0:T, 0:128], bcT.t[:, 0:T], k.identb[:, :], r=[bcT.key, "identb"], w=[("ps", b)])
            btok = k.bpool.get()
            k.op("dve", lambda e: e.tensor_copy(out=btok.t[0:T, 0:128], in_=pvb[0:T, 0:128]), r=[("ps", b)], w=[btok.key])
            k.pfree(b)
            bz = proj_tok(k, t, s_z, 260)
            sz = k.bpool.get()
            silu_to(k, sz.t[0:T, 0:256], k.ps[bz][0:T, 0:256], [("ps", bz)], [sz.key])
            li = k.rr("LAB", 4)
            lab = k.LAB[li]
            lk = ("LAB", li)
            k.op("dve", lambda e: e.tensor_tensor(out=lab[0:T, 0:4], in0=k.ps[bz][0:T, 256:260], in1=k.PRM[0:T, 4 * g:4 * g + 4], op=ALU.add),
                 r=[("ps", bz), "PRM"], w=[lk])
            k.pfree(bz)
            k.op("act", lambda e: e.activation(out=lab[0:T, 0:4], in_=lab[0:T, 0:4], func=AF.Exp), r=[lk], w=[lk])
            k.op("pool", lambda e: e.tensor_scalar(out=lab[0:T, 0:4], in0=lab[0:T, 0:4], scalar1=1.0, scalar2=None, op0=ALU.add), r=[lk], w=[lk])
            k.op("act", lambda e: e.activation(out=lab[0:T, 0:4], in_=lab[0:T, 0:4], func=AF.Ln), r=[lk], w=[lk])
            k.op("dve", lambda e: e.tensor_tensor(out=lab[0:T, 4:8], in0=lab[0:T, 0:4], in1=k.PRM[0:T, 96 + 4 * g:96 + 4 * g + 4], op=ALU.mult),
                 r=[lk, "PRM"], w=[lk])
            vdt = k.bpool.get()
            k.op("pool", lambda e: e.tensor_tensor(out=vdt.t[0:T, 0:256].rearrange("p (r d) -> p r d", r=4), in0=xs.t[0:T, 0:256].rearrange("p (r d) -> p r d", r=4),
                                                   in1=lab[0:T, 0:4].unsqueeze(2).to_broadcast([T, 4, 64]), op=ALU.mult), r=[xs.key, lk], w=[vdt.key])
            return {"bcT": bcT, "xs": xs, "btok": btok, "sz": sz, "lab": lab, "lk": lk, "vdt": vdt}

        def stage_b(t, A):
            T = t.T
            prep = decay_prep(k, t.kind, A["lab"][0:T, 4:8], [A["lk"]], 4)
            bcv = A["bcT"].t[:, 0:2 * T].rearrange("p (c t) -> p c t", c=2)

            def sload(b_, nc_, dst):
                stg = k.fpool.get()
                k.dma(stg.t[:, 0:256].rearrange("p (j n) -> p j n", j=2),
                      k.din["state_ssd"][b_, 4 * g:4 * g + 4, :, :].rearrange("(j h) p n -> (h p) j n", j=2), w=[stg.key])
                bb = k.psum()
                for jj in range(2):
                    k.tr(k.ps[bb][:, jj * 128:(jj + 1) * 128], stg.t[:, jj * 128:(jj + 1) * 128], k.c("ident"), r=[stg.key, "CONST"], w=[("ps", bb)])
                stg.free()
                k.op("dve", lambda e: e.tensor_copy(out=dst.t[:, 0:256], in_=k.ps[bb][:, 0:256]), r=[("ps", bb)], w=[dst.key])
                k.pfree(bb)

            def st_out(src_ap, src_key, dst_ap):
                bb = k.psum()
                for jj in range(2):
                    k.tr(k.ps[bb][:, jj * 128:(jj + 1) * 128], src_ap[:, jj * 128:(jj + 1) * 128], k.c("ident"), r=[src_key, "CONST"], w=[("ps", bb)])
                stg = k.fpool.get()
                k.op("act", lambda e: e.activation(out=stg.t[:, 0:256], in_=k.ps[bb][:, 0:256], func=AF.Copy), r=[("ps", bb)], w=[stg.key])
                k.pfree(bb)
                k.dma(dst_ap.rearrange("(j h) p n -> (h p) j n", j=2), stg.t[:, 0:256].rearrange("p (j n) -> p j n", j=2), r=[stg.key])
                stg.free()

            def sstore(b_, nc_, src):
                st_out(src.t[:, 0:256], src.key, k.dout["s_ssd"][b_, 4 * g:4 * g + 4, :, :])

            pstore = None
            if t.i == NPT - 1:
                def pstore(nc_):
                    st_out(k.SF[:, 0, 0:256], ("SF", 0), k.dout["p_ssd"][4 * g:4 * g + 4, :, :])
            y = scan_tile(k, t, prep, bcv[:, 1:2, :], A["bcT"].key, bcv[:, 0:1, :], A["bcT"].key, A["btok"].t[:, 0:128], A["btok"].key,
                          A["vdt"].t, A["vdt"].key, 4, 64, 1, t.i == 0, sload, sstore, pstore)
            prep["dec"].free()
            tmp = k.fpool.get()
            k.op("pool", lambda e: e.tensor_tensor(out=tmp.t[0:T, 0:256].rearrange("p (r d) -> p r d", r=4), in0=A["xs"].t[0:T, 0:256].rearrange("p (r d) -> p r d", r=4),
                                                   in1=k.PRM[0:T, 64 + 4 * g:64 + 4 * g + 4].unsqueeze(2).to_broadcast([T, 4, 64]), op=ALU.mult),
                 r=[A["xs"].key, "PRM"], w=[tmp.key])
            k.op("dve", lambda e: e.tensor_tensor(out=y.t[0:T, 0:256], in0=y.t[0:T, 0:256], in1=tmp.t[0:T, 0:256], op=ALU.add), r=[y.key, tmp.key], w=[y.key])
            tmp.free()
            k.op("dve", lambda e: e.tensor_tensor(out=y.t[0:T, 0:256], in0=y.t[0:T, 0:256], in1=A["sz"].t[0:T, 0:256], op=ALU.mult), r=[y.key, A["sz"].key], w=[y.key])
            stt = k.st[k.rr("st", 4)]
            emit_rstd(k, t, stt, y.t[0:T, 0:256], [y.key], 256, 0)
            yn = k.bpool.get()
            k.op("dve", lambda e: e.scalar_tensor_tensor(out=yn.t[0:T, 0:256], in0=y.t[0:T, 0:256], scalar=stt[0:T, 1:2], in1=nwb.t[0:T, 0:256],
                                                          op0=ALU.mult, op1=ALU.mult), r=[y.key, ("st", id(stt)), nwb.key], w=[yn.key])
            y.free()
            for nm in ("bcT", "xs", "btok", "sz", "vdt"):
                A[nm].free()
            ynT = transpose_to(k, t, yn, 2)
            yn.free()
            out_proj_add(k, t, ynT, 2, s_o)
            ynT.free()
            if t.i >= NPT - 1:
                conv_state_out(k, t, s_x, segs, k.dout["p_ssd_conv"], k.dout["s_ssd_conv"])

        A = stage_a(TILES[0])
        for ti in range(NT):
            An = stage_a(TILES[ti + 1]) if ti + 1 < NT else None
            stage_b(TILES[ti], A)
            A = An
        nwb.free()


def gdn_chain(k, T, A, nlev):
    identb = k.identb[0:T, 0:T]
    v3 = lambda buf: buf.t[0:T, 0:2 * T].rearrange("p (r t) -> p r t", r=2)
    sl = lambda buf, r_: buf.t[0:T, r_ * T:(r_ + 1) * T]
    b = k.psum()
    pvb = k.ps[b][:, :].bitcast(BF16)
    for r_ in range(2):
        k.tr(pvb[0:T, r_ * T:(r_ + 1) * T], sl(A, r_), identb, r=[A.key, "identb"], w=[("ps", b)])
    AT = k.bpool.get()
    k.op("act", lambda e: e.activation(out=AT.t[0:T, 0:2 * T], in_=pvb[0:T, 0:2 * T], func=AF.Copy), r=[("ps", b)], w=[AT.key])
    k.pfree(b)
    Dm = None
    DT = None
    for lv in range(nlev):
        last = (lv == nlev - 1)
        mT = k.c("bmT%d" % lv)[0:T, 0:T]
        AoT = k.bpool.get()
        k.op("pool", lambda e: e.tensor_tensor(out=v3(AoT), in0=v3(AT), in1=mT.unsqueeze(1).to_broadcast([T, 2, T]), op=ALU.mult),
             r=[AT.key, "CONST"], w=[AoT.key])
        dk = [Dm.key] if Dm is not None else ["identb"]
        dtk = [DT.key] if DT is not None else ["identb"]
        Dv = (lambda r_: sl(Dm, r_)) if Dm is not None else (lambda r_: identb)
        DTv = (lambda r_: sl(DT, r_)) if DT is not None else (lambda r_: identb)
        be = k.psum()
        for r_ in range(2):
            k.mm(k.ps[be][0:T, r_ * T:(r_ + 1) * T], sl(AoT, r_), Dv(r_), True, True, r=[AoT.key] + dk, w=[("ps", be)])
        E = k.bpool.get()
        k.op("act", lambda e: e.activation(out=E.t[0:T, 0:2 * T], in_=k.ps[be][0:T, 0:2 * T], func=AF.Copy), r=[("ps", be)], w=[E.key])
        k.pfree(be)
        AoT.free()
        bft = k.psum()
        for r_ in range(2):
            k.mm(k.ps[bft][0:T, r_ * T:(r_ + 1) * T], sl(E, r_), DTv(r_), True, True, r=[E.key] + dtk, w=[("ps", bft)])
        DTn = k.bpool.get()
        if DT is not None:
            k.op("dve", lambda e: e.tensor_tensor(out=DTn.t[0:T, 0:2 * T], in0=DT.t[0:T, 0:2 * T], in1=k.ps[bft][0:T, 0:2 * T], op=ALU.add),
                 r=[DT.key, ("ps", bft)], w=[DTn.key])
        else:
            k.op("dve", lambda e: e.tensor_tensor(out=v3(DTn), in0=k.ps[bft][0:T, 0:2 * T].rearrange("p (r t) -> p r t", r=2),
                                                  in1=identb.unsqueeze(1).to_broadcast([T, 2, T]), op=ALU.add), r=["identb", ("ps", bft)], w=[DTn.key])
        k.pfree(bft)
        Dn = None
        if not last:
            bf_ = k.psum()
            for r_ in range(2):
                k.mm(k.ps[bf_][0:T, r_ * T:(r_ + 1) * T], DTv(r_), sl(E, r_), True, True, r=[E.key] + dtk, w=[("ps", bf_)])
            Dn = k.bpool.get()
            if Dm is not None:
                k.op("dve", lambda e: e.tensor_tensor(out=Dn.t[0:T, 0:2 * T], in0=Dm.t[0:T, 0:2 * T], in1=k.ps[bf_][0:T, 0:2 * T], op=ALU.add),
                     r=[Dm.key, ("ps", bf_)], w=[Dn.key])
            else:
                k.op("dve", lambda e: e.tensor_tensor(out=v3(Dn), in0=k.ps[bf_][0:T, 0:2 * T].rearrange("p (r t) -> p r t", r=2),
                                                      in1=identb.unsqueeze(1).to_broadcast([T, 2, T]), op=ALU.add), r=["identb", ("ps", bf_)], w=[Dn.key])
            k.pfree(bf_)
        E.free()
        if Dm is not None:
            Dm.free()
        if DT is not None:
            DT.free()
        Dm, DT = Dn, DTn
    AT.free()
    return DT


def from_gdn(k, layer):
    win = k.din["gdn_w_in"]
    wout = k.din["gdn_w_out"]
    k.dma(k.PRM[:, 0:16], k.din["gdn_dt_bias"][0, :].partition_broadcast(128), w=["PRM"])
    k.dma(k.PRM[:, 16:32], k.din["gdn_a_log"][0, :].partition_broadcast(128), w=["PRM"])
    k.op("act", lambda e: e.activation(out=k.PRM[:, 32:48], in_=k.PRM[:, 16:32], func=AF.Exp), r=["PRM"], w=["PRM"])
    k.op("dve", lambda e: e.tensor_scalar(out=k.PRM[:, 32:48], in0=k.PRM[:, 32:48], scalar1=-1.0, scalar2=None, op0=ALU.mult), r=["PRM"], w=["PRM"])
    load_conv_params(k, k.din["gdn_conv_w"], None)
    gnw = k.fpool.get()
    k.dma(gnw.t[:, 0:128], k.din["gdn_norm"][0, :].partition_broadcast(128), w=[gnw.key])
    for kh in range(8):
        v8 = lambda tt: tt[:, :].rearrange("p (k n) -> p k n", k=8)
        segs = [(0, 128, kh * 128), (128, 128, 1024 + kh * 128), (256, 256, 2048 + kh * 256)]
        s_x = k.load_w([((lambda tt, c0_=c0_, n_=n_: v8(tt)[:, :, c0_:c0_ + n_]),
                         win[:, ch0:ch0 + n_].rearrange("(k p) n -> p k n", p=128)) for (c0_, n_, ch0) in segs])
        s_z = k.load_w([(lambda tt: v8(tt)[:, :, 0:256], win[:, 4096 + kh * 256:4096 + (kh + 1) * 256].rearrange("(k p) n -> p k n", p=128)),
                        (lambda tt: v8(tt)[:, :, 256:258], win[:, 6144 + 2 * kh:6144 + 2 * kh + 2].rearrange("(k p) n -> p k n", p=128)),
                        (lambda tt: v8(tt)[:, :, 258:260], win[:, 6160 + 2 * kh:6160 + 2 * kh + 2].rearrange("(k p) n -> p k n", p=128))])
        s_o = k.load_w([(lambda tt: tt[:, 0:2048].rearrange("p (c n) -> p c n", c=2), wout[kh * 256:(kh + 1) * 256, :].rearrange("(c p) n -> p c n", p=128))])
        chunk_ids = [kh, 8 + kh, 16 + 2 * kh, 17 + 2 * kh]
        conv_in = {"src": k.din["state_gdn_conv"], "segs": segs}

        def stage_a(t):
            T = t.T
            acc = conv_feature_major(k, t, s_x, chunk_ids, None, t.i == 0, conv_in, False)
            act4 = k.fpool.get()
            silu_to(k, act4.t[:, 0:4 * T], acc.t[:, 0:4 * T], [acc.key], [act4.key])
            acc.free()
            sq = k.fpool.get()
            k.op("pool", lambda e: e.tensor_tensor(out=sq.t[:, 0:2 * T], in0=act4.t[:, 0:2 * T], in1=act4.t[:, 0:2 * T], op=ALU.mult), r=[act4.key], w=[sq.key])
            b = k.psum()
            k.mm(k.ps[b][:, 0:2 * T], k.c("ones"), sq.t[:, 0:2 * T], True, True, r=["CONST", sq.key], w=[("ps", b)])
            k.op("dve", lambda e: e.tensor_scalar(out=sq.t[:, 0:2 * T], in0=k.ps[b][:, 0:2 * T], scalar1=RMS_EPS, scalar2=None, op0=ALU.add),
                 r=[("ps", b)], w=[sq.key])
            k.pfree(b)
            k.op("act", lambda e: e.activation(out=sq.t[:, 0:2 * T], in_=sq.t[:, 0:2 * T], func=AF.Ln), r=[sq.key], w=[sq.key])
            k.op("act", lambda e: e.activation(out=sq.t[:, 0:2 * T], in_=sq.t[:, 0:2 * T], func=AF.Exp, scale=-0.5), r=[sq.key], w=[sq.key])
            qkn = k.bpool.get()
            k.op("dve", lambda e: e.scalar_tensor_tensor(out=qkn.t[:, 0:T], in0=act4.t[:, 0:T], scalar=float(128.0 ** -0.5), in1=sq.t[:, 0:T],
                                                          op0=ALU.mult, op1=ALU.mult), r=[act4.key, sq.key], w=[qkn.key])
            k.op("pool", lambda e: e.tensor_tensor(out=qkn.t[:, T:2 * T], in0=act4.t[:, T:2 * T], in1=sq.t[:, T:2 * T], op=ALU.mult),
                 r=[act4.key, sq.key], w=[qkn.key])
            sq.free()
            b = k.psum()
            pvb = k.ps[b][:, :].bitcast(BF16)
            k.tr(pvb[0:T, 0:128], qkn.t[:, T:2 * T], k.identb[:, :], r=[qkn.key, "identb"], w=[("ps", b)])
            ktok = k.bpool.get()
            k.op("dve", lambda e: e.tensor_copy(out=ktok.t[0:T, 0:128], in_=pvb[0:T, 0:128]), r=[("ps", b)], w=[ktok.key])
            k.pfree(b)
            b = k.psum()
            for c in range(2):
                k.tr(k.ps[b][0:T, c * 128:(c + 1) * 128], act4.t[:, (2 + c) * T:(3 + c) * T], k.c("ident"), r=[act4.key, "CONST"], w=[("ps", b)])
            vtok = k.fpool.get()
            k.op("act", lambda e: e.activation(out=vtok.t[0:T, 0:256], in_=k.ps[b][0:T, 0:256], func=AF.Copy), r=[("ps", b)], w=[vtok.key])
            k.pfree(b)
            act4.free()
            bz = proj_tok(k, t, s_z, 260)
            sz = k.bpool.get()
            silu_to(k, sz.t[0:T, 0:256], k.ps[bz][0:T, 0:256], [("ps", bz)], [sz.key])
            li = k.rr("LAB", 4)
            lab = k.LAB[li]
            lk = ("LAB", li)
            k.op("act", lambda e: e.activation(out=lab[0:T, 0:2], in_=k.ps[bz][0:T, 256:258], func=AF.Exp, scale=-1.0), r=[("ps", bz)], w=[lk])
            k.op("dve", lambda e: e.tensor_tensor(out=lab[0:T, 4:6], in0=k.ps[bz][0:T, 258:260], in1=k.PRM[0:T, 2 * kh:2 * kh + 2], op=ALU.add),
                 r=[("ps", bz), "PRM"], w=[lk])
            k.pfree(bz)
            k.op("pool", lambda e: e.tensor_scalar(out=lab[0:T, 0:2], in0=lab[0:T, 0:2], scalar1=1.0, scalar2=None, op0=ALU.add), r=[lk], w=[lk])
            k.op("dve", lambda e: e.reciprocal(out=lab[0:T, 0:2], in_=lab[0:T, 0:2]), r=[lk], w=[lk])
            k.op("dve", lambda e: e.tensor_scalar(out=lab[0:T, 2:4], in0=lab[0:T, 0:2], scalar1=-1.0, scalar2=None, op0=ALU.mult), r=[lk], w=[lk])
            k.op("act", lambda e: e.activation(out=lab[0:T, 4:6], in_=lab[0:T, 4:6], func=AF.Exp), r=[lk], w=[lk])
            k.op("pool", lambda e: e.tensor_scalar(out=lab[0:T, 4:6], in0=lab[0:T, 4:6], scalar1=1.0, scalar2=None, op0=ALU.add), r=[lk], w=[lk])
            k.op("act", lambda e: e.activation(out=lab[0:T, 4:6], in_=lab[0:T, 4:6], func=AF.Ln), r=[lk], w=[lk])
            k.op("dve", lambda e: e.tensor_tensor(out=lab[0:T, 4:6], in0=lab[0:T, 4:6], in1=k.PRM[0:T, 32 + 2 * kh:32 + 2 * kh + 2], op=ALU.mult),
                 r=[lk, "PRM"], w=[lk])
            return {"qkn": qkn, "ktok": ktok, "vtok": vtok, "sz": sz, "lab": lab, "lk": lk}

        def stage_b(t, A_):
            T = t.T
            nseq = t.nseq
            first = (t.i == 0)
            lab, lk = A_["lab"], A_["lk"]
            qkn, ktok, vtok = A_["qkn"], A_["ktok"], A_["vtok"]
            qT = qkn.t[:, 0:T]
            kT = qkn.t[:, T:2 * T]
            prep = decay_prep(k, t.kind, lab[0:T, 4:6], [lk], 2)
            sm, smk, dec = prep["sm"], prep["smk"], prep["dec"]
            strict = kconst(k, t.kind, "strict")
            bg = k.psum()
            k.mm(k.ps[bg][0:T, 0:T], kT, kT, True, True, r=[qkn.key], w=[("ps", bg)])
            k.mm(k.ps[bg][0:T, T:2 * T], kT, qT, True, True, r=[qkn.key], w=[("ps", bg)])
            bd_ = k.psum()
            for r_ in range(2):
                k.tr(k.ps[bd_][0:T, r_ * T:(r_ + 1) * T], dec.t[0:T, r_ * T:(r_ + 1) * T], k.c("ident")[0:T, 0:T], r=[dec.key, "CONST"], w=[("ps", bd_)])
            dsb = k.fpool.get()
            k.op("dve", lambda e: e.tensor_tensor(out=dsb.t[0:T, 0:2 * T].rearrange("p (r t) -> p r t", r=2),
                                                  in0=k.ps[bd_][0:T, 0:2 * T].rearrange("p (r t) -> p r t", r=2),
                                                  in1=strict.unsqueeze(1).to_broadcast([T, 2, T]), op=ALU.mult), r=[("ps", bd_), "CONST"], w=[dsb.key])
            k.pfree(bd_)
            Am = k.bpool.get()
            for r_ in range(2):
                k.op("dve", lambda e, r_=r_: e.scalar_tensor_tensor(out=Am.t[0:T, r_ * T:(r_ + 1) * T], in0=k.ps[bg][0:T, 0:T], scalar=lab[0:T, 2 + r_:3 + r_],
                                                                     in1=dsb.t[0:T, r_ * T:(r_ + 1) * T], op0=ALU.mult, op1=ALU.mult),
                     r=[("ps", bg), lk, dsb.key], w=[Am.key])
            dsb.free()
            attnT = k.bpool.get()
            k.op("dve", lambda e: e.tensor_tensor(out=attnT.t[0:T, 0:2 * T].rearrange("p (r t) -> p r t", r=2),
                                                  in0=dec.t[0:T, 0:2 * T].rearrange("p (r t) -> p r t", r=2),
                                                  in1=k.ps[bg][0:T, T:2 * T].unsqueeze(1).to_broadcast([T, 2, T]), op=ALU.mult),
                 r=[dec.key, ("ps", bg)], w=[attnT.key])
            k.pfree(bg)
            dec.free()
            TTd = gdn_chain(k, T, Am, 7 if not t.sample else 2)
            Am.free()
            TTf = k.bpool.get()
            for r_ in range(2):
                k.op("act", lambda e, r_=r_: e.activation(out=TTf.t[0:T, r_ * T:(r_ + 1) * T], in_=TTd.t[0:T, r_ * T:(r_ + 1) * T], func=AF.Copy,
                                                            scale=lab[0:T, 2 + r_:3 + r_]), r=[TTd.key, lk], w=[TTf.key])
            TTd.free()
            tmpb = k.bpool.get()
            bks = None
            if not (nseq == 1 and first):
                bks = k.psum()
                if nseq == 1:
                    for r_ in range(2):
                        k.mm(k.ps[bks][0:T, r_ * 256:r_ * 256 + 128], kT, k.SB[:, 0, r_ * 128:(r_ + 1) * 128], True, True, r=[qkn.key, ("SB", 0)], w=[("ps", bks)])
                        k.mm(k.ps[bks][0:T, r_ * 256 + 128:r_ * 256 + 256], qT, k.SB[:, 0, r_ * 128:(r_ + 1) * 128], True, True, r=[qkn.key, ("SB", 0)], w=[("ps", bks)])
                else:
                    sbl = []
                    for b_ in range(nseq):
                        sf = k.fpool.get()
                        k.dma(sf.t[:, 0:256].rearrange("p (r v) -> p r v", r=2), k.din["state_gdn"][b_, 2 * kh:2 * kh + 2, :, :].rearrange("r k v -> k r v"), w=[sf.key])
                        sb = k.bpool.get()
                        k.op("act", lambda e, sb=sb, sf=sf: e.activation(out=sb.t[:, 0:256], in_=sf.t[:, 0:256], func=AF.Copy), r=[sf.key], w=[sb.key])
                        sf.free()
                        qm = k.bpool.get()
                        k.op("pool", lambda e, qm=qm, b_=b_: e.tensor_tensor(out=qm.t[:, 0:2 * T].rearrange("p (c t) -> p c t", c=2),
                                                                            in0=qkn.t[:, 0:2 * T].rearrange("p (c t) -> p c t", c=2),
                                                                            in1=k.colmaskb[:, b_, :].unsqueeze(1).to_broadcast([128, 2, T]), op=ALU.mult),
                             r=[qkn.key, "colmaskb"], w=[qm.key])
                        for r_ in range(2):
                            k.P.op("pe", lambda e, qm=qm, sb=sb, r_=r_, b_=b_: e.matmul(k.ps[bks][0:T, r_ * 256:r_ * 256 + 128], lhsT=qm.t[:, T:2 * T],
                                                                                         rhs=sb.t[:, r_ * 128:(r_ + 1) * 128], start=(b_ == 0 and r_ == 0),
                                                                                         stop=(b_ == nseq - 1), skip_group_check=True),
                                   r=[qm.key, sb.key], w=[("ps", bks)])
                            k.P.op("pe", lambda e, qm=qm, sb=sb, r_=r_, b_=b_: e.matmul(k.ps[bks][0:T, r_ * 256 + 128:r_ * 256 + 256], lhsT=qm.t[:, 0:T],
                                                                                         rhs=sb.t[:, r_ * 128:(r_ + 1) * 128], start=False,
                                                                                         stop=(b_ == nseq - 1), skip_group_check=True),
                                   r=[qm.key, sb.key], w=[("ps", bks)])
                        qm.free()
                        sb.free()
                for r_ in range(2):
                    k.op("dve", lambda e, r_=r_: e.scalar_tensor_tensor(out=tmpb.t[0:T, r_ * 128:(r_ + 1) * 128], in0=k.ps[bks][0:T, r_ * 256:r_ * 256 + 128],
                                                                         scalar=sm[0:T, 4 + r_:5 + r_], in1=vtok.t[0:T, r_ * 128:(r_ + 1) * 128],
                                                                         op0=ALU.mult, op1=ALU.subtract), r=[("ps", bks), smk, vtok.key], w=[tmpb.key])
            else:
                k.op("act", lambda e: e.activation(out=tmpb.t[0:T, 0:256], in_=vtok.t[0:T, 0:256], func=AF.Copy, scale=-1.0), r=[vtok.key], w=[tmpb.key])
            bv = k.psum()
            for r_ in range(2):
                k.mm(k.ps[bv][0:T, r_ * 128:(r_ + 1) * 128], TTf.t[0:T, r_ * T:(r_ + 1) * T], tmpb.t[0:T, r_ * 128:(r_ + 1) * 128], True, True,
                     r=[TTf.key, tmpb.key], w=[("ps", bv)])
            TTf.free()
            tmpb.free()
            vnew = k.bpool.get()
            k.op("act", lambda e: e.activation(out=vnew.t[0:T, 0:256], in_=k.ps[bv][0:T, 0:256], func=AF.Copy), r=[("ps", bv)], w=[vnew.key])
            k.pfree(bv)
            bo = k.psum()
            for r_ in range(2):
                k.mm(k.ps[bo][0:T, r_ * 128:(r_ + 1) * 128], attnT.t[0:T, r_ * T:(r_ + 1) * T], vnew.t[0:T, r_ * 128:(r_ + 1) * 128], True, True,
                     r=[attnT.key, vnew.key], w=[("ps", bo)])
            attnT.free()
            o = k.fpool.get()
            if bks is not None:
                tq = k.fpool.get()
                for r_ in range(2):
                    k.op("act", lambda e, r_=r_: e.activation(out=tq.t[0:T, r_ * 128:(r_ + 1) * 128], in_=k.ps[bks][0:T, r_ * 256 + 128:r_ * 256 + 256], func=AF.Copy,
                                                                scale=sm[0:T, 4 + r_:5 + r_]), r=[("ps", bks), smk], w=[tq.key])
                k.pfree(bks)
                k.op("dve", lambda e: e.tensor_tensor(out=o.t[0:T, 0:256], in0=tq.t[0:T, 0:256], in1=k.ps[bo][0:T, 0:256], op=ALU.add), r=[tq.key, ("ps", bo)], w=[o.key])
                tq.free()
            else:
                k.op("act", lambda e: e.activation(out=o.t[0:T, 0:256], in_=k.ps[bo][0:T, 0:256], func=AF.Copy), r=[("ps", bo)], w=[o.key])
            k.pfree(bo)
            kd = k.bpool.get()
            for r_ in range(2):
                k.op("pool", lambda e, r_=r_: e.tensor_scalar_mul(out=kd.t[0:T, r_ * 128:(r_ + 1) * 128], in0=ktok.t[0:T, 0:128], scalar1=sm[0:T, 6 + r_:7 + r_]),
                     r=[ktok.key, smk], w=[kd.key])
            if nseq == 1:
                bs_ = k.psum()
                for r_ in range(2):
                    k.mm(k.ps[bs_][:, r_ * 128:(r_ + 1) * 128], kd.t[0:T, r_ * 128:(r_ + 1) * 128], vnew.t[0:T, r_ * 128:(r_ + 1) * 128], True, True,
                         r=[kd.key, vnew.key], w=[("ps", bs_)])
                for r_ in range(2):
                    if first:
                        k.op("act", lambda e, r_=r_: e.activation(out=k.SF[:, 0, r_ * 128:(r_ + 1) * 128], in_=k.ps[bs_][:, r_ * 128:(r_ + 1) * 128], func=AF.Copy),
                             r=[("ps", bs_)], w=[("SF", 0)])
                    else:
                        k.op("dve", lambda e, r_=r_: e.scalar_tensor_tensor(out=k.SF[:, 0, r_ * 128:(r_ + 1) * 128], in0=k.SF[:, 0, r_ * 128:(r_ + 1) * 128],
                                                                             scalar=sm[0:128, 8 + r_:9 + r_], in1=k.ps[bs_][:, r_ * 128:(r_ + 1) * 128],
                                                                             op0=ALU.mult, op1=ALU.add), r=[("SF", 0), smk, ("ps", bs_)], w=[("SF", 0)])
                k.pfree(bs_)
                k.op("act", lambda e: e.activation(out=k.SB[:, 0, 0:256], in_=k.SF[:, 0, 0:256], func=AF.Copy), r=[("SF", 0)], w=[("SB", 0)])
                if t.i == NPT - 1:
                    k.dma(k.dout["p_gdn"][2 * kh:2 * kh + 2, :, :].rearrange("r k v -> k r v"), k.SF[:, 0, 0:256].rearrange("p (r v) -> p r v", r=2), r=[("SF", 0)])
            else:
                for b_ in range(nseq):
                    sf = k.fpool.get()
                    k.dma(sf.t[:, 0:256].rearrange("p (r v) -> p r v", r=2), k.din["state_gdn"][b_, 2 * kh:2 * kh + 2, :, :].rearrange("r k v -> k r v"), w=[sf.key])
                    km = k.bpool.get()
                    k.op("pool", lambda e, km=km, b_=b_: e.tensor_scalar_mul(out=km.t[0:T, 0:256], in0=kd.t[0:T, 0:256], scalar1=k.c("rowmask_s")[0:T, b_:b_ + 1]),
                         r=[kd.key, "CONST"], w=[km.key])
                    bs_ = k.psum()
                    for r_ in range(2):
                        k.mm(k.ps[bs_][:, r_ * 128:(r_ + 1) * 128], km.t[0:T, r_ * 128:(r_ + 1) * 128], vnew.t[0:T, r_ * 128:(r_ + 1) * 128], True, True,
                             r=[km.key, vnew.key], w=[("ps", bs_)])
                    km.free()
                    for r_ in range(2):
                        k.op("dve", lambda e, r_=r_, sf=sf, bs_=bs_, b_=b_: e.scalar_tensor_tensor(
                            out=sf.t[:, r_ * 128:(r_ + 1) * 128], in0=sf.t[:, r_ * 128:(r_ + 1) * 128], scalar=sm[0:128, 16 + 2 * b_ + r_:17 + 2 * b_ + r_],
                            in1=k.ps[bs_][:, r_ * 128:(r_ + 1) * 128], op0=ALU.mult, op1=ALU.add), r=[sf.key, smk, ("ps", bs_)], w=[sf.key])
                    k.pfree(bs_)
                    k.dma(k.dout["s_gdn"][b_, 2 * kh:2 * kh + 2, :, :].rearrange("r k v -> k r v"), sf.t[:, 0:256].rearrange("p (r v) -> p r v", r=2), r=[sf.key])
                    sf.free()
            kd.free()
            vnew.free()
            stt = k.st[k.rr("st", 4)]
            sk_ = ("st", id(stt))
            yn = k.bpool.get()
            for r_ in range(2):
                emit_rstd(k, t, stt, o.t[0:T, r_ * 128:(r_ + 1) * 128], [o.key], 128, 4 * r_)
                k.op("dve", lambda e, r_=r_: e.scalar_tensor_tensor(out=o.t[0:T, r_ * 128:(r_ + 1) * 128], in0=o.t[0:T, r_ * 128:(r_ + 1) * 128],
                                                                     scalar=stt[0:T, 4 * r_ + 1:4 * r_ + 2], in1=gnw.t[0:T, 0:128], op0=ALU.mult, op1=ALU.mult),
                     r=[o.key, sk_, gnw.key], w=[o.key])
            k.op("dve", lambda e: e.tensor_tensor(out=yn.t[0:T, 0:256], in0=o.t[0:T, 0:256], in1=A_["sz"].t[0:T, 0:256], op=ALU.mult), r=[o.key, A_["sz"].key], w=[yn.key])
            o.free()
            for nm in ("qkn", "ktok", "vtok", "sz"):
                A_[nm].free()
            ynT = transpose_to(k, t, yn, 2)
            yn.free()
            out_proj_add(k, t, ynT, 2, s_o)
            ynT.free()
            if t.i >= NPT - 1:
                conv_state_out(k, t, s_x, segs, k.dout["p_gdn_conv"], k.dout["s_gdn_conv"])

        for ti in range(NT):
            A_ = stage_a(TILES[ti])
            stage_b(TILES[ti], A_)
    gnw.free()


def zero_unwritten_outputs(k):
    pass


_CACHE = {}


def _get_program():
    if "nc" not in _CACHE:
        pack, offs, cs, colmask = build_consts()
        offs = dict(offs)
        offs["_tot"] = pack.shape[1]
        _set_shapes(pack.shape[1])
        nc = bass.Bass("TRN2", target_bir_lowering=False)
        _CACHE["k"] = build_program(nc, offs)
        _CACHE["nc"] = nc
        _CACHE["used"] = set(_CACHE["k"].din.keys())
        _CACHE["pack"] = pack
        _CACHE["cs"] = cs
        _CACHE["colmask"] = colmask
    return _CACHE["nc"], _CACHE["pack"], _CACHE["cs"]


def kernel(x_prompt, x_sample, state_ret, state_ssd, state_ssd_conv, state_gdn, state_gdn_conv,
           norm_mix, norm_mlp, norm_final, ret_w_in, ret_w_out,
           ssd_w_in, ssd_conv_w, ssd_conv_b, ssd_dt_bias, ssd_a_log, ssd_d, ssd_norm, ssd_w_out,
           gdn_w_in, gdn_conv_w, gdn_dt_bias, gdn_a_log, gdn_norm, gdn_w_out,
           mlp_w_up, mlp_w_down):
    nc, pack, cs = _get_program()
    f = lambda a: np.ascontiguousarray(np.asarray(a, dtype=np.float32))
    norms = f(np.concatenate([np.asarray(norm_mix), np.asarray(norm_mlp), np.asarray(norm_final)[None, :]], axis=0))
    shared = {
        "norms": norms,
        "ret_w_in": f(ret_w_in), "ret_w_out": f(ret_w_out),
        "ssd_w_in": f(ssd_w_in[0]), "ssd_conv_w": f(ssd_conv_w[0]), "ssd_conv_b": f(ssd_conv_b), "ssd_dt_bias": f(ssd_dt_bias),
        "ssd_a_log": f(ssd_a_log), "ssd_d": f(ssd_d), "ssd_norm": f(ssd_norm), "ssd_w_out": f(ssd_w_out[0]),
        "gdn_w_in": f(gdn_w_in[0]), "gdn_conv_w": f(gdn_conv_w[0]), "gdn_dt_bias": f(gdn_dt_bias), "gdn_a_log": f(gdn_a_log),
        "gdn_norm": f(gdn_norm), "gdn_w_out": f(gdn_w_out[0]),
        "mlp_w_up": f(mlp_w_up), "mlp_w_down": f(mlp_w_down),
        "cpack": pack, "ropecs": cs, "colmask": _CACHE["colmask"],
    }
    xp = np.asarray(x_prompt, dtype=np.float32)
    xs = np.asarray(x_sample, dtype=np.float32)
    in_maps = []
    for c in range(8):
        sl = slice(16 * c, 16 * c + 16)
        m = dict(shared)
        m["x_prompt"] = f(xp[c])
        m["x_sample"] = f(xs[sl].reshape(TS, D))
        m["state_ret"] = f(np.asarray(state_ret)[:, sl])
        m["state_ssd"] = f(np.asarray(state_ssd)[0, sl])
        m["state_ssd_conv"] = f(np.asarray(state_ssd_conv)[0, sl])
        m["state_gdn"] = f(np.asarray(state_gdn)[0, sl])
        m["state_gdn_conv"] = f(np.asarray(state_gdn_conv)[0, sl])
        m = {kk: vv for kk, vv in m.items() if kk in _CACHE["used"]}
        in_maps.append(m)
    ncores = CFG.get("ncores", 8)
    res = run_bass_kernel_spmd(nc, in_maps[:ncores], core_ids=list(range(ncores)))
    R = res.results
    g = lambda nm: [np.asarray(R[c][nm]) if c < ncores else np.zeros_like(np.asarray(R[0][nm])) for c in range(8)]
    y_prompt = np.stack(g("y_prompt"), 0)
    y_sample = np.concatenate(g("y_sample"), 0).reshape(128, 4, D)
    p_ret = np.stack(g("p_ret"), 1)
    p_ssd = np.stack(g("p_ssd"), 0)[None]
    p_ssd_conv = np.stack(g("p_ssd_conv"), 0)[None]
    p_gdn = np.stack(g("p_gdn"), 0)[None]
    p_gdn_conv = np.stack(g("p_gdn_conv"), 0)[None]
    s_ret = np.concatenate(g("s_ret"), 1)
    s_ssd = np.concatenate(g("s_ssd"), 0)[None]
    s_ssd_conv = np.concatenate(g("s_ssd_conv"), 0)[None]
    s_gdn = np.concatenate(g("s_gdn"), 0)[None]
    s_gdn_conv = np.concatenate(g("s_gdn_conv"), 0)[None]
    return (y_prompt, y_sample, p_ret, p_ssd, p_ssd_conv, p_gdn, p_gdn_conv,
            s_ret, s_ssd, s_ssd_conv, s_gdn, s_gdn_conv)
```

```python
import contextlib
import math
import numpy as np
import concourse.bass as bass
import concourse.mybir as mybir
from concourse.bass_utils import run_bass_kernel_spmd

F32 = mybir.dt.float32
BF16 = mybir.dt.bfloat16
ALU = mybir.AluOpType
AF = mybir.ActivationFunctionType

ENGS = ("pe", "act", "dve", "pool", "sp")
NDMA_SEMS = 40

D = 1024
SEQ = 2048
NPT = 16
NT = 17
TS = 64
NTOK = SEQ + TS
DEPTH = 4
PAST_LEN = 16384
RMS_EPS = 1e-6
D_FF = 4096

CFG = {"mixers": (0, 1, 2, 3), "mlp": True, "nlayers": 4}


class Op:
    __slots__ = ("eng", "fn", "deps", "is_dma", "pos", "inc", "semi", "semv", "waits")

    def __init__(self, eng, fn, deps, is_dma):
        self.eng = eng
        self.fn = fn
        self.deps = deps
        self.is_dma = is_dma
        self.inc = False
        self.semi = -1
        self.semv = 0
        self.waits = None


import types as _types


def _freeze(fn):
    if fn.__closure__ is None:
        return fn
    cells = []
    for c in fn.__closure__:
        try:
            cells.append(_types.CellType(c.cell_contents))
        except ValueError:
            cells.append(c)
    g = _types.FunctionType(fn.__code__, fn.__globals__, fn.__name__, fn.__defaults__, tuple(cells))
    g.__kwdefaults__ = fn.__kwdefaults__
    return g


import threading as _threading


class Coop:
    def __init__(self):
        self.yield_fn = None

    def run(self, fns):
        n = len(fns)
        sems = [_threading.Semaphore(0) for _ in range(n)]
        main = _threading.Semaphore(0)
        done = [False] * n
        exc = []
        state = {"cur": 0}

        def nxt(me):
            for d in range(1, n + 1):
                j = (me + d) % n
                if not done[j]:
                    return j
            return None

        def switch():
            me = state["cur"]
            j = nxt(me)
            if j is None or j == me:
                return
            state["cur"] = j
            sems[j].release()
            sems[me].acquire()

        def worker(i):
            sems[i].acquire()
            try:
                fns[i]()
            except BaseException as e:
                exc.append(e)
            finally:
                done[i] = True
                j = nxt(i)
                if j is None:
                    main.release()
                else:
                    state["cur"] = j
                    sems[j].release()

        ths = [_threading.Thread(target=worker, args=(i,)) for i in range(n)]
        for t in ths:
            t.start()
        self.yield_fn = switch
        sems[0].release()
        main.acquire()
        self.yield_fn = None
        for t in ths:
            t.join()
        if exc:
            raise exc[0]


COOP = Coop()


class Prog:
    def __init__(self, nc):
        self.nc = nc
        self.ops = []
        self.lastw = {}
        self.readers = {}

    def op(self, eng, fn, r=(), w=(), dma=False):
        deps = set()
        lastw = self.lastw
        readers = self.readers
        for k in r:
            lw = lastw.get(k)
            if lw is not None:
                deps.add(lw)
            if type(k) is tuple and k[0] == "ps":
                rd = readers.get(k)
                if rd:
                    for j_ in rd:
                        if self.ops[j_].eng != eng:
                            deps.add(j_)
        for k in w:
            lw = lastw.get(k)
            if lw is not None:
                deps.add(lw)
            rd = readers.get(k)
            if rd:
                deps.update(rd)
        idx = len(self.ops)
        self.ops.append(Op(eng, _freeze(fn), deps, dma))
        for k in w:
            lastw[k] = idx
            readers[k] = []
        for k in r:
            if k in w:
                continue
            readers.setdefault(k, []).append(idx)
        if COOP.yield_fn is not None:
            COOP.yield_fn()
        return idx

    def plan(self):
        ops = self.ops
        streams = {e: [] for e in ENGS}
        for i, o in enumerate(ops):
            o.pos = len(streams[o.eng])
            streams[o.eng].append(i)
        clock = {e: {f: -1 for f in ENGS} for e in ENGS}
        dma_seen = {e: set() for e in ENGS}
        vcs = [None] * len(ops)
        dma_count = {"sp": 0, "pool": 0, "act": 0}
        dma_base = {"sp": (0, 28), "pool": (28, 12), "act": (0, 28)}
        sem_last = [None] * NDMA_SEMS
        sem_val = [0] * NDMA_SEMS
        for i, o in enumerate(ops):
            E = o.eng
            ck = clock[E]
            waits = []
            deps = o.deps
            if o.is_dma:
                base_, n_ = dma_base[E]
                s = base_ + dma_count[E] % n_
                dma_count[E] += 1
                if sem_last[s] is not None:
                    deps = set(deps)
                    deps.add(sem_last[s])
                sem_last[s] = i
                sem_val[s] += 16
                o.semi = s
                o.semv = sem_val[s]
            need = {}
            for j in deps:
                oj = ops[j]
                if oj.is_dma:
                    if j not in dma_seen[E]:
                        waits.append(("dma", j))
                        dma_seen[E].add(j)
                        vj = vcs[j]
                        for f in ENGS:
                            if vj[f] > ck[f]:
                                ck[f] = vj[f]
                else:
                    F = oj.eng
                    if F == E and E in ("pe", "sp"):
                        continue
                    if oj.pos > ck[F] and oj.pos > need.get(F, -1):
                        need[F] = oj.pos
            for F, p in need.items():
                if p > ck[F]:
                    j = streams[F][p]
                    ops[j].inc = True
                    waits.append(("eng", F, j))
                    vj = vcs[j]
                    for f in ENGS:
                        if vj[f] > ck[f]:
                            ck[f] = vj[f]
                    if p > ck[F]:
                        ck[F] = p
            o.waits = waits
            o.deps = None
            if o.is_dma:
                vcs[i] = dict(ck)
            else:
                v = dict(ck)
                v[E] = o.pos
                vcs[i] = v
                if E == "pe":
                    ck[E] = o.pos
        cnt = {e: 0 for e in ENGS}
        for e in ENGS:
            for i in streams[e]:
                o = ops[i]
                if (not o.is_dma) and o.inc:
                    cnt[e] += 1
                    o.semv = cnt[e]
        self.streams = streams
        self.counts = cnt
        return streams

    def emit(self):
        nc = self.nc
        ops = self.ops
        streams = self.plan()
        with contextlib.ExitStack() as es:
            esem = {e: es.enter_context(nc.semaphore("s_" + e)) for e in ENGS}
            dsem = [es.enter_context(nc.semaphore("d%d" % k)) for k in range(NDMA_SEMS)]
            block = es.enter_context(nc.Block())

            def run(e, eng):
                for i in streams[e]:
                    o = ops[i]
                    for wt in o.waits:
                        if wt[0] == "dma":
                            oj = ops[wt[1]]
                            eng.wait_ge(dsem[oj.semi], oj.semv)
                        else:
                            oj = ops[wt[2]]
                            eng.wait_ge(esem[oj.eng], oj.semv)
                    ins = o.fn(eng)
                    if o.is_dma:
                        ins.then_inc(dsem[o.semi], 16)
                    elif o.inc:
                        ins.then_inc(esem[e], 1)
                    o.fn = None

            @block.tensor
            def _(eng):
                run("pe", eng)

            @block.scalar
            def _(eng):
                run("act", eng)

            @block.vector
            def _(eng):
                run("dve", eng)

            @block.gpsimd
            def _(eng):
                run("pool", eng)

            @block.sync
            def _(eng):
                run("sp", eng)
                vals = {}
                for o in ops:
                    if o.is_dma:
                        vals[o.semi] = max(vals.get(o.semi, 0), o.semv)
                for s, v in sorted(vals.items()):
                    eng.wait_ge(dsem[s], v)
                for e in ("pe", "act", "dve", "pool"):
                    if self.counts[e] > 0:
                        eng.wait_ge(esem[e], self.counts[e])


def _mask_consts(T, L):
    idx = np.arange(T)
    same = (idx[:, None] // L) == (idx[None, :] // L)
    tri = (same & (idx[:, None] <= idx[None, :])).astype(np.float32)
    seq = same.astype(np.float32)
    neg = np.where(same & (idx[None, :] >= idx[:, None]), 0.0, -30000.0).astype(np.float32)
    negt = np.where(same & (idx[None, :] <= idx[:, None]), 0.0, -30000.0).astype(np.float32)
    strict = (same & (idx[None, :] < idx[:, None])).astype(np.float32)
    return tri, seq, neg, negt, strict


def build_consts():
    items = []
    items.append(("ident", np.eye(128, dtype=np.float32)))
    items.append(("ones", np.ones((128, 128), np.float32)))
    for nm, a in zip(("tri_p", "seq_p", "neg_p", "negt_p", "strict_p"), _mask_consts(128, 128)):
        if nm != "seq_p":
            items.append((nm, a))
    for nm, a in zip(("tri_s", "seq_s", "neg_s", "negt_s", "strict_s"), _mask_consts(TS, 4)):
        items.append((nm, a))
    rowmask = (np.arange(TS)[:, None] // 4 == np.arange(16)[None, :]).astype(np.float32)
    items.append(("rowmask_s", rowmask))
    colmask = np.ascontiguousarray(np.broadcast_to(rowmask.T[None, :, :], (128, 16, TS)).reshape(128, 16 * TS))
    ii = np.arange(128)
    for lv in range(7):
        bsz = 1 << lv
        same = (ii[:, None] // (2 * bsz)) == (ii[None, :] // (2 * bsz))
        mT = same & ((ii[None, :] % (2 * bsz)) >= bsz) & ((ii[:, None] % (2 * bsz)) < bsz)
        items.append(("bmT%d" % lv, mT.astype(np.float32)))
    lg = np.log1p(-np.exp2(-5.0 - np.arange(4, dtype=np.float32))).astype(np.float32)
    items.append(("laret", np.broadcast_to(lg[None, :], (128, 4)).copy()))
    offs = {}
    tot = 0
    for nm, a in items:
        offs[nm] = (tot, a.shape[0], a.shape[1])
        tot += a.shape[1]
    pack = np.zeros((128, tot), np.float32)
    for nm, a in items:
        o, r, c = offs[nm]
        pack[:r, o:o + c] = a
    half = 128
    inv = (np.float32(10000.0) ** (-np.arange(half, dtype=np.float32) / np.float32(half))).astype(np.float32)
    pos = np.zeros((NT, 128), np.float32)
    for i in range(NPT):
        pos[i] = np.arange(128, dtype=np.float32) + 128 * i
    pos[NPT, :TS] = (np.arange(TS) % 4).astype(np.float32) + np.float32(PAST_LEN)
    ang = (pos[:, :, None] * inv[None, None, :]).astype(np.float32)
    cs = np.stack([np.cos(ang.astype(np.float64)), np.sin(ang.astype(np.float64))], axis=2).astype(np.float32)
    return pack, offs, np.ascontiguousarray(cs), colmask


class TileInfo:
    def __init__(self, i):
        self.i = i
        self.sample = i == NPT
        self.T = TS if self.sample else 128
        self.c0 = i * 128
        self.kind = "s" if self.sample else "p"
        self.nseq = 16 if self.sample else 1


TILES = [TileInfo(i) for i in range(NT)]


NFA = 12
NBA = 16


class Buf:
    __slots__ = ("pool", "i", "t", "key")

    def __init__(self, pool, i):
        self.pool = pool
        self.i = i
        self.t = pool.ts[i]
        self.key = (pool.name, i)

    def free(self):
        self.pool.free.append(self.i)


class BufPool:
    def __init__(self, name, ts):
        self.name = name
        self.ts = ts
        self.free = list(range(len(ts)))

    def get(self):
        assert self.free, "pool %s exhausted" % self.name
        return Buf(self, self.free.pop(0))


class K:
    def __init__(self, nc, offs):
        self.nc = nc
        self.P = Prog(nc)
        self.offs = offs
        self.uid = 0
        nc_ = nc
        dt = nc_.dram_tensor
        class _Lazy(dict):
            def __missing__(d, nm):
                v = dt(nm, list(IN_SHAPES[nm]), F32, kind="ExternalInput").ap()
                d[nm] = v
                return v
        self.din = _Lazy()
        self.dout = {}
        for nm, shp in OUT_SHAPES.items():
            self.dout[nm] = dt(nm, list(shp), F32, kind="ExternalOutput").ap()
        a = nc_.alloc_sbuf_tensor
        self.X = a("X", [128, NT, D], F32)
        self.XNT = a("XNT", [128, 8, NTOK], BF16)
        self.NRING = 4
        self.ring = [a("wr%d" % k, [128, 4096], BF16) for k in range(self.NRING)]
        self.ring_i = 0
        self.CONST = a("CONST", [128, offs["_tot"]], F32)
        self.identb = a("identb", [128, 128], BF16)
        self.colmaskb = a("colmaskb", [128, 16, TS], BF16)
        self.NW = a("NW", [128, 9, 8], F32)
        self.xnb = [a("xnb%d" % k, [128, D], BF16) for k in range(2)]
        self.st = [a("st%d" % k, [128, 8], F32) for k in range(4)]
        self.PAR = a("PAR", [128, D], F32)
        self.SF = a("SF", [128, 2, 512], F32)
        self.SB = a("SB", [128, 2, 512], BF16)
        self.SM = [a("SM%d" % k, [128, 160], F32) for k in range(4)]
        self.PRM = a("PRM", [128, 128], F32)
        self.CW = a("CW", [128, 32, 4], F32)
        self.CB = a("CB", [128, 32], F32)
        self.UB = [a("UB%d" % k, [128, 528], F32) for k in range(2)]
        self.LAB = [a("LAB%d" % k, [128, 40], F32) for k in range(4)]
        self.fpool = BufPool("fa", [a("fa%d" % k, [128, 512], F32) for k in range(NFA)])
        self.bpool = BufPool("ba", [a("ba%d" % k, [128, 512], BF16) for k in range(NBA)])
        self.ps = [nc_.alloc_psum_tensor("ps%d" % k, [128, 512], F32) for k in range(8)]
        self.ps_free = list(range(8))
        self.cnt = {}
        print("sbuf bytes remaining", nc_.sbuf_bytes_remaining, flush=True)

    def c(self, name):
        o, r, cc = self.offs[name]
        return self.CONST[0:r, o:o + cc]

    def nid(self, pfx):
        self.uid += 1
        return "%s#%d" % (pfx, self.uid)

    def rr(self, key, n=2):
        v = self.cnt.get(key, 0)
        self.cnt[key] = v + 1
        return v % n

    def psum(self):
        b = self.ps_free.pop(0)
        return b

    def pfree(self, b):
        self.ps_free.append(b)

    def ringslot(self):
        s = self.ring_i % self.NRING
        self.ring_i += 1
        return s

    def op(self, *a, **k):
        return self.P.op(*a, **k)

    def dma(self, out, in_, r=(), w=(), eng="sp", slow=False):
        if slow:
            fn = lambda e: e.dma_start(out=out, in_=in_, allow_slow_non_contiguous=True)
        else:
            fn = lambda e: e.dma_start(out=out, in_=in_)
        return self.P.op(eng, fn, r=r, w=w, dma=True)

    def load_w(self, src_pieces, r_extra=()):
        s = self.ringslot()
        t = self.ring[s]
        for dstf, src in src_pieces:
            self.dma(dstf(t), src, w=[("ring", s)], eng="pool")
        return s

    def mm(self, out, lhsT, rhs, start, stop, r, w):
        self.P.op("pe", lambda e: e.matmul(out, lhsT=lhsT, rhs=rhs, start=start, stop=stop), r=r, w=w)

    def tr(self, out, in_, ident, r, w):
        self.P.op("pe", lambda e: e.transpose(out=out, in_=in_, identity=ident), r=r, w=w)


IN_SHAPES = {}
OUT_SHAPES = {}


def _set_shapes(ncst):
    IN_SHAPES.clear()
    IN_SHAPES.update({
        "x_prompt": (SEQ, D), "x_sample": (TS, D),
        "state_ret": (2, 16, 4, 256, 512), "state_ssd": (16, 32, 64, 128), "state_ssd_conv": (16, 3, 4096),
        "state_gdn": (16, 16, 128, 128), "state_gdn_conv": (16, 3, 4096),
        "norms": (9, D),
        "ret_w_in": (2, D, 6144), "ret_w_out": (2, 2048, D),
        "ssd_w_in": (D, 6176), "ssd_conv_w": (4, 4096), "ssd_conv_b": (1, 4096), "ssd_dt_bias": (1, 32),
        "ssd_a_log": (1, 32), "ssd_d": (1, 32), "ssd_norm": (1, 2048), "ssd_w_out": (2048, D),
        "gdn_w_in": (D, 6176), "gdn_conv_w": (4, 4096), "gdn_dt_bias": (1, 16), "gdn_a_log": (1, 16),
        "gdn_norm": (1, 128), "gdn_w_out": (2048, D),
        "mlp_w_up": (4, D, D_FF), "mlp_w_down": (4, D_FF, D),
        "cpack": (128, ncst), "ropecs": (NT, 128, 2, 128), "colmask": (128, 16 * TS),
    })
    OUT_SHAPES.clear()
    OUT_SHAPES.update({
        "y_prompt": (SEQ, D), "y_sample": (TS, D),
        "p_ret": (2, 4, 256, 512), "p_ssd": (32, 64, 128), "p_ssd_conv": (3, 4096),
        "p_gdn": (16, 128, 128), "p_gdn_conv": (3, 4096),
        "s_ret": (2, 16, 4, 256, 512), "s_ssd": (16, 32, 64, 128), "s_ssd_conv": (16, 3, 4096),
        "s_gdn": (16, 16, 128, 128), "s_gdn_conv": (16, 3, 4096),
    })


def emit_setup(k):
    k.dma(k.CONST[:, :], k.din["cpack"][:, :], w=["CONST"])
    k.op("act", lambda e: e.activation(out=k.identb[:, :], in_=k.c("ident"), func=AF.Copy), r=["CONST"], w=["identb"])
    for q in range(2):
        cb = k.fpool.get()
        k.dma(cb.t[:, :], k.din["colmask"][:, q * 512:(q + 1) * 512], w=[cb.key])
        k.op("dve", lambda e, cb=cb, q=q: e.tensor_copy(out=k.colmaskb[0:128, :, :].rearrange("p a b -> p (a b)")[:, q * 512:(q + 1) * 512], in_=cb.t[:, :]),
             r=[cb.key], w=["colmaskb"])
        cb.free()
    k.dma(k.PAR[0:9, :], k.din["norms"][:, :], w=["PAR"])
    b = k.psum()
    pv = k.ps[b][:, 0:72].rearrange("p (c l) -> p c l", c=8)
    for ch in range(8):
        k.tr(pv[:, ch, :], k.PAR[0:9, ch * 128:(ch + 1) * 128], k.c("ident")[0:9, 0:9], r=["PAR", "CONST"], w=[("ps", b)])
    k.op("dve", lambda e: e.tensor_copy(out=k.NW[:, :, :].rearrange("p l c -> p c l"), in_=pv), r=[("ps", b)], w=["NW"])
    k.pfree(b)
    for t in TILES:
        src = k.din["x_sample"][:, :] if t.sample else k.din["x_prompt"][t.c0:t.c0 + 128, :]
        k.dma(k.X[0:t.T, t.i, :], src, w=[("X", t.i)])


def emit_rstd(k, t, stt, src_ap, src_res, n, col):
    T = t.T
    jb = [k.bpool.get() for _ in range((n + 511) // 512)]
    for q, jbq in enumerate(jb):
        n0, n1 = q * 512, min(n, q * 512 + 512)
        k.op("act", lambda e, jbq=jbq, n0=n0, n1=n1, q=q: e.activation(out=jbq.t[0:T, 0:n1 - n0], in_=src_ap[:, n0:n1], func=AF.Square,
                                                                      accum_out=stt[0:T, col + 2 + q:col + 3 + q]),
             r=list(src_res), w=[jbq.key, ("st", id(stt))])
        jbq.free()
    if len(jb) == 2:
        k.op("dve", lambda e: e.tensor_tensor(out=stt[0:T, col:col + 1], in0=stt[0:T, col + 2:col + 3], in1=stt[0:T, col + 3:col + 4], op=ALU.add),
             r=[("st", id(stt))], w=[("st", id(stt))])
    else:
        k.op("dve", lambda e: e.tensor_copy(out=stt[0:T, col:col + 1], in_=stt[0:T, col + 2:col + 3]),
             r=[("st", id(stt))], w=[("st", id(stt))])
    k.op("dve", lambda e: e.tensor_scalar(out=stt[0:T, col + 1:col + 2], in0=stt[0:T, col:col + 1], scalar1=1.0 / n, scalar2=RMS_EPS,
                                          op0=ALU.mult, op1=ALU.add), r=[("st", id(stt))], w=[("st", id(stt))])
    k.op("act", lambda e: e.activation(out=stt[0:T, col + 1:col + 2], in_=stt[0:T, col + 1:col + 2], func=AF.Ln),
         r=[("st", id(stt))], w=[("st", id(stt))])
    k.op("act", lambda e: e.activation(out=stt[0:T, col + 1:col + 2], in_=stt[0:T, col + 1:col + 2], func=AF.Exp, scale=-0.5),
         r=[("st", id(stt))], w=[("st", id(stt))])


def emit_norm_xnt(k, widx):
    for t in TILES[:CFG.get("norm_tiles", NT)]:
        T = t.T
        stt = k.st[k.rr("st", 4)]
        xb = k.xnb[k.rr("xnb", 2)]
        xbk = ("xnb", id(xb))
        emit_rstd(k, t, stt, k.X[0:T, t.i, :], [("X", t.i)], D, 0)
        k.op("dve", lambda e, T=T, xb=xb, stt=stt, t=t: e.tensor_scalar_mul(out=xb[0:T, :], in0=k.X[0:T, t.i, :], scalar1=stt[0:T, 1:2]),
             r=[("X", t.i), ("st", id(stt))], w=[xbk])
        b = k.psum()
        pv = k.ps[b][:, :].bitcast(BF16).rearrange("p (c m) -> p c m", c=8)
        for ch in range(8):
            k.tr(pv[:, ch, 0:T], xb[0:T, ch * 128:(ch + 1) * 128], k.identb[0:T, 0:T], r=[xbk, "identb"], w=[("ps", b)])
        nwb = k.NW[:, widx, :].unsqueeze(2).to_broadcast([128, 8, T])
        k.op("dve", lambda e, T=T, t=t, pv=pv, nwb=nwb: e.tensor_tensor(out=k.XNT[:, :, t.c0:t.c0 + T], in0=pv[:, :, 0:T], in1=nwb, op=ALU.mult),
             r=[("ps", b), "NW"], w=[("XNT", t.i)])
        k.pfree(b)


def emit_mlp(k, layer):
    wu = k.din["mlp_w_up"]
    wd = k.din["mlp_w_down"]
    blocks = [(0, 512, [0, 1, 2, 3]), (512, 512, [4, 5, 6, 7]), (1024, 512, [8, 9, 10, 11]), (1536, 512, [12, 13, 14, 15]),
              (2048, TS, [16])]
    for g in range(8):
        su = k.load_w([(lambda t: t[:, :].rearrange("p (k n) -> p k n", k=8),
                        wu[layer, :, g * 512:(g + 1) * 512].rearrange("(k p) n -> p k n", p=128))])
        sd = k.load_w([(lambda t: t[:, :].rearrange("p (c n) -> p c n", c=4),
                        wd[layer, g * 512:(g + 1) * 512, :].rearrange("(c p) n -> p c n", p=128))])
        Wu = k.ring[su][:, :].rearrange("p (k n) -> p k n", k=8)
        Wd = k.ring[sd][:, :].rearrange("p (c n) -> p c n", c=4)
        for (c0, ncol, tl) in blocks:
            hTb = [k.bpool.get() for _ in range(4)]
            for c in range(4):
                b = k.psum()
                for kk in range(8):
                    k.mm(k.ps[b][:, 0:ncol], Wu[:, kk, c * 128:(c + 1) * 128], k.XNT[:, kk, c0:c0 + ncol], kk == 0, kk == 7,
                         r=[("ring", su)] + [("XNT", ti) for ti in tl], w=[("ps", b)])
                hr = k.bpool.get()
                k.op("act", lambda e, b=b, hr=hr, ncol=ncol: e.activation(out=hr.t[:, 0:ncol], in_=k.ps[b][:, 0:ncol], func=AF.Relu),
                     r=[("ps", b)], w=[hr.key])
                k.op("pool", lambda e, hr=hr, hTc=hTb[c], ncol=ncol: e.tensor_tensor(out=hTc.t[:, 0:ncol], in0=hr.t[:, 0:ncol], in1=hr.t[:, 0:ncol], op=ALU.mult),
                     r=[hr.key], w=[hTb[c].key])
                hr.free()
                k.pfree(b)
            for j, ti in enumerate(tl):
                t = TILES[ti]
                T = t.T
                for half in range(2):
                    b = k.psum()
                    for c in range(4):
                        k.mm(k.ps[b][0:T, :], hTb[c].t[:, j * 128:j * 128 + T], Wd[:, c, half * 512:(half + 1) * 512], c == 0, c == 3,
                             r=[hTb[c].key, ("ring", sd)], w=[("ps", b)])
                    k.op("dve", lambda e, b=b, T=T, ti=ti, half=half: e.tensor_tensor(
                        out=k.X[0:T, ti, half * 512:(half + 1) * 512], in0=k.X[0:T, ti, half * 512:(half + 1) * 512],
                        in1=k.ps[b][0:T, :], op=ALU.add), r=[("ps", b), ("X", ti)], w=[("X", ti)])
                    k.pfree(b)
            for hb in hTb:
                hb.free()


def emit_final(k):
    k.dma(k.PAR[:, :], k.din["norms"][8, :].partition_broadcast(128), w=["PAR"])
    for t in TILES:
        T = t.T
        stt = k.st[k.rr("st", 4)]
        emit_rstd(k, t, stt, k.X[0:T, t.i, :], [("X", t.i)], D, 0)
        k.op("dve", lambda e, T=T, t=t, stt=stt: e.scalar_tensor_tensor(out=k.X[0:T, t.i, :], in0=k.X[0:T, t.i, :], scalar=stt[0:T, 1:2],
                                                                         in1=k.PAR[0:T, :], op0=ALU.mult, op1=ALU.mult),
             r=[("X", t.i), ("st", id(stt)), "PAR"], w=[("X", t.i)])
        dst = k.dout["y_sample"][:, :] if t.sample else k.dout["y_prompt"][t.c0:t.c0 + 128, :]
        k.dma(dst, k.X[0:T, t.i, :], r=[("X", t.i)])


def build_program(nc, offs):
    k = K(nc, offs)
    emit_setup(k)
    for layer in range(CFG["nlayers"]):
        emit_norm_xnt(k, layer)
        if layer in CFG["mixers"]:
            kind = layer % 3
            if kind == 0:
                from_ret(k, layer)
            elif kind == 1:
                from_ssd(k, layer)
            else:
                from_gdn(k, layer)
        if CFG["mlp"]:
            emit_norm_xnt(k, 4 + layer)
            emit_mlp(k, layer)
    emit_final(k)
    zero_unwritten_outputs(k)
    k.P.emit()
    return k


def kconst(k, kind, nm):
    return k.c("%s_%s" % (nm, kind)) if not (nm == "seq" and kind == "p") else k.c("ones")


def decay_prep(k, kind, la, la_res, R):
    T = 128 if kind == "p" else TS
    nseq = 1 if kind == "p" else 16
    smi = k.rr("SM", 4)
    sm = k.SM[smi]
    smk = ("SM", smi)
    tri = kconst(k, kind, "tri")
    seq = kconst(k, kind, "seq")
    neg = kconst(k, kind, "neg")
    ones = k.c("ones")
    b = k.psum()
    k.mm(k.ps[b][0:T, 0:R], tri, la, True, True, r=["CONST"] + la_res, w=[("ps", b)])
    k.mm(k.ps[b][0:T, R:2 * R], seq, la, True, True, r=["CONST"] + la_res, w=[("ps", b)])
    k.op("act", lambda e: e.activation(out=sm[0:T, 0:2 * R], in_=k.ps[b][0:T, 0:2 * R], func=AF.Copy), r=[("ps", b)], w=[smk])
    k.pfree(b)
    bm = k.fpool.get()
    bmv = bm.t[0:T, 0:R * T].rearrange("p (r t) -> p r t", r=R)
    k.op("pool", lambda e: e.tensor_tensor(out=bmv, in0=tri.unsqueeze(1).to_broadcast([T, R, T]), in1=la.unsqueeze(2).to_broadcast([T, R, T]),
                                           op=ALU.mult), r=["CONST"] + la_res, w=[bm.key])
    b2 = k.psum()
    k.mm(k.ps[b2][0:T, 0:R * T], ones[0:T, 0:T], bm.t[0:T, 0:R * T], True, True, r=["CONST", bm.key], w=[("ps", b2)])
    bm.free()
    dec = k.fpool.get()
    decv = dec.t[0:T, 0:R * T].rearrange("p (r t) -> p r t", r=R)
    crv = k.ps[b2][0:T, 0:R * T].rearrange("p (r t) -> p r t", r=R)
    for r_ in range(R):
        k.op("dve", lambda e, r_=r_: e.scalar_tensor_tensor(out=decv[:, r_, :], in0=crv[:, r_, :], scalar=sm[0:T, r_:r_ + 1], in1=neg,
                                                              op0=ALU.subtract, op1=ALU.add), r=[("ps", b2), smk, "CONST"], w=[dec.key])
    k.pfree(b2)
    k.op("act", lambda e: e.activation(out=dec.t[0:T, 0:R * T], in_=dec.t[0:T, 0:R * T], func=AF.Exp), r=[dec.key], w=[dec.key])
    k.op("act", lambda e: e.activation(out=sm[0:T, 2 * R:3 * R], in_=sm[0:T, 0:R], func=AF.Exp), r=[smk], w=[smk])
    k.op("dve", lambda e: e.tensor_tensor(out=sm[0:T, 3 * R:4 * R], in0=sm[0:T, R:2 * R], in1=sm[0:T, 0:R], op=ALU.subtract), r=[smk], w=[smk])
    k.op("act", lambda e: e.activation(out=sm[0:T, 3 * R:4 * R], in_=sm[0:T, 3 * R:4 * R], func=AF.Exp), r=[smk], w=[smk])
    if nseq == 1:
        k.op("act", lambda e: e.activation(out=sm[0:T, 4 * R:5 * R], in_=sm[0:T, R:2 * R], func=AF.Exp), r=[smk], w=[smk])
    else:
        lam = sm[0:T, 96:96 + 16 * R].rearrange("p (b r) -> p b r", b=16)
        k.op("pool", lambda e: e.tensor_tensor(out=lam, in0=la.unsqueeze(1).to_broadcast([T, 16, R]),
                                               in1=k.c("rowmask_s").unsqueeze(2).to_broadcast([T, 16, R]), op=ALU.mult),
             r=["CONST", smk] + la_res, w=[smk])
        b3 = k.psum()
        k.mm(k.ps[b3][0:128, 0:16 * R], ones[0:T, 0:128], sm[0:T, 96:96 + 16 * R], True, True, r=["CONST", smk], w=[("ps", b3)])
        k.op("act", lambda e: e.activation(out=sm[0:128, 16:16 + 16 * R], in_=k.ps[b3][0:128, 0:16 * R], func=AF.Exp), r=[("ps", b3)], w=[smk])
        k.pfree(b3)
    return {"sm": sm, "smk": smk, "dec": dec, "T": T, "R": R, "kind": kind}


def scan_tile(k, t, prep, qT, qT_key, kT, kT_key, ktok, ktok_key, v, v_key, R, Pd, NC, first, sload, sstore, pstore):
    T = t.T
    W = R * Pd
    sm, smk, dec = prep["sm"], prep["smk"], prep["dec"]
    nseq = t.nseq
    bs = k.psum()
    for nc_ in range(NC):
        k.mm(k.ps[bs][0:T, 0:T], kT[:, nc_, 0:T], qT[:, nc_, 0:T], nc_ == 0, nc_ == NC - 1, r=[kT_key, qT_key], w=[("ps", bs)])
    at = k.bpool.get()
    k.op("dve", lambda e: e.tensor_tensor(out=at.t[0:T, 0:R * T].rearrange("p (r t) -> p r t", r=R),
                                          in0=dec.t[0:T, 0:R * T].rearrange("p (r t) -> p r t", r=R),
                                          in1=k.ps[bs][0:T, 0:T].unsqueeze(1).to_broadcast([T, R, T]), op=ALU.mult),
         r=[dec.key, ("ps", bs)], w=[at.key])
    k.pfree(bs)
    by1 = k.psum()
    for r_ in range(R):
        k.mm(k.ps[by1][0:T, r_ * Pd:(r_ + 1) * Pd], at.t[0:T, r_ * T:(r_ + 1) * T], v[0:T, r_ * Pd:(r_ + 1) * Pd], True, True,
             r=[at.key, v_key], w=[("ps", by1)])
    at.free()
    vw = k.bpool.get()
    k.op("pool", lambda e: e.tensor_tensor(out=vw.t[0:T, 0:W].rearrange("p (r d) -> p r d", r=R), in0=v[0:T, 0:W].rearrange("p (r d) -> p r d", r=R),
                                           in1=sm[0:T, 3 * R:4 * R].unsqueeze(2).to_broadcast([T, R, Pd]), op=ALU.mult),
         r=[v_key, smk], w=[vw.key])
    y = k.fpool.get()
    if nseq == 1:
        if first:
            k.op("act", lambda e: e.activation(out=y.t[0:T, 0:W], in_=k.ps[by1][0:T, 0:W], func=AF.Copy), r=[("ps", by1)], w=[y.key])
            k.pfree(by1)
        else:
            by2 = k.psum()
            for nc_ in range(NC):
                k.mm(k.ps[by2][0:T, 0:W], qT[:, nc_, 0:T], k.SB[:, nc_, 0:W], nc_ == 0, nc_ == NC - 1, r=[qT_key, ("SB", nc_)], w=[("ps", by2)])
            tmp = k.fpool.get()
            k.op("dve", lambda e: e.tensor_tensor(out=tmp.t[0:T, 0:W].rearrange("p (r d) -> p r d", r=R),
                                                  in0=k.ps[by2][0:T, 0:W].rearrange("p (r d) -> p r d", r=R),
                                                  in1=sm[0:T, 2 * R:3 * R].unsqueeze(2).to_broadcast([T, R, Pd]), op=ALU.mult),
                 r=[("ps", by2), smk], w=[tmp.key])
            k.pfree(by2)
            k.op("dve", lambda e: e.tensor_tensor(out=y.t[0:T, 0:W], in0=tmp.t[0:T, 0:W], in1=k.ps[by1][0:T, 0:W], op=ALU.add),
                 r=[tmp.key, ("ps", by1)], w=[y.key])
            tmp.free()
            k.pfree(by1)
        for nc_ in range(NC):
            bd = k.psum()
            k.mm(k.ps[bd][0:128, 0:W], ktok[0:T, nc_ * 128:(nc_ + 1) * 128], vw.t[0:T, 0:W], True, True, r=[ktok_key, vw.key], w=[("ps", bd)])
            if first:
                k.op("act", lambda e, bd=bd, nc_=nc_: e.activation(out=k.SF[:, nc_, 0:W], in_=k.ps[bd][0:128, 0:W], func=AF.Copy),
                     r=[("ps", bd)], w=[("SF", nc_)])
            elif R == 1:
                k.op("dve", lambda e, bd=bd, nc_=nc_: e.scalar_tensor_tensor(out=k.SF[:, nc_, 0:W], in0=k.SF[:, nc_, 0:W], scalar=sm[0:128, 4:5],
                                                                              in1=k.ps[bd][0:128, 0:W], op0=ALU.mult, op1=ALU.add),
                     r=[("SF", nc_), smk, ("ps", bd)], w=[("SF", nc_)])
            else:
                k.op("pool", lambda e, nc_=nc_: e.tensor_tensor(out=k.SF[:, nc_, 0:W].rearrange("p (r d) -> p r d", r=R),
                                                                 in0=k.SF[:, nc_, 0:W].rearrange("p (r d) -> p r d", r=R),
                                                                 in1=sm[0:128, 4 * R:5 * R].unsqueeze(2).to_broadcast([128, R, Pd]), op=ALU.mult),
                     r=[("SF", nc_), smk], w=[("SF", nc_)])
            if (not first) and R != 1:
                k.op("dve", lambda e, bd=bd, nc_=nc_: e.tensor_tensor(out=k.SF[:, nc_, 0:W], in0=k.SF[:, nc_, 0:W], in1=k.ps[bd][0:128, 0:W], op=ALU.add),
                     r=[("SF", nc_), ("ps", bd)], w=[("SF", nc_)])
            k.pfree(bd)
            k.op("act", lambda e, nc_=nc_: e.activation(out=k.SB[:, nc_, 0:W], in_=k.SF[:, nc_, 0:W], func=AF.Copy), r=[("SF", nc_)], w=[("SB", nc_)])
            if pstore is not None:
                pstore(nc_)
    else:
        by2 = k.psum()
        for b_ in range(nseq):
            sf = [k.fpool.get() for _ in range(NC)]
            sb = [k.bpool.get() for _ in range(NC)]
            for nc_ in range(NC):
                sload(b_, nc_, sf[nc_])
                k.op("act", lambda e, nc_=nc_, sf=sf, sb=sb: e.activation(out=sb[nc_].t[:, 0:W], in_=sf[nc_].t[:, 0:W], func=AF.Copy),
                     r=[sf[nc_].key], w=[sb[nc_].key])
            qm = k.bpool.get()
            qmv = qm.t[:, 0:NC * T].rearrange("p (c t) -> p c t", c=NC)
            k.op("pool", lambda e, b_=b_, qmv=qmv: e.tensor_tensor(out=qmv, in0=qT[:, :, 0:T], in1=k.colmaskb[:, b_, :].unsqueeze(1).to_broadcast([128, NC, T]),
                                                                   op=ALU.mult), r=[qT_key, "colmaskb"], w=[qm.key])
            for nc_ in range(NC):
                k.mm(k.ps[by2][0:T, 0:W], qmv[:, nc_, :], sb[nc_].t[:, 0:W], (b_ == 0 and nc_ == 0), (b_ == nseq - 1 and nc_ == NC - 1),
                     r=[qm.key, sb[nc_].key], w=[("ps", by2)])
            qm.free()
            km = k.bpool.get()
            k.op("pool", lambda e, b_=b_, km=km: e.tensor_scalar_mul(out=km.t[0:T, 0:NC * 128], in0=ktok[0:T, 0:NC * 128], scalar1=k.c("rowmask_s")[0:T, b_:b_ + 1]),
                 r=[ktok_key, "CONST"], w=[km.key])
            for nc_ in range(NC):
                bd = k.psum()
                k.mm(k.ps[bd][0:128, 0:W], km.t[0:T, nc_ * 128:(nc_ + 1) * 128], vw.t[0:T, 0:W], True, True, r=[km.key, vw.key], w=[("ps", bd)])
                if R == 1:
                    k.op("dve", lambda e, nc_=nc_, sf=sf, bd=bd, b_=b_: e.scalar_tensor_tensor(out=sf[nc_].t[:, 0:W], in0=sf[nc_].t[:, 0:W],
                                                                                               scalar=sm[0:128, 16 + b_:17 + b_], in1=k.ps[bd][0:128, 0:W],
                                                                                               op0=ALU.mult, op1=ALU.add),
                         r=[sf[nc_].key, smk, ("ps", bd)], w=[sf[nc_].key])
                else:
                    k.op("pool", lambda e, nc_=nc_, sf=sf, b_=b_: e.tensor_tensor(out=sf[nc_].t[:, 0:W].rearrange("p (r d) -> p r d", r=R),
                                                                                 in0=sf[nc_].t[:, 0:W].rearrange("p (r d) -> p r d", r=R),
                                                                                 in1=sm[0:128, 16 + b_ * R:16 + (b_ + 1) * R].unsqueeze(2).to_broadcast([128, R, Pd]),
                                                                                 op=ALU.mult), r=[sf[nc_].key, smk], w=[sf[nc_].key])
                    k.op("dve", lambda e, nc_=nc_, sf=sf, bd=bd: e.tensor_tensor(out=sf[nc_].t[:, 0:W], in0=sf[nc_].t[:, 0:W], in1=k.ps[bd][0:128, 0:W], op=ALU.add),
                         r=[sf[nc_].key, ("ps", bd)], w=[sf[nc_].key])
                k.pfree(bd)
                sstore(b_, nc_, sf[nc_])
            km.free()
            for x_ in sf + sb:
                x_.free()
        tmp = k.fpool.get()
        k.op("dve", lambda e: e.tensor_tensor(out=tmp.t[0:T, 0:W].rearrange("p (r d) -> p r d", r=R),
                                              in0=k.ps[by2][0:T, 0:W].rearrange("p (r d) -> p r d", r=R),
                                              in1=sm[0:T, 2 * R:3 * R].unsqueeze(2).to_broadcast([T, R, Pd]), op=ALU.mult),
             r=[("ps", by2), smk], w=[tmp.key])
        k.pfree(by2)
        k.op("dve", lambda e: e.tensor_tensor(out=y.t[0:T, 0:W], in0=tmp.t[0:T, 0:W], in1=k.ps[by1][0:T, 0:W], op=ALU.add),
             r=[tmp.key, ("ps", by1)], w=[y.key])
        tmp.free()
        k.pfree(by1)
    vw.free()
    return y


def proj_tok(k, t, slot, ncols, c_off=0):
    Wv = k.ring[slot][:, :].rearrange("p (k n) -> p k n", k=8)
    b = k.psum()
    for kk in range(8):
        k.mm(k.ps[b][0:t.T, 0:ncols], k.XNT[:, kk, t.c0:t.c0 + t.T], Wv[:, kk, c_off:c_off + ncols], kk == 0, kk == 7,
             r=[("XNT", t.i), ("ring", slot)], w=[("ps", b)])
    return b


def out_proj_add(k, t, ygT, nchunks, slot):
    T = t.T
    Wo = k.ring[slot][:, 0:nchunks * 1024].rearrange("p (c n) -> p c n", c=nchunks)
    yv = ygT.t[:, 0:nchunks * 128].rearrange("p (c t) -> p c t", c=nchunks)
    for half in range(2):
        b = k.psum()
        for c in range(nchunks):
            k.mm(k.ps[b][0:T, :], yv[:, c, 0:T], Wo[:, c, half * 512:(half + 1) * 512], c == 0, c == nchunks - 1,
                 r=[ygT.key, ("ring", slot)], w=[("ps", b)])
        k.op("dve", lambda e, b=b, half=half: e.tensor_tensor(out=k.X[0:T, t.i, half * 512:(half + 1) * 512], in0=k.X[0:T, t.i, half * 512:(half + 1) * 512],
                                                              in1=k.ps[b][0:T, :], op=ALU.add), r=[("ps", b), ("X", t.i)], w=[("X", t.i)])
        k.pfree(b)


def transpose_to(k, t, src, nchunks, eng="act"):
    T = t.T
    b = k.psum()
    pv = k.ps[b][:, :].bitcast(BF16).rearrange("p (c m) -> p c m", c=8)
    for c in range(nchunks):
        k.tr(pv[:, c, 0:T], src.t[0:T, c * 128:(c + 1) * 128], k.identb[0:T, 0:T], r=[src.key, "identb"], w=[("ps", b)])
    dst = k.bpool.get()
    dv = dst.t[:, 0:nchunks * 128].rearrange("p (c t) -> p c t", c=nchunks)
    if eng == "act":
        k.op("act", lambda e: e.activation(out=dv[:, :, 0:T], in_=pv[:, 0:nchunks, 0:T], func=AF.Copy), r=[("ps", b)], w=[dst.key])
    else:
        k.op("dve", lambda e: e.tensor_copy(out=dv[:, :, 0:T], in_=pv[:, 0:nchunks, 0:T]), r=[("ps", b)], w=[dst.key])
    k.pfree(b)
    return dst


def from_ret(k, layer):
    j = layer // 3
    win = k.din["ret_w_in"]
    wout = k.din["ret_w_out"]
    s_ret_in = k.din["state_ret"]
    for h in range(4):
        v8 = lambda tt: tt[:, :].rearrange("p (k n) -> p k n", k=8)
        s_qk = k.load_w([(lambda tt: v8(tt)[:, :, 0:256], win[j, :, h * 256:(h + 1) * 256].rearrange("(k p) n -> p k n", p=128)),
                         (lambda tt: v8(tt)[:, :, 256:512], win[j, :, 1024 + h * 256:1024 + (h + 1) * 256].rearrange("(k p) n -> p k n", p=128))])
        s_v = k.load_w([(v8, win[j, :, 2048 + h * 512:2048 + (h + 1) * 512].rearrange("(k p) n -> p k n", p=128))])
        s_g = k.load_w([(v8, win[j, :, 4096 + h * 512:4096 + (h + 1) * 512].rearrange("(k p) n -> p k n", p=128))])
        s_o = k.load_w([(lambda tt: tt[:, :].rearrange("p (c n) -> p c n", c=4), wout[j, h * 512:(h + 1) * 512, :].rearrange("(c p) n -> p c n", p=128))])
        preps = {}
        for kind in ("p", "s"):
            T_ = 128 if kind == "p" else TS
            preps[kind] = decay_prep(k, kind, k.c("laret")[0:T_, h:h + 1], ["CONST"], 1)

        def stage_a(t):
            T = t.T
            bqk = proj_tok(k, t, s_qk, 512)
            qk = k.fpool.get()
            k.op("act", lambda e: e.activation(out=qk.t[0:T, 0:256], in_=k.ps[bqk][0:T, 0:256], func=AF.Copy), r=[("ps", bqk)], w=[qk.key])
            k.op("act", lambda e: e.activation(out=qk.t[0:T, 256:512], in_=k.ps[bqk][0:T, 256:512], func=AF.Copy, scale=1.0 / 16.0),
                 r=[("ps", bqk)], w=[qk.key])
            k.pfree(bqk)
            cs = k.fpool.get()
            k.dma(cs.t[0:T, 0:256], k.din["ropecs"][t.i, 0:T, :, :].rearrange("p a b -> p (a b)"), w=[cs.key])
            qv = qk.t[0:T, :].rearrange("p (a h m) -> p a h m", a=2, h=2)
            x1, x2 = qv[:, :, 0, :], qv[:, :, 1, :]
            cosb = cs.t[0:T, 0:128].unsqueeze(1).to_broadcast([T, 2, 128])
            sinb = cs.t[0:T, 128:256].unsqueeze(1).to_broadcast([T, 2, 128])
            ta = k.fpool.get()
            tb = k.fpool.get()
            tav = ta.t[0:T, :].rearrange("p (u a m) -> p u a m", u=2, a=2)
            tbv = tb.t[0:T, :].rearrange("p (u a m) -> p u a m", u=2, a=2)
            k.op("pool", lambda e: e.tensor_tensor(out=tav[:, 0], in0=x1, in1=cosb, op=ALU.mult), r=[qk.key, cs.key], w=[ta.key])
            k.op("pool", lambda e: e.tensor_tensor(out=tav[:, 1], in0=x2, in1=sinb, op=ALU.mult), r=[qk.key, cs.key], w=[ta.key])
            k.op("pool", lambda e: e.tensor_tensor(out=tbv[:, 0], in0=x1, in1=sinb, op=ALU.mult), r=[qk.key, cs.key], w=[tb.key])
            k.op("pool", lambda e: e.tensor_tensor(out=tbv[:, 1], in0=x2, in1=cosb, op=ALU.mult), r=[qk.key, cs.key], w=[tb.key])
            rot = k.bpool.get()
            rv = rot.t[0:T, :].rearrange("p (a h m) -> p a h m", a=2, h=2)
            k.op("dve", lambda e: e.tensor_tensor(out=rv[:, :, 0, :], in0=tav[:, 0], in1=tav[:, 1], op=ALU.subtract), r=[ta.key], w=[rot.key])
            k.op("dve", lambda e: e.tensor_tensor(out=rv[:, :, 1, :], in0=tbv[:, 0], in1=tbv[:, 1], op=ALU.add), r=[tb.key], w=[rot.key])
            for x_ in (qk, cs, ta, tb):
                x_.free()
            qkT = transpose_to(k, t, rot, 4)
            bv = proj_tok(k, t, s_v, 512)
            vb = k.bpool.get()
            k.op("act", lambda e: e.activation(out=vb.t[0:T, :], in_=k.ps[bv][0:T, :], func=AF.Copy), r=[("ps", bv)], w=[vb.key])
            k.pfree(bv)
            bg = proj_tok(k, t, s_g, 512)
            sg = k.bpool.get()
            k.op("act", lambda e: e.activation(out=sg.t[0:T, :], in_=k.ps[bg][0:T, :], func=AF.Silu), r=[("ps", bg)], w=[sg.key])
            k.pfree(bg)
            return {"rot": rot, "qkT": qkT, "v": vb, "sg": sg}

        def stage_b(t, A):
            T = t.T
            qkv = A["qkT"].t[:, 0:512].rearrange("p (c t) -> p c t", c=4)
            first = (t.i == 0)

            def sload(b_, nc_, dst):
                k.dma(dst.t[:, :], s_ret_in[j, b_, h, nc_ * 128:(nc_ + 1) * 128, :], w=[dst.key])

            def sstore(b_, nc_, src):
                k.dma(k.dout["s_ret"][j, b_, h, nc_ * 128:(nc_ + 1) * 128, :], src.t[:, :], r=[src.key])

            pstore = None
            if t.i == NPT - 1:
                def pstore(nc_):
                    k.dma(k.dout["p_ret"][j, h, nc_ * 128:(nc_ + 1) * 128, :], k.SF[:, nc_, :], r=[("SF", nc_)])
            y = scan_tile(k, t, preps[t.kind], qkv[:, 0:2, :], A["qkT"].key, qkv[:, 2:4, :], A["qkT"].key,
                          A["rot"].t[:, 256:512], A["rot"].key, A["v"].t, A["v"].key, 1, 512, 2, first, sload, sstore, pstore)
            stt = k.st[k.rr("st", 4)]
            emit_rstd(k, t, stt, y.t[0:T, 0:512], [y.key], 512, 0)
            yg = k.bpool.get()
            k.op("dve", lambda e: e.scalar_tensor_tensor(out=yg.t[0:T, :], in0=y.t[0:T, :], scalar=stt[0:T, 1:2], in1=A["sg"].t[0:T, :],
                                                          op0=ALU.mult, op1=ALU.mult), r=[y.key, ("st", id(stt)), A["sg"].key], w=[yg.key])
            y.free()
            for nm in ("rot", "qkT", "v", "sg"):
                A[nm].free()
            ygT = transpose_to(k, t, yg, 4)
            yg.free()
            out_proj_add(k, t, ygT, 4, s_o)
            ygT.free()

        A = stage_a(TILES[0])
        for ti in range(NT):
            if ti + 1 < NT and CFG.get("coop", True):
                box = {}
                COOP.run([lambda: box.__setitem__("A", stage_a(TILES[ti + 1])), lambda: stage_b(TILES[ti], A)])
                A = box["A"]
            else:
                An = stage_a(TILES[ti + 1]) if ti + 1 < NT else None
                stage_b(TILES[ti], A)
                A = An
        for kind in ("p", "s"):
            preps[kind]["dec"].free()


def silu_to(k, out_ap, in_ap, r, w):
    k.op("act", lambda e: e.activation(out=out_ap, in_=in_ap, func=AF.Silu), r=r, w=w)


def load_conv_params(k, convw, convb):
    k.dma(k.PAR[0:16, :], convw.rearrange("t (q n) -> (t q) n", q=4), w=["PAR"])
    if convb is not None:
        k.dma(k.PAR[32:36, :], convb.rearrange("o (q n) -> (o q) n", q=4), w=["PAR"])
    b = k.psum()
    pv = k.ps[b][:, 0:128].rearrange("p (c x) -> p c x", c=8)
    for c8 in range(8):
        k.tr(pv[:, c8, :], k.PAR[0:16, c8 * 128:(c8 + 1) * 128], k.c("ident")[0:16, 0:16], r=["PAR", "CONST"], w=[("ps", b)])
    for q in range(4):
        k.op("dve", lambda e, q=q: e.tensor_copy(out=k.CW[:, q * 8:(q + 1) * 8, :], in_=pv.rearrange("p c (t q) -> p c t q", q=4)[:, :, :, q]),
             r=[("ps", b)], w=["CW"])
    k.pfree(b)
    if convb is not None:
        b = k.psum()
        pv2 = k.ps[b][:, 0:32].rearrange("p (c q) -> p c q", c=8)
        for c8 in range(8):
            k.tr(pv2[:, c8, :], k.PAR[32:36, c8 * 128:(c8 + 1) * 128], k.c("ident")[32:36, 32:36], r=["PAR", "CONST"], w=[("ps", b)])
        k.op("dve", lambda e: e.tensor_copy(out=k.CB[:, :].rearrange("p (q c) -> p c q", q=4), in_=pv2), r=[("ps", b)], w=["CB"])
        k.pfree(b)


def conv_feature_major(k, t, slot, chunk_ids, halo_src, first, conv_in, has_bias):
    T = t.T
    Wv = k.ring[slot][:, :].rearrange("p (k n) -> p k n", k=8)
    b = k.psum()
    for c in range(4):
        for kk in range(8):
            k.mm(k.ps[b][:, c * T:(c + 1) * T], Wv[:, kk, c * 128:(c + 1) * 128], k.XNT[:, kk, t.c0:t.c0 + T], kk == 0, kk == 7,
                 r=[("ring", slot), ("XNT", t.i)], w=[("ps", b)])
    ui = k.rr("UB", 2)
    U = k.UB[ui]
    uk = ("UB", ui)
    acc = k.fpool.get()
    if not t.sample:
        Uv = U[:, 0:4 * 131].rearrange("p (c x) -> p c x", c=4)
        k.op("act", lambda e: e.activation(out=Uv[:, :, 3:131], in_=k.ps[b][:, 0:512].rearrange("p (c x) -> p c x", c=4), func=AF.Copy),
             r=[("ps", b)], w=[uk])
        if first:
            k.op("pool", lambda e: e.memset(Uv[:, :, 0:3], 0.0), w=[uk])
        else:
            pU = k.UB[1 - ui][:, 0:4 * 131].rearrange("p (c x) -> p c x", c=4)
            k.op("pool", lambda e: e.tensor_copy(out=Uv[:, :, 0:3], in_=pU[:, :, 128:131]), r=[("UB", 1 - ui)], w=[uk])
        accv = acc.t[:, 0:512].rearrange("p (c x) -> p c x", c=4)
        srcs = lambda c, tap: Uv[:, c, tap:tap + 128]
        outs = lambda c: accv[:, c, :]
    else:
        Uv = U[:, 0:4 * 112].rearrange("p (c b x) -> p c b x", c=4, b=16)
        k.op("act", lambda e: e.activation(out=Uv[:, :, :, 3:7], in_=k.ps[b][:, 0:256].rearrange("p (c b x) -> p c b x", c=4, b=16), func=AF.Copy),
             r=[("ps", b)], w=[uk])
        stg = k.fpool.get()
        for (c0_, n_, ch0) in conv_in["segs"]:
            k.dma(stg.t[0:48, c0_:c0_ + n_], conv_in["src"][:, :, ch0:ch0 + n_].rearrange("b j c -> (b j) c"), w=[stg.key])
        bh = k.psum()
        ph = k.ps[bh][:, 0:192].rearrange("p (c x) -> p c x", c=4)
        for c in range(4):
            k.tr(ph[:, c, :], stg.t[0:48, c * 128:(c + 1) * 128], k.c("ident")[0:48, 0:48], r=[stg.key, "CONST"], w=[("ps", bh)])
        stg.free()
        k.op("act", lambda e: e.activation(out=Uv[:, :, :, 0:3], in_=ph.rearrange("p c (b x) -> p c b x", b=16), func=AF.Copy), r=[("ps", bh)], w=[uk])
        k.pfree(bh)
        accv = acc.t[:, 0:256].rearrange("p (c b x) -> p c b x", c=4, b=16)
        srcs = lambda c, tap: Uv[:, c, :, tap:tap + 4]
        outs = lambda c: accv[:, c, :, :]
    k.pfree(b)
    for c in range(4):
        ch = chunk_ids[c]
        eng = "dve"
        if has_bias:
            k.op(eng, lambda e, c=c, ch=ch: e.tensor_scalar(out=outs(c), in0=srcs(c, 0), scalar1=k.CW[:, ch, 0:1], scalar2=k.CB[:, ch:ch + 1],
                                                            op0=ALU.mult, op1=ALU.add), r=[uk, "CW", "CB"], w=[acc.key])
        else:
            k.op(eng, lambda e, c=c, ch=ch: e.tensor_scalar_mul(out=outs(c), in0=srcs(c, 0), scalar1=k.CW[:, ch, 0:1]), r=[uk, "CW"], w=[acc.key])
        for tap in range(1, 4):
            k.op(eng, lambda e, c=c, ch=ch, tap=tap: e.scalar_tensor_tensor(out=outs(c), in0=srcs(c, tap), scalar=k.CW[:, ch, tap:tap + 1], in1=outs(c),
                                                                           op0=ALU.mult, op1=ALU.add), r=[uk, "CW", acc.key], w=[acc.key])
    return acc


def conv_state_out(k, t, slot, segs, dst_p, dst_s):
    T = t.T
    b = proj_tok(k, t, slot, 512)
    stg = k.fpool.get()
    k.op("act", lambda e: e.activation(out=stg.t[0:T, :], in_=k.ps[b][0:T, :], func=AF.Copy), r=[("ps", b)], w=[stg.key])
    k.pfree(b)
    for (c0_, n_, ch0) in segs:
        if not t.sample:
            k.dma(dst_p[0:3, ch0:ch0 + n_], stg.t[125:128, c0_:c0_ + n_], r=[stg.key])
        else:
            for jj in range(3):
                k.dma(dst_s[:, jj, ch0:ch0 + n_], stg.t[1 + jj:64:4, c0_:c0_ + n_], r=[stg.key])
    stg.free()


def from_ssd(k, layer):
    win = k.din["ssd_w_in"]
    wout = k.din["ssd_w_out"]
    k.dma(k.PRM[:, 0:32], k.din["ssd_dt_bias"][0, :].partition_broadcast(128), w=["PRM"])
    k.dma(k.PRM[:, 32:64], k.din["ssd_a_log"][0, :].partition_broadcast(128), w=["PRM"])
    k.dma(k.PRM[:, 64:96], k.din["ssd_d"][0, :].partition_broadcast(128), w=["PRM"])
    k.op("act", lambda e: e.activation(out=k.PRM[:, 96:128], in_=k.PRM[:, 32:64], func=AF.Exp), r=["PRM"], w=["PRM"])
    k.op("dve", lambda e: e.tensor_scalar(out=k.PRM[:, 96:128], in0=k.PRM[:, 96:128], scalar1=-1.0, scalar2=None, op0=ALU.mult), r=["PRM"], w=["PRM"])
    load_conv_params(k, k.din["ssd_conv_w"], k.din["ssd_conv_b"])
    for g in range(8):
        v8 = lambda tt: tt[:, :].rearrange("p (k n) -> p k n", k=8)
        segs = [(0, 256, g * 256), (256, 128, 2048 + g * 128), (384, 128, 3072 + g * 128)]
        s_x = k.load_w([((lambda tt, c0_=c0_, n_=n_: v8(tt)[:, :, c0_:c0_ + n_]),
                         win[:, 2048 + ch0:2048 + ch0 + n_].rearrange("(k p) n -> p k n", p=128)) for (c0_, n_, ch0) in segs])
        s_z = k.load_w([(lambda tt: v8(tt)[:, :, 0:256], win[:, g * 256:(g + 1) * 256].rearrange("(k p) n -> p k n", p=128)),
                        (lambda tt: v8(tt)[:, :, 256:260], win[:, 6144 + 4 * g:6144 + 4 * g + 4].rearrange("(k p) n -> p k n", p=128))])
        s_o = k.load_w([(lambda tt: tt[:, 0:2048].rearrange("p (c n) -> p c n", c=2), wout[g * 256:(g + 1) * 256, :].rearrange("(c p) n -> p c n", p=128))])
        nwb = k.fpool.get()
        k.dma(nwb.t[:, 0:256], k.din["ssd_norm"][0, g * 256:(g + 1) * 256].partition_broadcast(128), w=[nwb.key])
        chunk_ids = [2 * g, 2 * g + 1, 16 + g, 24 + g]
        conv_in = {"src": k.din["state_ssd_conv"], "segs": segs}

        def stage_a(t):
            T = t.T
            acc = conv_feature_major(k, t, s_x, chunk_ids, None, t.i == 0, conv_in, True)
            n4 = 4 * T
            xsT = k.fpool.get()
            bcT = k.bpool.get()
            silu_to(k, xsT.t[:, 0:2 * T], acc.t[:, 0:2 * T], [acc.key], [xsT.key])
            silu_to(k, bcT.t[:, 0:2 * T], acc.t[:, 2 * T:4 * T], [acc.key], [bcT.key])
            acc.free()
            b = k.psum()
            for c in range(2):
                k.tr(k.ps[b][0:T, c * 128:(c + 1) * 128], xsT.t[:, c * T:(c + 1) * T], k.c("ident"), r=[xsT.key, "CONST"], w=[("ps", b)])
            xs = k.fpool.get()
            k.op("act", lambda e: e.activation(out=xs.t[0:T, 0:256], in_=k.ps[b][0:T, 0:256], func=AF.Copy), r=[("ps", b)], w=[xs.key])
            k.pfree(b)
            xsT.free()
            b = k.psum()
            pvb = k.ps[b][:, :].bitcast(BF16)
            k.tr(pvb[0:T, 0:128], bcT.t[:, 0:T], k.identb[:, :], r=[bcT.key, "identb"], w=[("ps", b)])
            btok = k.bpool.get()
            k.op("dve", lambda e: e.tensor_copy(out=btok.t[0:T, 0:128], in_=pvb[0:T, 0:128]), r=[("ps", b)], w=[btok.key])
            k.pfree(b)
            bz = proj_tok(k, t, s_z, 260)
            sz = k.bpool.get()
            silu_to(k, sz.t[0:T, 0:256], k.ps[bz][0:T, 0:256], [("ps", bz)], [sz.key])
            li = k.rr("LAB", 4)
            lab = k.LAB[li]
            lk = ("LAB", li)
            k.op("dve", lambda e: e.tensor_tensor(out=lab[0:T, 0:4], in0=k.ps[bz][0:T, 256:260], in1=k.PRM[0:T, 4 * g:4 * g + 4], op=ALU.add),
                 r=[("ps", bz), "PRM"], w=[lk])
            k.pfree(bz)
            k.op("act", lambda e: e.activation(out=lab[0:T, 0:4], in_=lab[0:T, 0:4], func=AF.Exp), r=[lk], w=[lk])
            k.op("pool", lambda e: e.tensor_scalar(out=lab[0:T, 0:4], in0=lab[0:T, 0:4], scalar1=1.0, scalar2=None, op0=ALU.add), r=[lk], w=[lk])
            k.op("act", lambda e: e.activation(out=lab[0:T, 0:4], in_=lab[0:T, 0:4], func=AF.Ln), r=[lk], w=[lk])
            k.op("dve", lambda e: e.tensor_tensor(out=lab[0:T, 4:8], in0=lab[0:T, 0:4], in1=k.PRM[0:T, 96 + 4 * g:96 + 4 * g + 4], op=ALU.mult),
                 r=[lk, "PRM"], w=[lk])
            vdt = k.bpool.get()
            k.op("pool", lambda e: e.tensor_tensor(out=vdt.t[0:T, 0:256].rearrange("p (r d) -> p r d", r=4), in0=xs.t[0:T, 0:256].rearrange("p (r d) -> p r d", r=4),
                                                   in1=lab[0:T, 0:4].unsqueeze(2).to_broadcast([T, 4, 64]), op=ALU.mult), r=[xs.key, lk], w=[vdt.key])
            return {"bcT": bcT, "xs": xs, "btok": btok, "sz": sz, "lab": lab, "lk": lk, "vdt": vdt}

        def stage_b(t, A):
            T = t.T
            prep = decay_prep(k, t.kind, A["lab"][0:T, 4:8], [A["lk"]], 4)
            bcv = A["bcT"].t[:, 0:2 * T].rearrange("p (c t) -> p c t", c=2)

            def sload(b_, nc_, dst):
                stg = k.fpool.get()
                k.dma(stg.t[:, 0:256].rearrange("p (j n) -> p j n", j=2),
                      k.din["state_ssd"][b_, 4 * g:4 * g + 4, :, :].rearrange("(j h) p n -> (h p) j n", j=2), w=[stg.key])
                bb = k.psum()
                for jj in range(2):
                    k.tr(k.ps[bb][:, jj * 128:(jj + 1) * 128], stg.t[:, jj * 128:(jj + 1) * 128], k.c("ident"), r=[stg.key, "CONST"], w=[("ps", bb)])
                stg.free()
                k.op("dve", lambda e: e.tensor_copy(out=dst.t[:, 0:256], in_=k.ps[bb][:, 0:256]), r=[("ps", bb)], w=[dst.key])
                k.pfree(bb)

            def st_out(src_ap, src_key, dst_ap):
                bb = k.psum()
                for jj in range(2):
                    k.tr(k.ps[bb][:, jj * 128:(jj + 1) * 128], src_ap[:, jj * 128:(jj + 1) * 128], k.c("ident"), r=[src_key, "CONST"], w=[("ps", bb)])
                stg = k.fpool.get()
                k.op("act", lambda e: e.activation(out=stg.t[:, 0:256], in_=k.ps[bb][:, 0:256], func=AF.Copy), r=[("ps", bb)], w=[stg.key])
                k.pfree(bb)
                k.dma(dst_ap.rearrange("(j h) p n -> (h p) j n", j=2), stg.t[:, 0:256].rearrange("p (j n) -> p j n", j=2), r=[stg.key])
                stg.free()

            def sstore(b_, nc_, src):
                st_out(src.t[:, 0:256], src.key, k.dout["s_ssd"][b_, 4 * g:4 * g + 4, :, :])

            pstore = None
            if t.i == NPT - 1:
                def pstore(nc_):
                    st_out(k.SF[:, 0, 0:256], ("SF", 0), k.dout["p_ssd"][4 * g:4 * g + 4, :, :])
            y = scan_tile(k, t, prep, bcv[:, 1:2, :], A["bcT"].key, bcv[:, 0:1, :], A["bcT"].key, A["btok"].t[:, 0:128], A["btok"].key,
                          A["vdt"].t, A["vdt"].key, 4, 64, 1, t.i == 0, sload, sstore, pstore)
            prep["dec"].free()
            tmp = k.fpool.get()
            k.op("pool", lambda e: e.tensor_tensor(out=tmp.t[0:T, 0:256].rearrange("p (r d) -> p r d", r=4), in0=A["xs"].t[0:T, 0:256].rearrange("p (r d) -> p r d", r=4),
                                                   in1=k.PRM[0:T, 64 + 4 * g:64 + 4 * g + 4].unsqueeze(2).to_broadcast([T, 4, 64]), op=ALU.mult),
                 r=[A["xs"].key, "PRM"], w=[tmp.key])
            k.op("dve", lambda e: e.tensor_tensor(out=y.t[0:T, 0:256], in0=y.t[0:T, 0:256], in1=tmp.t[0:T, 0:256], op=ALU.add), r=[y.key, tmp.key], w=[y.key])
            tmp.free()
            k.op("dve", lambda e: e.tensor_tensor(out=y.t[0:T, 0:256], in0=y.t[0:T, 0:256], in1=A["sz"].t[0:T, 0:256], op=ALU.mult), r=[y.key, A["sz"].key], w=[y.key])
            stt = k.st[k.rr("st", 4)]
            emit_rstd(k, t, stt, y.t[0:T, 0:256], [y.key], 256, 0)
            yn = k.bpool.get()
            k.op("dve", lambda e: e.scalar_tensor_tensor(out=yn.t[0:T, 0:256], in0=y.t[0:T, 0:256], scalar=stt[0:T, 1:2], in1=nwb.t[0:T, 0:256],
                                                          op0=ALU.mult, op1=ALU.mult), r=[y.key, ("st", id(stt)), nwb.key], w=[yn.key])
            y.free()
            for nm in ("bcT", "xs", "btok", "sz", "vdt"):
                A[nm].free()
            ynT = transpose_to(k, t, yn, 2)
            yn.free()
            out_proj_add(k, t, ynT, 2, s_o)
            ynT.free()
            if t.i >= NPT - 1:
                conv_state_out(k, t, s_x, segs, k.dout["p_ssd_conv"], k.dout["s_ssd_conv"])

        A = stage_a(TILES[0])
        for ti in range(NT):
            if ti + 1 < NT and CFG.get("coop", True):
                box = {}
                COOP.run([lambda: box.__setitem__("A", stage_a(TILES[ti + 1])), lambda: stage_b(TILES[ti], A)])
                A = box["A"]
            else:
                An = stage_a(TILES[ti + 1]) if ti + 1 < NT else None
                stage_b(TILES[ti], A)
                A = An
        nwb.free()


def gdn_chain(k, T, A, nlev):
    identb = k.identb[0:T, 0:T]
    v3 = lambda buf: buf.t[0:T, 0:2 * T].rearrange("p (r t) -> p r t", r=2)
    sl = lambda buf, r_: buf.t[0:T, r_ * T:(r_ + 1) * T]
    b = k.psum()
    pvb = k.ps[b][:, :].bitcast(BF16)
    for r_ in range(2):
        k.tr(pvb[0:T, r_ * T:(r_ + 1) * T], sl(A, r_), identb, r=[A.key, "identb"], w=[("ps", b)])
    AT = k.bpool.get()
    k.op("act", lambda e: e.activation(out=AT.t[0:T, 0:2 * T], in_=pvb[0:T, 0:2 * T], func=AF.Copy), r=[("ps", b)], w=[AT.key])
    k.pfree(b)
    Dm = None
    DT = None
    for lv in range(nlev):
        last = (lv == nlev - 1)
        mT = k.c("bmT%d" % lv)[0:T, 0:T]
        AoT = k.bpool.get()
        k.op("pool", lambda e: e.tensor_tensor(out=v3(AoT), in0=v3(AT), in1=mT.unsqueeze(1).to_broadcast([T, 2, T]), op=ALU.mult),
             r=[AT.key, "CONST"], w=[AoT.key])
        dk = [Dm.key] if Dm is not None else ["identb"]
        dtk = [DT.key] if DT is not None else ["identb"]
        Dv = (lambda r_: sl(Dm, r_)) if Dm is not None else (lambda r_: identb)
        DTv = (lambda r_: sl(DT, r_)) if DT is not None else (lambda r_: identb)
        be = k.psum()
        for r_ in range(2):
            k.mm(k.ps[be][0:T, r_ * T:(r_ + 1) * T], sl(AoT, r_), Dv(r_), True, True, r=[AoT.key] + dk, w=[("ps", be)])
        E = k.bpool.get()
        k.op("act", lambda e: e.activation(out=E.t[0:T, 0:2 * T], in_=k.ps[be][0:T, 0:2 * T], func=AF.Copy), r=[("ps", be)], w=[E.key])
        k.pfree(be)
        AoT.free()
        bft = k.psum()
        for r_ in range(2):
            k.mm(k.ps[bft][0:T, r_ * T:(r_ + 1) * T], sl(E, r_), DTv(r_), True, True, r=[E.key] + dtk, w=[("ps", bft)])
        DTn = k.bpool.get()
        if DT is not None:
            k.op("dve", lambda e: e.tensor_tensor(out=DTn.t[0:T, 0:2 * T], in0=DT.t[0:T, 0:2 * T], in1=k.ps[bft][0:T, 0:2 * T], op=ALU.add),
                 r=[DT.key, ("ps", bft)], w=[DTn.key])
        else:
            k.op("dve", lambda e: e.tensor_tensor(out=v3(DTn), in0=k.ps[bft][0:T, 0:2 * T].rearrange("p (r t) -> p r t", r=2),
                                                  in1=identb.unsqueeze(1).to_broadcast([T, 2, T]), op=ALU.add), r=["identb", ("ps", bft)], w=[DTn.key])
        k.pfree(bft)
        Dn = None
        if not last:
            bf_ = k.psum()
            for r_ in range(2):
                k.mm(k.ps[bf_][0:T, r_ * T:(r_ + 1) * T], DTv(r_), sl(E, r_), True, True, r=[E.key] + dtk, w=[("ps", bf_)])
            Dn = k.bpool.get()
            if Dm is not None:
                k.op("dve", lambda e: e.tensor_tensor(out=Dn.t[0:T, 0:2 * T], in0=Dm.t[0:T, 0:2 * T], in1=k.ps[bf_][0:T, 0:2 * T], op=ALU.add),
                     r=[Dm.key, ("ps", bf_)], w=[Dn.key])
            else:
                k.op("dve", lambda e: e.tensor_tensor(out=v3(Dn), in0=k.ps[bf_][0:T, 0:2 * T].rearrange("p (r t) -> p r t", r=2),
                                                      in1=identb.unsqueeze(1).to_broadcast([T, 2, T]), op=ALU.add), r=["identb", ("ps", bf_)], w=[Dn.key])
            k.pfree(bf_)
        E.free()
        if Dm is not None:
            Dm.free()
        if DT is not None:
            DT.free()
        Dm, DT = Dn, DTn
    AT.free()
    return DT


def from_gdn(k, layer):
    win = k.din["gdn_w_in"]
    wout = k.din["gdn_w_out"]
    k.dma(k.PRM[:, 0:16], k.din["gdn_dt_bias"][0, :].partition_broadcast(128), w=["PRM"])
    k.dma(k.PRM[:, 16:32], k.din["gdn_a_log"][0, :].partition_broadcast(128), w=["PRM"])
    k.op("act", lambda e: e.activation(out=k.PRM[:, 32:48], in_=k.PRM[:, 16:32], func=AF.Exp), r=["PRM"], w=["PRM"])
    k.op("dve", lambda e: e.tensor_scalar(out=k.PRM[:, 32:48], in0=k.PRM[:, 32:48], scalar1=-1.0, scalar2=None, op0=ALU.mult), r=["PRM"], w=["PRM"])
    load_conv_params(k, k.din["gdn_conv_w"], None)
    gnw = k.fpool.get()
    k.dma(gnw.t[:, 0:128], k.din["gdn_norm"][0, :].partition_broadcast(128), w=[gnw.key])
    for kh in range(8):
        v8 = lambda tt: tt[:, :].rearrange("p (k n) -> p k n", k=8)
        segs = [(0, 128, kh * 128), (128, 128, 1024 + kh * 128), (256, 256, 2048 + kh * 256)]
        s_x = k.load_w([((lambda tt, c0_=c0_, n_=n_: v8(tt)[:, :, c0_:c0_ + n_]),
                         win[:, ch0:ch0 + n_].rearrange("(k p) n -> p k n", p=128)) for (c0_, n_, ch0) in segs])
        s_z = k.load_w([(lambda tt: v8(tt)[:, :, 0:256], win[:, 4096 + kh * 256:4096 + (kh + 1) * 256].rearrange("(k p) n -> p k n", p=128)),
                        (lambda tt: v8(tt)[:, :, 256:258], win[:, 6144 + 2 * kh:6144 + 2 * kh + 2].rearrange("(k p) n -> p k n", p=128)),
                        (lambda tt: v8(tt)[:, :, 258:260], win[:, 6160 + 2 * kh:6160 + 2 * kh + 2].rearrange("(k p) n -> p k n", p=128))])
        s_o = k.load_w([(lambda tt: tt[:, 0:2048].rearrange("p (c n) -> p c n", c=2), wout[kh * 256:(kh + 1) * 256, :].rearrange("(c p) n -> p c n", p=128))])
        chunk_ids = [kh, 8 + kh, 16 + 2 * kh, 17 + 2 * kh]
        conv_in = {"src": k.din["state_gdn_conv"], "segs": segs}

        def stage_a(t):
            T = t.T
            acc = conv_feature_major(k, t, s_x, chunk_ids, None, t.i == 0, conv_in, False)
            act4 = k.fpool.get()
            silu_to(k, act4.t[:, 0:4 * T], acc.t[:, 0:4 * T], [acc.key], [act4.key])
            acc.free()
            sq = k.fpool.get()
            k.op("pool", lambda e: e.tensor_tensor(out=sq.t[:, 0:2 * T], in0=act4.t[:, 0:2 * T], in1=act4.t[:, 0:2 * T], op=ALU.mult), r=[act4.key], w=[sq.key])
            b = k.psum()
            k.mm(k.ps[b][:, 0:2 * T], k.c("ones"), sq.t[:, 0:2 * T], True, True, r=["CONST", sq.key], w=[("ps", b)])
            k.op("dve", lambda e: e.tensor_scalar(out=sq.t[:, 0:2 * T], in0=k.ps[b][:, 0:2 * T], scalar1=RMS_EPS, scalar2=None, op0=ALU.add),
                 r=[("ps", b)], w=[sq.key])
            k.pfree(b)
            k.op("act", lambda e: e.activation(out=sq.t[:, 0:2 * T], in_=sq.t[:, 0:2 * T], func=AF.Ln), r=[sq.key], w=[sq.key])
            k.op("act", lambda e: e.activation(out=sq.t[:, 0:2 * T], in_=sq.t[:, 0:2 * T], func=AF.Exp, scale=-0.5), r=[sq.key], w=[sq.key])
            qkn = k.bpool.get()
            k.op("dve", lambda e: e.scalar_tensor_tensor(out=qkn.t[:, 0:T], in0=act4.t[:, 0:T], scalar=float(128.0 ** -0.5), in1=sq.t[:, 0:T],
                                                          op0=ALU.mult, op1=ALU.mult), r=[act4.key, sq.key], w=[qkn.key])
            k.op("pool", lambda e: e.tensor_tensor(out=qkn.t[:, T:2 * T], in0=act4.t[:, T:2 * T], in1=sq.t[:, T:2 * T], op=ALU.mult),
                 r=[act4.key, sq.key], w=[qkn.key])
            sq.free()
            b = k.psum()
            pvb = k.ps[b][:, :].bitcast(BF16)
            k.tr(pvb[0:T, 0:128], qkn.t[:, T:2 * T], k.identb[:, :], r=[qkn.key, "identb"], w=[("ps", b)])
            ktok = k.bpool.get()
            k.op("dve", lambda e: e.tensor_copy(out=ktok.t[0:T, 0:128], in_=pvb[0:T, 0:128]), r=[("ps", b)], w=[ktok.key])
            k.pfree(b)
            b = k.psum()
            for c in range(2):
                k.tr(k.ps[b][0:T, c * 128:(c + 1) * 128], act4.t[:, (2 + c) * T:(3 + c) * T], k.c("ident"), r=[act4.key, "CONST"], w=[("ps", b)])
            vtok = k.fpool.get()
            k.op("act", lambda e: e.activation(out=vtok.t[0:T, 0:256], in_=k.ps[b][0:T, 0:256], func=AF.Copy), r=[("ps", b)], w=[vtok.key])
            k.pfree(b)
            act4.free()
            bz = proj_tok(k, t, s_z, 260)
            sz = k.bpool.get()
            silu_to(k, sz.t[0:T, 0:256], k.ps[bz][0:T, 0:256], [("ps", bz)], [sz.key])
            li = k.rr("LAB", 4)
            lab = k.LAB[li]
            lk = ("LAB", li)
            k.op("act", lambda e: e.activation(out=lab[0:T, 0:2], in_=k.ps[bz][0:T, 256:258], func=AF.Exp, scale=-1.0), r=[("ps", bz)], w=[lk])
            k.op("dve", lambda e: e.tensor_tensor(out=lab[0:T, 4:6], in0=k.ps[bz][0:T, 258:260], in1=k.PRM[0:T, 2 * kh:2 * kh + 2], op=ALU.add),
                 r=[("ps", bz), "PRM"], w=[lk])
            k.pfree(bz)
            k.op("pool", lambda e: e.tensor_scalar(out=lab[0:T, 0:2], in0=lab[0:T, 0:2], scalar1=1.0, scalar2=None, op0=ALU.add), r=[lk], w=[lk])
            k.op("dve", lambda e: e.reciprocal(out=lab[0:T, 0:2], in_=lab[0:T, 0:2]), r=[lk], w=[lk])
            k.op("dve", lambda e: e.tensor_scalar(out=lab[0:T, 2:4], in0=lab[0:T, 0:2], scalar1=-1.0, scalar2=None, op0=ALU.mult), r=[lk], w=[lk])
            k.op("act", lambda e: e.activation(out=lab[0:T, 4:6], in_=lab[0:T, 4:6], func=AF.Exp), r=[lk], w=[lk])
            k.op("pool", lambda e: e.tensor_scalar(out=lab[0:T, 4:6], in0=lab[0:T, 4:6], scalar1=1.0, scalar2=None, op0=ALU.add), r=[lk], w=[lk])
            k.op("act", lambda e: e.activation(out=lab[0:T, 4:6], in_=lab[0:T, 4:6], func=AF.Ln), r=[lk], w=[lk])
            k.op("dve", lambda e: e.tensor_tensor(out=lab[0:T, 4:6], in0=lab[0:T, 4:6], in1=k.PRM[0:T, 32 + 2 * kh:32 + 2 * kh + 2], op=ALU.mult),
                 r=[lk, "PRM"], w=[lk])
            return {"qkn": qkn, "ktok": ktok, "vtok": vtok, "sz": sz, "lab": lab, "lk": lk}

        def stage_b(t, A_):
            T = t.T
            nseq = t.nseq
            first = (t.i == 0)
            lab, lk = A_["lab"], A_["lk"]
            qkn, ktok, vtok = A_["qkn"], A_["ktok"], A_["vtok"]
            qT = qkn.t[:, 0:T]
            kT = qkn.t[:, T:2 * T]
            prep = decay_prep(k, t.kind, lab[0:T, 4:6], [lk], 2)
            sm, smk, dec = prep["sm"], prep["smk"], prep["dec"]
            strict = kconst(k, t.kind, "strict")
            bg = k.psum()
            k.mm(k.ps[bg][0:T, 0:T], kT, kT, True, True, r=[qkn.key], w=[("ps", bg)])
            k.mm(k.ps[bg][0:T, T:2 * T], kT, qT, True, True, r=[qkn.key], w=[("ps", bg)])
            bd_ = k.psum()
            for r_ in range(2):
                k.tr(k.ps[bd_][0:T, r_ * T:(r_ + 1) * T], dec.t[0:T, r_ * T:(r_ + 1) * T], k.c("ident")[0:T, 0:T], r=[dec.key, "CONST"], w=[("ps", bd_)])
            dsb = k.fpool.get()
            k.op("dve", lambda e: e.tensor_tensor(out=dsb.t[0:T, 0:2 * T].rearrange("p (r t) -> p r t", r=2),
                                                  in0=k.ps[bd_][0:T, 0:2 * T].rearrange("p (r t) -> p r t", r=2),
                                                  in1=strict.unsqueeze(1).to_broadcast([T, 2, T]), op=ALU.mult), r=[("ps", bd_), "CONST"], w=[dsb.key])
            k.pfree(bd_)
            Am = k.bpool.get()
            for r_ in range(2):
                k.op("dve", lambda e, r_=r_: e.scalar_tensor_tensor(out=Am.t[0:T, r_ * T:(r_ + 1) * T], in0=k.ps[bg][0:T, 0:T], scalar=lab[0:T, 2 + r_:3 + r_],
                                                                     in1=dsb.t[0:T, r_ * T:(r_ + 1) * T], op0=ALU.mult, op1=ALU.mult),
                     r=[("ps", bg), lk, dsb.key], w=[Am.key])
            dsb.free()
            attnT = k.bpool.get()
            k.op("dve", lambda e: e.tensor_tensor(out=attnT.t[0:T, 0:2 * T].rearrange("p (r t) -> p r t", r=2),
                                                  in0=dec.t[0:T, 0:2 * T].rearrange("p (r t) -> p r t", r=2),
                                                  in1=k.ps[bg][0:T, T:2 * T].unsqueeze(1).to_broadcast([T, 2, T]), op=ALU.mult),
                 r=[dec.key, ("ps", bg)], w=[attnT.key])
            k.pfree(bg)
            dec.free()
            TTd = gdn_chain(k, T, Am, 7 if not t.sample else 2)
            Am.free()
            TTf = k.bpool.get()
            for r_ in range(2):
                k.op("act", lambda e, r_=r_: e.activation(out=TTf.t[0:T, r_ * T:(r_ + 1) * T], in_=TTd.t[0:T, r_ * T:(r_ + 1) * T], func=AF.Copy,
                                                            scale=lab[0:T, 2 + r_:3 + r_]), r=[TTd.key, lk], w=[TTf.key])
            TTd.free()
            tmpb = k.bpool.get()
            bks = None
            if not (nseq == 1 and first):
                bks = k.psum()
                if nseq == 1:
                    for r_ in range(2):
                        k.mm(k.ps[bks][0:T, r_ * 256:r_ * 256 + 128], kT, k.SB[:, 0, r_ * 128:(r_ + 1) * 128], True, True, r=[qkn.key, ("SB", 0)], w=[("ps", bks)])
                        k.mm(k.ps[bks][0:T, r_ * 256 + 128:r_ * 256 + 256], qT, k.SB[:, 0, r_ * 128:(r_ + 1) * 128], True, True, r=[qkn.key, ("SB", 0)], w=[("ps", bks)])
                else:
                    sbl = []
                    for b_ in range(nseq):
                        sf = k.fpool.get()
                        k.dma(sf.t[:, 0:256].rearrange("p (r v) -> p r v", r=2), k.din["state_gdn"][b_, 2 * kh:2 * kh + 2, :, :].rearrange("r k v -> k r v"), w=[sf.key])
                        sb = k.bpool.get()
                        k.op("act", lambda e, sb=sb, sf=sf: e.activation(out=sb.t[:, 0:256], in_=sf.t[:, 0:256], func=AF.Copy), r=[sf.key], w=[sb.key])
                        sf.free()
                        qm = k.bpool.get()
                        k.op("pool", lambda e, qm=qm, b_=b_: e.tensor_tensor(out=qm.t[:, 0:2 * T].rearrange("p (c t) -> p c t", c=2),
                                                                            in0=qkn.t[:, 0:2 * T].rearrange("p (c t) -> p c t", c=2),
                                                                            in1=k.colmaskb[:, b_, :].unsqueeze(1).to_broadcast([128, 2, T]), op=ALU.mult),
                             r=[qkn.key, "colmaskb"], w=[qm.key])
                        for r_ in range(2):
                            k.P.op("pe", lambda e, qm=qm, sb=sb, r_=r_, b_=b_: e.matmul(k.ps[bks][0:T, r_ * 256:r_ * 256 + 128], lhsT=qm.t[:, T:2 * T],
                                                                                         rhs=sb.t[:, r_ * 128:(r_ + 1) * 128], start=(b_ == 0 and r_ == 0),
                                                                                         stop=(b_ == nseq - 1), skip_group_check=True),
                                   r=[qm.key, sb.key], w=[("ps", bks)])
                            k.P.op("pe", lambda e, qm=qm, sb=sb, r_=r_, b_=b_: e.matmul(k.ps[bks][0:T, r_ * 256 + 128:r_ * 256 + 256], lhsT=qm.t[:, 0:T],
                                                                                         rhs=sb.t[:, r_ * 128:(r_ + 1) * 128], start=False,
                                                                                         stop=(b_ == nseq - 1), skip_group_check=True),
                                   r=[qm.key, sb.key], w=[("ps", bks)])
                        qm.free()
                        sb.free()
                for r_ in range(2):
                    k.op("dve", lambda e, r_=r_: e.scalar_tensor_tensor(out=tmpb.t[0:T, r_ * 128:(r_ + 1) * 128], in0=k.ps[bks][0:T, r_ * 256:r_ * 256 + 128],
                                                                         scalar=sm[0:T, 4 + r_:5 + r_], in1=vtok.t[0:T, r_ * 128:(r_ + 1) * 128],
                                                                         op0=ALU.mult, op1=ALU.subtract), r=[("ps", bks), smk, vtok.key], w=[tmpb.key])
            else:
                k.op("act", lambda e: e.activation(out=tmpb.t[0:T, 0:256], in_=vtok.t[0:T, 0:256], func=AF.Copy, scale=-1.0), r=[vtok.key], w=[tmpb.key])
            bv = k.psum()
            for r_ in range(2):
                k.mm(k.ps[bv][0:T, r_ * 128:(r_ + 1) * 128], TTf.t[0:T, r_ * T:(r_ + 1) * T], tmpb.t[0:T, r_ * 128:(r_ + 1) * 128], True, True,
                     r=[TTf.key, tmpb.key], w=[("ps", bv)])
            TTf.free()
            tmpb.free()
            vnew = k.bpool.get()
            k.op("act", lambda e: e.activation(out=vnew.t[0:T, 0:256], in_=k.ps[bv][0:T, 0:256], func=AF.Copy), r=[("ps", bv)], w=[vnew.key])
            k.pfree(bv)
            bo = k.psum()
            for r_ in range(2):
                k.mm(k.ps[bo][0:T, r_ * 128:(r_ + 1) * 128], attnT.t[0:T, r_ * T:(r_ + 1) * T], vnew.t[0:T, r_ * 128:(r_ + 1) * 128], True, True,
                     r=[attnT.key, vnew.key], w=[("ps", bo)])
            attnT.free()
            o = k.fpool.get()
            if bks is not None:
                tq = k.fpool.get()
                for r_ in range(2):
                    k.op("act", lambda e, r_=r_: e.activation(out=tq.t[0:T, r_ * 128:(r_ + 1) * 128], in_=k.ps[bks][0:T, r_ * 256 + 128:r_ * 256 + 256], func=AF.Copy,
                                                                scale=sm[0:T, 4 + r_:5 + r_]), r=[("ps", bks), smk], w=[tq.key])
                k.pfree(bks)
                k.op("dve", lambda e: e.tensor_tensor(out=o.t[0:T, 0:256], in0=tq.t[0:T, 0:256], in1=k.ps[bo][0:T, 0:256], op=ALU.add), r=[tq.key, ("ps", bo)], w=[o.key])
                tq.free()
            else:
                k.op("act", lambda e: e.activation(out=o.t[0:T, 0:256], in_=k.ps[bo][0:T, 0:256], func=AF.Copy), r=[("ps", bo)], w=[o.key])
            k.pfree(bo)
            kd = k.bpool.get()
            for r_ in range(2):
                k.op("pool", lambda e, r_=r_: e.tensor_scalar_mul(out=kd.t[0:T, r_ * 128:(r_ + 1) * 128], in0=ktok.t[0:T, 0:128], scalar1=sm[0:T, 6 + r_:7 + r_]),
                     r=[ktok.key, smk], w=[kd.key])
            if nseq == 1:
                bs_ = k.psum()
                for r_ in range(2):
                    k.mm(k.ps[bs_][:, r_ * 128:(r_ + 1) * 128], kd.t[0:T, r_ * 128:(r_ + 1) * 128], vnew.t[0:T, r_ * 128:(r_ + 1) * 128], True, True,
                         r=[kd.key, vnew.key], w=[("ps", bs_)])
                for r_ in range(2):
                    if first:
                        k.op("act", lambda e, r_=r_: e.activation(out=k.SF[:, 0, r_ * 128:(r_ + 1) * 128], in_=k.ps[bs_][:, r_ * 128:(r_ + 1) * 128], func=AF.Copy),
                             r=[("ps", bs_)], w=[("SF", 0)])
                    else:
                        k.op("dve", lambda e, r_=r_: e.scalar_tensor_tensor(out=k.SF[:, 0, r_ * 128:(r_ + 1) * 128], in0=k.SF[:, 0, r_ * 128:(r_ + 1) * 128],
                                                                             scalar=sm[0:128, 8 + r_:9 + r_], in1=k.ps[bs_][:, r_ * 128:(r_ + 1) * 128],
                                                                             op0=ALU.mult, op1=ALU.add), r=[("SF", 0), smk, ("ps", bs_)], w=[("SF", 0)])
                k.pfree(bs_)
                k.op("act", lambda e: e.activation(out=k.SB[:, 0, 0:256], in_=k.SF[:, 0, 0:256], func=AF.Copy), r=[("SF", 0)], w=[("SB", 0)])
                if t.i == NPT - 1:
                    k.dma(k.dout["p_gdn"][2 * kh:2 * kh + 2, :, :].rearrange("r k v -> k r v"), k.SF[:, 0, 0:256].rearrange("p (r v) -> p r v", r=2), r=[("SF", 0)])
            else:
                for b_ in range(nseq):
                    sf = k.fpool.get()
                    k.dma(sf.t[:, 0:256].rearrange("p (r v) -> p r v", r=2), k.din["state_gdn"][b_, 2 * kh:2 * kh + 2, :, :].rearrange("r k v -> k r v"), w=[sf.key])
                    km = k.bpool.get()
                    k.op("pool", lambda e, km=km, b_=b_: e.tensor_scalar_mul(out=km.t[0:T, 0:256], in0=kd.t[0:T, 0:256], scalar1=k.c("rowmask_s")[0:T, b_:b_ + 1]),
                         r=[kd.key, "CONST"], w=[km.key])
                    bs_ = k.psum()
                    for r_ in range(2):
                        k.mm(k.ps[bs_][:, r_ * 128:(r_ + 1) * 128], km.t[0:T, r_ * 128:(r_ + 1) * 128], vnew.t[0:T, r_ * 128:(r_ + 1) * 128], True, True,
                             r=[km.key, vnew.key], w=[("ps", bs_)])
                    km.free()
                    for r_ in range(2):
                        k.op("dve", lambda e, r_=r_, sf=sf, bs_=bs_, b_=b_: e.scalar_tensor_tensor(
                            out=sf.t[:, r_ * 128:(r_ + 1) * 128], in0=sf.t[:, r_ * 128:(r_ + 1) * 128], scalar=sm[0:128, 16 + 2 * b_ + r_:17 + 2 * b_ + r_],
                            in1=k.ps[bs_][:, r_ * 128:(r_ + 1) * 128], op0=ALU.mult, op1=ALU.add), r=[sf.key, smk, ("ps", bs_)], w=[sf.key])
                    k.pfree(bs_)
                    k.dma(k.dout["s_gdn"][b_, 2 * kh:2 * kh + 2, :, :].rearrange("r k v -> k r v"), sf.t[:, 0:256].rearrange("p (r v) -> p r v", r=2), r=[sf.key])
                    sf.free()
            kd.free()
            vnew.free()
            stt = k.st[k.rr("st", 4)]
            sk_ = ("st", id(stt))
            yn = k.bpool.get()
            for r_ in range(2):
                emit_rstd(k, t, stt, o.t[0:T, r_ * 128:(r_ + 1) * 128], [o.key], 128, 4 * r_)
                k.op("dve", lambda e, r_=r_: e.scalar_tensor_tensor(out=o.t[0:T, r_ * 128:(r_ + 1) * 128], in0=o.t[0:T, r_ * 128:(r_ + 1) * 128],
                                                                     scalar=stt[0:T, 4 * r_ + 1:4 * r_ + 2], in1=gnw.t[0:T, 0:128], op0=ALU.mult, op1=ALU.mult),
                     r=[o.key, sk_, gnw.key], w=[o.key])
            k.op("dve", lambda e: e.tensor_tensor(out=yn.t[0:T, 0:256], in0=o.t[0:T, 0:256], in1=A_["sz"].t[0:T, 0:256], op=ALU.mult), r=[o.key, A_["sz"].key], w=[yn.key])
            o.free()
            for nm in ("qkn", "ktok", "vtok", "sz"):
                A_[nm].free()
            ynT = transpose_to(k, t, yn, 2)
            yn.free()
            out_proj_add(k, t, ynT, 2, s_o)
            ynT.free()
            if t.i >= NPT - 1:
                conv_state_out(k, t, s_x, segs, k.dout["p_gdn_conv"], k.dout["s_gdn_conv"])

        for ti in range(NT):
            A_ = stage_a(TILES[ti])
            stage_b(TILES[ti], A_)
    gnw.free()


def zero_unwritten_outputs(k):
    pass


_CACHE = {}


def _get_program():
    if "nc" not in _CACHE:
        pack, offs, cs, colmask = build_consts()
        offs = dict(offs)
        offs["_tot"] = pack.shape[1]
        _set_shapes(pack.shape[1])
        nc = bass.Bass("TRN2", target_bir_lowering=False)
        _CACHE["k"] = build_program(nc, offs)
        _CACHE["nc"] = nc
        _CACHE["used"] = set(_CACHE["k"].din.keys())
        _CACHE["pack"] = pack
        _CACHE["cs"] = cs
        _CACHE["colmask"] = colmask
    return _CACHE["nc"], _CACHE["pack"], _CACHE["cs"]


def kernel(x_prompt, x_sample, state_ret, state_ssd, state_ssd_conv, state_gdn, state_gdn_conv,
           norm_mix, norm_mlp, norm_final, ret_w_in, ret_w_out,
           ssd_w_in, ssd_conv_w, ssd_conv_b, ssd_dt_bias, ssd_a_log, ssd_d, ssd_norm, ssd_w_out,
           gdn_w_in, gdn_conv_w, gdn_dt_bias, gdn_a_log, gdn_norm, gdn_w_out,
           mlp_w_up, mlp_w_down):
    nc, pack, cs = _get_program()
    f = lambda a: np.ascontiguousarray(np.asarray(a, dtype=np.float32))
    norms = f(np.concatenate([np.asarray(norm_mix), np.asarray(norm_mlp), np.asarray(norm_final)[None, :]], axis=0))
    shared = {
        "norms": norms,
        "ret_w_in": f(ret_w_in), "ret_w_out": f(ret_w_out),
        "ssd_w_in": f(ssd_w_in[0]), "ssd_conv_w": f(ssd_conv_w[0]), "ssd_conv_b": f(ssd_conv_b), "ssd_dt_bias": f(ssd_dt_bias),
        "ssd_a_log": f(ssd_a_log), "ssd_d": f(ssd_d), "ssd_norm": f(ssd_norm), "ssd_w_out": f(ssd_w_out[0]),
        "gdn_w_in": f(gdn_w_in[0]), "gdn_conv_w": f(gdn_conv_w[0]), "gdn_dt_bias": f(gdn_dt_bias), "gdn_a_log": f(gdn_a_log),
        "gdn_norm": f(gdn_norm), "gdn_w_out": f(gdn_w_out[0]),
        "mlp_w_up": f(mlp_w_up), "mlp_w_down": f(mlp_w_down),
        "cpack": pack, "ropecs": cs, "colmask": _CACHE["colmask"],
    }
    xp = np.asarray(x_prompt, dtype=np.float32)
    xs = np.asarray(x_sample, dtype=np.float32)
    in_maps = []
    for c in range(8):
        sl = slice(16 * c, 16 * c + 16)
        m = dict(shared)
        m["x_prompt"] = f(xp[c])
        m["x_sample"] = f(xs[sl].reshape(TS, D))
        m["state_ret"] = f(np.asarray(state_ret)[:, sl])
        m["state_ssd"] = f(np.asarray(state_ssd)[0, sl])
        m["state_ssd_conv"] = f(np.asarray(state_ssd_conv)[0, sl])
        m["state_gdn"] = f(np.asarray(state_gdn)[0, sl])
        m["state_gdn_conv"] = f(np.asarray(state_gdn_conv)[0, sl])
        m = {kk: vv for kk, vv in m.items() if kk in _CACHE["used"]}
        in_maps.append(m)
    ncores = CFG.get("ncores", 8)
    res = run_bass_kernel_spmd(nc, in_maps[:ncores], core_ids=list(range(ncores)))
    R = res.results
    g = lambda nm: [np.asarray(R[c][nm]) if c < ncores else np.zeros_like(np.asarray(R[0][nm])) for c in range(8)]
    y_prompt = np.stack(g("y_prompt"), 0)
    y_sample = np.concatenate(g("y_sample"), 0).reshape(128, 4, D)
    p_ret = np.stack(g("p_ret"), 1)
    p_ssd = np.stack(g("p_ssd"), 0)[None]
    p_ssd_conv = np.stack(g("p_ssd_conv"), 0)[None]
    p_gdn = np.stack(g("p_gdn"), 0)[None]
    p_gdn_conv = np.stack(g("p_gdn_conv"), 0)[None]
    s_ret = np.concatenate(g("s_ret"), 1)
    s_ssd = np.concatenate(g("s_ssd"), 0)[None]
    s_ssd_conv = np.concatenate(g("s_ssd_conv"), 0)[None]
    s_gdn = np.concatenate(g("s_gdn"), 0)[None]
    s_gdn_conv = np.concatenate(g("s_gdn_conv"), 0)[None]
    return (y_prompt, y_sample, p_ret, p_ssd, p_ssd_conv, p_gdn, p_gdn_conv,
            s_ret, s_ssd, s_ssd_conv, s_gdn, s_gdn_conv)
```

```python
import contextlib
import math
import numpy as np
import concourse.bass as bass
import concourse.mybir as mybir
from concourse.bass_utils import run_bass_kernel_spmd

F32 = mybir.dt.float32
BF16 = mybir.dt.bfloat16
ALU = mybir.AluOpType
AF = mybir.ActivationFunctionType

ENGS = ("pe", "act", "dve", "pool", "sp")
NDMA_SEMS = 40

D = 1024
SEQ = 2048
NPT = 16
NT = 17
TS = 64
NTOK = SEQ + TS
DEPTH = 4
PAST_LEN = 16384
RMS_EPS = 1e-6
D_FF = 4096

CFG = {"mixers": (0, 1, 2, 3), "mlp": True, "nlayers": 4}


class Op:
    __slots__ = ("eng", "fn", "deps", "is_dma", "pos", "inc", "semi", "semv", "waits")

    def __init__(self, eng, fn, deps, is_dma):
        self.eng = eng
        self.fn = fn
        self.deps = deps
        self.is_dma = is_dma
        self.inc = False
        self.semi = -1
        self.semv = 0
        self.waits = None


import types as _types


def _freeze(fn):
    if fn.__closure__ is None:
        return fn
    cells = []
    for c in fn.__closure__:
        try:
            cells.append(_types.CellType(c.cell_contents))
        except ValueError:
            cells.append(c)
    g = _types.FunctionType(fn.__code__, fn.__globals__, fn.__name__, fn.__defaults__, tuple(cells))
    g.__kwdefaults__ = fn.__kwdefaults__
    return g


import threading as _threading


class Coop:
    def __init__(self):
        self.yield_fn = None

    def run(self, fns):
        n = len(fns)
        sems = [_threading.Semaphore(0) for _ in range(n)]
        main = _threading.Semaphore(0)
        done = [False] * n
        exc = []
        state = {"cur": 0}

        def nxt(me):
            for d in range(1, n + 1):
                j = (me + d) % n
                if not done[j]:
                    return j
            return None

        def switch():
            me = state["cur"]
            j = nxt(me)
            if j is None or j == me:
                return
            state["cur"] = j
            sems[j].release()
            sems[me].acquire()

        def worker(i):
            sems[i].acquire()
            try:
                fns[i]()
            except BaseException as e:
                exc.append(e)
            finally:
                done[i] = True
                j = nxt(i)
                if j is None:
                    main.release()
                else:
                    state["cur"] = j
                    sems[j].release()

        ths = [_threading.Thread(target=worker, args=(i,)) for i in range(n)]
        for t in ths:
            t.start()
        self.yield_fn = switch
        sems[0].release()
        main.acquire()
        self.yield_fn = None
        for t in ths:
            t.join()
        if exc:
            raise exc[0]


COOP = Coop()


class Prog:
    def __init__(self, nc):
        self.nc = nc
        self.ops = []
        self.lastw = {}
        self.readers = {}

    def op(self, eng, fn, r=(), w=(), dma=False):
        deps = set()
        lastw = self.lastw
        readers = self.readers
        for k in r:
            lw = lastw.get(k)
            if lw is not None:
                deps.add(lw)
            if type(k) is tuple and k[0] == "ps":
                rd = readers.get(k)
                if rd:
                    for j_ in rd:
                        if self.ops[j_].eng != eng:
                            deps.add(j_)
        for k in w:
            lw = lastw.get(k)
            if lw is not None:
                deps.add(lw)
            rd = readers.get(k)
            if rd:
                deps.update(rd)
        idx = len(self.ops)
        self.ops.append(Op(eng, _freeze(fn), deps, dma))
        for k in w:
            lastw[k] = idx
            readers[k] = []
        for k in r:
            if k in w:
                continue
            readers.setdefault(k, []).append(idx)
        if COOP.yield_fn is not None:
            COOP.yield_fn()
        return idx

    def plan(self):
        ops = self.ops
        streams = {e: [] for e in ENGS}
        for i, o in enumerate(ops):
            o.pos = len(streams[o.eng])
            streams[o.eng].append(i)
        clock = {e: {f: -1 for f in ENGS} for e in ENGS}
        dma_seen = {e: set() for e in ENGS}
        vcs = [None] * len(ops)
        dma_count = {"sp": 0, "pool": 0, "act": 0}
        dma_base = {"sp": (0, 28), "pool": (28, 12), "act": (0, 28)}
        sem_last = [None] * NDMA_SEMS
        sem_val = [0] * NDMA_SEMS
        for i, o in enumerate(ops):
            E = o.eng
            ck = clock[E]
            waits = []
            deps = o.deps
            if o.is_dma:
                base_, n_ = dma_base[E]
                s = base_ + dma_count[E] % n_
                dma_count[E] += 1
                if sem_last[s] is not None:
                    deps = set(deps)
                    deps.add(sem_last[s])
                sem_last[s] = i
                sem_val[s] += 16
                o.semi = s
                o.semv = sem_val[s]
            need = {}
            for j in deps:
                oj = ops[j]
                if oj.is_dma:
                    if j not in dma_seen[E]:
                        waits.append(("dma", j))
                        dma_seen[E].add(j)
                        vj = vcs[j]
                        for f in ENGS:
                            if vj[f] > ck[f]:
                                ck[f] = vj[f]
                else:
                    F = oj.eng
                    if F == E and E in ("pe", "sp"):
                        continue
                    if oj.pos > ck[F] and oj.pos > need.get(F, -1):
                        need[F] = oj.pos
            for F, p in need.items():
                if p > ck[F]:
                    j = streams[F][p]
                    ops[j].inc = True
                    waits.append(("eng", F, j))
                    vj = vcs[j]
                    for f in ENGS:
                        if vj[f] > ck[f]:
                            ck[f] = vj[f]
                    if p > ck[F]:
                        ck[F] = p
            o.waits = waits
            o.deps = None
            if o.is_dma:
                vcs[i] = dict(ck)
            else:
                v = dict(ck)
                v[E] = o.pos
                vcs[i] = v
                if E == "pe":
                    ck[E] = o.pos
        cnt = {e: 0 for e in ENGS}
        for e in ENGS:
            for i in streams[e]:
                o = ops[i]
                if (not o.is_dma) and o.inc:
                    cnt[e] += 1
                    o.semv = cnt[e]
        self.streams = streams
        self.counts = cnt
        return streams

    def emit(self):
        nc = self.nc
        ops = self.ops
        streams = self.plan()
        with contextlib.ExitStack() as es:
            esem = {e: es.enter_context(nc.semaphore("s_" + e)) for e in ENGS}
            dsem = [es.enter_context(nc.semaphore("d%d" % k)) for k in range(NDMA_SEMS)]
            block = es.enter_context(nc.Block())

            def run(e, eng):
                for i in streams[e]:
                    o = ops[i]
                    for wt in o.waits:
                        if wt[0] == "dma":
                            oj = ops[wt[1]]
                            eng.wait_ge(dsem[oj.semi], oj.semv)
                        else:
                            oj = ops[wt[2]]
                            eng.wait_ge(esem[oj.eng], oj.semv)
                    ins = o.fn(eng)
                    if o.is_dma:
                        ins.then_inc(dsem[o.semi], 16)
                    elif o.inc:
                        ins.then_inc(esem[e], 1)
                    o.fn = None

            @block.tensor
            def _(eng):
                run("pe", eng)

            @block.scalar
            def _(eng):
                run("act", eng)

            @block.vector
            def _(eng):
                run("dve", eng)

            @block.gpsimd
            def _(eng):
                run("pool", eng)

            @block.sync
            def _(eng):
                run("sp", eng)
                vals = {}
                for o in ops:
                    if o.is_dma:
                        vals[o.semi] = max(vals.get(o.semi, 0), o.semv)
                for s, v in sorted(vals.items()):
                    eng.wait_ge(dsem[s], v)
                for e in ("pe", "act", "dve", "pool"):
                    if self.counts[e] > 0:
                        eng.wait_ge(esem[e], self.counts[e])


def _mask_consts(T, L):
    idx = np.arange(T)
    same = (idx[:, None] // L) == (idx[None, :] // L)
    tri = (same & (idx[:, None] <= idx[None, :])).astype(np.float32)
    seq = same.astype(np.float32)
    neg = np.where(same & (idx[None, :] >= idx[:, None]), 0.0, -30000.0).astype(np.float32)
    negt = np.where(same & (idx[None, :] <= idx[:, None]), 0.0, -30000.0).astype(np.float32)
    strict = (same & (idx[None, :] < idx[:, None])).astype(np.float32)
    return tri, seq, neg, negt, strict


def build_consts():
    items = []
    items.append(("ident", np.eye(128, dtype=np.float32)))
    items.append(("ones", np.ones((128, 128), np.float32)))
    for nm, a in zip(("tri_p", "seq_p", "neg_p", "negt_p", "strict_p"), _mask_consts(128, 128)):
        if nm != "seq_p":
            items.append((nm, a))
    for nm, a in zip(("tri_s", "seq_s", "neg_s", "negt_s", "strict_s"), _mask_consts(TS, 4)):
        items.append((nm, a))
    rowmask = (np.arange(TS)[:, None] // 4 == np.arange(16)[None, :]).astype(np.float32)
    items.append(("rowmask_s", rowmask))
    colmask = np.ascontiguousarray(np.broadcast_to(rowmask.T[None, :, :], (128, 16, TS)).reshape(128, 16 * TS))
    ii = np.arange(128)
    for lv in range(7):
        bsz = 1 << lv
        same = (ii[:, None] // (2 * bsz)) == (ii[None, :] // (2 * bsz))
        mT = same & ((ii[None, :] % (2 * bsz)) >= bsz) & ((ii[:, None] % (2 * bsz)) < bsz)
        items.append(("bmT%d" % lv, mT.astype(np.float32)))
    lg = np.log1p(-np.exp2(-5.0 - np.arange(4, dtype=np.float32))).astype(np.float32)
    items.append(("laret", np.broadcast_to(lg[None, :], (128, 4)).copy()))
    offs = {}
    tot = 0
    for nm, a in items:
        offs[nm] = (tot, a.shape[0], a.shape[1])
        tot += a.shape[1]
    pack = np.zeros((128, tot), np.float32)
    for nm, a in items:
        o, r, c = offs[nm]
        pack[:r, o:o + c] = a
    half = 128
    inv = (np.float32(10000.0) ** (-np.arange(half, dtype=np.float32) / np.float32(half))).astype(np.float32)
    pos = np.zeros((NT, 128), np.float32)
    for i in range(NPT):
        pos[i] = np.arange(128, dtype=np.float32) + 128 * i
    pos[NPT, :TS] = (np.arange(TS) % 4).astype(np.float32) + np.float32(PAST_LEN)
    ang = (pos[:, :, None] * inv[None, None, :]).astype(np.float32)
    cs = np.stack([np.cos(ang.astype(np.float64)), np.sin(ang.astype(np.float64))], axis=2).astype(np.float32)
    return pack, offs, np.ascontiguousarray(cs), colmask


class TileInfo:
    def __init__(self, i):
        self.i = i
        self.sample = i == NPT
        self.T = TS if self.sample else 128
        self.c0 = i * 128
        self.kind = "s" if self.sample else "p"
        self.nseq = 16 if self.sample else 1


TILES = [TileInfo(i) for i in range(NT)]


NFA = 12
NBA = 16
NHA = 10


class Buf:
    __slots__ = ("pool", "i", "t", "key")

    def __init__(self, pool, i):
        self.pool = pool
        self.i = i
        self.t = pool.ts[i]
        self.key = (pool.name, i)

    def free(self):
        self.pool.free.append(self.i)


class BufPool:
    def __init__(self, name, ts):
        self.name = name
        self.ts = ts
        self.free = list(range(len(ts)))

    def get(self):
        assert self.free, "pool %s exhausted" % self.name
        return Buf(self, self.free.pop(0))


class K:
    def __init__(self, nc, offs):
        self.nc = nc
        self.P = Prog(nc)
        self.offs = offs
        self.uid = 0
        nc_ = nc
        dt = nc_.dram_tensor
        class _Lazy(dict):
            def __missing__(d, nm):
                v = dt(nm, list(IN_SHAPES[nm]), F32, kind="ExternalInput").ap()
                d[nm] = v
                return v
        self.din = _Lazy()
        self.dout = {}
        for nm, shp in OUT_SHAPES.items():
            self.dout[nm] = dt(nm, list(shp), F32, kind="ExternalOutput").ap()
        a = nc_.alloc_sbuf_tensor
        self.X = a("X", [128, NT, D], F32)
        self.XNT = a("XNT", [128, 8, NTOK], BF16)
        self.NRING = 4
        self.ring = [a("wr%d" % k, [128, 4096], BF16) for k in range(self.NRING)]
        self.ring_i = 0
        self.CONST = a("CONST", [128, offs["_tot"]], F32)
        self.identb = a("identb", [128, 128], BF16)
        self.colmaskb = a("colmaskb", [128, 16, TS], BF16)
        self.NW = a("NW", [128, 9, 8], F32)
        self.st = [a("st%d" % k, [128, 8], F32) for k in range(4)]
        self.PAR = a("PAR", [128, D], F32)
        self.SF = a("SF", [128, 2, 512], F32)
        self.SB = a("SB", [128, 2, 512], BF16)
        self.SM = [a("SM%d" % k, [128, 160], F32) for k in range(4)]
        self.PRM = a("PRM", [128, 128], F32)
        self.CW = a("CW", [128, 32, 4], F32)
        self.CB = a("CB", [128, 32], F32)
        self.UB = [a("UB%d" % k, [128, 528], F32) for k in range(2)]
        self.LAB = [a("LAB%d" % k, [128, 40], F32) for k in range(4)]
        self.fpool = BufPool("fa", [a("fa%d" % k, [128, 512], F32) for k in range(NFA)])
        self.bpool = BufPool("ba", [a("ba%d" % k, [128, 512], BF16) for k in range(NBA)])
        self.hpool = BufPool("ha", [a("ha%d" % k, [128, 256], BF16) for k in range(NHA)])
        self.ps = [nc_.alloc_psum_tensor("ps%d" % k, [128, 512], F32) for k in range(8)]
        self.ps_free = list(range(8))
        self.cnt = {}
        print("sbuf bytes remaining", nc_.sbuf_bytes_remaining, flush=True)

    def c(self, name):
        o, r, cc = self.offs[name]
        return self.CONST[0:r, o:o + cc]

    def nid(self, pfx):
        self.uid += 1
        return "%s#%d" % (pfx, self.uid)

    def rr(self, key, n=2):
        v = self.cnt.get(key, 0)
        self.cnt[key] = v + 1
        return v % n

    def psum(self):
        b = self.ps_free.pop(0)
        return b

    def pfree(self, b):
        self.ps_free.append(b)

    def ringslot(self):
        s = self.ring_i % self.NRING
        self.ring_i += 1
        return s

    def op(self, *a, **k):
        return self.P.op(*a, **k)

    def dma(self, out, in_, r=(), w=(), eng="sp", slow=False):
        if slow:
            fn = lambda e: e.dma_start(out=out, in_=in_, allow_slow_non_contiguous=True)
        else:
            fn = lambda e: e.dma_start(out=out, in_=in_)
        return self.P.op(eng, fn, r=r, w=w, dma=True)

    def load_w(self, src_pieces, r_extra=()):
        s = self.ringslot()
        t = self.ring[s]
        for dstf, src in src_pieces:
            self.dma(dstf(t), src, w=[("ring", s)], eng="pool")
        return s

    def mm(self, out, lhsT, rhs, start, stop, r, w):
        self.P.op("pe", lambda e: e.matmul(out, lhsT=lhsT, rhs=rhs, start=start, stop=stop), r=r, w=w)

    def tr(self, out, in_, ident, r, w):
        self.P.op("pe", lambda e: e.transpose(out=out, in_=in_, identity=ident), r=r, w=w)


IN_SHAPES = {}
OUT_SHAPES = {}


def _set_shapes(ncst):
    IN_SHAPES.clear()
    IN_SHAPES.update({
        "x_prompt": (SEQ, D), "x_sample": (TS, D),
        "state_ret": (2, 16, 4, 256, 512), "state_ssd": (16, 32, 64, 128), "state_ssd_conv": (16, 3, 4096),
        "state_gdn": (16, 16, 128, 128), "state_gdn_conv": (16, 3, 4096),
        "norms": (9, D),
        "ret_w_in": (2, D, 6144), "ret_w_out": (2, 2048, D),
        "ssd_w_in": (D, 6176), "ssd_conv_w": (4, 4096), "ssd_conv_b": (1, 4096), "ssd_dt_bias": (1, 32),
        "ssd_a_log": (1, 32), "ssd_d": (1, 32), "ssd_norm": (1, 2048), "ssd_w_out": (2048, D),
        "gdn_w_in": (D, 6176), "gdn_conv_w": (4, 4096), "gdn_dt_bias": (1, 16), "gdn_a_log": (1, 16),
        "gdn_norm": (1, 128), "gdn_w_out": (2048, D),
        "mlp_w_up": (4, D, D_FF), "mlp_w_down": (4, D_FF, D),
        "cpack": (128, ncst), "ropecs": (NT, 128, 2, 128), "colmask": (128, 16 * TS),
    })
    OUT_SHAPES.clear()
    OUT_SHAPES.update({
        "y_prompt": (SEQ, D), "y_sample": (TS, D),
        "p_ret": (2, 4, 256, 512), "p_ssd": (32, 64, 128), "p_ssd_conv": (3, 4096),
        "p_gdn": (16, 128, 128), "p_gdn_conv": (3, 4096),
        "s_ret": (2, 16, 4, 256, 512), "s_ssd": (16, 32, 64, 128), "s_ssd_conv": (16, 3, 4096),
        "s_gdn": (16, 16, 128, 128), "s_gdn_conv": (16, 3, 4096),
    })


def emit_setup(k):
    k.dma(k.CONST[:, :], k.din["cpack"][:, :], w=["CONST"])
    k.op("act", lambda e: e.activation(out=k.identb[:, :], in_=k.c("ident"), func=AF.Copy), r=["CONST"], w=["identb"])
    for q in range(2):
        cb = k.fpool.get()
        k.dma(cb.t[:, :], k.din["colmask"][:, q * 512:(q + 1) * 512], w=[cb.key])
        k.op("dve", lambda e, cb=cb, q=q: e.tensor_copy(out=k.colmaskb[0:128, :, :].rearrange("p a b -> p (a b)")[:, q * 512:(q + 1) * 512], in_=cb.t[:, :]),
             r=[cb.key], w=["colmaskb"])
        cb.free()
    k.dma(k.PAR[0:9, :], k.din["norms"][:, :], w=["PAR"])
    b = k.psum()
    pv = k.ps[b][:, 0:72].rearrange("p (c l) -> p c l", c=8)
    for ch in range(8):
        k.tr(pv[:, ch, :], k.PAR[0:9, ch * 128:(ch + 1) * 128], k.c("ident")[0:9, 0:9], r=["PAR", "CONST"], w=[("ps", b)])
    k.op("dve", lambda e: e.tensor_copy(out=k.NW[:, :, :].rearrange("p l c -> p c l"), in_=pv), r=[("ps", b)], w=["NW"])
    k.pfree(b)
    for t in TILES:
        src = k.din["x_sample"][:, :] if t.sample else k.din["x_prompt"][t.c0:t.c0 + 128, :]
        k.dma(k.X[0:t.T, t.i, :], src, w=[("X", t.i)])


def emit_rstd(k, t, stt, src_ap, src_res, n, col):
    T = t.T
    jb = [k.bpool.get() for _ in range((n + 511) // 512)]
    for q, jbq in enumerate(jb):
        n0, n1 = q * 512, min(n, q * 512 + 512)
        k.op("act", lambda e, jbq=jbq, n0=n0, n1=n1, q=q: e.activation(out=jbq.t[0:T, 0:n1 - n0], in_=src_ap[:, n0:n1], func=AF.Square,
                                                                      accum_out=stt[0:T, col + 2 + q:col + 3 + q]),
             r=list(src_res), w=[jbq.key, ("st", id(stt))])
        jbq.free()
    if len(jb) == 2:
        k.op("dve", lambda e: e.tensor_tensor(out=stt[0:T, col:col + 1], in0=stt[0:T, col + 2:col + 3], in1=stt[0:T, col + 3:col + 4], op=ALU.add),
             r=[("st", id(stt))], w=[("st", id(stt))])
    else:
        k.op("dve", lambda e: e.tensor_copy(out=stt[0:T, col:col + 1], in_=stt[0:T, col + 2:col + 3]),
             r=[("st", id(stt))], w=[("st", id(stt))])
    k.op("dve", lambda e: e.tensor_scalar(out=stt[0:T, col + 1:col + 2], in0=stt[0:T, col:col + 1], scalar1=1.0 / n, scalar2=RMS_EPS,
                                          op0=ALU.mult, op1=ALU.add), r=[("st", id(stt))], w=[("st", id(stt))])
    k.op("act", lambda e: e.activation(out=stt[0:T, col + 1:col + 2], in_=stt[0:T, col + 1:col + 2], func=AF.Ln),
         r=[("st", id(stt))], w=[("st", id(stt))])
    k.op("act", lambda e: e.activation(out=stt[0:T, col + 1:col + 2], in_=stt[0:T, col + 1:col + 2], func=AF.Exp, scale=-0.5),
         r=[("st", id(stt))], w=[("st", id(stt))])


def emit_norm_xnt(k, widx):
    for t in TILES[:CFG.get("norm_tiles", NT)]:
        T = t.T
        stt = k.st[k.rr("st", 4)]
        xbs = [k.bpool.get() for _ in range(2)]
        emit_rstd(k, t, stt, k.X[0:T, t.i, :], [("X", t.i)], D, 0)
        for q in range(2):
            k.op("dve", lambda e, T=T, q=q, stt=stt, t=t: e.tensor_scalar_mul(out=xbs[q].t[0:T, :], in0=k.X[0:T, t.i, q * 512:(q + 1) * 512], scalar1=stt[0:T, 1:2]),
                 r=[("X", t.i), ("st", id(stt))], w=[xbs[q].key])
        b = k.psum()
        pv = k.ps[b][:, :].bitcast(BF16).rearrange("p (c m) -> p c m", c=8)
        for ch in range(8):
            k.tr(pv[:, ch, 0:T], xbs[ch // 4].t[0:T, (ch % 4) * 128:(ch % 4 + 1) * 128], k.identb[0:T, 0:T], r=[xbs[ch // 4].key, "identb"], w=[("ps", b)])
        for xb_ in xbs:
            xb_.free()
        nwb = k.NW[:, widx, :].unsqueeze(2).to_broadcast([128, 8, T])
        k.op("dve", lambda e, T=T, t=t, pv=pv, nwb=nwb: e.tensor_tensor(out=k.XNT[:, :, t.c0:t.c0 + T], in0=pv[:, :, 0:T], in1=nwb, op=ALU.mult),
             r=[("ps", b), "NW"], w=[("XNT", t.i)])
        k.pfree(b)


def emit_mlp(k, layer):
    wu = k.din["mlp_w_up"]
    wd = k.din["mlp_w_down"]
    blocks = [(0, 512, [0, 1, 2, 3]), (512, 512, [4, 5, 6, 7]), (1024, 512, [8, 9, 10, 11]), (1536, 512, [12, 13, 14, 15]),
              (2048, TS, [16])]
    for g in range(8):
        su = k.load_w([(lambda t: t[:, :].rearrange("p (k n) -> p k n", k=8),
                        wu[layer, :, g * 512:(g + 1) * 512].rearrange("(k p) n -> p k n", p=128))])
        sd = k.load_w([(lambda t: t[:, :].rearrange("p (c n) -> p c n", c=4),
                        wd[layer, g * 512:(g + 1) * 512, :].rearrange("(c p) n -> p c n", p=128))])
        Wu = k.ring[su][:, :].rearrange("p (k n) -> p k n", k=8)
        Wd = k.ring[sd][:, :].rearrange("p (c n) -> p c n", c=4)
        for (c0, ncol, tl) in blocks:
            hTb = [k.bpool.get() for _ in range(4)]
            for c in range(4):
                b = k.psum()
                for kk in range(8):
                    k.mm(k.ps[b][:, 0:ncol], Wu[:, kk, c * 128:(c + 1) * 128], k.XNT[:, kk, c0:c0 + ncol], kk == 0, kk == 7,
                         r=[("ring", su)] + [("XNT", ti) for ti in tl], w=[("ps", b)])
                hr = k.bpool.get()
                k.op("act", lambda e, b=b, hr=hr, ncol=ncol: e.activation(out=hr.t[:, 0:ncol], in_=k.ps[b][:, 0:ncol], func=AF.Relu),
                     r=[("ps", b)], w=[hr.key])
                k.op("pool", lambda e, hr=hr, hTc=hTb[c], ncol=ncol: e.tensor_tensor(out=hTc.t[:, 0:ncol], in0=hr.t[:, 0:ncol], in1=hr.t[:, 0:ncol], op=ALU.mult),
                     r=[hr.key], w=[hTb[c].key])
                hr.free()
                k.pfree(b)
            for j, ti in enumerate(tl):
                t = TILES[ti]
                T = t.T
                for half in range(2):
                    b = k.psum()
                    for c in range(4):
                        k.mm(k.ps[b][0:T, :], hTb[c].t[:, j * 128:j * 128 + T], Wd[:, c, half * 512:(half + 1) * 512], c == 0, c == 3,
                             r=[hTb[c].key, ("ring", sd)], w=[("ps", b)])
                    k.op("dve", lambda e, b=b, T=T, ti=ti, half=half: e.tensor_tensor(
                        out=k.X[0:T, ti, half * 512:(half + 1) * 512], in0=k.X[0:T, ti, half * 512:(half + 1) * 512],
                        in1=k.ps[b][0:T, :], op=ALU.add), r=[("ps", b), ("X", ti)], w=[("X", ti)])
                    k.pfree(b)
            for hb in hTb:
                hb.free()


def emit_final(k):
    k.dma(k.PAR[:, :], k.din["norms"][8, :].partition_broadcast(128), w=["PAR"])
    for t in TILES:
        T = t.T
        stt = k.st[k.rr("st", 4)]
        emit_rstd(k, t, stt, k.X[0:T, t.i, :], [("X", t.i)], D, 0)
        k.op("dve", lambda e, T=T, t=t, stt=stt: e.scalar_tensor_tensor(out=k.X[0:T, t.i, :], in0=k.X[0:T, t.i, :], scalar=stt[0:T, 1:2],
                                                                         in1=k.PAR[0:T, :], op0=ALU.mult, op1=ALU.mult),
             r=[("X", t.i), ("st", id(stt)), "PAR"], w=[("X", t.i)])
        dst = k.dout["y_sample"][:, :] if t.sample else k.dout["y_prompt"][t.c0:t.c0 + 128, :]
        k.dma(dst, k.X[0:T, t.i, :], r=[("X", t.i)])


def build_program(nc, offs):
    k = K(nc, offs)
    emit_setup(k)
    for layer in range(CFG["nlayers"]):
        emit_norm_xnt(k, layer)
        if layer in CFG["mixers"]:
            kind = layer % 3
            if kind == 0:
                from_ret(k, layer)
            elif kind == 1:
                from_ssd(k, layer)
            else:
                from_gdn(k, layer)
        if CFG["mlp"]:
            emit_norm_xnt(k, 4 + layer)
            emit_mlp(k, layer)
    emit_final(k)
    zero_unwritten_outputs(k)
    k.P.emit()
    return k


def kconst(k, kind, nm):
    return k.c("%s_%s" % (nm, kind)) if not (nm == "seq" and kind == "p") else k.c("ones")


def decay_prep(k, kind, la, la_res, R):
    T = 128 if kind == "p" else TS
    nseq = 1 if kind == "p" else 16
    smi = k.rr("SM", 4)
    sm = k.SM[smi]
    smk = ("SM", smi)
    tri = kconst(k, kind, "tri")
    seq = kconst(k, kind, "seq")
    neg = kconst(k, kind, "neg")
    ones = k.c("ones")
    b = k.psum()
    k.mm(k.ps[b][0:T, 0:R], tri, la, True, True, r=["CONST"] + la_res, w=[("ps", b)])
    k.mm(k.ps[b][0:T, R:2 * R], seq, la, True, True, r=["CONST"] + la_res, w=[("ps", b)])
    k.op("act", lambda e: e.activation(out=sm[0:T, 0:2 * R], in_=k.ps[b][0:T, 0:2 * R], func=AF.Copy), r=[("ps", b)], w=[smk])
    k.pfree(b)
    bm = k.fpool.get()
    bmv = bm.t[0:T, 0:R * T].rearrange("p (r t) -> p r t", r=R)
    k.op("pool", lambda e: e.tensor_tensor(out=bmv, in0=tri.unsqueeze(1).to_broadcast([T, R, T]), in1=la.unsqueeze(2).to_broadcast([T, R, T]),
                                           op=ALU.mult), r=["CONST"] + la_res, w=[bm.key])
    b2 = k.psum()
    k.mm(k.ps[b2][0:T, 0:R * T], ones[0:T, 0:T], bm.t[0:T, 0:R * T], True, True, r=["CONST", bm.key], w=[("ps", b2)])
    bm.free()
    dec = k.fpool.get()
    decv = dec.t[0:T, 0:R * T].rearrange("p (r t) -> p r t", r=R)
    crv = k.ps[b2][0:T, 0:R * T].rearrange("p (r t) -> p r t", r=R)
    for r_ in range(R):
        k.op("dve", lambda e, r_=r_: e.scalar_tensor_tensor(out=decv[:, r_, :], in0=crv[:, r_, :], scalar=sm[0:T, r_:r_ + 1], in1=neg,
                                                              op0=ALU.subtract, op1=ALU.add), r=[("ps", b2), smk, "CONST"], w=[dec.key])
    k.pfree(b2)
    k.op("act", lambda e: e.activation(out=dec.t[0:T, 0:R * T], in_=dec.t[0:T, 0:R * T], func=AF.Exp), r=[dec.key], w=[dec.key])
    k.op("act", lambda e: e.activation(out=sm[0:T, 2 * R:3 * R], in_=sm[0:T, 0:R], func=AF.Exp), r=[smk], w=[smk])
    k.op("dve", lambda e: e.tensor_tensor(out=sm[0:T, 3 * R:4 * R], in0=sm[0:T, R:2 * R], in1=sm[0:T, 0:R], op=ALU.subtract), r=[smk], w=[smk])
    k.op("act", lambda e: e.activation(out=sm[0:T, 3 * R:4 * R], in_=sm[0:T, 3 * R:4 * R], func=AF.Exp), r=[smk], w=[smk])
    if nseq == 1:
        k.op("act", lambda e: e.activation(out=sm[0:T, 4 * R:5 * R], in_=sm[0:T, R:2 * R], func=AF.Exp), r=[smk], w=[smk])
    else:
        lam = sm[0:T, 96:96 + 16 * R].rearrange("p (b r) -> p b r", b=16)
        k.op("pool", lambda e: e.tensor_tensor(out=lam, in0=la.unsqueeze(1).to_broadcast([T, 16, R]),
                                               in1=k.c("rowmask_s").unsqueeze(2).to_broadcast([T, 16, R]), op=ALU.mult),
             r=["CONST", smk] + la_res, w=[smk])
        b3 = k.psum()
        k.mm(k.ps[b3][0:128, 0:16 * R], ones[0:T, 0:128], sm[0:T, 96:96 + 16 * R], True, True, r=["CONST", smk], w=[("ps", b3)])
        k.op("act", lambda e: e.activation(out=sm[0:128, 16:16 + 16 * R], in_=k.ps[b3][0:128, 0:16 * R], func=AF.Exp), r=[("ps", b3)], w=[smk])
        k.pfree(b3)
    return {"sm": sm, "smk": smk, "dec": dec, "T": T, "R": R, "kind": kind}


def scan_tile(k, t, prep, qT, qT_key, kT, kT_key, ktok, ktok_key, v, v_key, R, Pd, NC, first, sload, sstore, pstore):
    T = t.T
    W = R * Pd
    sm, smk, dec = prep["sm"], prep["smk"], prep["dec"]
    nseq = t.nseq
    bs = k.psum()
    for nc_ in range(NC):
        k.mm(k.ps[bs][0:T, 0:T], kT[:, nc_, 0:T], qT[:, nc_, 0:T], nc_ == 0, nc_ == NC - 1, r=[kT_key, qT_key], w=[("ps", bs)])
    at = k.bpool.get()
    k.op("dve", lambda e: e.tensor_tensor(out=at.t[0:T, 0:R * T].rearrange("p (r t) -> p r t", r=R),
                                          in0=dec.t[0:T, 0:R * T].rearrange("p (r t) -> p r t", r=R),
                                          in1=k.ps[bs][0:T, 0:T].unsqueeze(1).to_broadcast([T, R, T]), op=ALU.mult),
         r=[dec.key, ("ps", bs)], w=[at.key])
    k.pfree(bs)
    by1 = k.psum()
    for r_ in range(R):
        k.mm(k.ps[by1][0:T, r_ * Pd:(r_ + 1) * Pd], at.t[0:T, r_ * T:(r_ + 1) * T], v[0:T, r_ * Pd:(r_ + 1) * Pd], True, True,
             r=[at.key, v_key], w=[("ps", by1)])
    at.free()
    vw = k.bpool.get()
    k.op("pool", lambda e: e.tensor_tensor(out=vw.t[0:T, 0:W].rearrange("p (r d) -> p r d", r=R), in0=v[0:T, 0:W].rearrange("p (r d) -> p r d", r=R),
                                           in1=sm[0:T, 3 * R:4 * R].unsqueeze(2).to_broadcast([T, R, Pd]), op=ALU.mult),
         r=[v_key, smk], w=[vw.key])
    y = k.fpool.get()
    if nseq == 1:
        if first:
            k.op("act", lambda e: e.activation(out=y.t[0:T, 0:W], in_=k.ps[by1][0:T, 0:W], func=AF.Copy), r=[("ps", by1)], w=[y.key])
            k.pfree(by1)
        else:
            by2 = k.psum()
            for nc_ in range(NC):
                k.mm(k.ps[by2][0:T, 0:W], qT[:, nc_, 0:T], k.SB[:, nc_, 0:W], nc_ == 0, nc_ == NC - 1, r=[qT_key, ("SB", nc_)], w=[("ps", by2)])
            tmp = k.fpool.get()
            k.op("dve", lambda e: e.tensor_tensor(out=tmp.t[0:T, 0:W].rearrange("p (r d) -> p r d", r=R),
                                                  in0=k.ps[by2][0:T, 0:W].rearrange("p (r d) -> p r d", r=R),
                                                  in1=sm[0:T, 2 * R:3 * R].unsqueeze(2).to_broadcast([T, R, Pd]), op=ALU.mult),
                 r=[("ps", by2), smk], w=[tmp.key])
            k.pfree(by2)
            k.op("dve", lambda e: e.tensor_tensor(out=y.t[0:T, 0:W], in0=tmp.t[0:T, 0:W], in1=k.ps[by1][0:T, 0:W], op=ALU.add),
                 r=[tmp.key, ("ps", by1)], w=[y.key])
            tmp.free()
            k.pfree(by1)
        for nc_ in range(NC):
            bd = k.psum()
            k.mm(k.ps[bd][0:128, 0:W], ktok[0:T, nc_ * 128:(nc_ + 1) * 128], vw.t[0:T, 0:W], True, True, r=[ktok_key, vw.key], w=[("ps", bd)])
            if first:
                k.op("act", lambda e, bd=bd, nc_=nc_: e.activation(out=k.SF[:, nc_, 0:W], in_=k.ps[bd][0:128, 0:W], func=AF.Copy),
                     r=[("ps", bd)], w=[("SF", nc_)])
            elif R == 1:
                k.op("dve", lambda e, bd=bd, nc_=nc_: e.scalar_tensor_tensor(out=k.SF[:, nc_, 0:W], in0=k.SF[:, nc_, 0:W], scalar=sm[0:128, 4:5],
                                                                              in1=k.ps[bd][0:128, 0:W], op0=ALU.mult, op1=ALU.add),
                     r=[("SF", nc_), smk, ("ps", bd)], w=[("SF", nc_)])
            else:
                k.op("pool", lambda e, nc_=nc_: e.tensor_tensor(out=k.SF[:, nc_, 0:W].rearrange("p (r d) -> p r d", r=R),
                                                                 in0=k.SF[:, nc_, 0:W].rearrange("p (r d) -> p r d", r=R),
                                                                 in1=sm[0:128, 4 * R:5 * R].unsqueeze(2).to_broadcast([128, R, Pd]), op=ALU.mult),
                     r=[("SF", nc_), smk], w=[("SF", nc_)])
            if (not first) and R != 1:
                k.op("dve", lambda e, bd=bd, nc_=nc_: e.tensor_tensor(out=k.SF[:, nc_, 0:W], in0=k.SF[:, nc_, 0:W], in1=k.ps[bd][0:128, 0:W], op=ALU.add),
                     r=[("SF", nc_), ("ps", bd)], w=[("SF", nc_)])
            k.pfree(bd)
            k.op("act", lambda e, nc_=nc_: e.activation(out=k.SB[:, nc_, 0:W], in_=k.SF[:, nc_, 0:W], func=AF.Copy), r=[("SF", nc_)], w=[("SB", nc_)])
            if pstore is not None:
                pstore(nc_)
    else:
        by2 = k.psum()
        for b_ in range(nseq):
            sf = [k.fpool.get() for _ in range(NC)]
            sb = [k.bpool.get() for _ in range(NC)]
            for nc_ in range(NC):
                sload(b_, nc_, sf[nc_])
                k.op("act", lambda e, nc_=nc_, sf=sf, sb=sb: e.activation(out=sb[nc_].t[:, 0:W], in_=sf[nc_].t[:, 0:W], func=AF.Copy),
                     r=[sf[nc_].key], w=[sb[nc_].key])
            qm = k.bpool.get()
            qmv = qm.t[:, 0:NC * T].rearrange("p (c t) -> p c t", c=NC)
            k.op("pool", lambda e, b_=b_, qmv=qmv: e.tensor_tensor(out=qmv, in0=qT[:, :, 0:T], in1=k.colmaskb[:, b_, :].unsqueeze(1).to_broadcast([128, NC, T]),
                                                                   op=ALU.mult), r=[qT_key, "colmaskb"], w=[qm.key])
            for nc_ in range(NC):
                k.mm(k.ps[by2][0:T, 0:W], qmv[:, nc_, :], sb[nc_].t[:, 0:W], (b_ == 0 and nc_ == 0), (b_ == nseq - 1 and nc_ == NC - 1),
                     r=[qm.key, sb[nc_].key], w=[("ps", by2)])
            qm.free()
            km = k.bpool.get()
            k.op("pool", lambda e, b_=b_, km=km: e.tensor_scalar_mul(out=km.t[0:T, 0:NC * 128], in0=ktok[0:T, 0:NC * 128], scalar1=k.c("rowmask_s")[0:T, b_:b_ + 1]),
                 r=[ktok_key, "CONST"], w=[km.key])
            for nc_ in range(NC):
                bd = k.psum()
                k.mm(k.ps[bd][0:128, 0:W], km.t[0:T, nc_ * 128:(nc_ + 1) * 128], vw.t[0:T, 0:W], True, True, r=[km.key, vw.key], w=[("ps", bd)])
                if R == 1:
                    k.op("dve", lambda e, nc_=nc_, sf=sf, bd=bd, b_=b_: e.scalar_tensor_tensor(out=sf[nc_].t[:, 0:W], in0=sf[nc_].t[:, 0:W],
                                                                                               scalar=sm[0:128, 16 + b_:17 + b_], in1=k.ps[bd][0:128, 0:W],
                                                                                               op0=ALU.mult, op1=ALU.add),
                         r=[sf[nc_].key, smk, ("ps", bd)], w=[sf[nc_].key])
                else:
                    k.op("pool", lambda e, nc_=nc_, sf=sf, b_=b_: e.tensor_tensor(out=sf[nc_].t[:, 0:W].rearrange("p (r d) -> p r d", r=R),
                                                                                 in0=sf[nc_].t[:, 0:W].rearrange("p (r d) -> p r d", r=R),
                                                                                 in1=sm[0:128, 16 + b_ * R:16 + (b_ + 1) * R].unsqueeze(2).to_broadcast([128, R, Pd]),
                                                                                 op=ALU.mult), r=[sf[nc_].key, smk], w=[sf[nc_].key])
                    k.op("dve", lambda e, nc_=nc_, sf=sf, bd=bd: e.tensor_tensor(out=sf[nc_].t[:, 0:W], in0=sf[nc_].t[:, 0:W], in1=k.ps[bd][0:128, 0:W], op=ALU.add),
                         r=[sf[nc_].key, ("ps", bd)], w=[sf[nc_].key])
                k.pfree(bd)
                sstore(b_, nc_, sf[nc_])
            km.free()
            for x_ in sf + sb:
                x_.free()
        tmp = k.fpool.get()
        k.op("dve", lambda e: e.tensor_tensor(out=tmp.t[0:T, 0:W].rearrange("p (r d) -> p r d", r=R),
                                              in0=k.ps[by2][0:T, 0:W].rearrange("p (r d) -> p r d", r=R),
                                              in1=sm[0:T, 2 * R:3 * R].unsqueeze(2).to_broadcast([T, R, Pd]), op=ALU.mult),
             r=[("ps", by2), smk], w=[tmp.key])
        k.pfree(by2)
        k.op("dve", lambda e: e.tensor_tensor(out=y.t[0:T, 0:W], in0=tmp.t[0:T, 0:W], in1=k.ps[by1][0:T, 0:W], op=ALU.add),
             r=[tmp.key, ("ps", by1)], w=[y.key])
        tmp.free()
        k.pfree(by1)
    vw.free()
    return y


def proj_tok(k, t, slot, ncols, c_off=0):
    Wv = k.ring[slot][:, :].rearrange("p (k n) -> p k n", k=8)
    b = k.psum()
    for kk in range(8):
        k.mm(k.ps[b][0:t.T, 0:ncols], k.XNT[:, kk, t.c0:t.c0 + t.T], Wv[:, kk, c_off:c_off + ncols], kk == 0, kk == 7,
             r=[("XNT", t.i), ("ring", slot)], w=[("ps", b)])
    return b


def out_proj_add(k, t, ygT, nchunks, slot):
    T = t.T
    Wo = k.ring[slot][:, 0:nchunks * 1024].rearrange("p (c n) -> p c n", c=nchunks)
    yv = ygT.t[:, 0:nchunks * 128].rearrange("p (c t) -> p c t", c=nchunks)
    for half in range(2):
        b = k.psum()
        for c in range(nchunks):
            k.mm(k.ps[b][0:T, :], yv[:, c, 0:T], Wo[:, c, half * 512:(half + 1) * 512], c == 0, c == nchunks - 1,
                 r=[ygT.key, ("ring", slot)], w=[("ps", b)])
        k.op("dve", lambda e, b=b, half=half: e.tensor_tensor(out=k.X[0:T, t.i, half * 512:(half + 1) * 512], in0=k.X[0:T, t.i, half * 512:(half + 1) * 512],
                                                              in1=k.ps[b][0:T, :], op=ALU.add), r=[("ps", b), ("X", t.i)], w=[("X", t.i)])
        k.pfree(b)


def transpose_to(k, t, src, nchunks, eng="act"):
    T = t.T
    b = k.psum()
    pv = k.ps[b][:, :].bitcast(BF16).rearrange("p (c m) -> p c m", c=8)
    for c in range(nchunks):
        k.tr(pv[:, c, 0:T], src.t[0:T, c * 128:(c + 1) * 128], k.identb[0:T, 0:T], r=[src.key, "identb"], w=[("ps", b)])
    dst = k.bpool.get()
    dv = dst.t[:, 0:nchunks * 128].rearrange("p (c t) -> p c t", c=nchunks)
    if eng == "act":
        k.op("act", lambda e: e.activation(out=dv[:, :, 0:T], in_=pv[:, 0:nchunks, 0:T], func=AF.Copy), r=[("ps", b)], w=[dst.key])
    else:
        k.op("dve", lambda e: e.tensor_copy(out=dv[:, :, 0:T], in_=pv[:, 0:nchunks, 0:T]), r=[("ps", b)], w=[dst.key])
    k.pfree(b)
    return dst


def from_ret(k, layer):
    j = layer // 3
    win = k.din["ret_w_in"]
    wout = k.din["ret_w_out"]
    s_ret_in = k.din["state_ret"]
    for h in range(4):
        v8 = lambda tt: tt[:, :].rearrange("p (k n) -> p k n", k=8)
        s_qk = k.load_w([(lambda tt: v8(tt)[:, :, 0:256], win[j, :, h * 256:(h + 1) * 256].rearrange("(k p) n -> p k n", p=128)),
                         (lambda tt: v8(tt)[:, :, 256:512], win[j, :, 1024 + h * 256:1024 + (h + 1) * 256].rearrange("(k p) n -> p k n", p=128))])
        s_v = k.load_w([(v8, win[j, :, 2048 + h * 512:2048 + (h + 1) * 512].rearrange("(k p) n -> p k n", p=128))])
        s_g = k.load_w([(v8, win[j, :, 4096 + h * 512:4096 + (h + 1) * 512].rearrange("(k p) n -> p k n", p=128))])
        s_o = k.load_w([(lambda tt: tt[:, :].rearrange("p (c n) -> p c n", c=4), wout[j, h * 512:(h + 1) * 512, :].rearrange("(c p) n -> p c n", p=128))])
        preps = {}
        for kind in ("p", "s"):
            T_ = 128 if kind == "p" else TS
            preps[kind] = decay_prep(k, kind, k.c("laret")[0:T_, h:h + 1], ["CONST"], 1)

        def stage_a(t):
            T = t.T
            bqk = proj_tok(k, t, s_qk, 512)
            qk = k.fpool.get()
            k.op("act", lambda e: e.activation(out=qk.t[0:T, 0:256], in_=k.ps[bqk][0:T, 0:256], func=AF.Copy), r=[("ps", bqk)], w=[qk.key])
            k.op("act", lambda e: e.activation(out=qk.t[0:T, 256:512], in_=k.ps[bqk][0:T, 256:512], func=AF.Copy, scale=1.0 / 16.0),
                 r=[("ps", bqk)], w=[qk.key])
            k.pfree(bqk)
            cs = k.fpool.get()
            k.dma(cs.t[0:T, 0:256], k.din["ropecs"][t.i, 0:T, :, :].rearrange("p a b -> p (a b)"), w=[cs.key])
            qv = qk.t[0:T, :].rearrange("p (a h m) -> p a h m", a=2, h=2)
            x1, x2 = qv[:, :, 0, :], qv[:, :, 1, :]
            cosb = cs.t[0:T, 0:128].unsqueeze(1).to_broadcast([T, 2, 128])
            sinb = cs.t[0:T, 128:256].unsqueeze(1).to_broadcast([T, 2, 128])
            ta = k.fpool.get()
            tb = k.fpool.get()
            tav = ta.t[0:T, :].rearrange("p (u a m) -> p u a m", u=2, a=2)
            tbv = tb.t[0:T, :].rearrange("p (u a m) -> p u a m", u=2, a=2)
            k.op("pool", lambda e: e.tensor_tensor(out=tav[:, 0], in0=x1, in1=cosb, op=ALU.mult), r=[qk.key, cs.key], w=[ta.key])
            k.op("pool", lambda e: e.tensor_tensor(out=tav[:, 1], in0=x2, in1=sinb, op=ALU.mult), r=[qk.key, cs.key], w=[ta.key])
            k.op("pool", lambda e: e.tensor_tensor(out=tbv[:, 0], in0=x1, in1=sinb, op=ALU.mult), r=[qk.key, cs.key], w=[tb.key])
            k.op("pool", lambda e: e.tensor_tensor(out=tbv[:, 1], in0=x2, in1=cosb, op=ALU.mult), r=[qk.key, cs.key], w=[tb.key])
            rot = k.bpool.get()
            rv = rot.t[0:T, :].rearrange("p (a h m) -> p a h m", a=2, h=2)
            k.op("dve", lambda e: e.tensor_tensor(out=rv[:, :, 0, :], in0=tav[:, 0], in1=tav[:, 1], op=ALU.subtract), r=[ta.key], w=[rot.key])
            k.op("dve", lambda e: e.tensor_tensor(out=rv[:, :, 1, :], in0=tbv[:, 0], in1=tbv[:, 1], op=ALU.add), r=[tb.key], w=[rot.key])
            for x_ in (qk, cs, ta, tb):
                x_.free()
            qkT = transpose_to(k, t, rot, 4)
            bv = proj_tok(k, t, s_v, 512)
            vb = k.bpool.get()
            k.op("act", lambda e: e.activation(out=vb.t[0:T, :], in_=k.ps[bv][0:T, :], func=AF.Copy), r=[("ps", bv)], w=[vb.key])
            k.pfree(bv)
            bg = proj_tok(k, t, s_g, 512)
            sg = k.bpool.get()
            k.op("act", lambda e: e.activation(out=sg.t[0:T, :], in_=k.ps[bg][0:T, :], func=AF.Silu), r=[("ps", bg)], w=[sg.key])
            k.pfree(bg)
            return {"rot": rot, "qkT": qkT, "v": vb, "sg": sg}

        def stage_b(t, A):
            T = t.T
            qkv = A["qkT"].t[:, 0:512].rearrange("p (c t) -> p c t", c=4)
            first = (t.i == 0)

            def sload(b_, nc_, dst):
                k.dma(dst.t[:, :], s_ret_in[j, b_, h, nc_ * 128:(nc_ + 1) * 128, :], w=[dst.key])

            def sstore(b_, nc_, src):
                k.dma(k.dout["s_ret"][j, b_, h, nc_ * 128:(nc_ + 1) * 128, :], src.t[:, :], r=[src.key])

            pstore = None
            if t.i == NPT - 1:
                def pstore(nc_):
                    k.dma(k.dout["p_ret"][j, h, nc_ * 128:(nc_ + 1) * 128, :], k.SF[:, nc_, :], r=[("SF", nc_)])
            y = scan_tile(k, t, preps[t.kind], qkv[:, 0:2, :], A["qkT"].key, qkv[:, 2:4, :], A["qkT"].key,
                          A["rot"].t[:, 256:512], A["rot"].key, A["v"].t, A["v"].key, 1, 512, 2, first, sload, sstore, pstore)
            stt = k.st[k.rr("st", 4)]
            emit_rstd(k, t, stt, y.t[0:T, 0:512], [y.key], 512, 0)
            yg = k.bpool.get()
            k.op("dve", lambda e: e.scalar_tensor_tensor(out=yg.t[0:T, :], in0=y.t[0:T, :], scalar=stt[0:T, 1:2], in1=A["sg"].t[0:T, :],
                                                          op0=ALU.mult, op1=ALU.mult), r=[y.key, ("st", id(stt)), A["sg"].key], w=[yg.key])
            y.free()
            for nm in ("rot", "qkT", "v", "sg"):
                A[nm].free()
            ygT = transpose_to(k, t, yg, 4)
            yg.free()
            out_proj_add(k, t, ygT, 4, s_o)
            ygT.free()

        A = stage_a(TILES[0])
        for ti in range(NT):
            if ti + 1 < NT and CFG.get("coop", True):
                box = {}
                COOP.run([lambda: box.__setitem__("A", stage_a(TILES[ti + 1])), lambda: stage_b(TILES[ti], A)])
                A = box["A"]
            else:
                An = stage_a(TILES[ti + 1]) if ti + 1 < NT else None
                stage_b(TILES[ti], A)
                A = An
        for kind in ("p", "s"):
            preps[kind]["dec"].free()


def silu_to(k, out_ap, in_ap, r, w):
    k.op("act", lambda e: e.activation(out=out_ap, in_=in_ap, func=AF.Silu), r=r, w=w)


def load_conv_params(k, convw, convb):
    k.dma(k.PAR[0:16, :], convw.rearrange("t (q n) -> (t q) n", q=4), w=["PAR"])
    if convb is not None:
        k.dma(k.PAR[32:36, :], convb.rearrange("o (q n) -> (o q) n", q=4), w=["PAR"])
    b = k.psum()
    pv = k.ps[b][:, 0:128].rearrange("p (c x) -> p c x", c=8)
    for c8 in range(8):
        k.tr(pv[:, c8, :], k.PAR[0:16, c8 * 128:(c8 + 1) * 128], k.c("ident")[0:16, 0:16], r=["PAR", "CONST"], w=[("ps", b)])
    for q in range(4):
        k.op("dve", lambda e, q=q: e.tensor_copy(out=k.CW[:, q * 8:(q + 1) * 8, :], in_=pv.rearrange("p c (t q) -> p c t q", q=4)[:, :, :, q]),
             r=[("ps", b)], w=["CW"])
    k.pfree(b)
    if convb is not None:
        b = k.psum()
        pv2 = k.ps[b][:, 0:32].rearrange("p (c q) -> p c q", c=8)
        for c8 in range(8):
            k.tr(pv2[:, c8, :], k.PAR[32:36, c8 * 128:(c8 + 1) * 128], k.c("ident")[32:36, 32:36], r=["PAR", "CONST"], w=[("ps", b)])
        k.op("dve", lambda e: e.tensor_copy(out=k.CB[:, :].rearrange("p (q c) -> p c q", q=4), in_=pv2), r=[("ps", b)], w=["CB"])
        k.pfree(b)


def conv_feature_major(k, t, slot, chunk_ids, halo_src, first, conv_in, has_bias):
    T = t.T
    Wv = k.ring[slot][:, :].rearrange("p (k n) -> p k n", k=8)
    b = k.psum()
    for c in range(4):
        for kk in range(8):
            k.mm(k.ps[b][:, c * T:(c + 1) * T], Wv[:, kk, c * 128:(c + 1) * 128], k.XNT[:, kk, t.c0:t.c0 + T], kk == 0, kk == 7,
                 r=[("ring", slot), ("XNT", t.i)], w=[("ps", b)])
    ui = k.rr("UB", 2)
    U = k.UB[ui]
    uk = ("UB", ui)
    acc = k.fpool.get()
    if not t.sample:
        Uv = U[:, 0:4 * 131].rearrange("p (c x) -> p c x", c=4)
        k.op("act", lambda e: e.activation(out=Uv[:, :, 3:131], in_=k.ps[b][:, 0:512].rearrange("p (c x) -> p c x", c=4), func=AF.Copy),
             r=[("ps", b)], w=[uk])
        if first:
            k.op("pool", lambda e: e.memset(Uv[:, :, 0:3], 0.0), w=[uk])
        else:
            pU = k.UB[1 - ui][:, 0:4 * 131].rearrange("p (c x) -> p c x", c=4)
            k.op("pool", lambda e: e.tensor_copy(out=Uv[:, :, 0:3], in_=pU[:, :, 128:131]), r=[("UB", 1 - ui)], w=[uk])
        accv = acc.t[:, 0:512].rearrange("p (c x) -> p c x", c=4)
        srcs = lambda c, tap: Uv[:, c, tap:tap + 128]
        outs = lambda c: accv[:, c, :]
    else:
        Uv = U[:, 0:4 * 112].rearrange("p (c b x) -> p c b x", c=4, b=16)
        k.op("act", lambda e: e.activation(out=Uv[:, :, :, 3:7], in_=k.ps[b][:, 0:256].rearrange("p (c b x) -> p c b x", c=4, b=16), func=AF.Copy),
             r=[("ps", b)], w=[uk])
        stg = k.fpool.get()
        for (c0_, n_, ch0) in conv_in["segs"]:
            k.dma(stg.t[0:48, c0_:c0_ + n_], conv_in["src"][:, :, ch0:ch0 + n_].rearrange("b j c -> (b j) c"), w=[stg.key])
        bh = k.psum()
        ph = k.ps[bh][:, 0:192].rearrange("p (c x) -> p c x", c=4)
        for c in range(4):
            k.tr(ph[:, c, :], stg.t[0:48, c * 128:(c + 1) * 128], k.c("ident")[0:48, 0:48], r=[stg.key, "CONST"], w=[("ps", bh)])
        stg.free()
        k.op("act", lambda e: e.activation(out=Uv[:, :, :, 0:3], in_=ph.rearrange("p c (b x) -> p c b x", b=16), func=AF.Copy), r=[("ps", bh)], w=[uk])
        k.pfree(bh)
        accv = acc.t[:, 0:256].rearrange("p (c b x) -> p c b x", c=4, b=16)
        srcs = lambda c, tap: Uv[:, c, :, tap:tap + 4]
        outs = lambda c: accv[:, c, :, :]
    k.pfree(b)
    for c in range(4):
        ch = chunk_ids[c]
        eng = "dve"
        if has_bias:
            k.op(eng, lambda e, c=c, ch=ch: e.tensor_scalar(out=outs(c), in0=srcs(c, 0), scalar1=k.CW[:, ch, 0:1], scalar2=k.CB[:, ch:ch + 1],
                                                            op0=ALU.mult, op1=ALU.add), r=[uk, "CW", "CB"], w=[acc.key])
        else:
            k.op(eng, lambda e, c=c, ch=ch: e.tensor_scalar_mul(out=outs(c), in0=srcs(c, 0), scalar1=k.CW[:, ch, 0:1]), r=[uk, "CW"], w=[acc.key])
        for tap in range(1, 4):
            k.op(eng, lambda e, c=c, ch=ch, tap=tap: e.scalar_tensor_tensor(out=outs(c), in0=srcs(c, tap), scalar=k.CW[:, ch, tap:tap + 1], in1=outs(c),
                                                                           op0=ALU.mult, op1=ALU.add), r=[uk, "CW", acc.key], w=[acc.key])
    return acc


def conv_state_out(k, t, slot, segs, dst_p, dst_s):
    T = t.T
    b = proj_tok(k, t, slot, 512)
    stg = k.fpool.get()
    k.op("act", lambda e: e.activation(out=stg.t[0:T, :], in_=k.ps[b][0:T, :], func=AF.Copy), r=[("ps", b)], w=[stg.key])
    k.pfree(b)
    for (c0_, n_, ch0) in segs:
        if not t.sample:
            k.dma(dst_p[0:3, ch0:ch0 + n_], stg.t[125:128, c0_:c0_ + n_], r=[stg.key])
        else:
            for jj in range(3):
                k.dma(dst_s[:, jj, ch0:ch0 + n_], stg.t[1 + jj:64:4, c0_:c0_ + n_], r=[stg.key])
    stg.free()


def from_ssd(k, layer):
    win = k.din["ssd_w_in"]
    wout = k.din["ssd_w_out"]
    k.dma(k.PRM[:, 0:32], k.din["ssd_dt_bias"][0, :].partition_broadcast(128), w=["PRM"])
    k.dma(k.PRM[:, 32:64], k.din["ssd_a_log"][0, :].partition_broadcast(128), w=["PRM"])
    k.dma(k.PRM[:, 64:96], k.din["ssd_d"][0, :].partition_broadcast(128), w=["PRM"])
    k.op("act", lambda e: e.activation(out=k.PRM[:, 96:128], in_=k.PRM[:, 32:64], func=AF.Exp), r=["PRM"], w=["PRM"])
    k.op("dve", lambda e: e.tensor_scalar(out=k.PRM[:, 96:128], in0=k.PRM[:, 96:128], scalar1=-1.0, scalar2=None, op0=ALU.mult), r=["PRM"], w=["PRM"])
    load_conv_params(k, k.din["ssd_conv_w"], k.din["ssd_conv_b"])
    for g in range(8):
        v8 = lambda tt: tt[:, :].rearrange("p (k n) -> p k n", k=8)
        segs = [(0, 256, g * 256), (256, 128, 2048 + g * 128), (384, 128, 3072 + g * 128)]
        s_x = k.load_w([((lambda tt, c0_=c0_, n_=n_: v8(tt)[:, :, c0_:c0_ + n_]),
                         win[:, 2048 + ch0:2048 + ch0 + n_].rearrange("(k p) n -> p k n", p=128)) for (c0_, n_, ch0) in segs])
        s_z = k.load_w([(lambda tt: v8(tt)[:, :, 0:256], win[:, g * 256:(g + 1) * 256].rearrange("(k p) n -> p k n", p=128)),
                        (lambda tt: v8(tt)[:, :, 256:260], win[:, 6144 + 4 * g:6144 + 4 * g + 4].rearrange("(k p) n -> p k n", p=128))])
        s_o = k.load_w([(lambda tt: tt[:, 0:2048].rearrange("p (c n) -> p c n", c=2), wout[g * 256:(g + 1) * 256, :].rearrange("(c p) n -> p c n", p=128))])
        nwb = k.fpool.get()
        k.dma(nwb.t[:, 0:256], k.din["ssd_norm"][0, g * 256:(g + 1) * 256].partition_broadcast(128), w=[nwb.key])
        chunk_ids = [2 * g, 2 * g + 1, 16 + g, 24 + g]
        conv_in = {"src": k.din["state_ssd_conv"], "segs": segs}

        def stage_a(t):
            T = t.T
            acc = conv_feature_major(k, t, s_x, chunk_ids, None, t.i == 0, conv_in, True)
            n4 = 4 * T
            xsT = k.fpool.get()
            bcT = k.bpool.get()
            silu_to(k, xsT.t[:, 0:2 * T], acc.t[:, 0:2 * T], [acc.key], [xsT.key])
            silu_to(k, bcT.t[:, 0:2 * T], acc.t[:, 2 * T:4 * T], [acc.key], [bcT.key])
            acc.free()
            b = k.psum()
            for c in range(2):
                k.tr(k.ps[b][0:T, c * 128:(c + 1) * 128], xsT.t[:, c * T:(c + 1) * T], k.c("ident"), r=[xsT.key, "CONST"], w=[("ps", b)])
            xs = k.fpool.get()
            k.op("act", lambda e: e.activation(out=xs.t[0:T, 0:256], in_=k.ps[b][0:T, 0:256], func=AF.Copy), r=[("ps", b)], w=[xs.key])
            k.pfree(b)
            xsT.free()
            b = k.psum()
            pvb = k.ps[b][:, :].bitcast(BF16)
            k.tr(pvb[0:T, 0:128], bcT.t[:, 0:T], k.identb[:, :], r=[bcT.key, "identb"], w=[("ps", b)])
            btok = k.bpool.get()
            k.op("dve", lambda e: e.tensor_copy(out=btok.t[0:T, 0:128], in_=pvb[0:T, 0:128]), r=[("ps", b)], w=[btok.key])
            k.pfree(b)
            bz = proj_tok(k, t, s_z, 260)
            sz = k.bpool.get()
            silu_to(k, sz.t[0:T, 0:256], k.ps[bz][0:T, 0:256], [("ps", bz)], [sz.key])
            li = k.rr("LAB", 4)
            lab = k.LAB[li]
            lk = ("LAB", li)
            k.op("dve", lambda e: e.tensor_tensor(out=lab[0:T, 0:4], in0=k.ps[bz][0:T, 256:260], in1=k.PRM[0:T, 4 * g:4 * g + 4], op=ALU.add),
                 r=[("ps", bz), "PRM"], w=[lk])
            k.pfree(bz)
            k.op("act", lambda e: e.activation(out=lab[0:T, 0:4], in_=lab[0:T, 0:4], func=AF.Exp), r=[lk], w=[lk])
            k.op("pool", lambda e: e.tensor_scalar(out=lab[0:T, 0:4], in0=lab[0:T, 0:4], scalar1=1.0, scalar2=None, op0=ALU.add), r=[lk], w=[lk])
            k.op("act", lambda e: e.activation(out=lab[0:T, 0:4], in_=lab[0:T, 0:4], func=AF.Ln), r=[lk], w=[lk])
            k.op("dve", lambda e: e.tensor_tensor(out=lab[0:T, 4:8], in0=lab[0:T, 0:4], in1=k.PRM[0:T, 96 + 4 * g:96 + 4 * g + 4], op=ALU.mult),
                 r=[lk, "PRM"], w=[lk])
            vdt = k.bpool.get()
            k.op("pool", lambda e: e.tensor_tensor(out=vdt.t[0:T, 0:256].rearrange("p (r d) -> p r d", r=4), in0=xs.t[0:T, 0:256].rearrange("p (r d) -> p r d", r=4),
                                                   in1=lab[0:T, 0:4].unsqueeze(2).to_broadcast([T, 4, 64]), op=ALU.mult), r=[xs.key, lk], w=[vdt.key])
            return {"bcT": bcT, "xs": xs, "btok": btok, "sz": sz, "lab": lab, "lk": lk, "vdt": vdt}

        def stage_b(t, A):
            T = t.T
            prep = decay_prep(k, t.kind, A["lab"][0:T, 4:8], [A["lk"]], 4)
            bcv = A["bcT"].t[:, 0:2 * T].rearrange("p (c t) -> p c t", c=2)

            def sload(b_, nc_, dst):
                stg = k.fpool.get()
                k.dma(stg.t[:, 0:256].rearrange("p (j n) -> p j n", j=2),
                      k.din["state_ssd"][b_, 4 * g:4 * g + 4, :, :].rearrange("(j h) p n -> (h p) j n", j=2), w=[stg.key])
                bb = k.psum()
                for jj in range(2):
                    k.tr(k.ps[bb][:, jj * 128:(jj + 1) * 128], stg.t[:, jj * 128:(jj + 1) * 128], k.c("ident"), r=[stg.key, "CONST"], w=[("ps", bb)])
                stg.free()
                k.op("dve", lambda e: e.tensor_copy(out=dst.t[:, 0:256], in_=k.ps[bb][:, 0:256]), r=[("ps", bb)], w=[dst.key])
                k.pfree(bb)

            def st_out(src_ap, src_key, dst_ap):
                bb = k.psum()
                for jj in range(2):
                    k.tr(k.ps[bb][:, jj * 128:(jj + 1) * 128], src_ap[:, jj * 128:(jj + 1) * 128], k.c("ident"), r=[src_key, "CONST"], w=[("ps", bb)])
                stg = k.fpool.get()
                k.op("act", lambda e: e.activation(out=stg.t[:, 0:256], in_=k.ps[bb][:, 0:256], func=AF.Copy), r=[("ps", bb)], w=[stg.key])
                k.pfree(bb)
                k.dma(dst_ap.rearrange("(j h) p n -> (h p) j n", j=2), stg.t[:, 0:256].rearrange("p (j n) -> p j n", j=2), r=[stg.key])
                stg.free()

            def sstore(b_, nc_, src):
                st_out(src.t[:, 0:256], src.key, k.dout["s_ssd"][b_, 4 * g:4 * g + 4, :, :])

            pstore = None
            if t.i == NPT - 1:
                def pstore(nc_):
                    st_out(k.SF[:, 0, 0:256], ("SF", 0), k.dout["p_ssd"][4 * g:4 * g + 4, :, :])
            y = scan_tile(k, t, prep, bcv[:, 1:2, :], A["bcT"].key, bcv[:, 0:1, :], A["bcT"].key, A["btok"].t[:, 0:128], A["btok"].key,
                          A["vdt"].t, A["vdt"].key, 4, 64, 1, t.i == 0, sload, sstore, pstore)
            prep["dec"].free()
            tmp = k.fpool.get()
            k.op("pool", lambda e: e.tensor_tensor(out=tmp.t[0:T, 0:256].rearrange("p (r d) -> p r d", r=4), in0=A["xs"].t[0:T, 0:256].rearrange("p (r d) -> p r d", r=4),
                                                   in1=k.PRM[0:T, 64 + 4 * g:64 + 4 * g + 4].unsqueeze(2).to_broadcast([T, 4, 64]), op=ALU.mult),
                 r=[A["xs"].key, "PRM"], w=[tmp.key])
            k.op("dve", lambda e: e.tensor_tensor(out=y.t[0:T, 0:256], in0=y.t[0:T, 0:256], in1=tmp.t[0:T, 0:256], op=ALU.add), r=[y.key, tmp.key], w=[y.key])
            tmp.free()
            k.op("dve", lambda e: e.tensor_tensor(out=y.t[0:T, 0:256], in0=y.t[0:T, 0:256], in1=A["sz"].t[0:T, 0:256], op=ALU.mult), r=[y.key, A["sz"].key], w=[y.key])
            stt = k.st[k.rr("st", 4)]
            emit_rstd(k, t, stt, y.t[0:T, 0:256], [y.key], 256, 0)
            yn = k.bpool.get()
            k.op("dve", lambda e: e.scalar_tensor_tensor(out=yn.t[0:T, 0:256], in0=y.t[0:T, 0:256], scalar=stt[0:T, 1:2], in1=nwb.t[0:T, 0:256],
                                                          op0=ALU.mult, op1=ALU.mult), r=[y.key, ("st", id(stt)), nwb.key], w=[yn.key])
            y.free()
            for nm in ("bcT", "xs", "btok", "sz", "vdt"):
                A[nm].free()
            ynT = transpose_to(k, t, yn, 2)
            yn.free()
            out_proj_add(k, t, ynT, 2, s_o)
            ynT.free()
            if t.i >= NPT - 1:
                conv_state_out(k, t, s_x, segs, k.dout["p_ssd_conv"], k.dout["s_ssd_conv"])

        A = stage_a(TILES[0])
        for ti in range(NT):
            if ti + 1 < NT and CFG.get("coop", True):
                box = {}
                COOP.run([lambda: box.__setitem__("A", stage_a(TILES[ti + 1])), lambda: stage_b(TILES[ti], A)])
                A = box["A"]
            else:
                An = stage_a(TILES[ti + 1]) if ti + 1 < NT else None
                stage_b(TILES[ti], A)
                A = An
        nwb.free()


def gdn_chain(k, T, A, nlev):
    identb = k.identb[0:T, 0:T]
    v3 = lambda buf: buf.t[0:T, 0:2 * T].rearrange("p (r t) -> p r t", r=2)
    sl = lambda buf, r_: buf.t[0:T, r_ * T:(r_ + 1) * T]
    b = k.psum()
    pvb = k.ps[b][:, :].bitcast(BF16)
    for r_ in range(2):
        k.tr(pvb[0:T, r_ * T:(r_ + 1) * T], sl(A, r_), identb, r=[A.key, "identb"], w=[("ps", b)])
    AT = k.hpool.get()
    k.op("act", lambda e: e.activation(out=AT.t[0:T, 0:2 * T], in_=pvb[0:T, 0:2 * T], func=AF.Copy), r=[("ps", b)], w=[AT.key])
    k.pfree(b)
    Dm = None
    DT = None
    for lv in range(nlev):
        last = (lv == nlev - 1)
        mT = k.c("bmT%d" % lv)[0:T, 0:T]
        AoT = k.hpool.get()
        k.op("pool", lambda e: e.tensor_tensor(out=v3(AoT), in0=v3(AT), in1=mT.unsqueeze(1).to_broadcast([T, 2, T]), op=ALU.mult),
             r=[AT.key, "CONST"], w=[AoT.key])
        dk = [Dm.key] if Dm is not None else ["identb"]
        dtk = [DT.key] if DT is not None else ["identb"]
        Dv = (lambda r_: sl(Dm, r_)) if Dm is not None else (lambda r_: identb)
        DTv = (lambda r_: sl(DT, r_)) if DT is not None else (lambda r_: identb)
        be = k.psum()
        for r_ in range(2):
            k.mm(k.ps[be][0:T, r_ * T:(r_ + 1) * T], sl(AoT, r_), Dv(r_), True, True, r=[AoT.key] + dk, w=[("ps", be)])
        E = k.hpool.get()
        k.op("act", lambda e: e.activation(out=E.t[0:T, 0:2 * T], in_=k.ps[be][0:T, 0:2 * T], func=AF.Copy), r=[("ps", be)], w=[E.key])
        k.pfree(be)
        AoT.free()
        bft = k.psum()
        for r_ in range(2):
            k.mm(k.ps[bft][0:T, r_ * T:(r_ + 1) * T], sl(E, r_), DTv(r_), True, True, r=[E.key] + dtk, w=[("ps", bft)])
        DTn = k.hpool.get()
        if DT is not None:
            k.op("dve", lambda e: e.tensor_tensor(out=DTn.t[0:T, 0:2 * T], in0=DT.t[0:T, 0:2 * T], in1=k.ps[bft][0:T, 0:2 * T], op=ALU.add),
                 r=[DT.key, ("ps", bft)], w=[DTn.key])
        else:
            k.op("dve", lambda e: e.tensor_tensor(out=v3(DTn), in0=k.ps[bft][0:T, 0:2 * T].rearrange("p (r t) -> p r t", r=2),
                                                  in1=identb.unsqueeze(1).to_broadcast([T, 2, T]), op=ALU.add), r=["identb", ("ps", bft)], w=[DTn.key])
        k.pfree(bft)
        Dn = None
        if not last:
            bf_ = k.psum()
            for r_ in range(2):
                k.mm(k.ps[bf_][0:T, r_ * T:(r_ + 1) * T], DTv(r_), sl(E, r_), True, True, r=[E.key] + dtk, w=[("ps", bf_)])
            Dn = k.hpool.get()
            if Dm is not None:
                k.op("dve", lambda e: e.tensor_tensor(out=Dn.t[0:T, 0:2 * T], in0=Dm.t[0:T, 0:2 * T], in1=k.ps[bf_][0:T, 0:2 * T], op=ALU.add),
                     r=[Dm.key, ("ps", bf_)], w=[Dn.key])
            else:
                k.op("dve", lambda e: e.tensor_tensor(out=v3(Dn), in0=k.ps[bf_][0:T, 0:2 * T].rearrange("p (r t) -> p r t", r=2),
                                                      in1=identb.unsqueeze(1).to_broadcast([T, 2, T]), op=ALU.add), r=["identb", ("ps", bf_)], w=[Dn.key])
            k.pfree(bf_)
        E.free()
        if Dm is not None:
            Dm.free()
        if DT is not None:
            DT.free()
        Dm, DT = Dn, DTn
    AT.free()
    return DT


def from_gdn(k, layer):
    win = k.din["gdn_w_in"]
    wout = k.din["gdn_w_out"]
    k.dma(k.PRM[:, 0:16], k.din["gdn_dt_bias"][0, :].partition_broadcast(128), w=["PRM"])
    k.dma(k.PRM[:, 16:32], k.din["gdn_a_log"][0, :].partition_broadcast(128), w=["PRM"])
    k.op("act", lambda e: e.activation(out=k.PRM[:, 32:48], in_=k.PRM[:, 16:32], func=AF.Exp), r=["PRM"], w=["PRM"])
    k.op("dve", lambda e: e.tensor_scalar(out=k.PRM[:, 32:48], in0=k.PRM[:, 32:48], scalar1=-1.0, scalar2=None, op0=ALU.mult), r=["PRM"], w=["PRM"])
    load_conv_params(k, k.din["gdn_conv_w"], None)
    gnw = k.fpool.get()
    k.dma(gnw.t[:, 0:128], k.din["gdn_norm"][0, :].partition_broadcast(128), w=[gnw.key])
    for kh in range(8):
        v8 = lambda tt: tt[:, :].rearrange("p (k n) -> p k n", k=8)
        segs = [(0, 128, kh * 128), (128, 128, 1024 + kh * 128), (256, 256, 2048 + kh * 256)]
        s_x = k.load_w([((lambda tt, c0_=c0_, n_=n_: v8(tt)[:, :, c0_:c0_ + n_]),
                         win[:, ch0:ch0 + n_].rearrange("(k p) n -> p k n", p=128)) for (c0_, n_, ch0) in segs])
        s_z = k.load_w([(lambda tt: v8(tt)[:, :, 0:256], win[:, 4096 + kh * 256:4096 + (kh + 1) * 256].rearrange("(k p) n -> p k n", p=128)),
                        (lambda tt: v8(tt)[:, :, 256:258], win[:, 6144 + 2 * kh:6144 + 2 * kh + 2].rearrange("(k p) n -> p k n", p=128)),
                        (lambda tt: v8(tt)[:, :, 258:260], win[:, 6160 + 2 * kh:6160 + 2 * kh + 2].rearrange("(k p) n -> p k n", p=128))])
        s_o = k.load_w([(lambda tt: tt[:, 0:2048].rearrange("p (c n) -> p c n", c=2), wout[kh * 256:(kh + 1) * 256, :].rearrange("(c p) n -> p c n", p=128))])
        chunk_ids = [kh, 8 + kh, 16 + 2 * kh, 17 + 2 * kh]
        conv_in = {"src": k.din["state_gdn_conv"], "segs": segs}

        def stage_a(t):
            T = t.T
            acc = conv_feature_major(k, t, s_x, chunk_ids, None, t.i == 0, conv_in, False)
            act4 = k.fpool.get()
            silu_to(k, act4.t[:, 0:4 * T], acc.t[:, 0:4 * T], [acc.key], [act4.key])
            acc.free()
            sq = k.fpool.get()
            k.op("pool", lambda e: e.tensor_tensor(out=sq.t[:, 0:2 * T], in0=act4.t[:, 0:2 * T], in1=act4.t[:, 0:2 * T], op=ALU.mult), r=[act4.key], w=[sq.key])
            b = k.psum()
            k.mm(k.ps[b][:, 0:2 * T], k.c("ones"), sq.t[:, 0:2 * T], True, True, r=["CONST", sq.key], w=[("ps", b)])
            k.op("dve", lambda e: e.tensor_scalar(out=sq.t[:, 0:2 * T], in0=k.ps[b][:, 0:2 * T], scalar1=RMS_EPS, scalar2=None, op0=ALU.add),
                 r=[("ps", b)], w=[sq.key])
            k.pfree(b)
            k.op("act", lambda e: e.activation(out=sq.t[:, 0:2 * T], in_=sq.t[:, 0:2 * T], func=AF.Ln), r=[sq.key], w=[sq.key])
            k.op("act", lambda e: e.activation(out=sq.t[:, 0:2 * T], in_=sq.t[:, 0:2 * T], func=AF.Exp, scale=-0.5), r=[sq.key], w=[sq.key])
            qkn = k.bpool.get()
            k.op("dve", lambda e: e.scalar_tensor_tensor(out=qkn.t[:, 0:T], in0=act4.t[:, 0:T], scalar=float(128.0 ** -0.5), in1=sq.t[:, 0:T],
                                                          op0=ALU.mult, op1=ALU.mult), r=[act4.key, sq.key], w=[qkn.key])
            k.op("pool", lambda e: e.tensor_tensor(out=qkn.t[:, T:2 * T], in0=act4.t[:, T:2 * T], in1=sq.t[:, T:2 * T], op=ALU.mult),
                 r=[act4.key, sq.key], w=[qkn.key])
            sq.free()
            b = k.psum()
            pvb = k.ps[b][:, :].bitcast(BF16)
            k.tr(pvb[0:T, 0:128], qkn.t[:, T:2 * T], k.identb[:, :], r=[qkn.key, "identb"], w=[("ps", b)])
            ktok = k.bpool.get()
            k.op("dve", lambda e: e.tensor_copy(out=ktok.t[0:T, 0:128], in_=pvb[0:T, 0:128]), r=[("ps", b)], w=[ktok.key])
            k.pfree(b)
            b = k.psum()
            for c in range(2):
                k.tr(k.ps[b][0:T, c * 128:(c + 1) * 128], act4.t[:, (2 + c) * T:(3 + c) * T], k.c("ident"), r=[act4.key, "CONST"], w=[("ps", b)])
            vtok = k.fpool.get()
            k.op("act", lambda e: e.activation(out=vtok.t[0:T, 0:256], in_=k.ps[b][0:T, 0:256], func=AF.Copy), r=[("ps", b)], w=[vtok.key])
            k.pfree(b)
            act4.free()
            bz = proj_tok(k, t, s_z, 260)
            sz = k.bpool.get()
            silu_to(k, sz.t[0:T, 0:256], k.ps[bz][0:T, 0:256], [("ps", bz)], [sz.key])
            li = k.rr("LAB", 4)
            lab = k.LAB[li]
            lk = ("LAB", li)
            k.op("act", lambda e: e.activation(out=lab[0:T, 0:2], in_=k.ps[bz][0:T, 256:258], func=AF.Exp, scale=-1.0), r=[("ps", bz)], w=[lk])
            k.op("dve", lambda e: e.tensor_tensor(out=lab[0:T, 4:6], in0=k.ps[bz][0:T, 258:260], in1=k.PRM[0:T, 2 * kh:2 * kh + 2], op=ALU.add),
                 r=[("ps", bz), "PRM"], w=[lk])
            k.pfree(bz)
            k.op("pool", lambda e: e.tensor_scalar(out=lab[0:T, 0:2], in0=lab[0:T, 0:2], scalar1=1.0, scalar2=None, op0=ALU.add), r=[lk], w=[lk])
            k.op("dve", lambda e: e.reciprocal(out=lab[0:T, 0:2], in_=lab[0:T, 0:2]), r=[lk], w=[lk])
            k.op("dve", lambda e: e.tensor_scalar(out=lab[0:T, 2:4], in0=lab[0:T, 0:2], scalar1=-1.0, scalar2=None, op0=ALU.mult), r=[lk], w=[lk])
            k.op("act", lambda e: e.activation(out=lab[0:T, 4:6], in_=lab[0:T, 4:6], func=AF.Exp), r=[lk], w=[lk])
            k.op("pool", lambda e: e.tensor_scalar(out=lab[0:T, 4:6], in0=lab[0:T, 4:6], scalar1=1.0, scalar2=None, op0=ALU.add), r=[lk], w=[lk])
            k.op("act", lambda e: e.activation(out=lab[0:T, 4:6], in_=lab[0:T, 4:6], func=AF.Ln), r=[lk], w=[lk])
            k.op("dve", lambda e: e.tensor_tensor(out=lab[0:T, 4:6], in0=lab[0:T, 4:6], in1=k.PRM[0:T, 32 + 2 * kh:32 + 2 * kh + 2], op=ALU.mult),
                 r=[lk, "PRM"], w=[lk])
            return {"qkn": qkn, "ktok": ktok, "vtok": vtok, "sz": sz, "lab": lab, "lk": lk}

        def stage_a2(t):
            A_ = stage_a(t)
            stage_b1(t, A_)
            return A_

        def stage_b1(t, A_):
            T = t.T
            lab, lk = A_["lab"], A_["lk"]
            qkn = A_["qkn"]
            qT = qkn.t[:, 0:T]
            kT = qkn.t[:, T:2 * T]
            prep = decay_prep(k, t.kind, lab[0:T, 4:6], [lk], 2)
            sm, smk, dec = prep["sm"], prep["smk"], prep["dec"]
            strict = kconst(k, t.kind, "strict")
            bg = k.psum()
            k.mm(k.ps[bg][0:T, 0:T], kT, kT, True, True, r=[qkn.key], w=[("ps", bg)])
            k.mm(k.ps[bg][0:T, T:2 * T], kT, qT, True, True, r=[qkn.key], w=[("ps", bg)])
            bd_ = k.psum()
            for r_ in range(2):
                k.tr(k.ps[bd_][0:T, r_ * T:(r_ + 1) * T], dec.t[0:T, r_ * T:(r_ + 1) * T], k.c("ident")[0:T, 0:T], r=[dec.key, "CONST"], w=[("ps", bd_)])
            dsb = k.fpool.get()
            k.op("dve", lambda e: e.tensor_tensor(out=dsb.t[0:T, 0:2 * T].rearrange("p (r t) -> p r t", r=2),
                                                  in0=k.ps[bd_][0:T, 0:2 * T].rearrange("p (r t) -> p r t", r=2),
                                                  in1=strict.unsqueeze(1).to_broadcast([T, 2, T]), op=ALU.mult), r=[("ps", bd_), "CONST"], w=[dsb.key])
            k.pfree(bd_)
            Am = k.hpool.get()
            for r_ in range(2):
                k.op("dve", lambda e, r_=r_: e.scalar_tensor_tensor(out=Am.t[0:T, r_ * T:(r_ + 1) * T], in0=k.ps[bg][0:T, 0:T], scalar=lab[0:T, 2 + r_:3 + r_],
                                                                     in1=dsb.t[0:T, r_ * T:(r_ + 1) * T], op0=ALU.mult, op1=ALU.mult),
                     r=[("ps", bg), lk, dsb.key], w=[Am.key])
            dsb.free()
            attnT = k.bpool.get()
            k.op("dve", lambda e: e.tensor_tensor(out=attnT.t[0:T, 0:2 * T].rearrange("p (r t) -> p r t", r=2),
                                                  in0=dec.t[0:T, 0:2 * T].rearrange("p (r t) -> p r t", r=2),
                                                  in1=k.ps[bg][0:T, T:2 * T].unsqueeze(1).to_broadcast([T, 2, T]), op=ALU.mult),
                 r=[dec.key, ("ps", bg)], w=[attnT.key])
            k.pfree(bg)
            dec.free()
            TTd = gdn_chain(k, T, Am, 7 if not t.sample else 2)
            Am.free()
            TTf = k.bpool.get()
            for r_ in range(2):
                k.op("act", lambda e, r_=r_: e.activation(out=TTf.t[0:T, r_ * T:(r_ + 1) * T], in_=TTd.t[0:T, r_ * T:(r_ + 1) * T], func=AF.Copy,
                                                            scale=lab[0:T, 2 + r_:3 + r_]), r=[TTd.key, lk], w=[TTf.key])
            TTd.free()
            A_["prep"] = prep
            A_["attnT"] = attnT
            A_["TTf"] = TTf

        def stage_b(t, A_):
            T = t.T
            nseq = t.nseq
            first = (t.i == 0)
            lab, lk = A_["lab"], A_["lk"]
            qkn, ktok, vtok = A_["qkn"], A_["ktok"], A_["vtok"]
            qT = qkn.t[:, 0:T]
            kT = qkn.t[:, T:2 * T]
            prep, attnT, TTf = A_["prep"], A_["attnT"], A_["TTf"]
            sm, smk = prep["sm"], prep["smk"]
            tmpb = k.bpool.get()
            bks = None
            if not (nseq == 1 and first):
                bks = k.psum()
                if nseq == 1:
                    for r_ in range(2):
                        k.mm(k.ps[bks][0:T, r_ * 256:r_ * 256 + 128], kT, k.SB[:, 0, r_ * 128:(r_ + 1) * 128], True, True, r=[qkn.key, ("SB", 0)], w=[("ps", bks)])
                        k.mm(k.ps[bks][0:T, r_ * 256 + 128:r_ * 256 + 256], qT, k.SB[:, 0, r_ * 128:(r_ + 1) * 128], True, True, r=[qkn.key, ("SB", 0)], w=[("ps", bks)])
                else:
                    sbl = []
                    for b_ in range(nseq):
                        sf = k.fpool.get()
                        k.dma(sf.t[:, 0:256].rearrange("p (r v) -> p r v", r=2), k.din["state_gdn"][b_, 2 * kh:2 * kh + 2, :, :].rearrange("r k v -> k r v"), w=[sf.key])
                        sb = k.bpool.get()
                        k.op("act", lambda e, sb=sb, sf=sf: e.activation(out=sb.t[:, 0:256], in_=sf.t[:, 0:256], func=AF.Copy), r=[sf.key], w=[sb.key])
                        sf.free()
                        qm = k.bpool.get()
                        k.op("pool", lambda e, qm=qm, b_=b_: e.tensor_tensor(out=qm.t[:, 0:2 * T].rearrange("p (c t) -> p c t", c=2),
                                                                            in0=qkn.t[:, 0:2 * T].rearrange("p (c t) -> p c t", c=2),
                                                                            in1=k.colmaskb[:, b_, :].unsqueeze(1).to_broadcast([128, 2, T]), op=ALU.mult),
                             r=[qkn.key, "colmaskb"], w=[qm.key])
                        for r_ in range(2):
                            k.P.op("pe", lambda e, qm=qm, sb=sb, r_=r_, b_=b_: e.matmul(k.ps[bks][0:T, r_ * 256:r_ * 256 + 128], lhsT=qm.t[:, T:2 * T],
                                                                                         rhs=sb.t[:, r_ * 128:(r_ + 1) * 128], start=(b_ == 0 and r_ == 0),
                                                                                         stop=(b_ == nseq - 1), skip_group_check=True),
                                   r=[qm.key, sb.key], w=[("ps", bks)])
                            k.P.op("pe", lambda e, qm=qm, sb=sb, r_=r_, b_=b_: e.matmul(k.ps[bks][0:T, r_ * 256 + 128:r_ * 256 + 256], lhsT=qm.t[:, 0:T],
                                                                                         rhs=sb.t[:, r_ * 128:(r_ + 1) * 128], start=False,
                                                                                         stop=(b_ == nseq - 1), skip_group_check=True),
                                   r=[qm.key, sb.key], w=[("ps", bks)])
                        qm.free()
                        sb.free()
                for r_ in range(2):
                    k.op("dve", lambda e, r_=r_: e.scalar_tensor_tensor(out=tmpb.t[0:T, r_ * 128:(r_ + 1) * 128], in0=k.ps[bks][0:T, r_ * 256:r_ * 256 + 128],
                                                                         scalar=sm[0:T, 4 + r_:5 + r_], in1=vtok.t[0:T, r_ * 128:(r_ + 1) * 128],
                                                                         op0=ALU.mult, op1=ALU.subtract), r=[("ps", bks), smk, vtok.key], w=[tmpb.key])
            else:
                k.op("act", lambda e: e.activation(out=tmpb.t[0:T, 0:256], in_=vtok.t[0:T, 0:256], func=AF.Copy, scale=-1.0), r=[vtok.key], w=[tmpb.key])
            bv = k.psum()
            for r_ in range(2):
                k.mm(k.ps[bv][0:T, r_ * 128:(r_ + 1) * 128], TTf.t[0:T, r_ * T:(r_ + 1) * T], tmpb.t[0:T, r_ * 128:(r_ + 1) * 128], True, True,
                     r=[TTf.key, tmpb.key], w=[("ps", bv)])
            TTf.free()
            tmpb.free()
            vnew = k.bpool.get()
            k.op("act", lambda e: e.activation(out=vnew.t[0:T, 0:256], in_=k.ps[bv][0:T, 0:256], func=AF.Copy), r=[("ps", bv)], w=[vnew.key])
            k.pfree(bv)
            bo = k.psum()
            for r_ in range(2):
                k.mm(k.ps[bo][0:T, r_ * 128:(r_ + 1) * 128], attnT.t[0:T, r_ * T:(r_ + 1) * T], vnew.t[0:T, r_ * 128:(r_ + 1) * 128], True, True,
                     r=[attnT.key, vnew.key], w=[("ps", bo)])
            attnT.free()
            o = k.fpool.get()
            if bks is not None:
                tq = k.fpool.get()
                for r_ in range(2):
                    k.op("act", lambda e, r_=r_: e.activation(out=tq.t[0:T, r_ * 128:(r_ + 1) * 128], in_=k.ps[bks][0:T, r_ * 256 + 128:r_ * 256 + 256], func=AF.Copy,
                                                                scale=sm[0:T, 4 + r_:5 + r_]), r=[("ps", bks), smk], w=[tq.key])
                k.pfree(bks)
                k.op("dve", lambda e: e.tensor_tensor(out=o.t[0:T, 0:256], in0=tq.t[0:T, 0:256], in1=k.ps[bo][0:T, 0:256], op=ALU.add), r=[tq.key, ("ps", bo)], w=[o.key])
                tq.free()
            else:
                k.op("act", lambda e: e.activation(out=o.t[0:T, 0:256], in_=k.ps[bo][0:T, 0:256], func=AF.Copy), r=[("ps", bo)], w=[o.key])
            k.pfree(bo)
            kd = k.bpool.get()
            for r_ in range(2):
                k.op("pool", lambda e, r_=r_: e.tensor_scalar_mul(out=kd.t[0:T, r_ * 128:(r_ + 1) * 128], in0=ktok.t[0:T, 0:128], scalar1=sm[0:T, 6 + r_:7 + r_]),
                     r=[ktok.key, smk], w=[kd.key])
            if nseq == 1:
                bs_ = k.psum()
                for r_ in range(2):
                    k.mm(k.ps[bs_][:, r_ * 128:(r_ + 1) * 128], kd.t[0:T, r_ * 128:(r_ + 1) * 128], vnew.t[0:T, r_ * 128:(r_ + 1) * 128], True, True,
                         r=[kd.key, vnew.key], w=[("ps", bs_)])
                for r_ in range(2):
                    if first:
                        k.op("act", lambda e, r_=r_: e.activation(out=k.SF[:, 0, r_ * 128:(r_ + 1) * 128], in_=k.ps[bs_][:, r_ * 128:(r_ + 1) * 128], func=AF.Copy),
                             r=[("ps", bs_)], w=[("SF", 0)])
                    else:
                        k.op("dve", lambda e, r_=r_: e.scalar_tensor_tensor(out=k.SF[:, 0, r_ * 128:(r_ + 1) * 128], in0=k.SF[:, 0, r_ * 128:(r_ + 1) * 128],
                                                                             scalar=sm[0:128, 8 + r_:9 + r_], in1=k.ps[bs_][:, r_ * 128:(r_ + 1) * 128],
                                                                             op0=ALU.mult, op1=ALU.add), r=[("SF", 0), smk, ("ps", bs_)], w=[("SF", 0)])
                k.pfree(bs_)
                k.op("act", lambda e: e.activation(out=k.SB[:, 0, 0:256], in_=k.SF[:, 0, 0:256], func=AF.Copy), r=[("SF", 0)], w=[("SB", 0)])
                if t.i == NPT - 1:
                    k.dma(k.dout["p_gdn"][2 * kh:2 * kh + 2, :, :].rearrange("r k v -> k r v"), k.SF[:, 0, 0:256].rearrange("p (r v) -> p r v", r=2), r=[("SF", 0)])
            else:
                for b_ in range(nseq):
                    sf = k.fpool.get()
                    k.dma(sf.t[:, 0:256].rearrange("p (r v) -> p r v", r=2), k.din["state_gdn"][b_, 2 * kh:2 * kh + 2, :, :].rearrange("r k v -> k r v"), w=[sf.key])
                    km = k.bpool.get()
                    k.op("pool", lambda e, km=km, b_=b_: e.tensor_scalar_mul(out=km.t[0:T, 0:256], in0=kd.t[0:T, 0:256], scalar1=k.c("rowmask_s")[0:T, b_:b_ + 1]),
                         r=[kd.key, "CONST"], w=[km.key])
                    bs_ = k.psum()
                    for r_ in range(2):
                        k.mm(k.ps[bs_][:, r_ * 128:(r_ + 1) * 128], km.t[0:T, r_ * 128:(r_ + 1) * 128], vnew.t[0:T, r_ * 128:(r_ + 1) * 128], True, True,
                             r=[km.key, vnew.key], w=[("ps", bs_)])
                    km.free()
                    for r_ in range(2):
                        k.op("dve", lambda e, r_=r_, sf=sf, bs_=bs_, b_=b_: e.scalar_tensor_tensor(
                            out=sf.t[:, r_ * 128:(r_ + 1) * 128], in0=sf.t[:, r_ * 128:(r_ + 1) * 128], scalar=sm[0:128, 16 + 2 * b_ + r_:17 + 2 * b_ + r_],
                            in1=k.ps[bs_][:, r_ * 128:(r_ + 1) * 128], op0=ALU.mult, op1=ALU.add), r=[sf.key, smk, ("ps", bs_)], w=[sf.key])
                    k.pfree(bs_)
                    k.dma(k.dout["s_gdn"][b_, 2 * kh:2 * kh + 2, :, :].rearrange("r k v -> k r v"), sf.t[:, 0:256].rearrange("p (r v) -> p r v", r=2), r=[sf.key])
                    sf.free()
            kd.free()
            vnew.free()
            stt = k.st[k.rr("st", 4)]
            sk_ = ("st", id(stt))
            yn = k.bpool.get()
            for r_ in range(2):
                emit_rstd(k, t, stt, o.t[0:T, r_ * 128:(r_ + 1) * 128], [o.key], 128, 4 * r_)
                k.op("dve", lambda e, r_=r_: e.scalar_tensor_tensor(out=o.t[0:T, r_ * 128:(r_ + 1) * 128], in0=o.t[0:T, r_ * 128:(r_ + 1) * 128],
                                                                     scalar=stt[0:T, 4 * r_ + 1:4 * r_ + 2], in1=gnw.t[0:T, 0:128], op0=ALU.mult, op1=ALU.mult),
                     r=[o.key, sk_, gnw.key], w=[o.key])
            k.op("dve", lambda e: e.tensor_tensor(out=yn.t[0:T, 0:256], in0=o.t[0:T, 0:256], in1=A_["sz"].t[0:T, 0:256], op=ALU.mult), r=[o.key, A_["sz"].key], w=[yn.key])
            o.free()
            for nm in ("qkn", "ktok", "vtok", "sz"):
                A_[nm].free()
            ynT = transpose_to(k, t, yn, 2)
            yn.free()
            out_proj_add(k, t, ynT, 2, s_o)
            ynT.free()
            if t.i >= NPT - 1:
                conv_state_out(k, t, s_x, segs, k.dout["p_gdn_conv"], k.dout["s_gdn_conv"])

        A_ = stage_a2(TILES[0])
        for ti in range(NT):
            if ti + 1 < NT:
                box = {}
                COOP.run([lambda: box.__setitem__("A", stage_a2(TILES[ti + 1])), lambda: stage_b(TILES[ti], A_)])
                A_ = box["A"]
            else:
                stage_b(TILES[ti], A_)
    gnw.free()


def zero_unwritten_outputs(k):
    pass


_CACHE = {}


def _get_program():
    if "nc" not in _CACHE:
        pack, offs, cs, colmask = build_consts()
        offs = dict(offs)
        offs["_tot"] = pack.shape[1]
        _set_shapes(pack.shape[1])
        nc = bass.Bass("TRN2", target_bir_lowering=False)
        _CACHE["k"] = build_program(nc, offs)
        _CACHE["nc"] = nc
        _CACHE["used"] = set(_CACHE["k"].din.keys())
        _CACHE["pack"] = pack
        _CACHE["cs"] = cs
        _CACHE["colmask"] = colmask
    return _CACHE["nc"], _CACHE["pack"], _CACHE["cs"]


def kernel(x_prompt, x_sample, state_ret, state_ssd, state_ssd_conv, state_gdn, state_gdn_conv,
           norm_mix, norm_mlp, norm_final, ret_w_in, ret_w_out,
           ssd_w_in, ssd_conv_w, ssd_conv_b, ssd_dt_bias, ssd_a_log, ssd_d, ssd_norm, ssd_w_out,
           gdn_w_in, gdn_conv_w, gdn_dt_bias, gdn_a_log, gdn_norm, gdn_w_out,
           mlp_w_up, mlp_w_down):
    nc, pack, cs = _get_program()
    f = lambda a: np.ascontiguousarray(np.asarray(a, dtype=np.float32))
    norms = f(np.concatenate([np.asarray(norm_mix), np.asarray(norm_mlp), np.asarray(norm_final)[None, :]], axis=0))
    shared = {
        "norms": norms,
        "ret_w_in": f(ret_w_in), "ret_w_out": f(ret_w_out),
        "ssd_w_in": f(ssd_w_in[0]), "ssd_conv_w": f(ssd_conv_w[0]), "ssd_conv_b": f(ssd_conv_b), "ssd_dt_bias": f(ssd_dt_bias),
        "ssd_a_log": f(ssd_a_log), "ssd_d": f(ssd_d), "ssd_norm": f(ssd_norm), "ssd_w_out": f(ssd_w_out[0]),
        "gdn_w_in": f(gdn_w_in[0]), "gdn_conv_w": f(gdn_conv_w[0]), "gdn_dt_bias": f(gdn_dt_bias), "gdn_a_log": f(gdn_a_log),
        "gdn_norm": f(gdn_norm), "gdn_w_out": f(gdn_w_out[0]),
        "mlp_w_up": f(mlp_w_up), "mlp_w_down": f(mlp_w_down),
        "cpack": pack, "ropecs": cs, "colmask": _CACHE["colmask"],
    }
    xp = np.asarray(x_prompt, dtype=np.float32)
    xs = np.asarray(x_sample, dtype=np.float32)
    in_maps = []
    for c in range(8):
        sl = slice(16 * c, 16 * c + 16)
        m = dict(shared)
        m["x_prompt"] = f(xp[c])
        m["x_sample"] = f(xs[sl].reshape(TS, D))
        m["state_ret"] = f(np.asarray(state_ret)[:, sl])
        m["state_ssd"] = f(np.asarray(state_ssd)[0, sl])
        m["state_ssd_conv"] = f(np.asarray(state_ssd_conv)[0, sl])
        m["state_gdn"] = f(np.asarray(state_gdn)[0, sl])
        m["state_gdn_conv"] = f(np.asarray(state_gdn_conv)[0, sl])
        m = {kk: vv for kk, vv in m.items() if kk in _CACHE["used"]}
        in_maps.append(m)
    ncores = CFG.get("ncores", 8)
    res = run_bass_kernel_spmd(nc, in_maps[:ncores], core_ids=list(range(ncores)))
    R = res.results
    g = lambda nm: [np.asarray(R[c][nm]) if c < ncores else np.zeros_like(np.asarray(R[0][nm])) for c in range(8)]
    y_prompt = np.stack(g("y_prompt"), 0)
    y_sample = np.concatenate(g("y_sample"), 0).reshape(128, 4, D)
    p_ret = np.stack(g("p_ret"), 1)
    p_ssd = np.stack(g("p_ssd"), 0)[None]
    p_ssd_conv = np.stack(g("p_ssd_conv"), 0)[None]
    p_gdn = np.stack(g("p_gdn"), 0)[None]
    p_gdn_conv = np.stack(g("p_gdn_conv"), 0)[None]
    s_ret = np.concatenate(g("s_ret"), 1)
    s_ssd = np.concatenate(g("s_ssd"), 0)[None]
    s_ssd_conv = np.concatenate(g("s_ssd_conv"), 0)[None]
    s_gdn = np.concatenate(g("s_gdn"), 0)[None]
    s_gdn_conv = np.concatenate(g("s_gdn_conv"), 0)[None]
    return (y_prompt, y_sample, p_ret, p_ssd, p_ssd_conv, p_gdn, p_gdn_conv,
            s_ret, s_ssd, s_ssd_conv, s_gdn, s_gdn_conv)
```

```python
import contextlib
import math
import numpy as np
import concourse.bass as bass
import concourse.mybir as mybir
from concourse.bass_utils import run_bass_kernel_spmd

F32 = mybir.dt.float32
BF16 = mybir.dt.bfloat16
ALU = mybir.AluOpType
AF = mybir.ActivationFunctionType

ENGS = ("pe", "act", "dve", "pool", "sp")
NDMA_SEMS = 40

D = 1024
SEQ = 2048
NPT = 16
NT = 17
TS = 64
NTOK = SEQ + TS
DEPTH = 4
PAST_LEN = 16384
RMS_EPS = 1e-6
D_FF = 4096

CFG = {"mixers": (0, 1, 2, 3), "mlp": True, "nlayers": 4}


class Op:
    __slots__ = ("eng", "fn", "deps", "is_dma", "pos", "inc", "semi", "semv", "waits")

    def __init__(self, eng, fn, deps, is_dma):
        self.eng = eng
        self.fn = fn
        self.deps = deps
        self.is_dma = is_dma
        self.inc = False
        self.semi = -1
        self.semv = 0
        self.waits = None


import types as _types


def _freeze(fn):
    if fn.__closure__ is None:
        return fn
    cells = []
    for c in fn.__closure__:
        try:
            cells.append(_types.CellType(c.cell_contents))
        except ValueError:
            cells.append(c)
    g = _types.FunctionType(fn.__code__, fn.__globals__, fn.__name__, fn.__defaults__, tuple(cells))
    g.__kwdefaults__ = fn.__kwdefaults__
    return g


import threading as _threading


class Coop:
    def __init__(self):
        self.yield_fn = None

    def run(self, fns):
        n = len(fns)
        sems = [_threading.Semaphore(0) for _ in range(n)]
        main = _threading.Semaphore(0)
        done = [False] * n
        exc = []
        state = {"cur": 0}

        def nxt(me):
            for d in range(1, n + 1):
                j = (me + d) % n
                if not done[j]:
                    return j
            return None

        def switch():
            me = state["cur"]
            j = nxt(me)
            if j is None or j == me:
                return
            state["cur"] = j
            sems[j].release()
            sems[me].acquire()

        def worker(i):
            sems[i].acquire()
            try:
                fns[i]()
            except BaseException as e:
                exc.append(e)
            finally:
                done[i] = True
                j = nxt(i)
                if j is None:
                    main.release()
                else:
                    state["cur"] = j
                    sems[j].release()

        ths = [_threading.Thread(target=worker, args=(i,)) for i in range(n)]
        for t in ths:
            t.start()
        self.yield_fn = switch
        sems[0].release()
        main.acquire()
        self.yield_fn = None
        for t in ths:
            t.join()
        if exc:
            raise exc[0]


COOP = Coop()


class Prog:
    def __init__(self, nc):
        self.nc = nc
        self.ops = []
        self.lastw = {}
        self.readers = {}

    def op(self, eng, fn, r=(), w=(), dma=False):
        deps = set()
        lastw = self.lastw
        readers = self.readers
        for k in r:
            lw = lastw.get(k)
            if lw is not None:
                deps.add(lw)
            if type(k) is tuple and k[0] == "ps":
                rd = readers.get(k)
                if rd:
                    for j_ in rd:
                        if self.ops[j_].eng != eng:
                            deps.add(j_)
        for k in w:
            lw = lastw.get(k)
            if lw is not None:
                deps.add(lw)
            rd = readers.get(k)
            if rd:
                deps.update(rd)
        idx = len(self.ops)
        self.ops.append(Op(eng, _freeze(fn), deps, dma))
        for k in w:
            lastw[k] = idx
            readers[k] = []
        for k in r:
            if k in w:
                continue
            readers.setdefault(k, []).append(idx)
        if COOP.yield_fn is not None:
            COOP.yield_fn()
        return idx

    def plan(self):
        ops = self.ops
        streams = {e: [] for e in ENGS}
        for i, o in enumerate(ops):
            o.pos = len(streams[o.eng])
            streams[o.eng].append(i)
        clock = {e: {f: -1 for f in ENGS} for e in ENGS}
        dma_seen = {e: set() for e in ENGS}
        vcs = [None] * len(ops)
        dma_count = {"sp": 0, "pool": 0, "act": 0}
        dma_base = {"sp": (0, 28), "pool": (28, 12), "act": (0, 28)}
        sem_last = [None] * NDMA_SEMS
        sem_val = [0] * NDMA_SEMS
        for i, o in enumerate(ops):
            E = o.eng
            ck = clock[E]
            waits = []
            deps = o.deps
            if o.is_dma:
                base_, n_ = dma_base[E]
                s = base_ + dma_count[E] % n_
                dma_count[E] += 1
                if sem_last[s] is not None:
                    deps = set(deps)
                    deps.add(sem_last[s])
                sem_last[s] = i
                sem_val[s] += 16
                o.semi = s
                o.semv = sem_val[s]
            need = {}
            for j in deps:
                oj = ops[j]
                if oj.is_dma:
                    if j not in dma_seen[E]:
                        waits.append(("dma", j))
                        dma_seen[E].add(j)
                        vj = vcs[j]
                        for f in ENGS:
                            if vj[f] > ck[f]:
                                ck[f] = vj[f]
                else:
                    F = oj.eng
                    if F == E and E in ("pe", "sp"):
                        continue
                    if oj.pos > ck[F] and oj.pos > need.get(F, -1):
                        need[F] = oj.pos
            for F, p in need.items():
                if p > ck[F]:
                    j = streams[F][p]
                    ops[j].inc = True
                    waits.append(("eng", F, j))
                    vj = vcs[j]
                    for f in ENGS:
                        if vj[f] > ck[f]:
                            ck[f] = vj[f]
                    if p > ck[F]:
                        ck[F] = p
            o.waits = waits
            o.deps = None
            if o.is_dma:
                vcs[i] = dict(ck)
            else:
                v = dict(ck)
                v[E] = o.pos
                vcs[i] = v
                if E == "pe":
                    ck[E] = o.pos
        cnt = {e: 0 for e in ENGS}
        for e in ENGS:
            for i in streams[e]:
                o = ops[i]
                if (not o.is_dma) and o.inc:
                    cnt[e] += 1
                    o.semv = cnt[e]
        self.streams = streams
        self.counts = cnt
        return streams

    def emit(self):
        nc = self.nc
        ops = self.ops
        streams = self.plan()
        with contextlib.ExitStack() as es:
            esem = {e: es.enter_context(nc.semaphore("s_" + e)) for e in ENGS}
            dsem = [es.enter_context(nc.semaphore("d%d" % k)) for k in range(NDMA_SEMS)]
            block = es.enter_context(nc.Block())

            def run(e, eng):
                for i in streams[e]:
                    o = ops[i]
                    for wt in o.waits:
                        if wt[0] == "dma":
                            oj = ops[wt[1]]
                            eng.wait_ge(dsem[oj.semi], oj.semv)
                        else:
                            oj = ops[wt[2]]
                            eng.wait_ge(esem[oj.eng], oj.semv)
                    ins = o.fn(eng)
                    if o.is_dma:
                        ins.then_inc(dsem[o.semi], 16)
                    elif o.inc:
                        ins.then_inc(esem[e], 1)
                    o.fn = None

            @block.tensor
            def _(eng):
                run("pe", eng)

            @block.scalar
            def _(eng):
                run("act", eng)

            @block.vector
            def _(eng):
                run("dve", eng)

            @block.gpsimd
            def _(eng):
                run("pool", eng)

            @block.sync
            def _(eng):
                run("sp", eng)
                vals = {}
                for o in ops:
                    if o.is_dma:
                        vals[o.semi] = max(vals.get(o.semi, 0), o.semv)
                for s, v in sorted(vals.items()):
                    eng.wait_ge(dsem[s], v)
                for e in ("pe", "act", "dve", "pool"):
                    if self.counts[e] > 0:
                        eng.wait_ge(esem[e], self.counts[e])


def _mask_consts(T, L):
    idx = np.arange(T)
    same = (idx[:, None] // L) == (idx[None, :] // L)
    tri = (same & (idx[:, None] <= idx[None, :])).astype(np.float32)
    seq = same.astype(np.float32)
    neg = np.where(same & (idx[None, :] >= idx[:, None]), 0.0, -30000.0).astype(np.float32)
    negt = np.where(same & (idx[None, :] <= idx[:, None]), 0.0, -30000.0).astype(np.float32)
    strict = (same & (idx[None, :] < idx[:, None])).astype(np.float32)
    return tri, seq, neg, negt, strict


def build_consts():
    items = []
    items.append(("ident", np.eye(128, dtype=np.float32)))
    items.append(("ones", np.ones((128, 128), np.float32)))
    for nm, a in zip(("tri_p", "seq_p", "neg_p", "negt_p", "strict_p"), _mask_consts(128, 128)):
        if nm != "seq_p":
            items.append((nm, a))
    for nm, a in zip(("tri_s", "seq_s", "neg_s", "negt_s", "strict_s"), _mask_consts(TS, 4)):
        items.append((nm, a))
    rowmask = (np.arange(TS)[:, None] // 4 == np.arange(16)[None, :]).astype(np.float32)
    items.append(("rowmask_s", rowmask))
    colmask = np.ascontiguousarray(np.broadcast_to(rowmask.T[None, :, :], (128, 16, TS)).reshape(128, 16 * TS))
    ii = np.arange(128)
    for lv in range(7):
        bsz = 1 << lv
        same = (ii[:, None] // (2 * bsz)) == (ii[None, :] // (2 * bsz))
        mT = same & ((ii[None, :] % (2 * bsz)) >= bsz) & ((ii[:, None] % (2 * bsz)) < bsz)
        items.append(("bmT%d" % lv, mT.astype(np.float32)))
        if lv == 0:
            items.append(("bm0", np.ascontiguousarray(mT.T).astype(np.float32)))
    lg = np.log1p(-np.exp2(-5.0 - np.arange(4, dtype=np.float32))).astype(np.float32)
    items.append(("laret", np.broadcast_to(lg[None, :], (128, 4)).copy()))
    offs = {}
    tot = 0
    for nm, a in items:
        offs[nm] = (tot, a.shape[0], a.shape[1])
        tot += a.shape[1]
    pack = np.zeros((128, tot), np.float32)
    for nm, a in items:
        o, r, c = offs[nm]
        pack[:r, o:o + c] = a
    half = 128
    inv = (np.float32(10000.0) ** (-np.arange(half, dtype=np.float32) / np.float32(half))).astype(np.float32)
    pos = np.zeros((NT, 128), np.float32)
    for i in range(NPT):
        pos[i] = np.arange(128, dtype=np.float32) + 128 * i
    pos[NPT, :TS] = (np.arange(TS) % 4).astype(np.float32) + np.float32(PAST_LEN)
    ang = (pos[:, :, None] * inv[None, None, :]).astype(np.float32)
    cs = np.stack([np.cos(ang.astype(np.float64)), np.sin(ang.astype(np.float64))], axis=2).astype(np.float32)
    return pack, offs, np.ascontiguousarray(cs), colmask


class TileInfo:
    def __init__(self, i):
        self.i = i
        self.sample = i == NPT
        self.T = TS if self.sample else 128
        self.c0 = i * 128
        self.kind = "s" if self.sample else "p"
        self.nseq = 16 if self.sample else 1


TILES = [TileInfo(i) for i in range(NT)]


NFA = 12
NBA = 16
NHA = 10


class Buf:
    __slots__ = ("pool", "i", "t", "key")

    def __init__(self, pool, i):
        self.pool = pool
        self.i = i
        self.t = pool.ts[i]
        self.key = (pool.name, i)

    def free(self):
        self.pool.free.append(self.i)


class BufPool:
    def __init__(self, name, ts):
        self.name = name
        self.ts = ts
        self.free = list(range(len(ts)))

    def get(self):
        assert self.free, "pool %s exhausted" % self.name
        return Buf(self, self.free.pop(0))


class K:
    def __init__(self, nc, offs):
        self.nc = nc
        self.P = Prog(nc)
        self.offs = offs
        self.uid = 0
        nc_ = nc
        dt = nc_.dram_tensor
        class _Lazy(dict):
            def __missing__(d, nm):
                v = dt(nm, list(IN_SHAPES[nm]), F32, kind="ExternalInput").ap()
                d[nm] = v
                return v
        self.din = _Lazy()
        self.dout = {}
        for nm, shp in OUT_SHAPES.items():
            self.dout[nm] = dt(nm, list(shp), F32, kind="ExternalOutput").ap()
        a = nc_.alloc_sbuf_tensor
        self.X = a("X", [128, NT, D], F32)
        self.XNT = a("XNT", [128, 8, NTOK], BF16)
        self.NRING = 4
        self.ring = [a("wr%d" % k, [128, 4096], BF16) for k in range(self.NRING)]
        self.ring_i = 0
        self.CONST = a("CONST", [128, offs["_tot"]], F32)
        self.identb = a("identb", [128, 128], BF16)
        self.colmaskb = a("colmaskb", [128, 16, TS], BF16)
        self.NW = a("NW", [128, 9, 8], F32)
        self.st = [a("st%d" % k, [128, 8], F32) for k in range(4)]
        self.PAR = a("PAR", [128, D], F32)
        self.SF = a("SF", [128, 2, 512], F32)
        self.SB = a("SB", [128, 2, 512], BF16)
        self.SM = [a("SM%d" % k, [128, 160], F32) for k in range(4)]
        self.PRM = a("PRM", [128, 128], F32)
        self.CW = a("CW", [128, 32, 4], F32)
        self.CB = a("CB", [128, 32], F32)
        self.UB = [a("UB%d" % k, [128, 528], F32) for k in range(2)]
        self.LAB = [a("LAB%d" % k, [128, 40], F32) for k in range(4)]
        self.fpool = BufPool("fa", [a("fa%d" % k, [128, 512], F32) for k in range(NFA)])
        self.bpool = BufPool("ba", [a("ba%d" % k, [128, 512], BF16) for k in range(NBA)])
        self.hpool = BufPool("ha", [a("ha%d" % k, [128, 256], BF16) for k in range(NHA)])
        self.ps = [nc_.alloc_psum_tensor("ps%d" % k, [128, 512], F32) for k in range(8)]
        self.ps_free = list(range(8))
        self.cnt = {}
        print("sbuf bytes remaining", nc_.sbuf_bytes_remaining, flush=True)

    def c(self, name):
        o, r, cc = self.offs[name]
        return self.CONST[0:r, o:o + cc]

    def nid(self, pfx):
        self.uid += 1
        return "%s#%d" % (pfx, self.uid)

    def rr(self, key, n=2):
        v = self.cnt.get(key, 0)
        self.cnt[key] = v + 1
        return v % n

    def psum(self):
        b = self.ps_free.pop(0)
        return b

    def pfree(self, b):
        self.ps_free.append(b)

    def ringslot(self):
        s = self.ring_i % self.NRING
        self.ring_i += 1
        return s

    def op(self, *a, **k):
        return self.P.op(*a, **k)

    def dma(self, out, in_, r=(), w=(), eng="sp", slow=False):
        if slow:
            fn = lambda e: e.dma_start(out=out, in_=in_, allow_slow_non_contiguous=True)
        else:
            fn = lambda e: e.dma_start(out=out, in_=in_)
        return self.P.op(eng, fn, r=r, w=w, dma=True)

    def load_w(self, src_pieces, r_extra=()):
        s = self.ringslot()
        t = self.ring[s]
        for dstf, src in src_pieces:
            self.dma(dstf(t), src, w=[("ring", s)], eng="pool")
        return s

    def mm(self, out, lhsT, rhs, start, stop, r, w):
        self.P.op("pe", lambda e: e.matmul(out, lhsT=lhsT, rhs=rhs, start=start, stop=stop), r=r, w=w)

    def tr(self, out, in_, ident, r, w):
        self.P.op("pe", lambda e: e.transpose(out=out, in_=in_, identity=ident), r=r, w=w)


IN_SHAPES = {}
OUT_SHAPES = {}


def _set_shapes(ncst):
    IN_SHAPES.clear()
    IN_SHAPES.update({
        "x_prompt": (SEQ, D), "x_sample": (TS, D),
        "state_ret": (2, 16, 4, 256, 512), "state_ssd": (16, 32, 64, 128), "state_ssd_conv": (16, 3, 4096),
        "state_gdn": (16, 16, 128, 128), "state_gdn_conv": (16, 3, 4096),
        "norms": (9, D),
        "ret_w_in": (2, D, 6144), "ret_w_out": (2, 2048, D),
        "ssd_w_in": (D, 6176), "ssd_conv_w": (4, 4096), "ssd_conv_b": (1, 4096), "ssd_dt_bias": (1, 32),
        "ssd_a_log": (1, 32), "ssd_d": (1, 32), "ssd_norm": (1, 2048), "ssd_w_out": (2048, D),
        "gdn_w_in": (D, 6176), "gdn_conv_w": (4, 4096), "gdn_dt_bias": (1, 16), "gdn_a_log": (1, 16),
        "gdn_norm": (1, 128), "gdn_w_out": (2048, D),
        "mlp_w_up": (4, D, D_FF), "mlp_w_down": (4, D_FF, D),
        "cpack": (128, ncst), "ropecs": (NT, 128, 2, 128), "colmask": (128, 16 * TS),
    })
    OUT_SHAPES.clear()
    OUT_SHAPES.update({
        "y_prompt": (SEQ, D), "y_sample": (TS, D),
        "p_ret": (2, 4, 256, 512), "p_ssd": (32, 64, 128), "p_ssd_conv": (3, 4096),
        "p_gdn": (16, 128, 128), "p_gdn_conv": (3, 4096),
        "s_ret": (2, 16, 4, 256, 512), "s_ssd": (16, 32, 64, 128), "s_ssd_conv": (16, 3, 4096),
        "s_gdn": (16, 16, 128, 128), "s_gdn_conv": (16, 3, 4096),
    })


def emit_setup(k):
    k.dma(k.CONST[:, :], k.din["cpack"][:, :], w=["CONST"])
    k.op("act", lambda e: e.activation(out=k.identb[:, :], in_=k.c("ident"), func=AF.Copy), r=["CONST"], w=["identb"])
    for q in range(2):
        cb = k.fpool.get()
        k.dma(cb.t[:, :], k.din["colmask"][:, q * 512:(q + 1) * 512], w=[cb.key])
        k.op("dve", lambda e, cb=cb, q=q: e.tensor_copy(out=k.colmaskb[0:128, :, :].rearrange("p a b -> p (a b)")[:, q * 512:(q + 1) * 512], in_=cb.t[:, :]),
             r=[cb.key], w=["colmaskb"])
        cb.free()
    k.dma(k.PAR[0:9, :], k.din["norms"][:, :], w=["PAR"])
    b = k.psum()
    pv = k.ps[b][:, 0:72].rearrange("p (c l) -> p c l", c=8)
    for ch in range(8):
        k.tr(pv[:, ch, :], k.PAR[0:9, ch * 128:(ch + 1) * 128], k.c("ident")[0:9, 0:9], r=["PAR", "CONST"], w=[("ps", b)])
    k.op("dve", lambda e: e.tensor_copy(out=k.NW[:, :, :].rearrange("p l c -> p c l"), in_=pv), r=[("ps", b)], w=["NW"])
    k.pfree(b)
    for t in TILES:
        src = k.din["x_sample"][:, :] if t.sample else k.din["x_prompt"][t.c0:t.c0 + 128, :]
        k.dma(k.X[0:t.T, t.i, :], src, w=[("X", t.i)])


def emit_rstd(k, t, stt, src_ap, src_res, n, col):
    T = t.T
    jb = [k.bpool.get() for _ in range((n + 511) // 512)]
    for q, jbq in enumerate(jb):
        n0, n1 = q * 512, min(n, q * 512 + 512)
        k.op("act", lambda e, jbq=jbq, n0=n0, n1=n1, q=q: e.activation(out=jbq.t[0:T, 0:n1 - n0], in_=src_ap[:, n0:n1], func=AF.Square,
                                                                      accum_out=stt[0:T, col + 2 + q:col + 3 + q]),
             r=list(src_res), w=[jbq.key, ("st", id(stt))])
        jbq.free()
    if len(jb) == 2:
        k.op("dve", lambda e: e.tensor_tensor(out=stt[0:T, col:col + 1], in0=stt[0:T, col + 2:col + 3], in1=stt[0:T, col + 3:col + 4], op=ALU.add),
             r=[("st", id(stt))], w=[("st", id(stt))])
    else:
        k.op("dve", lambda e: e.tensor_copy(out=stt[0:T, col:col + 1], in_=stt[0:T, col + 2:col + 3]),
             r=[("st", id(stt))], w=[("st", id(stt))])
    k.op("dve", lambda e: e.tensor_scalar(out=stt[0:T, col + 1:col + 2], in0=stt[0:T, col:col + 1], scalar1=1.0 / n, scalar2=RMS_EPS,
                                          op0=ALU.mult, op1=ALU.add), r=[("st", id(stt))], w=[("st", id(stt))])
    k.op("act", lambda e: e.activation(out=stt[0:T, col + 1:col + 2], in_=stt[0:T, col + 1:col + 2], func=AF.Ln),
         r=[("st", id(stt))], w=[("st", id(stt))])
    k.op("act", lambda e: e.activation(out=stt[0:T, col + 1:col + 2], in_=stt[0:T, col + 1:col + 2], func=AF.Exp, scale=-0.5),
         r=[("st", id(stt))], w=[("st", id(stt))])


def emit_norm_xnt(k, widx):
    for t in TILES[:CFG.get("norm_tiles", NT)]:
        T = t.T
        stt = k.st[k.rr("st", 4)]
        xbs = [k.bpool.get() for _ in range(2)]
        emit_rstd(k, t, stt, k.X[0:T, t.i, :], [("X", t.i)], D, 0)
        for q in range(2):
            k.op("dve", lambda e, T=T, q=q, stt=stt, t=t: e.tensor_scalar_mul(out=xbs[q].t[0:T, :], in0=k.X[0:T, t.i, q * 512:(q + 1) * 512], scalar1=stt[0:T, 1:2]),
                 r=[("X", t.i), ("st", id(stt))], w=[xbs[q].key])
        b = k.psum()
        pv = k.ps[b][:, :].bitcast(BF16).rearrange("p (c m) -> p c m", c=8)
        for ch in range(8):
            k.tr(pv[:, ch, 0:T], xbs[ch // 4].t[0:T, (ch % 4) * 128:(ch % 4 + 1) * 128], k.identb[0:T, 0:T], r=[xbs[ch // 4].key, "identb"], w=[("ps", b)])
        for xb_ in xbs:
            xb_.free()
        nwb = k.NW[:, widx, :].unsqueeze(2).to_broadcast([128, 8, T])
        k.op("dve", lambda e, T=T, t=t, pv=pv, nwb=nwb: e.tensor_tensor(out=k.XNT[:, :, t.c0:t.c0 + T], in0=pv[:, :, 0:T], in1=nwb, op=ALU.mult),
             r=[("ps", b), "NW"], w=[("XNT", t.i)])
        k.pfree(b)


def emit_mlp(k, layer):
    wu = k.din["mlp_w_up"]
    wd = k.din["mlp_w_down"]
    blocks = [(0, 512, [0, 1, 2, 3]), (512, 512, [4, 5, 6, 7]), (1024, 512, [8, 9, 10, 11]), (1536, 512, [12, 13, 14, 15]),
              (2048, TS, [16])]
    for g in range(8):
        su = k.load_w([(lambda t: t[:, :].rearrange("p (k n) -> p k n", k=8),
                        wu[layer, :, g * 512:(g + 1) * 512].rearrange("(k p) n -> p k n", p=128))])
        sd = k.load_w([(lambda t: t[:, :].rearrange("p (c n) -> p c n", c=4),
                        wd[layer, g * 512:(g + 1) * 512, :].rearrange("(c p) n -> p c n", p=128))])
        Wu = k.ring[su][:, :].rearrange("p (k n) -> p k n", k=8)
        Wd = k.ring[sd][:, :].rearrange("p (c n) -> p c n", c=4)
        for (c0, ncol, tl) in blocks:
            hTb = [k.bpool.get() for _ in range(4)]
            for c in range(4):
                b = k.psum()
                for kk in range(8):
                    k.mm(k.ps[b][:, 0:ncol], Wu[:, kk, c * 128:(c + 1) * 128], k.XNT[:, kk, c0:c0 + ncol], kk == 0, kk == 7,
                         r=[("ring", su)] + [("XNT", ti) for ti in tl], w=[("ps", b)])
                hr = k.bpool.get()
                k.op("act", lambda e, b=b, hr=hr, ncol=ncol: e.activation(out=hr.t[:, 0:ncol], in_=k.ps[b][:, 0:ncol], func=AF.Relu),
                     r=[("ps", b)], w=[hr.key])
                k.op("pool", lambda e, hr=hr, hTc=hTb[c], ncol=ncol: e.tensor_tensor(out=hTc.t[:, 0:ncol], in0=hr.t[:, 0:ncol], in1=hr.t[:, 0:ncol], op=ALU.mult),
                     r=[hr.key], w=[hTb[c].key])
                hr.free()
                k.pfree(b)
            for j, ti in enumerate(tl):
                t = TILES[ti]
                T = t.T
                for half in range(2):
                    b = k.psum()
                    for c in range(4):
                        k.mm(k.ps[b][0:T, :], hTb[c].t[:, j * 128:j * 128 + T], Wd[:, c, half * 512:(half + 1) * 512], c == 0, c == 3,
                             r=[hTb[c].key, ("ring", sd)], w=[("ps", b)])
                    k.op("dve", lambda e, b=b, T=T, ti=ti, half=half: e.tensor_tensor(
                        out=k.X[0:T, ti, half * 512:(half + 1) * 512], in0=k.X[0:T, ti, half * 512:(half + 1) * 512],
                        in1=k.ps[b][0:T, :], op=ALU.add), r=[("ps", b), ("X", ti)], w=[("X", ti)])
                    k.pfree(b)
            for hb in hTb:
                hb.free()


def emit_final(k):
    k.dma(k.PAR[:, :], k.din["norms"][8, :].partition_broadcast(128), w=["PAR"])
    for t in TILES:
        T = t.T
        stt = k.st[k.rr("st", 4)]
        emit_rstd(k, t, stt, k.X[0:T, t.i, :], [("X", t.i)], D, 0)
        k.op("dve", lambda e, T=T, t=t, stt=stt: e.scalar_tensor_tensor(out=k.X[0:T, t.i, :], in0=k.X[0:T, t.i, :], scalar=stt[0:T, 1:2],
                                                                         in1=k.PAR[0:T, :], op0=ALU.mult, op1=ALU.mult),
             r=[("X", t.i), ("st", id(stt)), "PAR"], w=[("X", t.i)])
        dst = k.dout["y_sample"][:, :] if t.sample else k.dout["y_prompt"][t.c0:t.c0 + 128, :]
        k.dma(dst, k.X[0:T, t.i, :], r=[("X", t.i)])


def build_program(nc, offs):
    k = K(nc, offs)
    emit_setup(k)
    for layer in range(CFG["nlayers"]):
        emit_norm_xnt(k, layer)
        if layer in CFG["mixers"]:
            kind = layer % 3
            if kind == 0:
                from_ret(k, layer)
            elif kind == 1:
                from_ssd(k, layer)
            else:
                from_gdn(k, layer)
        if CFG["mlp"]:
            emit_norm_xnt(k, 4 + layer)
            emit_mlp(k, layer)
    emit_final(k)
    zero_unwritten_outputs(k)
    k.P.emit()
    return k


def kconst(k, kind, nm):
    return k.c("%s_%s" % (nm, kind)) if not (nm == "seq" and kind == "p") else k.c("ones")


def decay_prep(k, kind, la, la_res, R):
    T = 128 if kind == "p" else TS
    nseq = 1 if kind == "p" else 16
    smi = k.rr("SM", 4)
    sm = k.SM[smi]
    smk = ("SM", smi)
    tri = kconst(k, kind, "tri")
    seq = kconst(k, kind, "seq")
    neg = kconst(k, kind, "neg")
    ones = k.c("ones")
    b = k.psum()
    k.mm(k.ps[b][0:T, 0:R], tri, la, True, True, r=["CONST"] + la_res, w=[("ps", b)])
    k.mm(k.ps[b][0:T, R:2 * R], seq, la, True, True, r=["CONST"] + la_res, w=[("ps", b)])
    k.op("act", lambda e: e.activation(out=sm[0:T, 0:2 * R], in_=k.ps[b][0:T, 0:2 * R], func=AF.Copy), r=[("ps", b)], w=[smk])
    k.pfree(b)
    bm = k.fpool.get()
    bmv = bm.t[0:T, 0:R * T].rearrange("p (r t) -> p r t", r=R)
    k.op("pool", lambda e: e.tensor_tensor(out=bmv, in0=tri.unsqueeze(1).to_broadcast([T, R, T]), in1=la.unsqueeze(2).to_broadcast([T, R, T]),
                                           op=ALU.mult), r=["CONST"] + la_res, w=[bm.key])
    b2 = k.psum()
    k.mm(k.ps[b2][0:T, 0:R * T], ones[0:T, 0:T], bm.t[0:T, 0:R * T], True, True, r=["CONST", bm.key], w=[("ps", b2)])
    bm.free()
    dec = k.fpool.get()
    decv = dec.t[0:T, 0:R * T].rearrange("p (r t) -> p r t", r=R)
    crv = k.ps[b2][0:T, 0:R * T].rearrange("p (r t) -> p r t", r=R)
    for r_ in range(R):
        k.op("dve", lambda e, r_=r_: e.scalar_tensor_tensor(out=decv[:, r_, :], in0=crv[:, r_, :], scalar=sm[0:T, r_:r_ + 1], in1=neg,
                                                              op0=ALU.subtract, op1=ALU.add), r=[("ps", b2), smk, "CONST"], w=[dec.key])
    k.pfree(b2)
    k.op("act", lambda e: e.activation(out=dec.t[0:T, 0:R * T], in_=dec.t[0:T, 0:R * T], func=AF.Exp), r=[dec.key], w=[dec.key])
    k.op("act", lambda e: e.activation(out=sm[0:T, 2 * R:3 * R], in_=sm[0:T, 0:R], func=AF.Exp), r=[smk], w=[smk])
    k.op("dve", lambda e: e.tensor_tensor(out=sm[0:T, 3 * R:4 * R], in0=sm[0:T, R:2 * R], in1=sm[0:T, 0:R], op=ALU.subtract), r=[smk], w=[smk])
    k.op("act", lambda e: e.activation(out=sm[0:T, 3 * R:4 * R], in_=sm[0:T, 3 * R:4 * R], func=AF.Exp), r=[smk], w=[smk])
    if nseq == 1:
        k.op("act", lambda e: e.activation(out=sm[0:T, 4 * R:5 * R], in_=sm[0:T, R:2 * R], func=AF.Exp), r=[smk], w=[smk])
    else:
        lam = sm[0:T, 96:96 + 16 * R].rearrange("p (b r) -> p b r", b=16)
        k.op("pool", lambda e: e.tensor_tensor(out=lam, in0=la.unsqueeze(1).to_broadcast([T, 16, R]),
                                               in1=k.c("rowmask_s").unsqueeze(2).to_broadcast([T, 16, R]), op=ALU.mult),
             r=["CONST", smk] + la_res, w=[smk])
        b3 = k.psum()
        k.mm(k.ps[b3][0:128, 0:16 * R], ones[0:T, 0:128], sm[0:T, 96:96 + 16 * R], True, True, r=["CONST", smk], w=[("ps", b3)])
        k.op("act", lambda e: e.activation(out=sm[0:128, 16:16 + 16 * R], in_=k.ps[b3][0:128, 0:16 * R], func=AF.Exp), r=[("ps", b3)], w=[smk])
        k.pfree(b3)
    return {"sm": sm, "smk": smk, "dec": dec, "T": T, "R": R, "kind": kind}


def scan_tile(k, t, prep, qT, qT_key, kT, kT_key, ktok, ktok_key, v, v_key, R, Pd, NC, first, sload, sstore, pstore):
    T = t.T
    W = R * Pd
    sm, smk, dec = prep["sm"], prep["smk"], prep["dec"]
    nseq = t.nseq
    bs = k.psum()
    for nc_ in range(NC):
        k.mm(k.ps[bs][0:T, 0:T], kT[:, nc_, 0:T], qT[:, nc_, 0:T], nc_ == 0, nc_ == NC - 1, r=[kT_key, qT_key], w=[("ps", bs)])
    at = k.bpool.get()
    k.op("dve", lambda e: e.tensor_tensor(out=at.t[0:T, 0:R * T].rearrange("p (r t) -> p r t", r=R),
                                          in0=dec.t[0:T, 0:R * T].rearrange("p (r t) -> p r t", r=R),
                                          in1=k.ps[bs][0:T, 0:T].unsqueeze(1).to_broadcast([T, R, T]), op=ALU.mult),
         r=[dec.key, ("ps", bs)], w=[at.key])
    k.pfree(bs)
    by1 = k.psum()
    for r_ in range(R):
        k.mm(k.ps[by1][0:T, r_ * Pd:(r_ + 1) * Pd], at.t[0:T, r_ * T:(r_ + 1) * T], v[0:T, r_ * Pd:(r_ + 1) * Pd], True, True,
             r=[at.key, v_key], w=[("ps", by1)])
    at.free()
    vw = k.bpool.get()
    k.op("pool", lambda e: e.tensor_tensor(out=vw.t[0:T, 0:W].rearrange("p (r d) -> p r d", r=R), in0=v[0:T, 0:W].rearrange("p (r d) -> p r d", r=R),
                                           in1=sm[0:T, 3 * R:4 * R].unsqueeze(2).to_broadcast([T, R, Pd]), op=ALU.mult),
         r=[v_key, smk], w=[vw.key])
    y = k.fpool.get()
    if nseq == 1:
        if first:
            k.op("act", lambda e: e.activation(out=y.t[0:T, 0:W], in_=k.ps[by1][0:T, 0:W], func=AF.Copy), r=[("ps", by1)], w=[y.key])
            k.pfree(by1)
        else:
            by2 = k.psum()
            for nc_ in range(NC):
                k.mm(k.ps[by2][0:T, 0:W], qT[:, nc_, 0:T], k.SB[:, nc_, 0:W], nc_ == 0, nc_ == NC - 1, r=[qT_key, ("SB", nc_)], w=[("ps", by2)])
            tmp = k.fpool.get()
            k.op("dve", lambda e: e.tensor_tensor(out=tmp.t[0:T, 0:W].rearrange("p (r d) -> p r d", r=R),
                                                  in0=k.ps[by2][0:T, 0:W].rearrange("p (r d) -> p r d", r=R),
                                                  in1=sm[0:T, 2 * R:3 * R].unsqueeze(2).to_broadcast([T, R, Pd]), op=ALU.mult),
                 r=[("ps", by2), smk], w=[tmp.key])
            k.pfree(by2)
            k.op("dve", lambda e: e.tensor_tensor(out=y.t[0:T, 0:W], in0=tmp.t[0:T, 0:W], in1=k.ps[by1][0:T, 0:W], op=ALU.add),
                 r=[tmp.key, ("ps", by1)], w=[y.key])
            tmp.free()
            k.pfree(by1)
        for nc_ in range(NC):
            bd = k.psum()
            k.mm(k.ps[bd][0:128, 0:W], ktok[0:T, nc_ * 128:(nc_ + 1) * 128], vw.t[0:T, 0:W], True, True, r=[ktok_key, vw.key], w=[("ps", bd)])
            if first:
                k.op("act", lambda e, bd=bd, nc_=nc_: e.activation(out=k.SF[:, nc_, 0:W], in_=k.ps[bd][0:128, 0:W], func=AF.Copy),
                     r=[("ps", bd)], w=[("SF", nc_)])
            elif R == 1:
                k.op("dve", lambda e, bd=bd, nc_=nc_: e.scalar_tensor_tensor(out=k.SF[:, nc_, 0:W], in0=k.SF[:, nc_, 0:W], scalar=sm[0:128, 4:5],
                                                                              in1=k.ps[bd][0:128, 0:W], op0=ALU.mult, op1=ALU.add),
                     r=[("SF", nc_), smk, ("ps", bd)], w=[("SF", nc_)])
            else:
                k.op("pool", lambda e, nc_=nc_: e.tensor_tensor(out=k.SF[:, nc_, 0:W].rearrange("p (r d) -> p r d", r=R),
                                                                 in0=k.SF[:, nc_, 0:W].rearrange("p (r d) -> p r d", r=R),
                                                                 in1=sm[0:128, 4 * R:5 * R].unsqueeze(2).to_broadcast([128, R, Pd]), op=ALU.mult),
                     r=[("SF", nc_), smk], w=[("SF", nc_)])
            if (not first) and R != 1:
                k.op("dve", lambda e, bd=bd, nc_=nc_: e.tensor_tensor(out=k.SF[:, nc_, 0:W], in0=k.SF[:, nc_, 0:W], in1=k.ps[bd][0:128, 0:W], op=ALU.add),
                     r=[("SF", nc_), ("ps", bd)], w=[("SF", nc_)])
            k.pfree(bd)
            k.op("act", lambda e, nc_=nc_: e.activation(out=k.SB[:, nc_, 0:W], in_=k.SF[:, nc_, 0:W], func=AF.Copy), r=[("SF", nc_)], w=[("SB", nc_)])
            if pstore is not None:
                pstore(nc_)
    else:
        by2 = k.psum()
        for b_ in range(nseq):
            sf = [k.fpool.get() for _ in range(NC)]
            sb = [k.bpool.get() for _ in range(NC)]
            for nc_ in range(NC):
                sload(b_, nc_, sf[nc_])
                k.op("act", lambda e, nc_=nc_, sf=sf, sb=sb: e.activation(out=sb[nc_].t[:, 0:W], in_=sf[nc_].t[:, 0:W], func=AF.Copy),
                     r=[sf[nc_].key], w=[sb[nc_].key])
            qm = k.bpool.get()
            qmv = qm.t[:, 0:NC * T].rearrange("p (c t) -> p c t", c=NC)
            k.op("pool", lambda e, b_=b_, qmv=qmv: e.tensor_tensor(out=qmv, in0=qT[:, :, 0:T], in1=k.colmaskb[:, b_, :].unsqueeze(1).to_broadcast([128, NC, T]),
                                                                   op=ALU.mult), r=[qT_key, "colmaskb"], w=[qm.key])
            for nc_ in range(NC):
                k.mm(k.ps[by2][0:T, 0:W], qmv[:, nc_, :], sb[nc_].t[:, 0:W], (b_ == 0 and nc_ == 0), (b_ == nseq - 1 and nc_ == NC - 1),
                     r=[qm.key, sb[nc_].key], w=[("ps", by2)])
            qm.free()
            km = k.bpool.get()
            k.op("pool", lambda e, b_=b_, km=km: e.tensor_scalar_mul(out=km.t[0:T, 0:NC * 128], in0=ktok[0:T, 0:NC * 128], scalar1=k.c("rowmask_s")[0:T, b_:b_ + 1]),
                 r=[ktok_key, "CONST"], w=[km.key])
            for nc_ in range(NC):
                bd = k.psum()
                k.mm(k.ps[bd][0:128, 0:W], km.t[0:T, nc_ * 128:(nc_ + 1) * 128], vw.t[0:T, 0:W], True, True, r=[km.key, vw.key], w=[("ps", bd)])
                if R == 1:
                    k.op("dve", lambda e, nc_=nc_, sf=sf, bd=bd, b_=b_: e.scalar_tensor_tensor(out=sf[nc_].t[:, 0:W], in0=sf[nc_].t[:, 0:W],
                                                                                               scalar=sm[0:128, 16 + b_:17 + b_], in1=k.ps[bd][0:128, 0:W],
                                                                                               op0=ALU.mult, op1=ALU.add),
                         r=[sf[nc_].key, smk, ("ps", bd)], w=[sf[nc_].key])
                else:
                    k.op("pool", lambda e, nc_=nc_, sf=sf, b_=b_: e.tensor_tensor(out=sf[nc_].t[:, 0:W].rearrange("p (r d) -> p r d", r=R),
                                                                                 in0=sf[nc_].t[:, 0:W].rearrange("p (r d) -> p r d", r=R),
                                                                                 in1=sm[0:128, 16 + b_ * R:16 + (b_ + 1) * R].unsqueeze(2).to_broadcast([128, R, Pd]),
                                                                                 op=ALU.mult), r=[sf[nc_].key, smk], w=[sf[nc_].key])
                    k.op("dve", lambda e, nc_=nc_, sf=sf, bd=bd: e.tensor_tensor(out=sf[nc_].t[:, 0:W], in0=sf[nc_].t[:, 0:W], in1=k.ps[bd][0:128, 0:W], op=ALU.add),
                         r=[sf[nc_].key, ("ps", bd)], w=[sf[nc_].key])
                k.pfree(bd)
                sstore(b_, nc_, sf[nc_])
            km.free()
            for x_ in sf + sb:
                x_.free()
        tmp = k.fpool.get()
        k.op("dve", lambda e: e.tensor_tensor(out=tmp.t[0:T, 0:W].rearrange("p (r d) -> p r d", r=R),
                                              in0=k.ps[by2][0:T, 0:W].rearrange("p (r d) -> p r d", r=R),
                                              in1=sm[0:T, 2 * R:3 * R].unsqueeze(2).to_broadcast([T, R, Pd]), op=ALU.mult),
             r=[("ps", by2), smk], w=[tmp.key])
        k.pfree(by2)
        k.op("dve", lambda e: e.tensor_tensor(out=y.t[0:T, 0:W], in0=tmp.t[0:T, 0:W], in1=k.ps[by1][0:T, 0:W], op=ALU.add),
             r=[tmp.key, ("ps", by1)], w=[y.key])
        tmp.free()
        k.pfree(by1)
    vw.free()
    return y


def proj_tok(k, t, slot, ncols, c_off=0):
    Wv = k.ring[slot][:, :].rearrange("p (k n) -> p k n", k=8)
    b = k.psum()
    for kk in range(8):
        k.mm(k.ps[b][0:t.T, 0:ncols], k.XNT[:, kk, t.c0:t.c0 + t.T], Wv[:, kk, c_off:c_off + ncols], kk == 0, kk == 7,
             r=[("XNT", t.i), ("ring", slot)], w=[("ps", b)])
    return b


def out_proj_add(k, t, ygT, nchunks, slot):
    T = t.T
    Wo = k.ring[slot][:, 0:nchunks * 1024].rearrange("p (c n) -> p c n", c=nchunks)
    yv = ygT.t[:, 0:nchunks * 128].rearrange("p (c t) -> p c t", c=nchunks)
    for half in range(2):
        b = k.psum()
        for c in range(nchunks):
            k.mm(k.ps[b][0:T, :], yv[:, c, 0:T], Wo[:, c, half * 512:(half + 1) * 512], c == 0, c == nchunks - 1,
                 r=[ygT.key, ("ring", slot)], w=[("ps", b)])
        k.op("dve", lambda e, b=b, half=half: e.tensor_tensor(out=k.X[0:T, t.i, half * 512:(half + 1) * 512], in0=k.X[0:T, t.i, half * 512:(half + 1) * 512],
                                                              in1=k.ps[b][0:T, :], op=ALU.add), r=[("ps", b), ("X", t.i)], w=[("X", t.i)])
        k.pfree(b)


def transpose_to(k, t, src, nchunks, eng="act"):
    T = t.T
    b = k.psum()
    pv = k.ps[b][:, :].bitcast(BF16).rearrange("p (c m) -> p c m", c=8)
    for c in range(nchunks):
        k.tr(pv[:, c, 0:T], src.t[0:T, c * 128:(c + 1) * 128], k.identb[0:T, 0:T], r=[src.key, "identb"], w=[("ps", b)])
    dst = k.bpool.get()
    dv = dst.t[:, 0:nchunks * 128].rearrange("p (c t) -> p c t", c=nchunks)
    if eng == "act":
        k.op("act", lambda e: e.activation(out=dv[:, :, 0:T], in_=pv[:, 0:nchunks, 0:T], func=AF.Copy), r=[("ps", b)], w=[dst.key])
    else:
        k.op("dve", lambda e: e.tensor_copy(out=dv[:, :, 0:T], in_=pv[:, 0:nchunks, 0:T]), r=[("ps", b)], w=[dst.key])
    k.pfree(b)
    return dst


def from_ret(k, layer):
    j = layer // 3
    win = k.din["ret_w_in"]
    wout = k.din["ret_w_out"]
    s_ret_in = k.din["state_ret"]
    for h in range(4):
        v8 = lambda tt: tt[:, :].rearrange("p (k n) -> p k n", k=8)
        s_qk = k.load_w([(lambda tt: v8(tt)[:, :, 0:256], win[j, :, h * 256:(h + 1) * 256].rearrange("(k p) n -> p k n", p=128)),
                         (lambda tt: v8(tt)[:, :, 256:512], win[j, :, 1024 + h * 256:1024 + (h + 1) * 256].rearrange("(k p) n -> p k n", p=128))])
        s_v = k.load_w([(v8, win[j, :, 2048 + h * 512:2048 + (h + 1) * 512].rearrange("(k p) n -> p k n", p=128))])
        s_g = k.load_w([(v8, win[j, :, 4096 + h * 512:4096 + (h + 1) * 512].rearrange("(k p) n -> p k n", p=128))])
        s_o = k.load_w([(lambda tt: tt[:, :].rearrange("p (c n) -> p c n", c=4), wout[j, h * 512:(h + 1) * 512, :].rearrange("(c p) n -> p c n", p=128))])
        preps = {}
        for kind in ("p", "s"):
            T_ = 128 if kind == "p" else TS
            preps[kind] = decay_prep(k, kind, k.c("laret")[0:T_, h:h + 1], ["CONST"], 1)

        def stage_a(t):
            T = t.T
            bqk = proj_tok(k, t, s_qk, 512)
            qk = k.fpool.get()
            k.op("act", lambda e: e.activation(out=qk.t[0:T, 0:256], in_=k.ps[bqk][0:T, 0:256], func=AF.Copy), r=[("ps", bqk)], w=[qk.key])
            k.op("act", lambda e: e.activation(out=qk.t[0:T, 256:512], in_=k.ps[bqk][0:T, 256:512], func=AF.Copy, scale=1.0 / 16.0),
                 r=[("ps", bqk)], w=[qk.key])
            k.pfree(bqk)
            cs = k.fpool.get()
            k.dma(cs.t[0:T, 0:256], k.din["ropecs"][t.i, 0:T, :, :].rearrange("p a b -> p (a b)"), w=[cs.key])
            qv = qk.t[0:T, :].rearrange("p (a h m) -> p a h m", a=2, h=2)
            x1, x2 = qv[:, :, 0, :], qv[:, :, 1, :]
            cosb = cs.t[0:T, 0:128].unsqueeze(1).to_broadcast([T, 2, 128])
            sinb = cs.t[0:T, 128:256].unsqueeze(1).to_broadcast([T, 2, 128])
            ta = k.fpool.get()
            tb = k.fpool.get()
            tav = ta.t[0:T, :].rearrange("p (u a m) -> p u a m", u=2, a=2)
            tbv = tb.t[0:T, :].rearrange("p (u a m) -> p u a m", u=2, a=2)
            k.op("pool", lambda e: e.tensor_tensor(out=tav[:, 0], in0=x1, in1=cosb, op=ALU.mult), r=[qk.key, cs.key], w=[ta.key])
            k.op("pool", lambda e: e.tensor_tensor(out=tav[:, 1], in0=x2, in1=sinb, op=ALU.mult), r=[qk.key, cs.key], w=[ta.key])
            k.op("pool", lambda e: e.tensor_tensor(out=tbv[:, 0], in0=x1, in1=sinb, op=ALU.mult), r=[qk.key, cs.key], w=[tb.key])
            k.op("pool", lambda e: e.tensor_tensor(out=tbv[:, 1], in0=x2, in1=cosb, op=ALU.mult), r=[qk.key, cs.key], w=[tb.key])
            rot = k.bpool.get()
            rv = rot.t[0:T, :].rearrange("p (a h m) -> p a h m", a=2, h=2)
            k.op("dve", lambda e: e.tensor_tensor(out=rv[:, :, 0, :], in0=tav[:, 0], in1=tav[:, 1], op=ALU.subtract), r=[ta.key], w=[rot.key])
            k.op("dve", lambda e: e.tensor_tensor(out=rv[:, :, 1, :], in0=tbv[:, 0], in1=tbv[:, 1], op=ALU.add), r=[tb.key], w=[rot.key])
            for x_ in (qk, cs, ta, tb):
                x_.free()
            qkT = transpose_to(k, t, rot, 4)
            bv = proj_tok(k, t, s_v, 512)
            vb = k.bpool.get()
            k.op("act", lambda e: e.activation(out=vb.t[0:T, :], in_=k.ps[bv][0:T, :], func=AF.Copy), r=[("ps", bv)], w=[vb.key])
            k.pfree(bv)
            bg = proj_tok(k, t, s_g, 512)
            sg = k.bpool.get()
            k.op("act", lambda e: e.activation(out=sg.t[0:T, :], in_=k.ps[bg][0:T, :], func=AF.Silu), r=[("ps", bg)], w=[sg.key])
            k.pfree(bg)
            return {"rot": rot, "qkT": qkT, "v": vb, "sg": sg}

        def stage_b(t, A):
            T = t.T
            qkv = A["qkT"].t[:, 0:512].rearrange("p (c t) -> p c t", c=4)
            first = (t.i == 0)

            def sload(b_, nc_, dst):
                k.dma(dst.t[:, :], s_ret_in[j, b_, h, nc_ * 128:(nc_ + 1) * 128, :], w=[dst.key])

            def sstore(b_, nc_, src):
                k.dma(k.dout["s_ret"][j, b_, h, nc_ * 128:(nc_ + 1) * 128, :], src.t[:, :], r=[src.key])

            pstore = None
            if t.i == NPT - 1:
                def pstore(nc_):
                    k.dma(k.dout["p_ret"][j, h, nc_ * 128:(nc_ + 1) * 128, :], k.SF[:, nc_, :], r=[("SF", nc_)])
            y = scan_tile(k, t, preps[t.kind], qkv[:, 0:2, :], A["qkT"].key, qkv[:, 2:4, :], A["qkT"].key,
                          A["rot"].t[:, 256:512], A["rot"].key, A["v"].t, A["v"].key, 1, 512, 2, first, sload, sstore, pstore)
            stt = k.st[k.rr("st", 4)]
            emit_rstd(k, t, stt, y.t[0:T, 0:512], [y.key], 512, 0)
            yg = k.bpool.get()
            k.op("dve", lambda e: e.scalar_tensor_tensor(out=yg.t[0:T, :], in0=y.t[0:T, :], scalar=stt[0:T, 1:2], in1=A["sg"].t[0:T, :],
                                                          op0=ALU.mult, op1=ALU.mult), r=[y.key, ("st", id(stt)), A["sg"].key], w=[yg.key])
            y.free()
            for nm in ("rot", "qkT", "v", "sg"):
                A[nm].free()
            ygT = transpose_to(k, t, yg, 4)
            yg.free()
            out_proj_add(k, t, ygT, 4, s_o)
            ygT.free()

        A = stage_a(TILES[0])
        for ti in range(NT):
            if ti + 1 < NT and CFG.get("coop", True):
                box = {}
                COOP.run([lambda: box.__setitem__("A", stage_a(TILES[ti + 1])), lambda: stage_b(TILES[ti], A)])
                A = box["A"]
            else:
                An = stage_a(TILES[ti + 1]) if ti + 1 < NT else None
                stage_b(TILES[ti], A)
                A = An
        for kind in ("p", "s"):
            preps[kind]["dec"].free()


def silu_to(k, out_ap, in_ap, r, w):
    k.op("act", lambda e: e.activation(out=out_ap, in_=in_ap, func=AF.Silu), r=r, w=w)


def load_conv_params(k, convw, convb):
    k.dma(k.PAR[0:16, :], convw.rearrange("t (q n) -> (t q) n", q=4), w=["PAR"])
    if convb is not None:
        k.dma(k.PAR[32:36, :], convb.rearrange("o (q n) -> (o q) n", q=4), w=["PAR"])
    b = k.psum()
    pv = k.ps[b][:, 0:128].rearrange("p (c x) -> p c x", c=8)
    for c8 in range(8):
        k.tr(pv[:, c8, :], k.PAR[0:16, c8 * 128:(c8 + 1) * 128], k.c("ident")[0:16, 0:16], r=["PAR", "CONST"], w=[("ps", b)])
    for q in range(4):
        k.op("dve", lambda e, q=q: e.tensor_copy(out=k.CW[:, q * 8:(q + 1) * 8, :], in_=pv.rearrange("p c (t q) -> p c t q", q=4)[:, :, :, q]),
             r=[("ps", b)], w=["CW"])
    k.pfree(b)
    if convb is not None:
        b = k.psum()
        pv2 = k.ps[b][:, 0:32].rearrange("p (c q) -> p c q", c=8)
        for c8 in range(8):
            k.tr(pv2[:, c8, :], k.PAR[32:36, c8 * 128:(c8 + 1) * 128], k.c("ident")[32:36, 32:36], r=["PAR", "CONST"], w=[("ps", b)])
        k.op("dve", lambda e: e.tensor_copy(out=k.CB[:, :].rearrange("p (q c) -> p c q", q=4), in_=pv2), r=[("ps", b)], w=["CB"])
        k.pfree(b)


def conv_feature_major(k, t, slot, chunk_ids, halo_src, first, conv_in, has_bias):
    T = t.T
    Wv = k.ring[slot][:, :].rearrange("p (k n) -> p k n", k=8)
    b = k.psum()
    for c in range(4):
        for kk in range(8):
            k.mm(k.ps[b][:, c * T:(c + 1) * T], Wv[:, kk, c * 128:(c + 1) * 128], k.XNT[:, kk, t.c0:t.c0 + T], kk == 0, kk == 7,
                 r=[("ring", slot), ("XNT", t.i)], w=[("ps", b)])
    ui = k.rr("UB", 2)
    U = k.UB[ui]
    uk = ("UB", ui)
    acc = k.fpool.get()
    if not t.sample:
        Uv = U[:, 0:4 * 131].rearrange("p (c x) -> p c x", c=4)
        k.op("act", lambda e: e.activation(out=Uv[:, :, 3:131], in_=k.ps[b][:, 0:512].rearrange("p (c x) -> p c x", c=4), func=AF.Copy),
             r=[("ps", b)], w=[uk])
        if first:
            k.op("pool", lambda e: e.memset(Uv[:, :, 0:3], 0.0), w=[uk])
        else:
            pU = k.UB[1 - ui][:, 0:4 * 131].rearrange("p (c x) -> p c x", c=4)
            k.op("pool", lambda e: e.tensor_copy(out=Uv[:, :, 0:3], in_=pU[:, :, 128:131]), r=[("UB", 1 - ui)], w=[uk])
        accv = acc.t[:, 0:512].rearrange("p (c x) -> p c x", c=4)
        srcs = lambda c, tap: Uv[:, c, tap:tap + 128]
        outs = lambda c: accv[:, c, :]
    else:
        Uv = U[:, 0:4 * 112].rearrange("p (c b x) -> p c b x", c=4, b=16)
        k.op("act", lambda e: e.activation(out=Uv[:, :, :, 3:7], in_=k.ps[b][:, 0:256].rearrange("p (c b x) -> p c b x", c=4, b=16), func=AF.Copy),
             r=[("ps", b)], w=[uk])
        stg = k.fpool.get()
        for (c0_, n_, ch0) in conv_in["segs"]:
            k.dma(stg.t[0:48, c0_:c0_ + n_], conv_in["src"][:, :, ch0:ch0 + n_].rearrange("b j c -> (b j) c"), w=[stg.key])
        bh = k.psum()
        ph = k.ps[bh][:, 0:192].rearrange("p (c x) -> p c x", c=4)
        for c in range(4):
            k.tr(ph[:, c, :], stg.t[0:48, c * 128:(c + 1) * 128], k.c("ident")[0:48, 0:48], r=[stg.key, "CONST"], w=[("ps", bh)])
        stg.free()
        k.op("act", lambda e: e.activation(out=Uv[:, :, :, 0:3], in_=ph.rearrange("p c (b x) -> p c b x", b=16), func=AF.Copy), r=[("ps", bh)], w=[uk])
        k.pfree(bh)
        accv = acc.t[:, 0:256].rearrange("p (c b x) -> p c b x", c=4, b=16)
        srcs = lambda c, tap: Uv[:, c, :, tap:tap + 4]
        outs = lambda c: accv[:, c, :, :]
    k.pfree(b)
    for c in range(4):
        ch = chunk_ids[c]
        eng = "dve"
        if has_bias:
            k.op(eng, lambda e, c=c, ch=ch: e.tensor_scalar(out=outs(c), in0=srcs(c, 0), scalar1=k.CW[:, ch, 0:1], scalar2=k.CB[:, ch:ch + 1],
                                                            op0=ALU.mult, op1=ALU.add), r=[uk, "CW", "CB"], w=[acc.key])
        else:
            k.op(eng, lambda e, c=c, ch=ch: e.tensor_scalar_mul(out=outs(c), in0=srcs(c, 0), scalar1=k.CW[:, ch, 0:1]), r=[uk, "CW"], w=[acc.key])
        for tap in range(1, 4):
            k.op(eng, lambda e, c=c, ch=ch, tap=tap: e.scalar_tensor_tensor(out=outs(c), in0=srcs(c, tap), scalar=k.CW[:, ch, tap:tap + 1], in1=outs(c),
                                                                           op0=ALU.mult, op1=ALU.add), r=[uk, "CW", acc.key], w=[acc.key])
    return acc


def conv_state_out(k, t, slot, segs, dst_p, dst_s):
    T = t.T
    b = proj_tok(k, t, slot, 512)
    stg = k.fpool.get()
    k.op("act", lambda e: e.activation(out=stg.t[0:T, :], in_=k.ps[b][0:T, :], func=AF.Copy), r=[("ps", b)], w=[stg.key])
    k.pfree(b)
    for (c0_, n_, ch0) in segs:
        if not t.sample:
            k.dma(dst_p[0:3, ch0:ch0 + n_], stg.t[125:128, c0_:c0_ + n_], r=[stg.key])
        else:
            for jj in range(3):
                k.dma(dst_s[:, jj, ch0:ch0 + n_], stg.t[1 + jj:64:4, c0_:c0_ + n_], r=[stg.key])
    stg.free()


def from_ssd(k, layer):
    win = k.din["ssd_w_in"]
    wout = k.din["ssd_w_out"]
    k.dma(k.PRM[:, 0:32], k.din["ssd_dt_bias"][0, :].partition_broadcast(128), w=["PRM"])
    k.dma(k.PRM[:, 32:64], k.din["ssd_a_log"][0, :].partition_broadcast(128), w=["PRM"])
    k.dma(k.PRM[:, 64:96], k.din["ssd_d"][0, :].partition_broadcast(128), w=["PRM"])
    k.op("act", lambda e: e.activation(out=k.PRM[:, 96:128], in_=k.PRM[:, 32:64], func=AF.Exp), r=["PRM"], w=["PRM"])
    k.op("dve", lambda e: e.tensor_scalar(out=k.PRM[:, 96:128], in0=k.PRM[:, 96:128], scalar1=-1.0, scalar2=None, op0=ALU.mult), r=["PRM"], w=["PRM"])
    load_conv_params(k, k.din["ssd_conv_w"], k.din["ssd_conv_b"])
    for g in range(8):
        v8 = lambda tt: tt[:, :].rearrange("p (k n) -> p k n", k=8)
        segs = [(0, 256, g * 256), (256, 128, 2048 + g * 128), (384, 128, 3072 + g * 128)]
        s_x = k.load_w([((lambda tt, c0_=c0_, n_=n_: v8(tt)[:, :, c0_:c0_ + n_]),
                         win[:, 2048 + ch0:2048 + ch0 + n_].rearrange("(k p) n -> p k n", p=128)) for (c0_, n_, ch0) in segs])
        s_z = k.load_w([(lambda tt: v8(tt)[:, :, 0:256], win[:, g * 256:(g + 1) * 256].rearrange("(k p) n -> p k n", p=128)),
                        (lambda tt: v8(tt)[:, :, 256:260], win[:, 6144 + 4 * g:6144 + 4 * g + 4].rearrange("(k p) n -> p k n", p=128))])
        s_o = k.load_w([(lambda tt: tt[:, 0:2048].rearrange("p (c n) -> p c n", c=2), wout[g * 256:(g + 1) * 256, :].rearrange("(c p) n -> p c n", p=128))])
        nwb = k.fpool.get()
        k.dma(nwb.t[:, 0:256], k.din["ssd_norm"][0, g * 256:(g + 1) * 256].partition_broadcast(128), w=[nwb.key])
        chunk_ids = [2 * g, 2 * g + 1, 16 + g, 24 + g]
        conv_in = {"src": k.din["state_ssd_conv"], "segs": segs}

        def stage_a(t):
            T = t.T
            acc = conv_feature_major(k, t, s_x, chunk_ids, None, t.i == 0, conv_in, True)
            n4 = 4 * T
            xsT = k.fpool.get()
            bcT = k.bpool.get()
            silu_to(k, xsT.t[:, 0:2 * T], acc.t[:, 0:2 * T], [acc.key], [xsT.key])
            silu_to(k, bcT.t[:, 0:2 * T], acc.t[:, 2 * T:4 * T], [acc.key], [bcT.key])
            acc.free()
            b = k.psum()
            for c in range(2):
                k.tr(k.ps[b][0:T, c * 128:(c + 1) * 128], xsT.t[:, c * T:(c + 1) * T], k.c("ident"), r=[xsT.key, "CONST"], w=[("ps", b)])
            xs = k.fpool.get()
            k.op("act", lambda e: e.activation(out=xs.t[0:T, 0:256], in_=k.ps[b][0:T, 0:256], func=AF.Copy), r=[("ps", b)], w=[xs.key])
            k.pfree(b)
            xsT.free()
            b = k.psum()
            pvb = k.ps[b][:, :].bitcast(BF16)
            k.tr(pvb[0:T, 0:128], bcT.t[:, 0:T], k.identb[:, :], r=[bcT.key, "identb"], w=[("ps", b)])
            btok = k.bpool.get()
            k.op("dve", lambda e: e.tensor_copy(out=btok.t[0:T, 0:128], in_=pvb[0:T, 0:128]), r=[("ps", b)], w=[btok.key])
            k.pfree(b)
            bz = proj_tok(k, t, s_z, 260)
            sz = k.bpool.get()
            silu_to(k, sz.t[0:T, 0:256], k.ps[bz][0:T, 0:256], [("ps", bz)], [sz.key])
            li = k.rr("LAB", 4)
            lab = k.LAB[li]
            lk = ("LAB", li)
            k.op("dve", lambda e: e.tensor_tensor(out=lab[0:T, 0:4], in0=k.ps[bz][0:T, 256:260], in1=k.PRM[0:T, 4 * g:4 * g + 4], op=ALU.add),
                 r=[("ps", bz), "PRM"], w=[lk])
            k.pfree(bz)
            k.op("act", lambda e: e.activation(out=lab[0:T, 0:4], in_=lab[0:T, 0:4], func=AF.Exp), r=[lk], w=[lk])
            k.op("pool", lambda e: e.tensor_scalar(out=lab[0:T, 0:4], in0=lab[0:T, 0:4], scalar1=1.0, scalar2=None, op0=ALU.add), r=[lk], w=[lk])
            k.op("act", lambda e: e.activation(out=lab[0:T, 0:4], in_=lab[0:T, 0:4], func=AF.Ln), r=[lk], w=[lk])
            k.op("dve", lambda e: e.tensor_tensor(out=lab[0:T, 4:8], in0=lab[0:T, 0:4], in1=k.PRM[0:T, 96 + 4 * g:96 + 4 * g + 4], op=ALU.mult),
                 r=[lk, "PRM"], w=[lk])
            vdt = k.bpool.get()
            k.op("pool", lambda e: e.tensor_tensor(out=vdt.t[0:T, 0:256].rearrange("p (r d) -> p r d", r=4), in0=xs.t[0:T, 0:256].rearrange("p (r d) -> p r d", r=4),
                                                   in1=lab[0:T, 0:4].unsqueeze(2).to_broadcast([T, 4, 64]), op=ALU.mult), r=[xs.key, lk], w=[vdt.key])
            return {"bcT": bcT, "xs": xs, "btok": btok, "sz": sz, "lab": lab, "lk": lk, "vdt": vdt}

        def stage_b(t, A):
            T = t.T
            prep = decay_prep(k, t.kind, A["lab"][0:T, 4:8], [A["lk"]], 4)
            bcv = A["bcT"].t[:, 0:2 * T].rearrange("p (c t) -> p c t", c=2)

            def sload(b_, nc_, dst):
                stg = k.fpool.get()
                k.dma(stg.t[:, 0:256].rearrange("p (j n) -> p j n", j=2),
                      k.din["state_ssd"][b_, 4 * g:4 * g + 4, :, :].rearrange("(j h) p n -> (h p) j n", j=2), w=[stg.key])
                bb = k.psum()
                for jj in range(2):
                    k.tr(k.ps[bb][:, jj * 128:(jj + 1) * 128], stg.t[:, jj * 128:(jj + 1) * 128], k.c("ident"), r=[stg.key, "CONST"], w=[("ps", bb)])
                stg.free()
                k.op("dve", lambda e: e.tensor_copy(out=dst.t[:, 0:256], in_=k.ps[bb][:, 0:256]), r=[("ps", bb)], w=[dst.key])
                k.pfree(bb)

            def st_out(src_ap, src_key, dst_ap):
                bb = k.psum()
                for jj in range(2):
                    k.tr(k.ps[bb][:, jj * 128:(jj + 1) * 128], src_ap[:, jj * 128:(jj + 1) * 128], k.c("ident"), r=[src_key, "CONST"], w=[("ps", bb)])
                stg = k.fpool.get()
                k.op("act", lambda e: e.activation(out=stg.t[:, 0:256], in_=k.ps[bb][:, 0:256], func=AF.Copy), r=[("ps", bb)], w=[stg.key])
                k.pfree(bb)
                k.dma(dst_ap.rearrange("(j h) p n -> (h p) j n", j=2), stg.t[:, 0:256].rearrange("p (j n) -> p j n", j=2), r=[stg.key])
                stg.free()

            def sstore(b_, nc_, src):
                st_out(src.t[:, 0:256], src.key, k.dout["s_ssd"][b_, 4 * g:4 * g + 4, :, :])

            pstore = None
            if t.i == NPT - 1:
                def pstore(nc_):
                    st_out(k.SF[:, 0, 0:256], ("SF", 0), k.dout["p_ssd"][4 * g:4 * g + 4, :, :])
            y = scan_tile(k, t, prep, bcv[:, 1:2, :], A["bcT"].key, bcv[:, 0:1, :], A["bcT"].key, A["btok"].t[:, 0:128], A["btok"].key,
                          A["vdt"].t, A["vdt"].key, 4, 64, 1, t.i == 0, sload, sstore, pstore)
            prep["dec"].free()
            tmp = k.fpool.get()
            k.op("pool", lambda e: e.tensor_tensor(out=tmp.t[0:T, 0:256].rearrange("p (r d) -> p r d", r=4), in0=A["xs"].t[0:T, 0:256].rearrange("p (r d) -> p r d", r=4),
                                                   in1=k.PRM[0:T, 64 + 4 * g:64 + 4 * g + 4].unsqueeze(2).to_broadcast([T, 4, 64]), op=ALU.mult),
                 r=[A["xs"].key, "PRM"], w=[tmp.key])
            k.op("dve", lambda e: e.tensor_tensor(out=y.t[0:T, 0:256], in0=y.t[0:T, 0:256], in1=tmp.t[0:T, 0:256], op=ALU.add), r=[y.key, tmp.key], w=[y.key])
            tmp.free()
            k.op("dve", lambda e: e.tensor_tensor(out=y.t[0:T, 0:256], in0=y.t[0:T, 0:256], in1=A["sz"].t[0:T, 0:256], op=ALU.mult), r=[y.key, A["sz"].key], w=[y.key])
            stt = k.st[k.rr("st", 4)]
            emit_rstd(k, t, stt, y.t[0:T, 0:256], [y.key], 256, 0)
            yn = k.bpool.get()
            k.op("dve", lambda e: e.scalar_tensor_tensor(out=yn.t[0:T, 0:256], in0=y.t[0:T, 0:256], scalar=stt[0:T, 1:2], in1=nwb.t[0:T, 0:256],
                                                          op0=ALU.mult, op1=ALU.mult), r=[y.key, ("st", id(stt)), nwb.key], w=[yn.key])
            y.free()
            for nm in ("bcT", "xs", "btok", "sz", "vdt"):
                A[nm].free()
            ynT = transpose_to(k, t, yn, 2)
            yn.free()
            out_proj_add(k, t, ynT, 2, s_o)
            ynT.free()
            if t.i >= NPT - 1:
                conv_state_out(k, t, s_x, segs, k.dout["p_ssd_conv"], k.dout["s_ssd_conv"])

        A = stage_a(TILES[0])
        for ti in range(NT):
            if ti + 1 < NT and CFG.get("coop", True):
                box = {}
                COOP.run([lambda: box.__setitem__("A", stage_a(TILES[ti + 1])), lambda: stage_b(TILES[ti], A)])
                A = box["A"]
            else:
                An = stage_a(TILES[ti + 1]) if ti + 1 < NT else None
                stage_b(TILES[ti], A)
                A = An
        nwb.free()


def gdn_chain(k, T, A, nlev):
    identb = k.identb[0:T, 0:T]
    v3 = lambda buf: buf.t[0:T, 0:2 * T].rearrange("p (r t) -> p r t", r=2)
    sl = lambda buf, r_: buf.t[0:T, r_ * T:(r_ + 1) * T]
    b = k.psum()
    pvb = k.ps[b][:, :].bitcast(BF16)
    for r_ in range(2):
        k.tr(pvb[0:T, r_ * T:(r_ + 1) * T], sl(A, r_), identb, r=[A.key, "identb"], w=[("ps", b)])
    AT = k.hpool.get()
    k.op("act", lambda e: e.activation(out=AT.t[0:T, 0:2 * T], in_=pvb[0:T, 0:2 * T], func=AF.Copy), r=[("ps", b)], w=[AT.key])
    k.pfree(b)
    Dm = None
    DT = None
    for lv in range(nlev):
        last = (lv == nlev - 1)
        mT = k.c("bmT%d" % lv)[0:T, 0:T]
        AoT = k.hpool.get()
        k.op("pool", lambda e: e.tensor_tensor(out=v3(AoT), in0=v3(AT), in1=mT.unsqueeze(1).to_broadcast([T, 2, T]), op=ALU.mult),
             r=[AT.key, "CONST"], w=[AoT.key])
        if lv == 0:
            idb2_ = identb.unsqueeze(1).to_broadcast([T, 2, T])
            DTn = k.hpool.get()
            k.op("dve", lambda e: e.tensor_tensor(out=v3(DTn), in0=v3(AoT), in1=idb2_, op=ALU.add), r=[AoT.key, "identb"], w=[DTn.key])
            AoT.free()
            Dn = None
            if not last:
                E0 = k.hpool.get()
                m0 = k.c("bm0")[0:T, 0:T]
                k.op("pool", lambda e: e.tensor_tensor(out=v3(E0), in0=v3(A), in1=m0.unsqueeze(1).to_broadcast([T, 2, T]), op=ALU.mult),
                     r=[A.key, "CONST"], w=[E0.key])
                Dn = k.hpool.get()
                k.op("pool", lambda e: e.tensor_tensor(out=v3(Dn), in0=v3(E0), in1=idb2_, op=ALU.add), r=[E0.key, "identb"], w=[Dn.key])
                E0.free()
            Dm, DT = Dn, DTn
            continue
        dk = [Dm.key] if Dm is not None else ["identb"]
        dtk = [DT.key] if DT is not None else ["identb"]
        Dv = (lambda r_: sl(Dm, r_)) if Dm is not None else (lambda r_: identb)
        DTv = (lambda r_: sl(DT, r_)) if DT is not None else (lambda r_: identb)
        be = k.psum()
        for r_ in range(2):
            k.mm(k.ps[be][0:T, r_ * T:(r_ + 1) * T], sl(AoT, r_), Dv(r_), True, True, r=[AoT.key] + dk, w=[("ps", be)])
        E = k.hpool.get()
        k.op("act", lambda e: e.activation(out=E.t[0:T, 0:2 * T], in_=k.ps[be][0:T, 0:2 * T], func=AF.Copy), r=[("ps", be)], w=[E.key])
        k.pfree(be)
        AoT.free()
        bft = k.psum()
        for r_ in range(2):
            k.mm(k.ps[bft][0:T, r_ * T:(r_ + 1) * T], sl(E, r_), DTv(r_), True, True, r=[E.key] + dtk, w=[("ps", bft)])
        DTn = k.hpool.get()
        if DT is not None:
            k.op("dve", lambda e: e.tensor_tensor(out=DTn.t[0:T, 0:2 * T], in0=DT.t[0:T, 0:2 * T], in1=k.ps[bft][0:T, 0:2 * T], op=ALU.add),
                 r=[DT.key, ("ps", bft)], w=[DTn.key])
        else:
            k.op("dve", lambda e: e.tensor_tensor(out=v3(DTn), in0=k.ps[bft][0:T, 0:2 * T].rearrange("p (r t) -> p r t", r=2),
                                                  in1=identb.unsqueeze(1).to_broadcast([T, 2, T]), op=ALU.add), r=["identb", ("ps", bft)], w=[DTn.key])
        k.pfree(bft)
        Dn = None
        if not last:
            bf_ = k.psum()
            for r_ in range(2):
                k.mm(k.ps[bf_][0:T, r_ * T:(r_ + 1) * T], DTv(r_), sl(E, r_), True, True, r=[E.key] + dtk, w=[("ps", bf_)])
            Dn = k.hpool.get()
            if Dm is not None:
                k.op("dve", lambda e: e.tensor_tensor(out=Dn.t[0:T, 0:2 * T], in0=Dm.t[0:T, 0:2 * T], in1=k.ps[bf_][0:T, 0:2 * T], op=ALU.add),
                     r=[Dm.key, ("ps", bf_)], w=[Dn.key])
            else:
                k.op("dve", lambda e: e.tensor_tensor(out=v3(Dn), in0=k.ps[bf_][0:T, 0:2 * T].rearrange("p (r t) -> p r t", r=2),
                                                      in1=identb.unsqueeze(1).to_broadcast([T, 2, T]), op=ALU.add), r=["identb", ("ps", bf_)], w=[Dn.key])
            k.pfree(bf_)
        E.free()
        if Dm is not None:
            Dm.free()
        if DT is not None:
            DT.free()
        Dm, DT = Dn, DTn
    AT.free()
    return DT


def from_gdn(k, layer):
    win = k.din["gdn_w_in"]
    wout = k.din["gdn_w_out"]
    k.dma(k.PRM[:, 0:16], k.din["gdn_dt_bias"][0, :].partition_broadcast(128), w=["PRM"])
    k.dma(k.PRM[:, 16:32], k.din["gdn_a_log"][0, :].partition_broadcast(128), w=["PRM"])
    k.op("act", lambda e: e.activation(out=k.PRM[:, 32:48], in_=k.PRM[:, 16:32], func=AF.Exp), r=["PRM"], w=["PRM"])
    k.op("dve", lambda e: e.tensor_scalar(out=k.PRM[:, 32:48], in0=k.PRM[:, 32:48], scalar1=-1.0, scalar2=None, op0=ALU.mult), r=["PRM"], w=["PRM"])
    load_conv_params(k, k.din["gdn_conv_w"], None)
    gnw = k.fpool.get()
    k.dma(gnw.t[:, 0:128], k.din["gdn_norm"][0, :].partition_broadcast(128), w=[gnw.key])
    for kh in range(8):
        v8 = lambda tt: tt[:, :].rearrange("p (k n) -> p k n", k=8)
        segs = [(0, 128, kh * 128), (128, 128, 1024 + kh * 128), (256, 256, 2048 + kh * 256)]
        s_x = k.load_w([((lambda tt, c0_=c0_, n_=n_: v8(tt)[:, :, c0_:c0_ + n_]),
                         win[:, ch0:ch0 + n_].rearrange("(k p) n -> p k n", p=128)) for (c0_, n_, ch0) in segs])
        s_z = k.load_w([(lambda tt: v8(tt)[:, :, 0:256], win[:, 4096 + kh * 256:4096 + (kh + 1) * 256].rearrange("(k p) n -> p k n", p=128)),
                        (lambda tt: v8(tt)[:, :, 256:258], win[:, 6144 + 2 * kh:6144 + 2 * kh + 2].rearrange("(k p) n -> p k n", p=128)),
                        (lambda tt: v8(tt)[:, :, 258:260], win[:, 6160 + 2 * kh:6160 + 2 * kh + 2].rearrange("(k p) n -> p k n", p=128))])
        s_o = k.load_w([(lambda tt: tt[:, 0:2048].rearrange("p (c n) -> p c n", c=2), wout[kh * 256:(kh + 1) * 256, :].rearrange("(c p) n -> p c n", p=128))])
        chunk_ids = [kh, 8 + kh, 16 + 2 * kh, 17 + 2 * kh]
        conv_in = {"src": k.din["state_gdn_conv"], "segs": segs}

        def stage_a(t):
            T = t.T
            acc = conv_feature_major(k, t, s_x, chunk_ids, None, t.i == 0, conv_in, False)
            act4 = k.fpool.get()
            silu_to(k, act4.t[:, 0:4 * T], acc.t[:, 0:4 * T], [acc.key], [act4.key])
            acc.free()
            sq = k.fpool.get()
            k.op("pool", lambda e: e.tensor_tensor(out=sq.t[:, 0:2 * T], in0=act4.t[:, 0:2 * T], in1=act4.t[:, 0:2 * T], op=ALU.mult), r=[act4.key], w=[sq.key])
            b = k.psum()
            k.mm(k.ps[b][:, 0:2 * T], k.c("ones"), sq.t[:, 0:2 * T], True, True, r=["CONST", sq.key], w=[("ps", b)])
            k.op("dve", lambda e: e.tensor_scalar(out=sq.t[:, 0:2 * T], in0=k.ps[b][:, 0:2 * T], scalar1=RMS_EPS, scalar2=None, op0=ALU.add),
                 r=[("ps", b)], w=[sq.key])
            k.pfree(b)
            k.op("act", lambda e: e.activation(out=sq.t[:, 0:2 * T], in_=sq.t[:, 0:2 * T], func=AF.Ln), r=[sq.key], w=[sq.key])
            k.op("act", lambda e: e.activation(out=sq.t[:, 0:2 * T], in_=sq.t[:, 0:2 * T], func=AF.Exp, scale=-0.5), r=[sq.key], w=[sq.key])
            qkn = k.bpool.get()
            k.op("dve", lambda e: e.scalar_tensor_tensor(out=qkn.t[:, 0:T], in0=act4.t[:, 0:T], scalar=float(128.0 ** -0.5), in1=sq.t[:, 0:T],
                                                          op0=ALU.mult, op1=ALU.mult), r=[act4.key, sq.key], w=[qkn.key])
            k.op("pool", lambda e: e.tensor_tensor(out=qkn.t[:, T:2 * T], in0=act4.t[:, T:2 * T], in1=sq.t[:, T:2 * T], op=ALU.mult),
                 r=[act4.key, sq.key], w=[qkn.key])
            sq.free()
            b = k.psum()
            pvb = k.ps[b][:, :].bitcast(BF16)
            k.tr(pvb[0:T, 0:128], qkn.t[:, T:2 * T], k.identb[:, :], r=[qkn.key, "identb"], w=[("ps", b)])
            ktok = k.bpool.get()
            k.op("dve", lambda e: e.tensor_copy(out=ktok.t[0:T, 0:128], in_=pvb[0:T, 0:128]), r=[("ps", b)], w=[ktok.key])
            k.pfree(b)
            b = k.psum()
            for c in range(2):
                k.tr(k.ps[b][0:T, c * 128:(c + 1) * 128], act4.t[:, (2 + c) * T:(3 + c) * T], k.c("ident"), r=[act4.key, "CONST"], w=[("ps", b)])
            vtok = k.fpool.get()
            k.op("act", lambda e: e.activation(out=vtok.t[0:T, 0:256], in_=k.ps[b][0:T, 0:256], func=AF.Copy), r=[("ps", b)], w=[vtok.key])
            k.pfree(b)
            act4.free()
            bz = proj_tok(k, t, s_z, 260)
            sz = k.bpool.get()
            silu_to(k, sz.t[0:T, 0:256], k.ps[bz][0:T, 0:256], [("ps", bz)], [sz.key])
            li = k.rr("LAB", 4)
            lab = k.LAB[li]
            lk = ("LAB", li)
            k.op("act", lambda e: e.activation(out=lab[0:T, 0:2], in_=k.ps[bz][0:T, 256:258], func=AF.Exp, scale=-1.0), r=[("ps", bz)], w=[lk])
            k.op("dve", lambda e: e.tensor_tensor(out=lab[0:T, 4:6], in0=k.ps[bz][0:T, 258:260], in1=k.PRM[0:T, 2 * kh:2 * kh + 2], op=ALU.add),
                 r=[("ps", bz), "PRM"], w=[lk])
            k.pfree(bz)
            k.op("pool", lambda e: e.tensor_scalar(out=lab[0:T, 0:2], in0=lab[0:T, 0:2], scalar1=1.0, scalar2=None, op0=ALU.add), r=[lk], w=[lk])
            k.op("dve", lambda e: e.reciprocal(out=lab[0:T, 0:2], in_=lab[0:T, 0:2]), r=[lk], w=[lk])
            k.op("dve", lambda e: e.tensor_scalar(out=lab[0:T, 2:4], in0=lab[0:T, 0:2], scalar1=-1.0, scalar2=None, op0=ALU.mult), r=[lk], w=[lk])
            k.op("act", lambda e: e.activation(out=lab[0:T, 4:6], in_=lab[0:T, 4:6], func=AF.Exp), r=[lk], w=[lk])
            k.op("pool", lambda e: e.tensor_scalar(out=lab[0:T, 4:6], in0=lab[0:T, 4:6], scalar1=1.0, scalar2=None, op0=ALU.add), r=[lk], w=[lk])
            k.op("act", lambda e: e.activation(out=lab[0:T, 4:6], in_=lab[0:T, 4:6], func=AF.Ln), r=[lk], w=[lk])
            k.op("dve", lambda e: e.tensor_tensor(out=lab[0:T, 4:6], in0=lab[0:T, 4:6], in1=k.PRM[0:T, 32 + 2 * kh:32 + 2 * kh + 2], op=ALU.mult),
                 r=[lk, "PRM"], w=[lk])
            return {"qkn": qkn, "ktok": ktok, "vtok": vtok, "sz": sz, "lab": lab, "lk": lk}

        def stage_a2(t):
            A_ = stage_a(t)
            stage_b1(t, A_)
            return A_

        def stage_b1(t, A_):
            T = t.T
            lab, lk = A_["lab"], A_["lk"]
            qkn = A_["qkn"]
            qT = qkn.t[:, 0:T]
            kT = qkn.t[:, T:2 * T]
            prep = decay_prep(k, t.kind, lab[0:T, 4:6], [lk], 2)
            sm, smk, dec = prep["sm"], prep["smk"], prep["dec"]
            strict = kconst(k, t.kind, "strict")
            bg = k.psum()
            k.mm(k.ps[bg][0:T, 0:T], kT, kT, True, True, r=[qkn.key], w=[("ps", bg)])
            k.mm(k.ps[bg][0:T, T:2 * T], kT, qT, True, True, r=[qkn.key], w=[("ps", bg)])
            bd_ = k.psum()
            for r_ in range(2):
                k.tr(k.ps[bd_][0:T, r_ * T:(r_ + 1) * T], dec.t[0:T, r_ * T:(r_ + 1) * T], k.c("ident")[0:T, 0:T], r=[dec.key, "CONST"], w=[("ps", bd_)])
            dsb = k.fpool.get()
            k.op("dve", lambda e: e.tensor_tensor(out=dsb.t[0:T, 0:2 * T].rearrange("p (r t) -> p r t", r=2),
                                                  in0=k.ps[bd_][0:T, 0:2 * T].rearrange("p (r t) -> p r t", r=2),
                                                  in1=strict.unsqueeze(1).to_broadcast([T, 2, T]), op=ALU.mult), r=[("ps", bd_), "CONST"], w=[dsb.key])
            k.pfree(bd_)
            Am = k.hpool.get()
            for r_ in range(2):
                k.op("dve", lambda e, r_=r_: e.scalar_tensor_tensor(out=Am.t[0:T, r_ * T:(r_ + 1) * T], in0=k.ps[bg][0:T, 0:T], scalar=lab[0:T, 2 + r_:3 + r_],
                                                                     in1=dsb.t[0:T, r_ * T:(r_ + 1) * T], op0=ALU.mult, op1=ALU.mult),
                     r=[("ps", bg), lk, dsb.key], w=[Am.key])
            dsb.free()
            attnT = k.bpool.get()
            k.op("dve", lambda e: e.tensor_tensor(out=attnT.t[0:T, 0:2 * T].rearrange("p (r t) -> p r t", r=2),
                                                  in0=dec.t[0:T, 0:2 * T].rearrange("p (r t) -> p r t", r=2),
                                                  in1=k.ps[bg][0:T, T:2 * T].unsqueeze(1).to_broadcast([T, 2, T]), op=ALU.mult),
                 r=[dec.key, ("ps", bg)], w=[attnT.key])
            k.pfree(bg)
            dec.free()
            TTd = gdn_chain(k, T, Am, 7 if not t.sample else 2)
            Am.free()
            TTf = k.bpool.get()
            for r_ in range(2):
                k.op("act", lambda e, r_=r_: e.activation(out=TTf.t[0:T, r_ * T:(r_ + 1) * T], in_=TTd.t[0:T, r_ * T:(r_ + 1) * T], func=AF.Copy,
                                                            scale=lab[0:T, 2 + r_:3 + r_]), r=[TTd.key, lk], w=[TTf.key])
            TTd.free()
            A_["prep"] = prep
            A_["attnT"] = attnT
            A_["TTf"] = TTf

        def stage_b(t, A_):
            T = t.T
            nseq = t.nseq
            first = (t.i == 0)
            lab, lk = A_["lab"], A_["lk"]
            qkn, ktok, vtok = A_["qkn"], A_["ktok"], A_["vtok"]
            qT = qkn.t[:, 0:T]
            kT = qkn.t[:, T:2 * T]
            prep, attnT, TTf = A_["prep"], A_["attnT"], A_["TTf"]
            sm, smk = prep["sm"], prep["smk"]
            tmpb = k.bpool.get()
            bks = None
            if not (nseq == 1 and first):
                bks = k.psum()
                if nseq == 1:
                    for r_ in range(2):
                        k.mm(k.ps[bks][0:T, r_ * 256:r_ * 256 + 128], kT, k.SB[:, 0, r_ * 128:(r_ + 1) * 128], True, True, r=[qkn.key, ("SB", 0)], w=[("ps", bks)])
                        k.mm(k.ps[bks][0:T, r_ * 256 + 128:r_ * 256 + 256], qT, k.SB[:, 0, r_ * 128:(r_ + 1) * 128], True, True, r=[qkn.key, ("SB", 0)], w=[("ps", bks)])
                else:
                    sbl = []
                    for b_ in range(nseq):
                        sf = k.fpool.get()
                        k.dma(sf.t[:, 0:256].rearrange("p (r v) -> p r v", r=2), k.din["state_gdn"][b_, 2 * kh:2 * kh + 2, :, :].rearrange("r k v -> k r v"), w=[sf.key])
                        sb = k.bpool.get()
                        k.op("act", lambda e, sb=sb, sf=sf: e.activation(out=sb.t[:, 0:256], in_=sf.t[:, 0:256], func=AF.Copy), r=[sf.key], w=[sb.key])
                        sf.free()
                        qm = k.bpool.get()
                        k.op("pool", lambda e, qm=qm, b_=b_: e.tensor_tensor(out=qm.t[:, 0:2 * T].rearrange("p (c t) -> p c t", c=2),
                                                                            in0=qkn.t[:, 0:2 * T].rearrange("p (c t) -> p c t", c=2),
                                                                            in1=k.colmaskb[:, b_, :].unsqueeze(1).to_broadcast([128, 2, T]), op=ALU.mult),
                             r=[qkn.key, "colmaskb"], w=[qm.key])
                        for r_ in range(2):
                            k.P.op("pe", lambda e, qm=qm, sb=sb, r_=r_, b_=b_: e.matmul(k.ps[bks][0:T, r_ * 256:r_ * 256 + 128], lhsT=qm.t[:, T:2 * T],
                                                                                         rhs=sb.t[:, r_ * 128:(r_ + 1) * 128], start=(b_ == 0 and r_ == 0),
                                                                                         stop=(b_ == nseq - 1), skip_group_check=True),
                                   r=[qm.key, sb.key], w=[("ps", bks)])
                            k.P.op("pe", lambda e, qm=qm, sb=sb, r_=r_, b_=b_: e.matmul(k.ps[bks][0:T, r_ * 256 + 128:r_ * 256 + 256], lhsT=qm.t[:, 0:T],
                                                                                         rhs=sb.t[:, r_ * 128:(r_ + 1) * 128], start=False,
                                                                                         stop=(b_ == nseq - 1), skip_group_check=True),
                                   r=[qm.key, sb.key], w=[("ps", bks)])
                        qm.free()
                        sb.free()
                for r_ in range(2):
                    k.op("dve", lambda e, r_=r_: e.scalar_tensor_tensor(out=tmpb.t[0:T, r_ * 128:(r_ + 1) * 128], in0=k.ps[bks][0:T, r_ * 256:r_ * 256 + 128],
                                                                         scalar=sm[0:T, 4 + r_:5 + r_], in1=vtok.t[0:T, r_ * 128:(r_ + 1) * 128],
                                                                         op0=ALU.mult, op1=ALU.subtract), r=[("ps", bks), smk, vtok.key], w=[tmpb.key])
            else:
                k.op("act", lambda e: e.activation(out=tmpb.t[0:T, 0:256], in_=vtok.t[0:T, 0:256], func=AF.Copy, scale=-1.0), r=[vtok.key], w=[tmpb.key])
            bv = k.psum()
            for r_ in range(2):
                k.mm(k.ps[bv][0:T, r_ * 128:(r_ + 1) * 128], TTf.t[0:T, r_ * T:(r_ + 1) * T], tmpb.t[0:T, r_ * 128:(r_ + 1) * 128], True, True,
                     r=[TTf.key, tmpb.key], w=[("ps", bv)])
            TTf.free()
            tmpb.free()
            vnew = k.bpool.get()
            k.op("act", lambda e: e.activation(out=vnew.t[0:T, 0:256], in_=k.ps[bv][0:T, 0:256], func=AF.Copy), r=[("ps", bv)], w=[vnew.key])
            k.pfree(bv)
            bo = k.psum()
            for r_ in range(2):
                k.mm(k.ps[bo][0:T, r_ * 128:(r_ + 1) * 128], attnT.t[0:T, r_ * T:(r_ + 1) * T], vnew.t[0:T, r_ * 128:(r_ + 1) * 128], True, True,
                     r=[attnT.key, vnew.key], w=[("ps", bo)])
            attnT.free()
            o = k.fpool.get()
            if bks is not None:
                tq = k.fpool.get()
                for r_ in range(2):
                    k.op("act", lambda e, r_=r_: e.activation(out=tq.t[0:T, r_ * 128:(r_ + 1) * 128], in_=k.ps[bks][0:T, r_ * 256 + 128:r_ * 256 + 256], func=AF.Copy,
                                                                scale=sm[0:T, 4 + r_:5 + r_]), r=[("ps", bks), smk], w=[tq.key])
                k.pfree(bks)
                k.op("dve", lambda e: e.tensor_tensor(out=o.t[0:T, 0:256], in0=tq.t[0:T, 0:256], in1=k.ps[bo][0:T, 0:256], op=ALU.add), r=[tq.key, ("ps", bo)], w=[o.key])
                tq.free()
            else:
                k.op("act", lambda e: e.activation(out=o.t[0:T, 0:256], in_=k.ps[bo][0:T, 0:256], func=AF.Copy), r=[("ps", bo)], w=[o.key])
            k.pfree(bo)
            kd = k.bpool.get()
            for r_ in range(2):
                k.op("pool", lambda e, r_=r_: e.tensor_scalar_mul(out=kd.t[0:T, r_ * 128:(r_ + 1) * 128], in0=ktok.t[0:T, 0:128], scalar1=sm[0:T, 6 + r_:7 + r_]),
                     r=[ktok.key, smk], w=[kd.key])
            if nseq == 1:
                bs_ = k.psum()
                for r_ in range(2):
                    k.mm(k.ps[bs_][:, r_ * 128:(r_ + 1) * 128], kd.t[0:T, r_ * 128:(r_ + 1) * 128], vnew.t[0:T, r_ * 128:(r_ + 1) * 128], True, True,
                         r=[kd.key, vnew.key], w=[("ps", bs_)])
                for r_ in range(2):
                    if first:
                        k.op("act", lambda e, r_=r_: e.activation(out=k.SF[:, 0, r_ * 128:(r_ + 1) * 128], in_=k.ps[bs_][:, r_ * 128:(r_ + 1) * 128], func=AF.Copy),
                             r=[("ps", bs_)], w=[("SF", 0)])
                    else:
                        k.op("dve", lambda e, r_=r_: e.scalar_tensor_tensor(out=k.SF[:, 0, r_ * 128:(r_ + 1) * 128], in0=k.SF[:, 0, r_ * 128:(r_ + 1) * 128],
                                                                             scalar=sm[0:128, 8 + r_:9 + r_], in1=k.ps[bs_][:, r_ * 128:(r_ + 1) * 128],
                                                                             op0=ALU.mult, op1=ALU.add), r=[("SF", 0), smk, ("ps", bs_)], w=[("SF", 0)])
                k.pfree(bs_)
                k.op("act", lambda e: e.activation(out=k.SB[:, 0, 0:256], in_=k.SF[:, 0, 0:256], func=AF.Copy), r=[("SF", 0)], w=[("SB", 0)])
                if t.i == NPT - 1:
                    k.dma(k.dout["p_gdn"][2 * kh:2 * kh + 2, :, :].rearrange("r k v -> k r v"), k.SF[:, 0, 0:256].rearrange("p (r v) -> p r v", r=2), r=[("SF", 0)])
            else:
                for b_ in range(nseq):
                    sf = k.fpool.get()
                    k.dma(sf.t[:, 0:256].rearrange("p (r v) -> p r v", r=2), k.din["state_gdn"][b_, 2 * kh:2 * kh + 2, :, :].rearrange("r k v -> k r v"), w=[sf.key])
                    km = k.bpool.get()
                    k.op("pool", lambda e, km=km, b_=b_: e.tensor_scalar_mul(out=km.t[0:T, 0:256], in0=kd.t[0:T, 0:256], scalar1=k.c("rowmask_s")[0:T, b_:b_ + 1]),
                         r=[kd.key, "CONST"], w=[km.key])
                    bs_ = k.psum()
                    for r_ in range(2):
                        k.mm(k.ps[bs_][:, r_ * 128:(r_ + 1) * 128], km.t[0:T, r_ * 128:(r_ + 1) * 128], vnew.t[0:T, r_ * 128:(r_ + 1) * 128], True, True,
                             r=[km.key, vnew.key], w=[("ps", bs_)])
                    km.free()
                    for r_ in range(2):
                        k.op("dve", lambda e, r_=r_, sf=sf, bs_=bs_, b_=b_: e.scalar_tensor_tensor(
                            out=sf.t[:, r_ * 128:(r_ + 1) * 128], in0=sf.t[:, r_ * 128:(r_ + 1) * 128], scalar=sm[0:128, 16 + 2 * b_ + r_:17 + 2 * b_ + r_],
                            in1=k.ps[bs_][:, r_ * 128:(r_ + 1) * 128], op0=ALU.mult, op1=ALU.add), r=[sf.key, smk, ("ps", bs_)], w=[sf.key])
                    k.pfree(bs_)
                    k.dma(k.dout["s_gdn"][b_, 2 * kh:2 * kh + 2, :, :].rearrange("r k v -> k r v"), sf.t[:, 0:256].rearrange("p (r v) -> p r v", r=2), r=[sf.key])
                    sf.free()
            kd.free()
            vnew.free()
            stt = k.st[k.rr("st", 4)]
            sk_ = ("st", id(stt))
            yn = k.bpool.get()
            for r_ in range(2):
                emit_rstd(k, t, stt, o.t[0:T, r_ * 128:(r_ + 1) * 128], [o.key], 128, 4 * r_)
                k.op("dve", lambda e, r_=r_: e.scalar_tensor_tensor(out=o.t[0:T, r_ * 128:(r_ + 1) * 128], in0=o.t[0:T, r_ * 128:(r_ + 1) * 128],
                                                                     scalar=stt[0:T, 4 * r_ + 1:4 * r_ + 2], in1=gnw.t[0:T, 0:128], op0=ALU.mult, op1=ALU.mult),
                     r=[o.key, sk_, gnw.key], w=[o.key])
            k.op("dve", lambda e: e.tensor_tensor(out=yn.t[0:T, 0:256], in0=o.t[0:T, 0:256], in1=A_["sz"].t[0:T, 0:256], op=ALU.mult), r=[o.key, A_["sz"].key], w=[yn.key])
            o.free()
            for nm in ("qkn", "ktok", "vtok", "sz"):
                A_[nm].free()
            ynT = transpose_to(k, t, yn, 2)
            yn.free()
            out_proj_add(k, t, ynT, 2, s_o)
            ynT.free()
            if t.i >= NPT - 1:
                conv_state_out(k, t, s_x, segs, k.dout["p_gdn_conv"], k.dout["s_gdn_conv"])

        A_ = stage_a2(TILES[0])
        for ti in range(NT):
            if ti + 1 < NT:
                box = {}
                COOP.run([lambda: box.__setitem__("A", stage_a2(TILES[ti + 1])), lambda: stage_b(TILES[ti], A_)])
                A_ = box["A"]
            else:
                stage_b(TILES[ti], A_)
    gnw.free()


def zero_unwritten_outputs(k):
    pass


_CACHE = {}


def _get_program():
    if "nc" not in _CACHE:
        pack, offs, cs, colmask = build_consts()
        offs = dict(offs)
        offs["_tot"] = pack.shape[1]
        _set_shapes(pack.shape[1])
        nc = bass.Bass("TRN2", target_bir_lowering=False)
        _CACHE["k"] = build_program(nc, offs)
        _CACHE["nc"] = nc
        _CACHE["used"] = set(_CACHE["k"].din.keys())
        _CACHE["pack"] = pack
        _CACHE["cs"] = cs
        _CACHE["colmask"] = colmask
    return _CACHE["nc"], _CACHE["pack"], _CACHE["cs"]


def kernel(x_prompt, x_sample, state_ret, state_ssd, state_ssd_conv, state_gdn, state_gdn_conv,
           norm_mix, norm_mlp, norm_final, ret_w_in, ret_w_out,
           ssd_w_in, ssd_conv_w, ssd_conv_b, ssd_dt_bias, ssd_a_log, ssd_d, ssd_norm, ssd_w_out,
           gdn_w_in, gdn_conv_w, gdn_dt_bias, gdn_a_log, gdn_norm, gdn_w_out,
           mlp_w_up, mlp_w_down):
    nc, pack, cs = _get_program()
    f = lambda a: np.ascontiguousarray(np.asarray(a, dtype=np.float32))
    norms = f(np.concatenate([np.asarray(norm_mix), np.asarray(norm_mlp), np.asarray(norm_final)[None, :]], axis=0))
    shared = {
        "norms": norms,
        "ret_w_in": f(ret_w_in), "ret_w_out": f(ret_w_out),
        "ssd_w_in": f(ssd_w_in[0]), "ssd_conv_w": f(ssd_conv_w[0]), "ssd_conv_b": f(ssd_conv_b), "ssd_dt_bias": f(ssd_dt_bias),
        "ssd_a_log": f(ssd_a_log), "ssd_d": f(ssd_d), "ssd_norm": f(ssd_norm), "ssd_w_out": f(ssd_w_out[0]),
        "gdn_w_in": f(gdn_w_in[0]), "gdn_conv_w": f(gdn_conv_w[0]), "gdn_dt_bias": f(gdn_dt_bias), "gdn_a_log": f(gdn_a_log),
        "gdn_norm": f(gdn_norm), "gdn_w_out": f(gdn_w_out[0]),
        "mlp_w_up": f(mlp_w_up), "mlp_w_down": f(mlp_w_down),
        "cpack": pack, "ropecs": cs, "colmask": _CACHE["colmask"],
    }
    xp = np.asarray(x_prompt, dtype=np.float32)
    xs = np.asarray(x_sample, dtype=np.float32)
    in_maps = []
    for c in range(8):
        sl = slice(16 * c, 16 * c + 16)
        m = dict(shared)
        m["x_prompt"] = f(xp[c])
        m["x_sample"] = f(xs[sl].reshape(TS, D))
        m["state_ret"] = f(np.asarray(state_ret)[:, sl])
        m["state_ssd"] = f(np.asarray(state_ssd)[0, sl])
        m["state_ssd_conv"] = f(np.asarray(state_ssd_conv)[0, sl])
        m["state_gdn"] = f(np.asarray(state_gdn)[0, sl])
        m["state_gdn_conv"] = f(np.asarray(state_gdn_conv)[0, sl])
        m = {kk: vv for kk, vv in m.items() if kk in _CACHE["used"]}
        in_maps.append(m)
    ncores = CFG.get("ncores", 8)
    res = run_bass_kernel_spmd(nc, in_maps[:ncores], core_ids=list(range(ncores)))
    R = res.results
    g = lambda nm: [np.asarray(R[c][nm]) if c < ncores else np.zeros_like(np.asarray(R[0][nm])) for c in range(8)]
    y_prompt = np.stack(g("y_prompt"), 0)
    y_sample = np.concatenate(g("y_sample"), 0).reshape(128, 4, D)
    p_ret = np.stack(g("p_ret"), 1)
    p_ssd = np.stack(g("p_ssd"), 0)[None]
    p_ssd_conv = np.stack(g("p_ssd_conv"), 0)[None]
    p_gdn = np.stack(g("p_gdn"), 0)[None]
    p_gdn_conv = np.stack(g("p_gdn_conv"), 0)[None]
    s_ret = np.concatenate(g("s_ret"), 1)
    s_ssd = np.concatenate(g("s_ssd"), 0)[None]
    s_ssd_conv = np.concatenate(g("s_ssd_conv"), 0)[None]
    s_gdn = np.concatenate(g("s_gdn"), 0)[None]
    s_gdn_conv = np.concatenate(g("s_gdn_conv"), 0)[None]
    return (y_prompt, y_sample, p_ret, p_ssd, p_ssd_conv, p_gdn, p_gdn_conv,
            s_ret, s_ssd, s_ssd_conv, s_gdn, s_gdn_conv)
```
